# Optimizing a Trainium2 kernel written in Bass

```python
import jax, jax.numpy as jnp
from jax import lax
import numpy as np

D_MODEL = 1024
BATCH = 16
SEQ = 2048
DEPTH = 4

CHUNK = 64
Q_BLOCK = 128
PLE_DIM = 256
FOX_HEADS = 8
FOX_HEAD_DIM = 64
FOX_WIDTH = FOX_HEADS * FOX_HEAD_DIM
RWKV_HEADS = 8
RWKV_HEAD_DIM = 64
RWKV_WIDTH = RWKV_HEADS * RWKV_HEAD_DIM
DECAY_LORA = 64
AAA_LORA = 64
GATE_LORA = 128
VRES_LORA = 32
D_FF = 2816
CONV_WIDTH = 3
N_BRANCH = 2
FOX_COLS = 3 * FOX_WIDTH + FOX_HEADS
RWKV_COLS = 3 * RWKV_WIDTH + DECAY_LORA + AAA_LORA + GATE_LORA
GATE_COLS = N_BRANCH * D_MODEL
N_IN = FOX_COLS + RWKV_COLS + GATE_COLS
RMS_EPS = 1e-6
GN_EPS = 64e-5
NEG_BIG = -1e30

kernel_name = "hybrid_fox_rwkv7_convffn_ple"


def rmsnorm(x, g):
    xf = x.astype(jnp.float32)
    y = xf * lax.rsqrt(jnp.mean(xf * xf, axis=-1, keepdims=True) + RMS_EPS)
    return (y * g.astype(jnp.float32)).astype(x.dtype)


def token_shift(u):
    return jnp.concatenate([jnp.zeros_like(u[:, :1]), u[:, :-1]], axis=1)


def causal_dwconv(u, w, b):
    k_w = w.shape[0]
    s = u.shape[1]
    up = jnp.pad(u, ((0, 0), (k_w - 1, 0), (0, 0)))
    out = b
    for j in range(k_w):
        out = out + up[:, j:j + s] * w[j]
    return out


def fox_attention(q, k, v, log_f):
    s_len = q.shape[1]
    scale = FOX_HEAD_DIM ** -0.5
    c = jnp.cumsum(log_f.astype(jnp.float32), axis=1).transpose(0, 2, 1)
    outs = []
    for start in range(0, s_len, Q_BLOCK):
        end = start + Q_BLOCK
        qb = q[:, start:end]
        kb = k[:, :end]
        vb = v[:, :end]
        sc = jnp.einsum('bqhd,bkhd->bhqk', qb, kb).astype(jnp.float32) * scale
        bias = c[:, :, start:end, None] - c[:, :, None, :end]
        qpos = start + jnp.arange(Q_BLOCK)
        kpos = jnp.arange(end)
        mask = kpos[None, :] <= qpos[:, None]
        sc = jnp.where(mask, sc + bias, NEG_BIG)
        pr = jax.nn.softmax(sc, axis=-1).astype(v.dtype)
        outs.append(jnp.einsum('bhqk,bkhd->bqhd', pr, vb))
    return jnp.concatenate(outs, axis=1)


def rwkv7_scan(r, w, k, v, a, b):
    bsz, _, nh, n = r.shape

    def step(st, inp):
        r_t, w_t, k_t, v_t, a_t, b_t = inp
        sa = jnp.einsum('bhvk,bhk->bhv', st, a_t)
        st = st * w_t[:, :, None, :] + sa[..., None] * b_t[:, :, None, :] + v_t[..., None] * k_t[:, :, None, :]
        y = jnp.einsum('bhvk,bhk->bhv', st, r_t)
        return st, y

    xs = tuple(jnp.moveaxis(t.astype(jnp.float32), 1, 0) for t in (r, w, k, v, a, b))
    s0 = jnp.zeros((bsz, nh, n, n), jnp.float32)
    _, ys = lax.scan(step, s0, xs)
    return jnp.moveaxis(ys, 0, 1)


def setup_inputs(seed: int = 0) -> dict:
    key = jax.random.key(seed)
    ks = jax.random.split(key, 40)

    def nrm(k, shape, scale):
        return jax.random.normal(k, shape, jnp.float32) * scale

    L = DEPTH
    LV = DEPTH - 1
    return {
        "x": nrm(ks[0], (BATCH, SEQ, D_MODEL), 1.0),
        "p": nrm(ks[1], (DEPTH, BATCH, SEQ, PLE_DIM), 1.0),
        "g_mix": 1.0 + nrm(ks[2], (L, D_MODEL), 0.02),
        "w_in": nrm(ks[3], (L, D_MODEL, N_IN), D_MODEL ** -0.5),
        "b_f": 2.0 + nrm(ks[4], (L, FOX_HEADS), 0.5),
        "g_qnorm": 1.0 + nrm(ks[5], (L, FOX_HEAD_DIM), 0.02),
        "g_knorm": 1.0 + nrm(ks[6], (L, FOX_HEAD_DIM), 0.02),
        "mu_shift": jax.random.uniform(ks[7], (L, RWKV_COLS), jnp.float32),
        "w_decay_up": nrm(ks[8], (L, DECAY_LORA, RWKV_WIDTH), 0.1 * DECAY_LORA ** -0.5),
        "w0": nrm(ks[9], (L, RWKV_WIDTH), 0.5),
        "w_aaa_up": nrm(ks[10], (L, AAA_LORA, RWKV_WIDTH), 0.5 * AAA_LORA ** -0.5),
        "a0": nrm(ks[11], (L, RWKV_WIDTH), 0.1),
        "w_gate_up": nrm(ks[12], (L, GATE_LORA, RWKV_WIDTH), GATE_LORA ** -0.5),
        "k_k": 1.0 + nrm(ks[13], (L, RWKV_WIDTH), 0.1),
        "k_a": 1.0 + nrm(ks[14], (L, RWKV_WIDTH), 0.1),
        "r_k": nrm(ks[15], (L, RWKV_HEADS, RWKV_HEAD_DIM), 0.1),
        "gn_g": 1.0 + nrm(ks[16], (L, RWKV_WIDTH), 0.02),
        "gn_b": nrm(ks[17], (L, RWKV_WIDTH), 0.01),
        "w_vres_down": nrm(ks[18], (LV, D_MODEL, VRES_LORA), D_MODEL ** -0.5),
        "w_vres_up": nrm(ks[19], (LV, VRES_LORA, RWKV_WIDTH), 0.5 * VRES_LORA ** -0.5),
        "v0": nrm(ks[20], (LV, RWKV_WIDTH), 0.1),
        "w_o_fox": nrm(ks[21], (L, FOX_WIDTH, D_MODEL), FOX_WIDTH ** -0.5),
        "w_o_rwkv": nrm(ks[22], (L, RWKV_WIDTH, D_MODEL), RWKV_WIDTH ** -0.5),
        "w_out": nrm(ks[23], (L, D_MODEL, D_MODEL), D_MODEL ** -0.5),
        "g_ffn": 1.0 + nrm(ks[24], (L, D_MODEL), 0.02),
        "w_up": nrm(ks[25], (L, D_MODEL, 2 * D_FF), D_MODEL ** -0.5),
        "conv_w": nrm(ks[26], (L, CONV_WIDTH, 2 * D_FF), CONV_WIDTH ** -0.5),
        "conv_b": nrm(ks[27], (L, 2 * D_FF), 0.01),
        "w_down": nrm(ks[28], (L, D_FF, D_MODEL), D_FF ** -0.5),
        "g_ple": 1.0 + nrm(ks[29], (L, D_MODEL), 0.02),
        "w_ple_gate": nrm(ks[30], (L, D_MODEL, D_MODEL), D_MODEL ** -0.5),
        "w_ple_up": nrm(ks[31], (L, PLE_DIM, D_MODEL), PLE_DIM ** -0.5),
    }


def reference(x, p, g_mix, w_in, b_f, g_qnorm, g_knorm, mu_shift, w_decay_up, w0, w_aaa_up, a0,
              w_gate_up, k_k, k_a, r_k, gn_g, gn_b, w_vres_down, w_vres_up, v0, w_o_fox, w_o_rwkv,
              w_out, g_ffn, w_up, conv_w, conv_b, w_down, g_ple, w_ple_gate, w_ple_up):
    bsz, s_len, _ = x.shape
    v_first = None
    for i in range(DEPTH):
        h = rmsnorm(x, g_mix[i])
        z = h @ w_in[i]
        zf = z[..., :FOX_COLS]
        zr = z[..., FOX_COLS:FOX_COLS + RWKV_COLS]
        zg = z[..., FOX_COLS + RWKV_COLS:]

        q = zf[..., :FOX_WIDTH].reshape(bsz, s_len, FOX_HEADS, FOX_HEAD_DIM)
        k = zf[..., FOX_WIDTH:2 * FOX_WIDTH].reshape(bsz, s_len, FOX_HEADS, FOX_HEAD_DIM)
        v = zf[..., 2 * FOX_WIDTH:3 * FOX_WIDTH].reshape(bsz, s_len, FOX_HEADS, FOX_HEAD_DIM)
        log_f = jax.nn.log_sigmoid(zf[..., 3 * FOX_WIDTH:] + b_f[i])
        q = rmsnorm(q, g_qnorm[i])
        k = rmsnorm(k, g_knorm[i])
        y_fox = fox_attention(q, k, v, log_f).reshape(bsz, s_len, FOX_WIDTH)

        zr = zr + mu_shift[i] * (token_shift(zr) - zr)
        o0 = 0
        r = zr[..., o0:o0 + RWKV_WIDTH]; o0 += RWKV_WIDTH
        kr = zr[..., o0:o0 + RWKV_WIDTH]; o0 += RWKV_WIDTH
        vr = zr[..., o0:o0 + RWKV_WIDTH]; o0 += RWKV_WIDTH
        dw = zr[..., o0:o0 + DECAY_LORA]; o0 += DECAY_LORA
        da = zr[..., o0:o0 + AAA_LORA]; o0 += AAA_LORA
        dg = zr[..., o0:o0 + GATE_LORA]
        w_log = -jax.nn.softplus(-(w0[i] + jnp.tanh(dw) @ w_decay_up[i])) - 0.5
        decay = jnp.exp(-jnp.exp(w_log.astype(jnp.float32)))
        a = jax.nn.sigmoid(a0[i] + da @ w_aaa_up[i])
        g = jax.nn.sigmoid(dg) @ w_gate_up[i]
        kk = (kr * k_k[i]).reshape(bsz, s_len, RWKV_HEADS, RWKV_HEAD_DIM).astype(jnp.float32)
        kk = kk / jnp.maximum(jnp.sqrt(jnp.sum(kk * kk, axis=-1, keepdims=True)), 1e-12)
        kr = kr * (1.0 + (a - 1.0) * k_a[i])
        if i == 0:
            v_first = vr
        else:
            vmix = jax.nn.sigmoid(v0[i - 1] + (h @ w_vres_down[i - 1]) @ w_vres_up[i - 1])
            vr = vr + (v_first - vr) * vmix
        hs = (bsz, s_len, RWKV_HEADS, RWKV_HEAD_DIM)
        rh, kh, vh = r.reshape(hs), kr.reshape(hs), vr.reshape(hs)
        ah = a.reshape(hs).astype(jnp.float32)
        yr = rwkv7_scan(rh, decay.reshape(hs), kh, vh, -kk, kk * ah)
        mu = jnp.mean(yr, axis=-1, keepdims=True)
        var = jnp.mean(jnp.square(yr - mu), axis=-1, keepdims=True)
        yr = ((yr - mu) * lax.rsqrt(var + GN_EPS)).reshape(bsz, s_len, RWKV_WIDTH)
        yr = (yr * gn_g[i] + gn_b[i]).astype(x.dtype).reshape(hs)
        bonus = jnp.sum(rh * kh * r_k[i], axis=-1, keepdims=True) * vh
        y_rwkv = (yr + bonus).reshape(bsz, s_len, RWKV_WIDTH) * g

        gate_fox = jax.nn.sigmoid(zg[..., :D_MODEL])
        gate_rwkv = jax.nn.sigmoid(zg[..., D_MODEL:])
        merged = gate_fox * (y_fox @ w_o_fox[i]) + gate_rwkv * (y_rwkv @ w_o_rwkv[i])
        x = x + merged @ w_out[i]

        h2 = rmsnorm(x, g_ffn[i])
        u = causal_dwconv(h2 @ w_up[i], conv_w[i], conv_b[i])
        x = x + (jax.nn.gelu(u[..., :D_FF], approximate=True) * u[..., D_FF:]) @ w_down[i]

        ple_gate = jax.nn.sigmoid(rmsnorm(x, g_ple[i]) @ w_ple_gate[i])
        x = x + ple_gate * (p[i] @ w_ple_up[i])
    return x
```

```python
import numpy as np
import concourse.bass as bass
import concourse.mybir as mybir
from concourse.bass_utils import run_bass_kernel_spmd

F32 = mybir.dt.float32
BF16 = mybir.dt.bfloat16
AF = mybir.ActivationFunctionType
ALU = mybir.AluOpType
AX = mybir.AxisListType

D = 1024
KC = 8
FOXW = 512
RW = 512
NIN = 5384
DFF = 2816
NJ = 22
PLE = 256
RMS_EPS = 1e-6
GN_EPS = 64e-5
CH = 64
TT = 512


class Res:
    __slots__ = ("w", "rd")

    def __init__(self):
        self.w = None
        self.rd = {}


class Tl:
    def __init__(self, t):
        self.t = t
        self._r = {}

    def R(self, key=None):
        r = self._r.get(key)
        if r is None:
            r = Res()
            self._r[key] = r
        return r

    def __getitem__(self, idx):
        return self.t[idx]


class Sch:
    NDS = 6

    def __init__(self, nc):
        self.nc = nc
        self.e = dict(pe=nc.tensor, act=nc.scalar, dve=nc.vector, pool=nc.gpsimd, sp=nc.sync)
        self.sem = {k: nc.alloc_semaphore(name=f"s_{k}") for k in self.e}
        self.cnt = {k: 0 for k in self.e}
        self.seen = {k: {} for k in self.e}
        self.dsem = {q: [nc.alloc_semaphore(name=f"d_{q}{i}") for i in range(self.NDS)]
                     for q in ("sp", "pool", "act")}
        self.dcnt = {q: 0 for q in self.dsem}
        self.semh = {}
        for k in self.e:
            self.semh[k] = self.sem[k]
        for q in self.dsem:
            for i, h in enumerate(self.dsem[q]):
                self.semh[(q, i)] = h
        self.n_ins = 0

    def _wait(self, E, key, val):
        if self.seen[E].get(key, 0) >= val:
            return
        self.e[E].wait_ge(self.semh[key], val)
        self.seen[E][key] = val
        self.n_ins += 1

    def _collect(self, reads, writes):
        deps = {}

        def add(tok):
            k, v = tok
            if deps.get(k, 0) < v:
                deps[k] = v

        for r in reads:
            if r.w is not None:
                add(r.w)
        for w in writes:
            if w.w is not None:
                add(w.w)
            for k, v in w.rd.items():
                add((k, v))
        return deps

    def _commit(self, tok, reads, writes):
        k, v = tok
        for w in writes:
            w.w = tok
            w.rd = {}
        for r in reads:
            if r.rd.get(k, 0) < v:
                r.rd[k] = v

    def op(self, E, fn, r=(), w=()):
        deps = self._collect(r, w)
        for k, v in deps.items():
            if E == "pe" and k == "pe":
                continue
            self._wait(E, k, v)
        ins = fn(self.e[E])
        self.cnt[E] += 1
        ins.then_inc(self.sem[E], 1)
        self.n_ins += 1
        self._commit((E, self.cnt[E]), r, w)

    def dma(self, q, out, in_, r=(), w=(), **kw):
        deps = self._collect(r, w)
        for k, v in deps.items():
            self._wait(q, k, v)
        n = self.dcnt[q]
        i = n % self.NDS
        gen = n // self.NDS
        if gen > 0:
            self._wait(q, (q, i), 16 * gen)
        ins = self.e[q].dma_start(out=out, in_=in_, **kw)
        ins.then_inc(self.dsem[q][i], 16)
        self.dcnt[q] = n + 1
        self.n_ins += 1
        self._commit(((q, i), 16 * (gen + 1)), r, w)

    def finish(self):
        for q in self.dsem:
            n = self.dcnt[q]
            for i in range(self.NDS):
                cnt_i = (n - i + self.NDS - 1) // self.NDS
                if cnt_i > 0:
                    self._wait("sp", (q, i), 16 * cnt_i)
        for k in self.e:
            if k != "sp" and self.cnt[k] > 0:
                self._wait("sp", k, self.cnt[k])


def _cols(vec):
    v = np.asarray(vec, np.float32).reshape(-1)
    n = (v.size + 127) // 128
    buf = np.zeros(n * 128, np.float32)
    buf[: v.size] = v
    return buf.reshape(n, 128).T


def make_consts():
    c = {}
    c["ident_f"] = np.eye(128, dtype=np.float32)
    c["ones_f"] = np.ones((128, 512), np.float32)
    blk = np.zeros((128, 128), np.float32)
    blk[:64, :64] = 1.0
    blk[64:, 64:] = 1.0
    c["blk2_f"] = blk
    s = np.arange(128)[:, None]
    t = np.arange(128)[None, :]
    c["trimask_f"] = np.where(s > t, -30000.0, 0.0).astype(np.float32)
    t5 = np.arange(512)[None, :]
    c["fullmask"] = np.concatenate([np.where(t5 < r * 128 + s, -30000.0, 0.0).astype(np.float32) for r in range(4)], axis=1)
    sm = np.ones((128, 512), np.float32)
    sm[:, ::CH] = 0.0
    c["scanmask"] = sm
    s64 = np.arange(64)[:, None]
    t64 = np.arange(64)[None, :]
    strict_up = (t64 > s64).astype(np.float32)
    incl_up = (t64 >= s64).astype(np.float32)
    one = np.concatenate([strict_up, incl_up], axis=1)
    c["mask_S"] = np.tile(one, (1, 4))
    strict_lo = (t64 < s64).astype(np.float32)
    c["mask_A"] = np.tile(strict_lo, (1, 8))
    c["ident8"] = np.tile(np.eye(64, dtype=np.float32), (1, 8))
    return c


CONST_SHAPES = {"ident_f": (128, 128), "ones_f": (128, 512), "blk2_f": (128, 128), "trimask_f": (128, 128),
                "scanmask": (128, 512), "fullmask": (128, 2048), "mask_S": (64, 512), "mask_A": (64, 512), "ident8": (64, 512)}

COLS = {}
_o = 0
for _name, _n in [("g_mix", 8), ("g_ffn", 8), ("g_ple", 8), ("g_q", 1), ("g_k", 1), ("b_f", 1), ("mu", 14),
                  ("w0", 4), ("a0", 4), ("k_k", 4), ("k_a", 4), ("r_k", 4), ("gn_g", 4), ("gn_b", 4), ("v0", 4),
                  ("cw0", 44), ("cw1", 44), ("cw2", 44), ("cb", 44)]:
    COLS[_name] = (_o, _n)
    _o += _n
NCOL = _o


def make_colpack(inp, depth):
    pk = np.zeros((128, depth, NCOL), np.float32)

    def put(l, name, arr):
        o, n = COLS[name]
        a = _cols(arr)
        assert a.shape[1] == n, (name, a.shape, n)
        pk[:, l, o:o + n] = a

    for l in range(depth):
        put(l, "g_mix", inp["g_mix"][l])
        put(l, "g_ffn", inp["g_ffn"][l])
        put(l, "g_ple", inp["g_ple"][l])
        put(l, "g_q", np.tile(inp["g_qnorm"][l], 2))
        put(l, "g_k", np.tile(inp["g_knorm"][l], 2))
        put(l, "b_f", inp["b_f"][l])
        put(l, "mu", inp["mu_shift"][l])
        for nm, key in [("w0", "w0"), ("a0", "a0"), ("k_k", "k_k"), ("k_a", "k_a"), ("gn_g", "gn_g"),
                        ("gn_b", "gn_b")]:
            put(l, nm, inp[key][l])
        put(l, "r_k", inp["r_k"][l].reshape(-1))
        if l >= 1:
            put(l, "v0", inp["v0"][l - 1])
        for j in range(3):
            put(l, f"cw{j}", inp["conv_w"][l][j])
        put(l, "cb", inp["conv_b"][l])
    return pk.reshape(128, depth * NCOL)


def in_groups():
    g = []
    for i in range(4):
        g.append(("q", i * 128, 128, i))
    for i in range(4):
        g.append(("k", 512 + i * 128, 128, i))
    g.append(("f", 1536, 8, 0))
    for i in range(14):
        g.append(("rw", 1544 + i * 128, 128, i))
    for i in range(16):
        g.append(("gate", 3336 + i * 128, 128, i))
    return g


class Prog:
    def __init__(self, S, NB, depth, debug=False, stages="ABCDE"):
        self.S, self.NB, self.depth, self.debug, self.stages = S, NB, depth, debug, stages
        self.T = S * NB
        self.NT = self.T // TT
        self.TPS = S // TT
        nc = bass.Bass("TRN2", target_bir_lowering=False)
        self.nc = nc
        self.s = Sch(nc)
        self._ps_i = 0
        self.build()

    def psb_(self, name, shape, dt=F32):
        return Tl(self.nc.alloc_sbuf_tensor(name, list(shape), dt))

    def sb(self, name, shape, dt=F32):
        esz = 2 if dt == BF16 else 4
        nbytes = int(np.prod(shape[1:])) * esz
        nbytes = (nbytes + 63) // 64 * 64
        off = self.arena_off
        assert off + nbytes <= self.arena_end, (name, off, nbytes, self.arena_end)
        self.arena_off = off + nbytes
        self._uid += 1
        return Tl(self.nc.alloc_sbuf_tensor_at(f"{name}_{self._uid}", list(shape), dt, offset=off))

    def stage_begin(self):
        s = self.s
        for E in s.e:
            for k in s.e:
                if s.cnt[k] > 0:
                    s._wait(E, k, s.cnt[k])
            for q in s.dsem:
                n = s.dcnt[q]
                for i in range(s.NDS):
                    cnt_i = (n - i + s.NDS - 1) // s.NDS
                    if cnt_i > 0:
                        s._wait(E, (q, i), 16 * cnt_i)
        self.arena_off = self.arena_start
        self.ep = {}

    def dram(self, name, shape, dt, kind="Internal"):
        if kind == "Internal" and self.debug:
            kind = "ExternalOutput"
        return Tl(self.nc.dram_tensor(name, list(shape), dt, kind=kind))

    def ps(self):
        p = self.psb[self._ps_i % 8]
        self._ps_i += 1
        return p

    def build(self):
        nc, s = self.nc, self.s
        T, L = self.T, self.depth
        self.xT = self.dram("xT", [D, T], F32, kind="ExternalInput")
        self.pT = self.dram("pT", [L, PLE, T], F32, kind="ExternalInput")
        self.out = self.dram("outT", [D, T], F32, kind="ExternalOutput")
        W = {}
        for name, shp in [("w_in", (L, D, NIN)), ("w_decay_up", (L, 64, RW)), ("w_aaa_up", (L, 64, RW)),
                          ("w_gate_up", (L, 128, RW)), ("w_vres_down", (max(L - 1, 1), D, 32)),
                          ("w_vres_up", (max(L - 1, 1), 32, RW)), ("w_o_fox", (L, FOXW, D)),
                          ("w_o_rwkv", (L, RW, D)), ("w_out", (L, D, D)), ("w_up", (L, D, 2 * DFF)),
                          ("w_down", (L, DFF, D)), ("w_ple_gate", (L, D, D)), ("w_ple_up", (L, PLE, D))]:
            W[name] = self.dram(name, shp, F32, kind="ExternalInput")
        self.W = W
        self.colpack_d = self.dram("colpack", [128, L * NCOL], F32, kind="ExternalInput")
        self.const_d = {k: self.dram("c_" + k, list(v), F32, kind="ExternalInput") for k, v in CONST_SHAPES.items()}
        self.xs = self.dram("xs", [D, T], F32)
        self.qa = self.dram("qa", [8, 70, T], BF16)
        self.ka = self.dram("ka", [8, 70, T], BF16)
        self.zr = self.dram("zr", [1792, T], F32)
        self.vf = self.dram("vf", [RW, T], F32)
        self.gt = self.dram("gt", [2 * D, T], BF16)
        self.yf = self.dram("yf", [FOXW, T], BF16)
        self.yr = self.dram("yr", [RW, T], BF16)
        self.vt = self.dram("vt", [T, FOXW], BF16)
        self.actT = self.dram("actT", [DFF, T], BF16)
        self.psb = [Tl(nc.alloc_psum_tensor(f"ps{i}", [128, 512], F32)) for i in range(8)]
        self.colpack = self.psb_("colpack_s", [128, L * NCOL])
        s.dma("sp", self.colpack[:], self.colpack_d[:, :], r=[self.colpack_d.R()], w=[self.colpack.R()])
        self.c = {}
        for k, shp in CONST_SHAPES.items():
            if k in ("fullmask", "trimask_f"):
                continue
            self.c[k] = self.psb_("cs_" + k, shp)
            s.dma("sp", self.c[k][:], self.const_d[k][:, :], r=[self.const_d[k].R()], w=[self.c[k].R()])
        for k in ["ident_f", "blk2_f", "fullmask"]:
            self.c[k + "_b"] = self.psb_("cb_" + k, CONST_SHAPES[k], BF16)
            s.dma("pool", self.c[k + "_b"][:], self.const_d[k][:, :], r=[self.const_d[k].R()],
                  w=[self.c[k + "_b"].R()])
        self.c["ones_b"] = self.psb_("cb_ones", [128, 512], BF16)
        s.dma("pool", self.c["ones_b"][:], self.const_d["ones_f"][:, :], r=[self.const_d["ones_f"].R()],
              w=[self.c["ones_b"].R()])
        self.carry = self.psb_("carry", [128, 16])
        self.ccar = self.psb_("ccar", [8, 2])
        self.ucarry = self.psb_("ucarry", [128, 2 * NJ, 2])
        self.negb = self.psb_("negb", [8, 4])
        for ll in range(L):
            s.op("dve", lambda e, ll=ll: e.tensor_scalar(out=self.negb[:, ll:ll + 1], in0=self.col(ll, "b_f", 0, 0, 8),
                                                          scalar1=-1.0, scalar2=None, op0=ALU.mult),
                 r=[self.colpack.R()], w=[self.negb.R()])
        self._uid = 0
        base0 = int(nc.sbuf_base)
        self.arena_start = (base0 + 63) // 64 * 64
        left = (int(nc.sbuf_bytes_remaining) - 256 - (self.arena_start - base0)) // 64 * 64
        slab = nc.alloc_sbuf_tensor("arena", [128, (left + self.arena_start - base0) // 4], F32)
        self.arena_end = self.arena_start + left
        assert int(nc.sbuf_base) >= self.arena_end, (nc.sbuf_base, self.arena_end)
        self.stage_begin()
        ow = self.sb("ones_wide", [3, T], BF16)
        s.op("dve", lambda e: e.memset(ow[:], 1.0), w=[ow.R()])
        for h in range(8):
            s.dma("sp", self.qa[h, 67:70, :], ow[:], r=[ow.R()], w=[self.qa.R(("ones", h))])
            s.dma("sp", self.ka[h, 64:67, :], ow[:], r=[ow.R()], w=[self.ka.R(("ones", h))])
        for l in range(L):
            src = self.xT if l == 0 else self.xs
            if "A" in self.stages:
                self.stage_A(l, src)
            if "B" in self.stages:
                self.stage_B(l)
            if "C" in self.stages:
                self.stage_C(l)
            if "D" in self.stages:
                self.stage_D(l, src)
            if "E" in self.stages or "1" in self.stages:
                self.stage_E1(l)
            if "E" in self.stages or "2" in self.stages:
                self.stage_E2(l, last=(l == L - 1))
        s.finish()

    def col(self, l, name, j=0, p0=0, p1=128):
        o, n = COLS[name]
        assert j < n
        c0 = l * NCOL + o + j
        return self.colpack[p0:p1, c0:c0 + 1]

    def rmsnorm_tile(self, l, gname, xt, ht, tag):
        s = self.s
        sq = ht
        s.op("act", lambda e: e.activation(out=sq[:], in_=xt[:], func=AF.Square), r=[xt.R()],
             w=[ht.R(kc) for kc in range(KC)])
        ps = self.ps()
        for kc in range(KC):
            s.op("pe", lambda e, kc=kc: e.matmul(ps[:, :], lhsT=self.c["ones_b"][:, 0:128], rhs=sq[:, kc, :],
                                                 start=(kc == 0), stop=(kc == KC - 1)),
                 r=[self.c["ones_b"].R(), ht.R(kc)], w=[ps.R()])
        rs = self.tmpA("rn_rs", [128, TT], nbuf=1)
        self.rsqrt_ps(rs, ps, 1.0 / D, RMS_EPS, 1.0)
        for kc in range(KC):
            eng = "dve" if kc % 2 == 0 else "pool"
            if eng == "dve":
                s.op("dve", lambda e, kc=kc: e.scalar_tensor_tensor(out=ht[:, kc, :], in0=xt[:, kc, :],
                                                                     scalar=self.col(l, gname, kc), in1=rs[:],
                                                                     op0=ALU.mult, op1=ALU.mult),
                     r=[xt.R(), rs.R(), self.colpack.R()], w=[ht.R(kc)])
            else:
                tmp = self.tmpA("rn_nt", [128, TT], nbuf=2)
                s.op("pool", lambda e, kc=kc, tmp=tmp: e.tensor_tensor(out=tmp[:], in0=xt[:, kc, :], in1=rs[:], op=ALU.mult),
                     r=[xt.R(), rs.R()], w=[tmp.R()])
                s.op("pool", lambda e, kc=kc, tmp=tmp: e.tensor_scalar(out=ht[:, kc, :], in0=tmp[:],
                                                               scalar1=self.col(l, gname, kc), scalar2=None,
                                                               op0=ALU.mult),
                     r=[tmp.R(), self.colpack.R()], w=[ht.R(kc)])

    def stage_A(self, l, src):
        nc, s = self.nc, self.s
        T = self.T
        self.stage_begin()
        self.wbig = self.sb("wbig", [128, KC * NIN], BF16)
        self.wbig_R = self.wbig.R()
        self.xt_t = [self.sb("xt0", [128, KC, TT])] * 2
        self.ht_t = [self.sb(f"ht{i}", [128, KC, TT], BF16) for i in range(2)]
        self.zraw = [self.sb(f"zraw{i}", [128, TT + 1]) for i in range(2)]
        W = self.W
        win = self.wbig
        winv = self.wbig.t[:, 0:KC * NIN].rearrange("p (k n) -> p k n", k=KC)
        wsrc = W["w_in"].t[l].rearrange("(k p) n -> p k n", p=128)
        for kc in range(KC):
            s.dma("pool", winv[:, kc, :], wsrc[:, kc, :], r=[W["w_in"].R()], w=[self.wbig_R])
        if l >= 1:
            self.wvd = self.sb("wvd", [128, KC, 32], BF16)
            self.wvu = self.sb("wvu", [32, RW], BF16)
            s.dma("pool", self.wvd[:], W["w_vres_down"].t[l - 1].rearrange("(k p) n -> p k n", p=128),
                  r=[W["w_vres_down"].R()], w=[self.wvd.R()])
            s.dma("pool", self.wvu[:], W["w_vres_up"].t[l - 1], r=[W["w_vres_up"].R()], w=[self.wvu.R()])
        groups = in_groups()
        srcv = src.t.rearrange("(k p) t -> p k t", p=128)
        self.deferred = []

        def run_deferred():
            run, self.deferred = self.deferred, []
            for f in run:
                f()

        def prep_tile(tt):
            xt = self.xt_t[tt % 2]
            ht = self.ht_t[tt % 2]
            for kc in range(KC):
                s.dma("sp", xt[:, kc, :], srcv[:, kc, tt * TT:(tt + 1) * TT], r=[src.R(("tile", tt))], w=[xt.R()])
            self.rmsnorm_tile(l, "g_mix", xt, ht, "A")

        prep_tile(0)
        for tt in range(self.NT):
            t0 = tt * TT
            seq_start = (tt % self.TPS == 0)
            xt = self.xt_t[tt % 2]
            ht = self.ht_t[tt % 2]
            hR = [ht.R(kc) for kc in range(KC)]
            for sub in range(4):
                ps = self.ps()
                for kc in range(KC):
                    s.op("pe", lambda e, kc=kc, sub=sub: e.matmul(ps[:, :], lhsT=ht[:, kc, sub * 128:(sub + 1) * 128],
                                                                   rhs=winv[:, kc, 1024:1536], start=(kc == 0),
                                                                   stop=(kc == KC - 1)),
                         r=[hR[kc], self.wbig_R], w=[ps.R()])
                blk = tt * 4 + sub
                vo = self.tmpA("vo", [128, FOXW], BF16)
                s.op("act", lambda e, vo=vo, ps=ps: e.activation(out=vo[:], in_=ps[:, :], func=AF.Copy),
                     r=[ps.R()], w=[vo.R()])
                s.dma("sp", self.vt.t[blk * 128:(blk + 1) * 128, :], vo[:], r=[vo.R()], w=[self.vt.R(blk)])
            if l >= 1:
                ps = self.ps()
                for kc in range(KC):
                    s.op("pe", lambda e, kc=kc, ps=ps: e.matmul(ps[0:32, :], lhsT=self.wvd[:, kc, :], rhs=ht[:, kc, :],
                                                                 start=(kc == 0), stop=(kc == KC - 1)),
                         r=[hR[kc], self.wvd.R()], w=[ps.R()])
                vd = self.tmpA("vd", [32, TT], BF16)
                s.op("act", lambda e, ps=ps: e.activation(out=vd[:], in_=ps[0:32, :], func=AF.Copy), r=[ps.R()],
                     w=[vd.R()])
            for gi, (kind, c0, wd, idx) in enumerate(groups):
                ps = self.ps()
                for kc in range(KC):
                    s.op("pe", lambda e, kc=kc, ps=ps, c0=c0, wd=wd: e.matmul(ps[0:wd, :], lhsT=winv[:, kc, c0:c0 + wd],
                                                                               rhs=ht[:, kc, :], start=(kc == 0),
                                                                               stop=(kc == KC - 1)),
                         r=[hR[kc], self.wbig_R], w=[ps.R()])
                run_deferred()
                if gi == 24 and tt + 1 < self.NT:
                    prep_tile(tt + 1)
                if kind in ("q", "k"):
                    self.epi_qk(l, kind, idx, ps, t0)
                elif kind == "f":
                    self.epi_f(l, ps, t0, seq_start)
                elif kind == "rw":
                    self.epi_rw(l, idx, ps, t0, tt, seq_start, vd if l >= 1 else None)
                else:
                    self.epi_gate(l, idx, ps, t0)
        run_deferred()

    def rsqrt_ps(self, out, ps, scale, eps, mult, np_=128):
        s = self.s
        if not hasattr(self, "_fconst"):
            self._fconst = {}
        def fc(v):
            if v not in self._fconst:
                t = self.psb_(f"fc{len(self._fconst)}", [128, 1])
                s.op("pool", lambda e: e.memset(t[:], float(v)), w=[t.R()])
                self._fconst[v] = t
            return self._fconst[v]
        be = fc(eps)
        bm = fc(float(np.log(mult)))
        s.op("act", lambda e: e.activation(out=out[0:np_, :], in_=ps[0:np_, :], func=AF.Ln, bias=be[0:np_, :], scale=float(scale)),
             r=[ps.R(), be.R()], w=[out.R()])
        s.op("act", lambda e: e.activation(out=out[0:np_, :], in_=out[0:np_, :], func=AF.Exp, bias=bm[0:np_, :], scale=-0.5),
             r=[out.R(), bm.R()], w=[out.R()])

    def ps_rot(self, lo, hi):
        key = (lo, hi)
        if not hasattr(self, "_psr"):
            self._psr = {}
        i = self._psr.get(key, 0)
        self._psr[key] = i + 1
        return self.psb[lo + i % (hi - lo)]

    def stage_B(self, l):
        s = self.s
        S, NB = self.S, self.NB
        self.stage_begin()
        QA = [self.sb(f"QA{i}", [70, S], BF16) for i in range(2)]
        KA = [self.sb(f"KA{i}", [70, S], BF16) for i in range(2)]
        Vb = [self.sb(f"Vb{i}", [128, S // 128, FOXW], BF16) for i in range(2)]
        PT = [self.sb(f"PT{i}", [128, TT], BF16) for i in range(4)]
        rden = [self.sb(f"rden{i}", [64, TT]) for i in range(2)]
        yt = [self.sb(f"yt{i}", [64, TT], BF16) for i in range(2)]
        fm = self.c["fullmask_b"]
        idb = self.c["ident_f_b"]
        onb = self.c["ones_b"]
        NQ = S // TT
        LOOK = 3
        groups = [(b, h) for b in range(NB) for h in range(8)]
        bufs = {}

        def load_group(gi):
            b, h = groups[gi]
            qa, ka = QA[gi % 2], KA[gi % 2]
            if h == 0:
                s.dma("sp", Vb[b % 2][:], self.vt.t[b * S:(b + 1) * S, :].rearrange("(c p) n -> p c n", p=128),
                      r=[self.vt.R(blk) for blk in range(b * S // 128, (b + 1) * S // 128)], w=[Vb[b % 2].R()])
            s.dma("sp", qa[:], self.qa.t[h, :, b * S:(b + 1) * S],
                  r=[self.qa.R(("ones", h))] + [self.qa.R(("qk", h, b * S + j * TT)) for j in range(NQ)] +
                    [self.qa.R(("c", jj, b * S + j * TT)) for j in range(NQ) for jj in range(3)], w=[qa.R()])
            s.dma("sp", ka[:], self.ka.t[h, :, b * S:(b + 1) * S],
                  r=[self.ka.R(("ones", h))] + [self.ka.R(("qk", h, b * S + j * TT)) for j in range(NQ)] +
                    [self.ka.R(("c", jj, b * S + j * TT)) for j in range(NQ) for jj in range(3)], w=[ka.R()])

        work = []
        for gi, (b, h) in enumerate(groups):
            for j in range(NQ):
                nch = 4 * (j + 1)
                for i in range(nch):
                    work.append((gi, b, h, j, i, nch))
        state = {}

        def emit_qk(w):
            gi, b, h, j, i, nch = w
            if j == 0 and i == 0:
                if gi == 0:
                    load_group(0)
                if gi + 1 < len(groups):
                    load_group(gi + 1)
            qa, ka = QA[gi % 2], KA[gi % 2]
            sc = self.ps_rot(4, 8)
            state[w] = sc
            r_ = i - 4 * j
            diag = r_ >= 0
            s.op("pe", lambda e: e.matmul(sc[:, :], lhsT=ka[:, i * 128:(i + 1) * 128], rhs=qa[:, j * TT:(j + 1) * TT], start=True,
                                          stop=not diag), r=[ka.R(), qa.R()], w=[sc.R()])
            if diag:
                s.op("pe", lambda e: e.matmul(sc[:, :], lhsT=idb[:], rhs=fm[:, r_ * 512:(r_ + 1) * 512], start=False, stop=True),
                     r=[idb.R(), fm.R()], w=[sc.R()])

        cnt = {"pt": 0, "acc": None, "den": None, "ep": 0}

        def emit_rest(w):
            gi, b, h, j, i, nch = w
            sc = state.pop(w)
            if i == 0:
                cnt["acc"] = self.ps_rot(0, 2)
                cnt["den"] = self.ps_rot(2, 4)
            acc, den = cnt["acc"], cnt["den"]
            pt = PT[cnt["pt"] % 4]
            cnt["pt"] += 1
            vb = Vb[b % 2]
            s.op("act", lambda e: e.activation(out=pt[:], in_=sc[:, :], func=AF.Exp), r=[sc.R()], w=[pt.R()])
            s.op("pe", lambda e: e.matmul(acc[0:64, :], lhsT=vb[:, i, h * 64:(h + 1) * 64], rhs=pt[:], start=(i == 0),
                                          stop=(i == nch - 1)), r=[vb.R(), pt.R()], w=[acc.R()])
            s.op("pe", lambda e: e.matmul(den[0:64, :], lhsT=onb[:, 0:64], rhs=pt[:], start=(i == 0), stop=(i == nch - 1)),
                 r=[onb.R(), pt.R()], w=[den.R()])
            if i == nch - 1:
                rd = rden[cnt["ep"] % 2]
                y = yt[cnt["ep"] % 2]
                cnt["ep"] += 1
                s.op("dve", lambda e: e.reciprocal(out=rd[:], in_=den[0:64, :]), r=[den.R()], w=[rd.R()])
                s.op("dve", lambda e: e.tensor_tensor(out=y[:], in0=acc[0:64, :], in1=rd[:], op=ALU.mult), r=[acc.R(), rd.R()],
                     w=[y.R()])
                t0 = b * S + j * TT
                s.dma("sp", self.yf.t[h * 64:(h + 1) * 64, t0:t0 + TT], y[:], r=[y.R()], w=[self.yf.R((h, t0))])

        for k in range(min(LOOK, len(work))):
            emit_qk(work[k])
        for k, w in enumerate(work):
            if k + LOOK < len(work):
                emit_qk(work[k + LOOK])
            emit_rest(w)

    def stage_C(self, l):
        import os
        cut = int(os.environ.get("CCUT", "9"))
        use_r = os.environ.get("NOF32R") is None
        RD = mybir.dt.float32r if use_r else F32
        s = self.s
        S, NB, T = self.S, self.NB, self.T
        W = self.W
        self.stage_begin()
        c = self.c
        idf, blkf, scanm, mS, mA, id8 = c["ident_f"], c["blk2_f"], c["scanmask"], c["mask_S"], c["mask_A"], c["ident8"]

        def V(E, fn, r, w):
            s.op(E, fn, r=[x.R() for x in r], w=[x.R() for x in w])

        def cp(E, out_ap, in_ap, r, w):
            if E == "act":
                V("act", lambda e: e.activation(out=out_ap, in_=in_ap, func=AF.Copy), r, w)
            else:
                V(E, lambda e: e.tensor_copy(out=out_ap, in_=in_ap), r, w)

        Wd = self.sb("Wd", [64, RW], BF16)
        Wa = self.sb("Wa", [64, RW], BF16)
        Wg = self.sb("Wg", [128, RW], BF16)
        s.dma("pool", Wd[:], W["w_decay_up"].t[l], r=[W["w_decay_up"].R()], w=[Wd.R()])
        s.dma("pool", Wa[:], W["w_aaa_up"].t[l], r=[W["w_aaa_up"].R()], w=[Wa.R()])
        s.dma("pool", Wg[:], W["w_gate_up"].t[l], r=[W["w_gate_up"].R()], w=[Wg.R()])
        omka = self.sb("omka", [128, 4])
        o_ka = l * NCOL + COLS["k_a"][0]
        V("dve", lambda e: e.tensor_scalar(out=omka[:], in0=self.colpack[:, o_ka:o_ka + 4], scalar1=-1.0, scalar2=1.0,
                                            op0=ALU.mult, op1=ALU.add), [self.colpack], [omka])
        epsg = self.sb("epsg", [64, 1])
        V("pool", lambda e: e.memset(epsg[:], GN_EPS), [], [epsg])
        AR = [self.sb(f"AR{i}", [128, 8, 2, CH], RD) for i in range(4)]
        BT = [self.sb(f"BT{i}", [128, TT], RD) for i in range(4)]
        KT = [self.sb(f"KT{i}", [128, TT], RD) for i in range(4)]
        ARo = [self.sb(f"ARo{i}", [64, 8, 2, CH], RD) for i in range(4)]
        BTo = [self.sb(f"BTo{i}", [64, TT], RD) for i in range(4)]
        KTo = [self.sb(f"KTo{i}", [64, TT], RD) for i in range(4)]
        VR = [self.sb(f"VR{i}", [128, TT]) for i in range(4)]
        G = [self.sb(f"G{i}", [128, TT], BF16) for i in range(4)]
        BG = [self.sb(f"BG{i}", [128, TT], BF16) for i in range(4)]
        PCp = self.sb("PCp", [128, 8]); PCo = self.sb("PCo", [64, 8])
        PCall = self.sb("PCall", [64, 8, 8])
        H = self.sb("H", [64, 512], RD)
        idr = self.sb("idr", [64, 64], RD)
        V("dve", lambda e: e.tensor_copy(out=idr[:], in_=idf[0:64, 0:64]), [idf], [idr])
        YN = self.sb("YN", [64, 8, 512])
        dwt = self.sb("dwt", [64, TT]); dat = self.sb("dat", [64, TT]); dgt = self.sb("dgt", [128, TT])
        tdw = self.sb("tdw", [64, TT], BF16); dab = self.sb("dab", [64, TT], BF16); sdg = self.sb("sdg", [128, TT], BF16)
        rT = self.sb("rT", [128, TT]); krT = self.sb("krT", [128, TT])
        sig = self.sb("sig", [128, TT]); aa = self.sb("aa", [128, TT]); kk = self.sb("kk", [128, TT])
        prod = self.sb("prod", [128, TT]); rn = self.sb("rn", [128, TT]); gf = self.sb("gf", [128, TT])
        Lc = self.sb("Lc", [128, TT])
        eL = self.sb("eL", [128, TT]); eLm = self.sb("eLm", [128, TT]); enL = self.sb("enL", [128, TT])
        Btok = self.sb("Btok", [64, 512], RD); Ktok = self.sb("Ktok", [64, 512], RD); Vtok = self.sb("Vtok", [64, 512], RD)
        SM = [self.sb(f"SM{i}", [64, 512], RD) for i in range(4)]
        Xa = [self.sb(f"Xa{i}", [64, 512], RD) for i in range(2)]
        XTa = [self.sb(f"XTa{i}", [64, 512], RD) for i in range(2)]
        TTa = [self.sb(f"TTa{i}", [64, 512], RD) for i in range(2)]
        W0s = self.sb("W0s", [64, 512], RD); Us = self.sb("Us", [64, 512], RD)
        YQ = self.sb("YQ", [64, 8, 512])
        st = {k: self.sb("st_" + k, [64, 64]) for k in ["sum", "sq", "m", "m2", "var", "rstd"]}
        po1 = self.sb("po1", [128, TT]); pob = self.sb("pob", [128, TT], BF16)

        def rr(ap):
            return ap

        def MM(e, out, lhsT, rhs, start, stop):
            return e.matmul(out, lhsT=rr(lhsT), rhs=rr(rhs), start=start, stop=stop)

        def colv(name, hp):
            return self.col(l, name, hp)

        def ar(h):
            return AR[h // 2] if h % 2 == 0 else ARo[h // 2]

        def bt(h):
            return BT[h // 2] if h % 2 == 0 else BTo[h // 2]

        def kt(h):
            return KT[h // 2] if h % 2 == 0 else KTo[h // 2]

        for tt in range(self.NT):
            t0 = tt * TT
            zr = self.zr
            s.dma("sp", dwt[:], zr.t[1536:1600, t0:t0 + TT], r=[zr.R((12, t0))], w=[dwt.R()])
            s.dma("sp", dat[:], zr.t[1600:1664, t0:t0 + TT], r=[zr.R((12, t0))], w=[dat.R()])
            s.dma("sp", dgt[:], zr.t[1664:1792, t0:t0 + TT], r=[zr.R((13, t0))], w=[dgt.R()])
            V("act", lambda e: e.activation(out=tdw[:], in_=dwt[:], func=AF.Tanh), [dwt], [tdw])
            V("pool", lambda e: e.tensor_copy(out=dab[:], in_=dat[:]), [dat], [dab])
            V("act", lambda e: e.activation(out=sdg[:], in_=dgt[:], func=AF.Sigmoid), [dgt], [sdg])
            for hp in range(4):
                hs = slice(hp * 128, (hp + 1) * 128)
                s.dma("sp", rT[:], zr.t[hp * 128:(hp + 1) * 128, t0:t0 + TT], r=[zr.R((hp, t0))], w=[rT.R()])
                s.dma("sp", krT[:], zr.t[512 + hp * 128:512 + (hp + 1) * 128, t0:t0 + TT], r=[zr.R((4 + hp, t0))], w=[krT.R()])
                s.dma("sp", VR[hp][:], zr.t[1024 + hp * 128:1024 + (hp + 1) * 128, t0:t0 + TT], r=[zr.R((8 + hp, t0))],
                      w=[VR[hp].R()])
                p1 = self.ps()
                V("pe", lambda e: e.matmul(p1[:, :], lhsT=Wd[:, hs], rhs=tdw[:], start=True, stop=True), [Wd, tdw], [p1])
                V("act", lambda e: e.activation(out=sig[:], in_=p1[:, :], func=AF.Sigmoid, bias=colv("w0", hp), scale=1.0),
                  [p1, self.colpack], [sig])
                p2 = self.ps()
                V("pe", lambda e: e.matmul(p2[:, :], lhsT=Wa[:, hs], rhs=dab[:], start=True, stop=True), [Wa, dab], [p2])
                V("act", lambda e: e.activation(out=aa[:], in_=p2[:, :], func=AF.Sigmoid, bias=colv("a0", hp), scale=1.0),
                  [p2, self.colpack], [aa])
                p3 = self.ps()
                V("pe", lambda e: e.matmul(p3[:, :], lhsT=Wg[:, hs], rhs=sdg[:], start=True, stop=True), [Wg, sdg], [p3])
                cp("act", gf[:], p3[:, :], [p3], [gf])
                cp("pool", G[hp][:], gf[:], [gf], [G[hp]])
                V("act", lambda e: e.activation(out=kk[:], in_=krT[:], func=AF.Copy, scale=colv("k_k", hp)),
                  [krT, self.colpack], [kk])
                V("pool", lambda e: e.tensor_tensor(out=prod[:], in0=kk[:], in1=kk[:], op=ALU.mult), [kk], [prod])
                p4 = self.ps()
                V("pe", lambda e: e.matmul(p4[:, :], lhsT=blkf[:], rhs=prod[:], start=True, stop=True), [blkf, prod], [p4])
                self.rsqrt_ps(rn, p4, 1.0, 1e-24, 1.0)
                V("dve", lambda e: e.tensor_tensor(out=kk[:], in0=kk[:], in1=rn[:], op=ALU.mult), [kk, rn], [kk])
                V("dve", lambda e: e.tensor_scalar(out=rn[:], in0=aa[:], scalar1=colv("k_a", hp), scalar2=omka[:, hp:hp + 1],
                                                    op0=ALU.mult, op1=ALU.add), [aa, self.colpack, omka], [rn])
                V("dve", lambda e: e.tensor_tensor(out=krT[:], in0=krT[:], in1=rn[:], op=ALU.mult), [krT, rn], [krT])
                V("pool", lambda e: e.tensor_tensor(out=aa[:], in0=kk[:], in1=aa[:], op=ALU.mult), [kk, aa], [aa])
                V("act", lambda e: e.activation(out=sig[:], in_=sig[:], func=AF.Copy, scale=-float(np.exp(-0.5))),
                  [sig], [sig])
                V("dve", lambda e: e.tensor_tensor_scan(out=Lc[:], data0=scanm[:], data1=sig[:], initial=0.0, op0=ALU.mult,
                                                         op1=ALU.add), [scanm, sig], [Lc])
                V("pool", lambda e: e.tensor_tensor(out=sig[:], in0=Lc[:], in1=sig[:], op=ALU.subtract), [Lc, sig], [sig])
                V("act", lambda e: e.activation(out=eL[:], in_=Lc[:], func=AF.Exp), [Lc], [eL])
                V("act", lambda e: e.activation(out=eLm[:], in_=sig[:], func=AF.Exp), [sig], [eLm])
                V("act", lambda e: e.activation(out=enL[:], in_=Lc[:], func=AF.Exp, scale=-1.0), [Lc], [enL])
                arv = AR[hp]
                V("dve", lambda e: e.scalar_tensor_tensor(out=arv[:, :, 0, :], in0=kk[:].rearrange("p (c t) -> p c t", t=CH),
                                                           scalar=-1.0, in1=eLm[:].rearrange("p (c t) -> p c t", t=CH),
                                                           op0=ALU.mult, op1=ALU.mult), [kk, eLm], [arv])
                V("dve", lambda e: e.tensor_tensor(out=arv[:, :, 1, :], in0=rT[:].rearrange("p (c t) -> p c t", t=CH),
                                                    in1=eL[:].rearrange("p (c t) -> p c t", t=CH), op=ALU.mult), [rT, eL], [arv])
                V("pool", lambda e: e.tensor_tensor(out=BT[hp][:], in0=aa[:], in1=enL[:], op=ALU.mult), [aa, enL], [BT[hp]])
                V("dve", lambda e: e.tensor_tensor(out=KT[hp][:], in0=krT[:], in1=enL[:], op=ALU.mult), [krT, enL], [KT[hp]])
                V("pool", lambda e: e.tensor_copy(out=PCp[:], in_=eL[:, CH - 1::CH]), [eL], [PCp])
                s.dma("sp", ARo[hp][:], AR[hp][64:128, :, :, :], r=[AR[hp].R()], w=[ARo[hp].R()])
                s.dma("sp", BTo[hp][:], BT[hp][64:128, :], r=[BT[hp].R()], w=[BTo[hp].R()])
                s.dma("sp", KTo[hp][:], KT[hp][64:128, :], r=[KT[hp].R()], w=[KTo[hp].R()])
                s.dma("sp", PCo[:], PCp[64:128, :], r=[PCp.R()], w=[PCo.R()])
                V("pool", lambda e: e.tensor_copy(out=PCall[:, :, 2 * hp], in_=PCp[0:64, :]), [PCp], [PCall])
                V("pool", lambda e: e.tensor_copy(out=PCall[:, :, 2 * hp + 1], in_=PCo[:]), [PCo], [PCall])
                V("dve", lambda e: e.scalar_tensor_tensor(out=prod[:], in0=rT[:], scalar=colv("r_k", hp), in1=krT[:],
                                                           op0=ALU.mult, op1=ALU.mult), [rT, krT, self.colpack], [prod])
                p5 = self.ps()
                V("pe", lambda e: e.matmul(p5[:, :], lhsT=blkf[:], rhs=prod[:], start=True, stop=True), [blkf, prod], [p5])
                V("dve", lambda e: e.tensor_tensor(out=rn[:], in0=p5[:, :], in1=VR[hp][:], op=ALU.mult), [p5, VR[hp]], [rn])
                V("pool", lambda e: e.tensor_tensor(out=BG[hp][:], in0=rn[:], in1=gf[:], op=ALU.mult), [rn, gf], [BG[hp]])
            for cc in range(8 if cut >= 1 else 0):
                cs = slice(cc * CH, (cc + 1) * CH)
                if tt % self.TPS == 0 and cc == 0:
                    V("pool", lambda e: e.memset(H[:].bitcast(F32), 0.0), [], [H])
                for srcs, dst, eng in [(BT, Btok, "act"), (KT, Ktok, "dve"), (VR, Vtok, "act")]:
                    pt_ = self.ps()
                    for hp in range(4):
                        V("pe", lambda e, hp=hp: e.transpose(out=pt_[0:64, hp * 128:(hp + 1) * 128], in_=srcs[hp][:, cs].bitcast(F32),
                                                             identity=idf[:]), [srcs[hp], idf], [pt_])
                    cp(eng, dst[:], pt_[0:64, :], [pt_], [dst])
                if cut < 2:
                    continue
                for hp in range(4):
                    pS = self.ps()
                    for par in range(2):
                        h = 2 * hp + par
                        rhs = ar(h)[0:64, cc, :, :].rearrange("p a t -> p (a t)")
                        V("pe", lambda e, par=par, h=h, rhs=rhs: MM(e, pS[0:64, par * 256:par * 256 + 128], lhsT=bt(h)[0:64, cs],
                                                                         rhs=rhs, start=True, stop=True), [bt(h), ar(h)], [pS])
                        V("pe", lambda e, par=par, h=h, rhs=rhs: MM(e, pS[0:64, par * 256 + 128:par * 256 + 256],
                                                                         lhsT=kt(h)[0:64, cs], rhs=rhs, start=True, stop=True),
                          [kt(h), ar(h)], [pS])
                    V("dve", lambda e, hp=hp, pS=pS: e.tensor_tensor(out=SM[hp][:], in0=pS[0:64, :], in1=mS[:], op=ALU.mult),
                      [pS, mS], [SM[hp]])
                pA = self.ps()
                for h in range(8):
                    V("pe", lambda e, h=h: MM(e, pA[0:64, h * 64:(h + 1) * 64], lhsT=ar(h)[0:64, cc, 0, :],
                                                    rhs=bt(h)[0:64, cs], start=True, stop=True), [ar(h), bt(h)], [pA])
                X, XT, Tt = Xa[0], XTa[0], TTa[0]
                V("dve", lambda e: e.tensor_tensor(out=X[:], in0=pA[0:64, :], in1=mA[:], op=ALU.mult), [pA, mA], [X])
                for hp in range(4):
                    V("pool", lambda e, hp=hp: e.tensor_copy(
                        out=XT[:, hp * 128:(hp + 1) * 128].rearrange("p (a t) -> p a t", a=2),
                        in_=SM[hp][:, :].rearrange("p (a t) -> p a t", a=2)[:, :, 0:64]), [SM[hp]], [XT])
                V("pool", lambda e: e.tensor_tensor(out=Tt[:], in0=XT[:], in1=id8[:], op=ALU.add), [XT, id8], [Tt])
                if cut < 3:
                    continue
                for k in range(1, 6):
                    Xn, XTn, Tn = Xa[k % 2], XTa[k % 2], TTa[k % 2]
                    pX = self.ps()
                    for h in range(8):
                        hsl = slice(h * 64, (h + 1) * 64)
                        V("pe", lambda e, hsl=hsl: MM(e, pX[0:64, hsl], lhsT=XT[:, hsl], rhs=X[:, hsl], start=True, stop=True),
                          [XT, X], [pX])
                    if k < 5:
                        pXT = self.ps()
                        for h in range(8):
                            hsl = slice(h * 64, (h + 1) * 64)
                            V("pe", lambda e, hsl=hsl: MM(e, pXT[0:64, hsl], lhsT=X[:, hsl], rhs=XT[:, hsl], start=True,
                                                                 stop=True), [XT, X], [pXT])
                    cp("act", Xn[:], pX[0:64, :], [pX], [Xn])
                    if k < 5:
                        cp("dve", XTn[:], pXT[0:64, :], [pXT], [XTn])
                    pT = self.ps()
                    for h in range(8):
                        hsl = slice(h * 64, (h + 1) * 64)
                        V("pe", lambda e, hsl=hsl: MM(e, pT[0:64, hsl], lhsT=Xn[:, hsl], rhs=Tt[:, hsl], start=True, stop=True),
                          [Xn, Tt], [pT])
                    V("dve", lambda e: e.tensor_tensor(out=Tn[:], in0=pT[0:64, :], in1=Tt[:], op=ALU.add), [pT, Tt], [Tn])
                    X, XT, Tt = Xn, XTn, Tn
                if cut < 4:
                    continue
                def hd(h):
                    return h // 2, (h % 2) * 256, slice(h * 64, (h + 1) * 64)
                pW = self.ps()
                for h in range(8):
                    hp, b0, hsl = hd(h)
                    V("pe", lambda e, hp=hp, b0=b0, hsl=hsl: MM(e, pW[0:64, hsl], lhsT=SM[hp][:, b0 + 128:b0 + 192],
                                                                     rhs=Vtok[:, hsl], start=True, stop=False), [SM[hp], Vtok], [pW])
                    V("pe", lambda e, h=h, hsl=hsl: MM(e, pW[0:64, hsl], lhsT=ar(h)[0:64, cc, 0, :], rhs=H[:, hsl], start=False,
                                                            stop=True), [ar(h), H], [pW])
                cp("act", W0s[:], pW[0:64, :], [pW], [W0s])
                pU = self.ps()
                for h in range(8):
                    hp, b0, hsl = hd(h)
                    V("pe", lambda e, hsl=hsl: MM(e, pU[0:64, hsl], lhsT=Tt[:, hsl], rhs=W0s[:, hsl], start=True, stop=True),
                      [Tt, W0s], [pU])
                cp("dve", Us[:], pU[0:64, :], [pU], [Us])
                pY = self.ps()
                for h in range(8):
                    hp, b0, hsl = hd(h)
                    V("pe", lambda e, hp=hp, b0=b0, hsl=hsl: MM(e, pY[0:64, hsl], lhsT=SM[hp][:, b0 + 192:b0 + 256],
                                                                     rhs=Vtok[:, hsl], start=True, stop=False), [SM[hp], Vtok], [pY])
                    V("pe", lambda e, hp=hp, b0=b0, hsl=hsl: MM(e, pY[0:64, hsl], lhsT=SM[hp][:, b0 + 64:b0 + 128],
                                                                     rhs=Us[:, hsl], start=False, stop=False), [SM[hp], Us], [pY])
                    V("pe", lambda e, h=h, hsl=hsl: MM(e, pY[0:64, hsl], lhsT=ar(h)[0:64, cc, 1, :], rhs=H[:, hsl], start=False,
                                                            stop=True), [ar(h), H], [pY])
                pH = self.ps()
                for h in range(8):
                    hp, b0, hsl = hd(h)
                    V("pe", lambda e, hsl=hsl: MM(e, pH[0:64, hsl], lhsT=Btok[:, hsl], rhs=Us[:, hsl], start=True, stop=False),
                      [Btok, Us], [pH])
                    V("pe", lambda e, hsl=hsl: MM(e, pH[0:64, hsl], lhsT=Ktok[:, hsl], rhs=Vtok[:, hsl], start=False, stop=False),
                      [Ktok, Vtok], [pH])
                    V("pe", lambda e, hsl=hsl: MM(e, pH[0:64, hsl], lhsT=idr[:], rhs=H[:, hsl], start=False, stop=True),
                      [idr, H], [pH])
                V("dve", lambda e: e.tensor_tensor(out=H[:].rearrange("p (h v) -> p h v", h=8),
                                                    in0=pH[0:64, :].rearrange("p (h v) -> p h v", h=8),
                                                    in1=PCall[:, cc, :].unsqueeze(2).broadcast_to([64, 8, 64]), op=ALU.mult),
                  [pH, PCall], [H])
                cp("act", YN[:, cc, :], pY[0:64, :], [pY], [YN])
            if cut >= 5:
                yr3 = YN[:].rearrange("p c (h v) -> p (c h) v", h=8)
                yq3 = YQ[:].rearrange("p c (h v) -> p (c h) v", h=8)
                V("act", lambda e: e.activation(out=YQ[:], in_=YN[:], func=AF.Square), [YN], [YQ])
                V("dve", lambda e: e.tensor_reduce(out=st["sum"][:], in_=yr3, axis=AX.X, op=ALU.add), [YN], [st["sum"]])
                V("dve", lambda e: e.tensor_reduce(out=st["sq"][:], in_=yq3, axis=AX.X, op=ALU.add), [YQ], [st["sq"]])
                V("pool", lambda e: e.tensor_scalar(out=st["m"][:], in0=st["sum"][:], scalar1=1.0 / 64, scalar2=None, op0=ALU.mult),
                  [st["sum"]], [st["m"]])
                V("pool", lambda e: e.tensor_tensor(out=st["m2"][:], in0=st["m"][:], in1=st["m"][:], op=ALU.mult), [st["m"]],
                  [st["m2"]])
                V("dve", lambda e: e.scalar_tensor_tensor(out=st["var"][:], in0=st["sq"][:], scalar=1.0 / 64, in1=st["m2"][:],
                                                           op0=ALU.mult, op1=ALU.subtract), [st["sq"], st["m2"]], [st["var"]])
                V("act", lambda e: e.activation(out=st["rstd"][:], in_=st["var"][:], func=AF.Ln, bias=epsg[:], scale=1.0),
                  [st["var"], epsg], [st["rstd"]])
                V("act", lambda e: e.activation(out=st["rstd"][:], in_=st["rstd"][:], func=AF.Exp, scale=-0.5), [st["rstd"]],
                  [st["rstd"]])
                V("dve", lambda e: e.tensor_tensor(out=yq3, in0=yr3, in1=st["m"][:].unsqueeze(2).broadcast_to([64, 64, 64]),
                                                    op=ALU.subtract), [YN, st["m"]], [YQ])
                V("dve", lambda e: e.tensor_tensor(out=yr3, in0=yq3, in1=st["rstd"][:].unsqueeze(2).broadcast_to([64, 64, 64]),
                                                    op=ALU.mult), [YQ, st["rstd"]], [YN])
            for hp in range(4 if cut >= 6 else 0):
                pO = self.ps()
                for cc in range(8):
                    V("pe", lambda e, cc=cc: e.transpose(out=pO[:, cc * CH:(cc + 1) * CH], in_=YN[:, cc, hp * 128:(hp + 1) * 128],
                                                         identity=idf[0:64, 0:64]), [YN, idf], [pO])
                V("dve", lambda e: e.tensor_scalar(out=po1[:], in0=pO[:, :], scalar1=colv("gn_g", hp), scalar2=colv("gn_b", hp),
                                                    op0=ALU.mult, op1=ALU.add), [pO, self.colpack], [po1])
                V("pool", lambda e: e.tensor_tensor(out=po1[:], in0=po1[:], in1=G[hp][:], op=ALU.mult), [po1, G[hp]], [po1])
                V("dve", lambda e: e.tensor_tensor(out=pob[:], in0=po1[:], in1=BG[hp][:], op=ALU.add), [po1, BG[hp]], [pob])
                s.dma("sp", self.yr.t[hp * 128:(hp + 1) * 128, t0:t0 + TT], pob[:], r=[pob.R()], w=[self.yr.R((hp, t0))])

    def load_w(self, dst, name, l, nk):
        Wt = self.W[name]
        srcv = Wt.t[l].rearrange("(k p) n -> p k n", p=128)
        for kc in range(nk):
            self.s.dma("pool", dst[:, kc, :], srcv[:, kc, :], r=[Wt.R()], w=[dst.R()])

    def stage_D(self, l, src):
        s = self.s
        self.stage_begin()
        wof = self.sb("wof", [128, 4, D], BF16)
        wor = self.sb("wor", [128, 4, D], BF16)
        wout = self.sb("wout", [128, KC, D], BF16)
        self.load_w(wof, "w_o_fox", l, 4)
        self.load_w(wor, "w_o_rwkv", l, 4)
        self.load_w(wout, "w_out", l, KC)
        yfT = [self.sb(f"yfT{i}", [128, 4, TT], BF16) for i in range(2)]
        yrT = [self.sb(f"yrT{i}", [128, 4, TT], BF16) for i in range(2)]
        gtT = [self.sb(f"gtT{i}", [128, 16, TT], BF16) for i in range(2)]
        xt_ = [self.sb(f"xD{i}", [128, KC, TT]) for i in range(2)]
        mg = self.sb("mg", [128, KC, TT], BF16)
        srcv = src.t.rearrange("(k p) t -> p k t", p=128)
        dstv = self.xs.t.rearrange("(k p) t -> p k t", p=128)
        for tt in range(self.NT):
            t0 = tt * TT
            yf, yr, gt, xt = yfT[tt % 2], yrT[tt % 2], gtT[tt % 2], xt_[tt % 2]
            s.dma("sp", yf[:], self.yf.t[:, t0:t0 + TT].rearrange("(k p) t -> p k t", p=128),
                  r=[self.yf.R((h, t0)) for h in range(8)], w=[yf.R()])
            s.dma("sp", yr[:], self.yr.t[:, t0:t0 + TT].rearrange("(k p) t -> p k t", p=128),
                  r=[self.yr.R((hp, t0)) for hp in range(4)], w=[yr.R()])
            s.dma("sp", gt[:], self.gt.t[:, t0:t0 + TT].rearrange("(k p) t -> p k t", p=128),
                  r=[self.gt.R((i, t0)) for i in range(16)], w=[gt.R()])
            for kc in range(KC):
                s.dma("sp", xt[:, kc, :], srcv[:, kc, t0:t0 + TT], r=[src.R(("tile", tt))], w=[xt.R()])
            for n in range(KC):
                ns = slice(n * 128, (n + 1) * 128)
                pa = self.ps()
                for kc in range(4):
                    s.op("pe", lambda e, kc=kc: e.matmul(pa[:, :], lhsT=wof[:, kc, ns], rhs=yf[:, kc, :], start=(kc == 0),
                                                         stop=(kc == 3)), r=[wof.R(), yf.R()], w=[pa.R()])
                pb = self.ps()
                for kc in range(4):
                    s.op("pe", lambda e, kc=kc: e.matmul(pb[:, :], lhsT=wor[:, kc, ns], rhs=yr[:, kc, :], start=(kc == 0),
                                                         stop=(kc == 3)), r=[wor.R(), yr.R()], w=[pb.R()])
                m1 = self.tmpA("m1", [128, TT])
                s.op("dve", lambda e, m1=m1: e.tensor_tensor(out=m1[:], in0=pa[:, :], in1=gt[:, n, :], op=ALU.mult),
                     r=[pa.R(), gt.R()], w=[m1.R()])
                m2 = self.tmpA("m2", [128, TT])
                s.op("dve", lambda e, m2=m2: e.tensor_tensor(out=m2[:], in0=pb[:, :], in1=gt[:, 8 + n, :], op=ALU.mult),
                     r=[pb.R(), gt.R()], w=[m2.R()])
                s.op("pool", lambda e, m1=m1, m2=m2: e.tensor_tensor(out=mg[:, n, :], in0=m1[:], in1=m2[:], op=ALU.add),
                     r=[m1.R(), m2.R()], w=[mg.R(n)])
            for n in range(KC):
                ns = slice(n * 128, (n + 1) * 128)
                po = self.ps()
                for kc in range(KC):
                    s.op("pe", lambda e, kc=kc: e.matmul(po[:, :], lhsT=wout[:, kc, ns], rhs=mg[:, kc, :], start=(kc == 0),
                                                         stop=(kc == KC - 1)), r=[wout.R(), mg.R(kc)], w=[po.R()])
                xo = self.tmpA("xo", [128, TT], nbuf=3)
                s.op("dve", lambda e, xo=xo: e.tensor_tensor(out=xo[:], in0=po[:, :], in1=xt[:, n, :], op=ALU.add),
                     r=[po.R(), xt.R()], w=[xo.R()])
                s.dma("sp", dstv[:, n, t0:t0 + TT], xo[:], r=[xo.R()], w=[self.xs.R(("tile", tt))])

    def stage_E1(self, l):
        s = self.s
        self.stage_begin()
        wup = self.sb("wup", [128, KC, 2 * DFF], BF16)
        self.load_w(wup, "w_up", l, KC)
        xt = self.sb("xE", [128, KC, TT])
        h2 = [self.sb(f"h2{i}", [128, KC, TT], BF16) for i in range(2)]
        ub = [self.sb(f"ub{i}", [128, TT + 2]) for i in range(2)]
        srcv = self.xs.t.rearrange("(k p) t -> p k t", p=128)
        K0 = float(2.0 * np.sqrt(2.0 / np.pi))
        def prep_tile(tt):
            for kc in range(KC):
                s.dma("sp", xt[:, kc, :], srcv[:, kc, tt * TT:(tt + 1) * TT], r=[self.xs.R(("tile", tt))], w=[xt.R()])
            self.rmsnorm_tile(l, "g_ffn", xt, h2[tt % 2], "E")

        prep_tile(0)
        for tt in range(self.NT):
            t0 = tt * TT
            seq_start = (tt % self.TPS == 0)
            ht = h2[tt % 2]
            hR = [ht.R(kc) for kc in range(KC)]
            for j in range(NJ):
                if j == 10 and tt + 1 < self.NT:
                    prep_tile(tt + 1)
                cres = []
                for half in range(2):
                    idx = half * NJ + j
                    c0 = idx * 128
                    pu = self.ps()
                    for kc in range(KC):
                        s.op("pe", lambda e, kc=kc, pu=pu, c0=c0: e.matmul(pu[:, :], lhsT=wup[:, kc, c0:c0 + 128], rhs=ht[:, kc, :],
                                                                           start=(kc == 0), stop=(kc == KC - 1)),
                             r=[hR[kc], wup.R()], w=[pu.R()])
                    u = ub[half]
                    s.op("act", lambda e, u=u, pu=pu: e.activation(out=u[:, 2:TT + 2], in_=pu[:, :], func=AF.Copy), r=[pu.R()],
                         w=[u.R()])
                    if seq_start:
                        s.op("pool", lambda e, u=u: e.memset(u[:, 0:2], 0.0), w=[u.R()])
                    else:
                        s.op("pool", lambda e, u=u, idx=idx: e.tensor_copy(out=u[:, 0:2], in_=self.ucarry[:, idx, :]),
                             r=[self.ucarry.R(idx)], w=[u.R()])
                    s.op("pool", lambda e, u=u, idx=idx: e.tensor_copy(out=self.ucarry[:, idx, :], in_=u[:, TT:TT + 2]),
                         r=[u.R()], w=[self.ucarry.R(idx)])
                    c1 = self.tmpA(f"c1{half}", [128, TT], nbuf=1)
                    s.op("dve", lambda e, u=u, c1=c1, idx=idx: e.tensor_scalar(out=c1[:], in0=u[:, 2:TT + 2],
                                                                               scalar1=self.col(l, "cw2", idx),
                                                                               scalar2=self.col(l, "cb", idx), op0=ALU.mult,
                                                                               op1=ALU.add), r=[u.R(), self.colpack.R()], w=[c1.R()])
                    c2 = self.tmpA(f"c2{half}", [128, TT], nbuf=1)
                    s.op("dve", lambda e, u=u, c1=c1, c2=c2, idx=idx: e.scalar_tensor_tensor(out=c2[:], in0=u[:, 1:TT + 1],
                                                                                           scalar=self.col(l, "cw1", idx), in1=c1[:],
                                                                                           op0=ALU.mult, op1=ALU.add),
                         r=[u.R(), c1.R(), self.colpack.R()], w=[c2.R()])
                    c3 = self.tmpA(f"c3{half}", [128, TT], nbuf=2)
                    s.op("dve", lambda e, u=u, c2=c2, c3=c3, idx=idx: e.scalar_tensor_tensor(out=c3[:], in0=u[:, 0:TT],
                                                                                           scalar=self.col(l, "cw0", idx), in1=c2[:],
                                                                                           op0=ALU.mult, op1=ALU.add),
                         r=[u.R(), c2.R(), self.colpack.R()], w=[c3.R()])
                    cres.append(c3)
                u1, u2 = cres
                sg = self.tmpA("gsg", [128, TT], nbuf=2)
                s.op("act", lambda e, sg=sg, u1=u1: e.activation(out=sg[:], in_=u1[:], func=AF.Gelu_apprx_tanh), r=[u1.R()],
                     w=[sg.R()])
                ao = self.tmpA("gao", [128, TT], BF16, nbuf=3)
                s.op("dve", lambda e, sg=sg, u2=u2, ao=ao: e.tensor_tensor(out=ao[:], in0=sg[:], in1=u2[:], op=ALU.mult),
                     r=[sg.R(), u2.R()], w=[ao.R()])
                s.dma("sp", self.actT.t[j * 128:(j + 1) * 128, t0:t0 + TT], ao[:], r=[ao.R()], w=[self.actT.R((j, t0))])

    def stage_E2(self, l, last):
        s = self.s
        self.stage_begin()
        wdn = self.sb("wdn", [128, NJ, D], BF16)
        wpg = self.sb("wpg", [128, KC, D], BF16)
        wpu = self.sb("wpu", [128, 2, D], BF16)
        self.load_w(wdn, "w_down", l, NJ)
        self.load_w(wpg, "w_ple_gate", l, KC)
        self.load_w(wpu, "w_ple_up", l, 2)
        at_ = [self.sb(f"at{i}", [128, NJ, TT], BF16) for i in range(2)]
        pt_ = [self.sb(f"pp{i}", [128, 2, TT], BF16) for i in range(2)]
        x2_ = [self.sb(f"x2{i}", [128, KC, TT]) for i in range(2)]
        h3_ = [self.sb(f"h3{i}", [128, KC, TT], BF16) for i in range(2)]
        srcv = self.xs.t.rearrange("(k p) t -> p k t", p=128)
        dst = self.out if last else self.xs
        dstv = dst.t.rearrange("(k p) t -> p k t", p=128)
        pv = self.pT.t[l].rearrange("(k p) t -> p k t", p=128)

        def down(tt):
            t0 = tt * TT
            at, pp, x2, h3 = at_[tt % 2], pt_[tt % 2], x2_[tt % 2], h3_[tt % 2]
            s.dma("sp", at[:], self.actT.t[:, t0:t0 + TT].rearrange("(k p) t -> p k t", p=128),
                  r=[self.actT.R((j, t0)) for j in range(NJ)], w=[at.R()])
            s.dma("pool", pp[:], pv[:, :, t0:t0 + TT], r=[self.pT.R()], w=[pp.R()])
            for kc in range(KC):
                s.dma("sp", x2[:, kc, :], srcv[:, kc, t0:t0 + TT], r=[self.xs.R(("tile", tt))], w=[x2.R()])
            for n in range(KC):
                ns = slice(n * 128, (n + 1) * 128)
                pd = self.ps()
                for j in range(NJ):
                    s.op("pe", lambda e, j=j: e.matmul(pd[:, :], lhsT=wdn[:, j, ns], rhs=at[:, j, :], start=(j == 0),
                                                       stop=(j == NJ - 1)), r=[wdn.R(), at.R()], w=[pd.R()])
                s.op("dve", lambda e: e.tensor_tensor(out=x2[:, n, :], in0=pd[:, :], in1=x2[:, n, :], op=ALU.add),
                     r=[pd.R(), x2.R()], w=[x2.R()])
            self.rmsnorm_tile(l, "g_ple", x2, h3, "F")

        def ple(tt):
            t0 = tt * TT
            pp, x2, h3 = pt_[tt % 2], x2_[tt % 2], h3_[tt % 2]
            for n in range(KC):
                ns = slice(n * 128, (n + 1) * 128)
                pg = self.ps()
                for kc in range(KC):
                    s.op("pe", lambda e, kc=kc: e.matmul(pg[:, :], lhsT=wpg[:, kc, ns], rhs=h3[:, kc, :], start=(kc == 0),
                                                         stop=(kc == KC - 1)), r=[wpg.R(), h3.R(kc)], w=[pg.R()])
                sgt = self.tmpA("sgt", [128, TT])
                s.op("act", lambda e, sgt=sgt: e.activation(out=sgt[:], in_=pg[:, :], func=AF.Sigmoid), r=[pg.R()], w=[sgt.R()])
                pq = self.ps()
                for kc in range(2):
                    s.op("pe", lambda e, kc=kc: e.matmul(pq[:, :], lhsT=wpu[:, kc, ns], rhs=pp[:, kc, :], start=(kc == 0),
                                                         stop=(kc == 1)), r=[wpu.R(), pp.R()], w=[pq.R()])
                s.op("dve", lambda e, sgt=sgt: e.tensor_tensor(out=sgt[:], in0=pq[:, :], in1=sgt[:], op=ALU.mult),
                     r=[pq.R(), sgt.R()], w=[sgt.R()])
                xo = self.tmpA("xo2", [128, TT], nbuf=2)
                s.op("pool", lambda e, sgt=sgt, xo=xo: e.tensor_tensor(out=xo[:], in0=sgt[:], in1=x2[:, n, :], op=ALU.add),
                     r=[sgt.R(), x2.R()], w=[xo.R()])
                s.dma("sp", dstv[:, n, t0:t0 + TT], xo[:], r=[xo.R()], w=[dst.R(("tile", tt))])

        down(0)
        for tt in range(self.NT):
            if tt + 1 < self.NT:
                down(tt + 1)
            ple(tt)

    def tmpA(self, name, shape, dt=F32, nbuf=2):
        key = ("tmp", name)
        if key not in self.ep:
            self.ep[key] = [[self.sb(f"tA_{name}{i}", shape, dt) for i in range(nbuf)], 0]
        lst = self.ep[key]
        t = lst[0][lst[1] % nbuf]
        lst[1] += 1
        return t

    def epi_qk(self, l, kind, idx, ps, t0):
        s = self.s
        sq = self.tmpA("qsq", [128, TT], BF16)
        s.op("act", lambda e: e.activation(out=sq[:], in_=ps[:, :], func=AF.Square), r=[ps.R()], w=[sq.R()])
        self.deferred.append(lambda: self.epi_qk2(l, kind, idx, ps, t0, sq))

    def epi_qk2(self, l, kind, idx, ps, t0, sq):
        s = self.s
        ps2 = self.ps()
        s.op("pe", lambda e: e.matmul(ps2[:, :], lhsT=self.c["blk2_f_b"][:], rhs=sq[:], start=True, stop=True),
             r=[self.c["blk2_f_b"].R(), sq.R()], w=[ps2.R()])
        rs2 = self.tmpA("qrs2", [128, TT], nbuf=1)
        self.rsqrt_ps(rs2, ps2, 1.0 / 64, RMS_EPS, 0.125 if kind == "q" else 1.0)
        o = self.tmpA("qo", [128, TT], BF16)
        gname = "g_q" if kind == "q" else "g_k"
        s.op("dve", lambda e: e.scalar_tensor_tensor(out=o[:], in0=ps[:, :], scalar=self.col(l, gname), in1=rs2[:],
                                                      op0=ALU.mult, op1=ALU.mult),
             r=[ps.R(), rs2.R(), self.colpack.R()], w=[o.R()])
        dst = self.qa if kind == "q" else self.ka
        for hh in range(2):
            h = idx * 2 + hh
            s.dma("sp", dst[h, 0:64, t0:t0 + TT], o[hh * 64:(hh + 1) * 64, :], r=[o.R()], w=[dst.R(("qk", h, t0))])

    def epi_f(self, l, ps, t0, seq_start):
        s = self.s
        e1 = self.tmpA("fe", [8, TT], nbuf=1)
        s.op("act", lambda e: e.activation(out=e1[:], in_=ps[0:8, :], func=AF.Exp, bias=self.negb[:, l:l + 1], scale=-1.0),
             r=[ps.R(), self.negb.R()], w=[e1.R()])
        l1 = e1
        one = self.c["ones_f"]
        s.op("act", lambda e: e.activation(out=l1[:], in_=e1[:], func=AF.Ln, bias=one[0:8, 0:1], scale=1.0),
             r=[e1.R(), one.R()], w=[l1.R()])
        c = self.tmpA("fc", [8, TT], nbuf=1)
        if seq_start:
            init = 0.0
            rr = []
        else:
            init = self.ccar[:, 0:1]
            rr = [self.ccar.R()]
        s.op("dve", lambda e: e.tensor_tensor_scan(out=c[:], data0=self.c["ones_f"][0:8, :], data1=l1[:], initial=init,
                                                    op0=ALU.mult, op1=ALU.subtract),
             r=[self.c["ones_f"].R(), l1.R()] + rr, w=[c.R()])
        s.op("dve", lambda e: e.tensor_copy(out=self.ccar[:, 0:1], in_=c[:, TT - 1:TT]), r=[c.R()], w=[self.ccar.R()])
        hi = self.tmpA("fhi", [8, TT], BF16, nbuf=1)
        s.op("dve", lambda e: e.tensor_copy(out=hi[:], in_=c[:]), r=[c.R()], w=[hi.R()])
        r1 = self.tmpA("fr1", [8, TT], nbuf=1)
        s.op("dve", lambda e: e.tensor_tensor(out=r1[:], in0=c[:], in1=hi[:], op=ALU.subtract), r=[c.R(), hi.R()],
             w=[r1.R()])
        mid = self.tmpA("fmid", [8, TT], BF16, nbuf=1)
        s.op("dve", lambda e: e.tensor_copy(out=mid[:], in_=r1[:]), r=[r1.R()], w=[mid.R()])
        r2 = self.tmpA("fe", [8, TT], nbuf=1)
        s.op("dve", lambda e: e.tensor_tensor(out=r2[:], in0=r1[:], in1=mid[:], op=ALU.subtract), r=[r1.R(), mid.R()],
             w=[r2.R()])
        lo = self.tmpA("flo", [8, TT], BF16, nbuf=1)
        s.op("dve", lambda e: e.tensor_copy(out=lo[:], in_=r2[:]), r=[r2.R()], w=[lo.R()])
        for j, part in enumerate([hi, mid, lo]):
            s.dma("sp", self.qa.t[:, 64 + j, t0:t0 + TT], part[:], r=[part.R()], w=[self.qa.R(("c", j, t0))])
            ng = self.tmpA("fng", [8, TT], BF16, nbuf=1)
            s.op("pool", lambda e, ng=ng, part=part: e.tensor_scalar(out=ng[:], in0=part[:], scalar1=-1.0, scalar2=None,
                                                                      op0=ALU.mult), r=[part.R()], w=[ng.R()])
            s.dma("sp", self.ka.t[:, 67 + j, t0:t0 + TT], ng[:], r=[ng.R()], w=[self.ka.R(("c", j, t0))])

    def epi_rw(self, l, idx, ps, t0, tt, seq_start, vd):
        s = self.s
        zb = self.zraw[idx % 2]
        s.op("act", lambda e: e.activation(out=zb[:, 1:TT + 1], in_=ps[:, :], func=AF.Copy), r=[ps.R()], w=[zb.R()])
        if seq_start:
            s.op("pool", lambda e: e.memset(zb[:, 0:1], 0.0), w=[zb.R()])
        else:
            s.op("pool", lambda e: e.tensor_copy(out=zb[:, 0:1], in_=self.carry[:, idx:idx + 1]),
                 r=[self.carry.R(idx)], w=[zb.R()])
        s.op("pool", lambda e: e.tensor_copy(out=self.carry[:, idx:idx + 1], in_=zb[:, TT:TT + 1]), r=[zb.R()],
             w=[self.carry.R(idx)])
        d = self.tmpA("rwd", [128, TT])
        s.op("pool", lambda e: e.tensor_tensor(out=d[:], in0=zb[:, 0:TT], in1=zb[:, 1:TT + 1], op=ALU.subtract),
             r=[zb.R()], w=[d.R()])
        o = self.tmpA("rwo", [128, TT], nbuf=2)
        s.op("dve", lambda e: e.scalar_tensor_tensor(out=o[:], in0=d[:], scalar=self.col(l, "mu", idx), in1=zb[:, 1:TT + 1],
                                                      op0=ALU.mult, op1=ALU.add),
             r=[d.R(), zb.R(), self.colpack.R()], w=[o.R()])
        if 8 <= idx < 12:
            vi = idx - 8
            if l == 0:
                s.dma("sp", self.vf.t[vi * 128:(vi + 1) * 128, t0:t0 + TT], o[:], r=[o.R()], w=[self.vf.R((vi, t0))])
            else:
                ps2 = self.ps()
                s.op("pe", lambda e: e.matmul(ps2[:, :], lhsT=self.wvu[:, vi * 128:(vi + 1) * 128], rhs=vd[:], start=True,
                                              stop=True), r=[self.wvu.R(), vd.R()], w=[ps2.R()])
                vm = self.tmpA("vm", [128, TT], nbuf=1)
                s.op("act", lambda e: e.activation(out=vm[:], in_=ps2[:, :], func=AF.Sigmoid, bias=self.col(l, "v0", vi),
                                                   scale=1.0), r=[ps2.R(), self.colpack.R()], w=[vm.R()])
                vfl = self.tmpA("vfl", [128, TT], nbuf=1)
                s.dma("sp", vfl[:], self.vf.t[vi * 128:(vi + 1) * 128, t0:t0 + TT], r=[self.vf.R((vi, t0))], w=[vfl.R()])
                dd = self.tmpA("vdd", [128, TT], nbuf=1)
                s.op("pool", lambda e: e.tensor_tensor(out=dd[:], in0=vfl[:], in1=o[:], op=ALU.subtract),
                     r=[vfl.R(), o.R()], w=[dd.R()])
                s.op("pool", lambda e: e.tensor_tensor(out=dd[:], in0=dd[:], in1=vm[:], op=ALU.mult), r=[dd.R(), vm.R()],
                     w=[dd.R()])
                o2 = self.tmpA("rwo2", [128, TT], nbuf=1)
                s.op("dve", lambda e: e.tensor_tensor(out=o2[:], in0=o[:], in1=dd[:], op=ALU.add), r=[o.R(), dd.R()],
                     w=[o2.R()])
                o = o2
        s.dma("sp", self.zr.t[idx * 128:(idx + 1) * 128, t0:t0 + TT], o[:], r=[o.R()], w=[self.zr.R((idx, t0))])

    def epi_gate(self, l, idx, ps, t0):
        s = self.s
        o = self.tmpA("go", [128, TT], BF16, nbuf=3)
        s.op("act", lambda e: e.activation(out=o[:], in_=ps[:, :], func=AF.Sigmoid), r=[ps.R()], w=[o.R()])
        s.dma("sp", self.gt.t[idx * 128:(idx + 1) * 128, t0:t0 + TT], o[:], r=[o.R()], w=[self.gt.R((idx, t0))])


def host_inputs(inp, S, NB, depth, core):
    b0 = core * NB
    x = np.asarray(inp["x"], np.float32)[b0:b0 + NB].reshape(NB * S, D)
    p = np.asarray(inp["p"], np.float32)[:depth, b0:b0 + NB].reshape(depth, NB * S, PLE)
    m = {"xT": np.ascontiguousarray(x.T), "pT": np.ascontiguousarray(p.transpose(0, 2, 1))}
    return m


def shared_inputs(inp, depth):
    m = {}
    for name in ["w_in", "w_decay_up", "w_aaa_up", "w_gate_up", "w_o_fox", "w_o_rwkv", "w_out", "w_up", "w_down",
                 "w_ple_gate", "w_ple_up"]:
        m[name] = np.ascontiguousarray(np.asarray(inp[name], np.float32)[:depth])
    for name in ["w_vres_down", "w_vres_up"]:
        m[name] = np.ascontiguousarray(np.asarray(inp[name], np.float32)[:max(depth - 1, 1)])
    m["colpack"] = make_colpack({k: np.asarray(v, np.float32) for k, v in inp.items()}, depth)
    for k, v in make_consts().items():
        m["c_" + k] = v
    return m


_PROG_CACHE = {}


def kernel(**inp):
    S, NBT, depth = 2048, 16, 4
    ncores = 8
    NB = NBT // ncores
    key = (S, NB, depth)
    if key not in _PROG_CACHE:
        _PROG_CACHE[key] = Prog(S, NB, depth)
    prog = _PROG_CACHE[key]
    sh = shared_inputs(inp, depth)
    in_maps = []
    for c in range(ncores):
        m = dict(sh)
        m.update(host_inputs(inp, S, NB, depth, c))
        in_maps.append(m)
    res = run_bass_kernel_spmd(prog.nc, in_maps, core_ids=list(range(ncores)))
    outs = []
    for c in range(ncores):
        o = res.results[c]["outT"]
        outs.append(np.ascontiguousarray(o.T).reshape(NB, S, D))
    return np.concatenate(outs, axis=0).astype(np.float32)
```

```python
import numpy as np
import concourse.bass as bass
import concourse.mybir as mybir
from concourse.bass_utils import run_bass_kernel_spmd

F32 = mybir.dt.float32
BF16 = mybir.dt.bfloat16
AF = mybir.ActivationFunctionType
ALU = mybir.AluOpType
AX = mybir.AxisListType

D = 1024
KC = 8
FOXW = 512
RW = 512
NIN = 5384
DFF = 2816
NJ = 22
PLE = 256
RMS_EPS = 1e-6
GN_EPS = 64e-5
CH = 64
TT = 512


class Res:
    __slots__ = ("w", "rd")

    def __init__(self):
        self.w = None
        self.rd = {}


class Tl:
    def __init__(self, t):
        self.t = t
        self._r = {}

    def R(self, key=None):
        r = self._r.get(key)
        if r is None:
            r = Res()
            self._r[key] = r
        return r

    def __getitem__(self, idx):
        return self.t[idx]


class Sch:
    NDS = 6

    def __init__(self, nc):
        self.nc = nc
        self.e = dict(pe=nc.tensor, act=nc.scalar, dve=nc.vector, pool=nc.gpsimd, sp=nc.sync)
        self.sem = {k: nc.alloc_semaphore(name=f"s_{k}") for k in self.e}
        self.cnt = {k: 0 for k in self.e}
        self.seen = {k: {} for k in self.e}
        self.dsem = {q: [nc.alloc_semaphore(name=f"d_{q}{i}") for i in range(self.NDS)]
                     for q in ("sp", "pool", "act")}
        self.dcnt = {q: 0 for q in self.dsem}
        self.semh = {}
        for k in self.e:
            self.semh[k] = self.sem[k]
        for q in self.dsem:
            for i, h in enumerate(self.dsem[q]):
                self.semh[(q, i)] = h
        self.n_ins = 0

    def _wait(self, E, key, val):
        if self.seen[E].get(key, 0) >= val:
            return
        self.e[E].wait_ge(self.semh[key], val)
        self.seen[E][key] = val
        self.n_ins += 1

    def _collect(self, reads, writes):
        deps = {}

        def add(tok):
            k, v = tok
            if deps.get(k, 0) < v:
                deps[k] = v

        for r in reads:
            if r.w is not None:
                add(r.w)
        for w in writes:
            if w.w is not None:
                add(w.w)
            for k, v in w.rd.items():
                add((k, v))
        return deps

    def _commit(self, tok, reads, writes):
        k, v = tok
        for w in writes:
            w.w = tok
            w.rd = {}
        for r in reads:
            if r.rd.get(k, 0) < v:
                r.rd[k] = v

    def op(self, E, fn, r=(), w=()):
        deps = self._collect(r, w)
        for k, v in deps.items():
            if E == "pe" and k == "pe":
                continue
            self._wait(E, k, v)
        ins = fn(self.e[E])
        self.cnt[E] += 1
        ins.then_inc(self.sem[E], 1)
        self.n_ins += 1
        self._commit((E, self.cnt[E]), r, w)

    def dma(self, q, out, in_, r=(), w=(), **kw):
        deps = self._collect(r, w)
        for k, v in deps.items():
            self._wait(q, k, v)
        n = self.dcnt[q]
        i = n % self.NDS
        gen = n // self.NDS
        if gen > 0:
            self._wait(q, (q, i), 16 * gen)
        ins = self.e[q].dma_start(out=out, in_=in_, **kw)
        ins.then_inc(self.dsem[q][i], 16)
        self.dcnt[q] = n + 1
        self.n_ins += 1
        self._commit(((q, i), 16 * (gen + 1)), r, w)

    def finish(self):
        for q in self.dsem:
            n = self.dcnt[q]
            for i in range(self.NDS):
                cnt_i = (n - i + self.NDS - 1) // self.NDS
                if cnt_i > 0:
                    self._wait("sp", (q, i), 16 * cnt_i)
        for k in self.e:
            if k != "sp" and self.cnt[k] > 0:
                self._wait("sp", k, self.cnt[k])


def _cols(vec):
    v = np.asarray(vec, np.float32).reshape(-1)
    n = (v.size + 127) // 128
    buf = np.zeros(n * 128, np.float32)
    buf[: v.size] = v
    return buf.reshape(n, 128).T


def make_consts():
    c = {}
    c["ident_f"] = np.eye(128, dtype=np.float32)
    c["ones_f"] = np.ones((128, 512), np.float32)
    blk = np.zeros((128, 128), np.float32)
    blk[:64, :64] = 1.0
    blk[64:, 64:] = 1.0
    c["blk2_f"] = blk
    s = np.arange(128)[:, None]
    t = np.arange(128)[None, :]
    c["trimask_f"] = np.where(s > t, -30000.0, 0.0).astype(np.float32)
    t5 = np.arange(512)[None, :]
    c["fullmask"] = np.concatenate([np.where(t5 < r * 128 + s, -30000.0, 0.0).astype(np.float32) for r in range(4)], axis=1)
    sm = np.ones((128, 512), np.float32)
    sm[:, ::CH] = 0.0
    c["scanmask"] = sm
    s64 = np.arange(64)[:, None]
    t64 = np.arange(64)[None, :]
    strict_up = (t64 > s64).astype(np.float32)
    incl_up = (t64 >= s64).astype(np.float32)
    one = np.concatenate([strict_up, incl_up], axis=1)
    c["mask_S"] = np.tile(one, (1, 4))
    strict_lo = (t64 < s64).astype(np.float32)
    c["mask_A"] = np.tile(strict_lo, (1, 8))
    c["ident8"] = np.tile(np.eye(64, dtype=np.float32), (1, 8))
    return c


CONST_SHAPES = {"ident_f": (128, 128), "ones_f": (128, 512), "blk2_f": (128, 128), "trimask_f": (128, 128),
                "scanmask": (128, 512), "fullmask": (128, 2048), "mask_S": (64, 512), "mask_A": (64, 512), "ident8": (64, 512)}

COLS = {}
_o = 0
for _name, _n in [("g_mix", 8), ("g_ffn", 8), ("g_ple", 8), ("g_q", 1), ("g_k", 1), ("b_f", 1), ("mu", 14),
                  ("w0", 4), ("a0", 4), ("k_k", 4), ("k_a", 4), ("r_k", 4), ("gn_g", 4), ("gn_b", 4), ("v0", 4),
                  ("cw0", 44), ("cw1", 44), ("cw2", 44), ("cb", 44)]:
    COLS[_name] = (_o, _n)
    _o += _n
NCOL = _o


def make_colpack(inp, depth):
    pk = np.zeros((128, depth, NCOL), np.float32)

    def put(l, name, arr):
        o, n = COLS[name]
        a = _cols(arr)
        assert a.shape[1] == n, (name, a.shape, n)
        pk[:, l, o:o + n] = a

    for l in range(depth):
        put(l, "g_mix", inp["g_mix"][l])
        put(l, "g_ffn", inp["g_ffn"][l])
        put(l, "g_ple", inp["g_ple"][l])
        put(l, "g_q", np.tile(inp["g_qnorm"][l], 2))
        put(l, "g_k", np.tile(inp["g_knorm"][l], 2))
        put(l, "b_f", inp["b_f"][l])
        put(l, "mu", inp["mu_shift"][l])
        for nm, key in [("w0", "w0"), ("a0", "a0"), ("k_k", "k_k"), ("k_a", "k_a"), ("gn_g", "gn_g"),
                        ("gn_b", "gn_b")]:
            put(l, nm, inp[key][l])
        put(l, "r_k", inp["r_k"][l].reshape(-1))
        if l >= 1:
            put(l, "v0", inp["v0"][l - 1])
        for j in range(3):
            put(l, f"cw{j}", inp["conv_w"][l][j])
        put(l, "cb", inp["conv_b"][l])
    return pk.reshape(128, depth * NCOL)


def in_groups():
    g = []
    for i in range(4):
        g.append(("q", i * 128, 128, i))
    for i in range(4):
        g.append(("k", 512 + i * 128, 128, i))
    g.append(("f", 1536, 8, 0))
    for i in range(14):
        g.append(("rw", 1544 + i * 128, 128, i))
    for i in range(16):
        g.append(("gate", 3336 + i * 128, 128, i))
    return g


class Prog:
    def __init__(self, S, NB, depth, debug=False, stages="ABCDE"):
        self.S, self.NB, self.depth, self.debug, self.stages = S, NB, depth, debug, stages
        self.T = S * NB
        self.NT = self.T // TT
        self.TPS = S // TT
        nc = bass.Bass("TRN2", target_bir_lowering=False)
        self.nc = nc
        self.s = Sch(nc)
        self._ps_i = 0
        self.build()

    def psb_(self, name, shape, dt=F32):
        return Tl(self.nc.alloc_sbuf_tensor(name, list(shape), dt))

    def sb(self, name, shape, dt=F32):
        esz = 2 if dt == BF16 else 4
        nbytes = int(np.prod(shape[1:])) * esz
        nbytes = (nbytes + 63) // 64 * 64
        off = self.arena_off
        assert off + nbytes <= self.arena_end, (name, off, nbytes, self.arena_end)
        self.arena_off = off + nbytes
        self._uid += 1
        return Tl(self.nc.alloc_sbuf_tensor_at(f"{name}_{self._uid}", list(shape), dt, offset=off))

    def stage_begin(self):
        s = self.s
        for E in s.e:
            for k in s.e:
                if s.cnt[k] > 0:
                    s._wait(E, k, s.cnt[k])
            for q in s.dsem:
                n = s.dcnt[q]
                for i in range(s.NDS):
                    cnt_i = (n - i + s.NDS - 1) // s.NDS
                    if cnt_i > 0:
                        s._wait(E, (q, i), 16 * cnt_i)
        self.arena_off = self.arena_start
        self.ep = {}

    def dram(self, name, shape, dt, kind="Internal"):
        if kind == "Internal" and self.debug:
            kind = "ExternalOutput"
        return Tl(self.nc.dram_tensor(name, list(shape), dt, kind=kind))

    def ps(self):
        p = self.psb[self._ps_i % 8]
        self._ps_i += 1
        return p

    def build(self):
        nc, s = self.nc, self.s
        T, L = self.T, self.depth
        self.xT = self.dram("xT", [D, T], F32, kind="ExternalInput")
        self.pT = self.dram("pT", [L, PLE, T], F32, kind="ExternalInput")
        self.out = self.dram("outT", [D, T], F32, kind="ExternalOutput")
        W = {}
        for name, shp in [("w_in", (L, D, NIN)), ("w_decay_up", (L, 64, RW)), ("w_aaa_up", (L, 64, RW)),
                          ("w_gate_up", (L, 128, RW)), ("w_vres_down", (max(L - 1, 1), D, 32)),
                          ("w_vres_up", (max(L - 1, 1), 32, RW)), ("w_o_fox", (L, FOXW, D)),
                          ("w_o_rwkv", (L, RW, D)), ("w_out", (L, D, D)), ("w_up", (L, D, 2 * DFF)),
                          ("w_down", (L, DFF, D)), ("w_ple_gate", (L, D, D)), ("w_ple_up", (L, PLE, D))]:
            W[name] = self.dram(name, shp, F32, kind="ExternalInput")
        self.W = W
        self.colpack_d = self.dram("colpack", [128, L * NCOL], F32, kind="ExternalInput")
        self.const_d = {k: self.dram("c_" + k, list(v), F32, kind="ExternalInput") for k, v in CONST_SHAPES.items()}
        self.xs = self.dram("xs", [D, T], F32)
        self.qa = self.dram("qa", [8, 70, T], BF16)
        self.ka = self.dram("ka", [8, 70, T], BF16)
        self.zr = self.dram("zr", [1792, T], F32)
        self.vf = self.dram("vf", [RW, T], F32)
        self.gt = self.dram("gt", [2 * D, T], BF16)
        self.yf = self.dram("yf", [FOXW, T], BF16)
        self.yr = self.dram("yr", [RW, T], BF16)
        self.vt = self.dram("vt", [T, FOXW], BF16)
        self.actT = self.dram("actT", [DFF, T], BF16)
        self.psb = [Tl(nc.alloc_psum_tensor(f"ps{i}", [128, 512], F32)) for i in range(8)]
        self.colpack = self.psb_("colpack_s", [128, L * NCOL])
        s.dma("sp", self.colpack[:], self.colpack_d[:, :], r=[self.colpack_d.R()], w=[self.colpack.R()])
        self.c = {}
        for k, shp in CONST_SHAPES.items():
            if k in ("fullmask", "trimask_f"):
                continue
            self.c[k] = self.psb_("cs_" + k, shp)
            s.dma("sp", self.c[k][:], self.const_d[k][:, :], r=[self.const_d[k].R()], w=[self.c[k].R()])
        for k in ["ident_f", "blk2_f", "fullmask"]:
            self.c[k + "_b"] = self.psb_("cb_" + k, CONST_SHAPES[k], BF16)
            s.dma("pool", self.c[k + "_b"][:], self.const_d[k][:, :], r=[self.const_d[k].R()],
                  w=[self.c[k + "_b"].R()])
        self.c["ones_b"] = self.psb_("cb_ones", [128, 512], BF16)
        s.dma("pool", self.c["ones_b"][:], self.const_d["ones_f"][:, :], r=[self.const_d["ones_f"].R()],
              w=[self.c["ones_b"].R()])
        self.carry = self.psb_("carry", [128, 16])
        self.ccar = self.psb_("ccar", [8, 2])
        self.ucarry = self.psb_("ucarry", [128, 2 * NJ, 2])
        self.negb = self.psb_("negb", [8, 4])
        for ll in range(L):
            s.op("dve", lambda e, ll=ll: e.tensor_scalar(out=self.negb[:, ll:ll + 1], in0=self.col(ll, "b_f", 0, 0, 8),
                                                          scalar1=-1.0, scalar2=None, op0=ALU.mult),
                 r=[self.colpack.R()], w=[self.negb.R()])
        self._uid = 0
        base0 = int(nc.sbuf_base)
        self.arena_start = (base0 + 63) // 64 * 64
        left = (int(nc.sbuf_bytes_remaining) - 256 - (self.arena_start - base0)) // 64 * 64
        slab = nc.alloc_sbuf_tensor("arena", [128, (left + self.arena_start - base0) // 4], F32)
        self.arena_end = self.arena_start + left
        assert int(nc.sbuf_base) >= self.arena_end, (nc.sbuf_base, self.arena_end)
        self.stage_begin()
        ow = self.sb("ones_wide", [3, T], BF16)
        s.op("dve", lambda e: e.memset(ow[:], 1.0), w=[ow.R()])
        for h in range(8):
            s.dma("sp", self.qa[h, 67:70, :], ow[:], r=[ow.R()], w=[self.qa.R(("ones", h))])
            s.dma("sp", self.ka[h, 64:67, :], ow[:], r=[ow.R()], w=[self.ka.R(("ones", h))])
        for l in range(L):
            src = self.xT if l == 0 else self.xs
            if "A" in self.stages:
                self.stage_A(l, src)
            if "B" in self.stages:
                self.stage_B(l)
            if "C" in self.stages:
                self.stage_C(l)
            if "D" in self.stages:
                self.stage_D(l, src)
            if "E" in self.stages or "1" in self.stages:
                self.stage_E1(l)
            if "E" in self.stages or "2" in self.stages:
                self.stage_E2(l, last=(l == L - 1))
        s.finish()

    def col(self, l, name, j=0, p0=0, p1=128):
        o, n = COLS[name]
        assert j < n
        c0 = l * NCOL + o + j
        return self.colpack[p0:p1, c0:c0 + 1]

    def rmsnorm_tile(self, l, gname, xt, ht, tag):
        s = self.s
        sq = ht
        s.op("act", lambda e: e.activation(out=sq[:], in_=xt[:], func=AF.Square), r=[xt.R()],
             w=[ht.R(kc) for kc in range(KC)])
        ps = self.ps()
        for kc in range(KC):
            s.op("pe", lambda e, kc=kc: e.matmul(ps[:, :], lhsT=self.c["ones_b"][:, 0:128], rhs=sq[:, kc, :],
                                                 start=(kc == 0), stop=(kc == KC - 1)),
                 r=[self.c["ones_b"].R(), ht.R(kc)], w=[ps.R()])
        rs = self.tmpA("rn_rs", [128, TT], nbuf=1)
        self.rsqrt_ps(rs, ps, 1.0 / D, RMS_EPS, 1.0)
        for kc in range(KC):
            eng = "dve" if kc % 2 == 0 else "pool"
            if eng == "dve":
                s.op("dve", lambda e, kc=kc: e.scalar_tensor_tensor(out=ht[:, kc, :], in0=xt[:, kc, :],
                                                                     scalar=self.col(l, gname, kc), in1=rs[:],
                                                                     op0=ALU.mult, op1=ALU.mult),
                     r=[xt.R(), rs.R(), self.colpack.R()], w=[ht.R(kc)])
            else:
                tmp = self.tmpA("rn_nt", [128, TT], nbuf=2)
                s.op("pool", lambda e, kc=kc, tmp=tmp: e.tensor_tensor(out=tmp[:], in0=xt[:, kc, :], in1=rs[:], op=ALU.mult),
                     r=[xt.R(), rs.R()], w=[tmp.R()])
                s.op("pool", lambda e, kc=kc, tmp=tmp: e.tensor_scalar(out=ht[:, kc, :], in0=tmp[:],
                                                               scalar1=self.col(l, gname, kc), scalar2=None,
                                                               op0=ALU.mult),
                     r=[tmp.R(), self.colpack.R()], w=[ht.R(kc)])

    def stage_A(self, l, src):
        nc, s = self.nc, self.s
        T = self.T
        self.stage_begin()
        self.wbig = self.sb("wbig", [128, KC * NIN], BF16)
        self.wbig_R = self.wbig.R()
        self.xt_t = [self.sb("xt0", [128, KC, TT])] * 2
        self.ht_t = [self.sb(f"ht{i}", [128, KC, TT], BF16) for i in range(2)]
        self.zraw = [self.sb(f"zraw{i}", [128, TT + 1]) for i in range(2)]
        W = self.W
        win = self.wbig
        winv = self.wbig.t[:, 0:KC * NIN].rearrange("p (k n) -> p k n", k=KC)
        wsrc = W["w_in"].t[l].rearrange("(k p) n -> p k n", p=128)
        for kc in range(KC):
            s.dma("pool", winv[:, kc, :], wsrc[:, kc, :], r=[W["w_in"].R()], w=[self.wbig_R])
        if l >= 1:
            self.wvd = self.sb("wvd", [128, KC, 32], BF16)
            self.wvu = self.sb("wvu", [32, RW], BF16)
            s.dma("pool", self.wvd[:], W["w_vres_down"].t[l - 1].rearrange("(k p) n -> p k n", p=128),
                  r=[W["w_vres_down"].R()], w=[self.wvd.R()])
            s.dma("pool", self.wvu[:], W["w_vres_up"].t[l - 1], r=[W["w_vres_up"].R()], w=[self.wvu.R()])
        groups = in_groups()
        srcv = src.t.rearrange("(k p) t -> p k t", p=128)
        self.deferred = []

        def run_deferred():
            run, self.deferred = self.deferred, []
            for f in run:
                f()

        def prep_tile(tt):
            xt = self.xt_t[tt % 2]
            ht = self.ht_t[tt % 2]
            for kc in range(KC):
                s.dma("sp", xt[:, kc, :], srcv[:, kc, tt * TT:(tt + 1) * TT], r=[src.R(("tile", tt))], w=[xt.R()])
            self.rmsnorm_tile(l, "g_mix", xt, ht, "A")

        prep_tile(0)
        for tt in range(self.NT):
            t0 = tt * TT
            seq_start = (tt % self.TPS == 0)
            xt = self.xt_t[tt % 2]
            ht = self.ht_t[tt % 2]
            hR = [ht.R(kc) for kc in range(KC)]
            for sub in range(4):
                ps = self.ps()
                for kc in range(KC):
                    s.op("pe", lambda e, kc=kc, sub=sub: e.matmul(ps[:, :], lhsT=ht[:, kc, sub * 128:(sub + 1) * 128],
                                                                   rhs=winv[:, kc, 1024:1536], start=(kc == 0),
                                                                   stop=(kc == KC - 1)),
                         r=[hR[kc], self.wbig_R], w=[ps.R()])
                blk = tt * 4 + sub
                vo = self.tmpA("vo", [128, FOXW], BF16)
                s.op("act", lambda e, vo=vo, ps=ps: e.activation(out=vo[:], in_=ps[:, :], func=AF.Copy),
                     r=[ps.R()], w=[vo.R()])
                s.dma("sp", self.vt.t[blk * 128:(blk + 1) * 128, :], vo[:], r=[vo.R()], w=[self.vt.R(blk)])
            if l >= 1:
                ps = self.ps()
                for kc in range(KC):
                    s.op("pe", lambda e, kc=kc, ps=ps: e.matmul(ps[0:32, :], lhsT=self.wvd[:, kc, :], rhs=ht[:, kc, :],
                                                                 start=(kc == 0), stop=(kc == KC - 1)),
                         r=[hR[kc], self.wvd.R()], w=[ps.R()])
                vd = self.tmpA("vd", [32, TT], BF16)
                s.op("act", lambda e, ps=ps: e.activation(out=vd[:], in_=ps[0:32, :], func=AF.Copy), r=[ps.R()],
                     w=[vd.R()])
            for gi, (kind, c0, wd, idx) in enumerate(groups):
                ps = self.ps()
                for kc in range(KC):
                    s.op("pe", lambda e, kc=kc, ps=ps, c0=c0, wd=wd: e.matmul(ps[0:wd, :], lhsT=winv[:, kc, c0:c0 + wd],
                                                                               rhs=ht[:, kc, :], start=(kc == 0),
                                                                               stop=(kc == KC - 1)),
                         r=[hR[kc], self.wbig_R], w=[ps.R()])
                run_deferred()
                if gi == 24 and tt + 1 < self.NT:
                    prep_tile(tt + 1)
                if kind in ("q", "k"):
                    self.epi_qk(l, kind, idx, ps, t0)
                elif kind == "f":
                    self.epi_f(l, ps, t0, seq_start)
                elif kind == "rw":
                    self.epi_rw(l, idx, ps, t0, tt, seq_start, vd if l >= 1 else None)
                else:
                    self.epi_gate(l, idx, ps, t0)
        run_deferred()

    def rsqrt_ps(self, out, ps, scale, eps, mult, np_=128):
        s = self.s
        if not hasattr(self, "_fconst"):
            self._fconst = {}
        def fc(v):
            if v not in self._fconst:
                t = self.psb_(f"fc{len(self._fconst)}", [128, 1])
                s.op("pool", lambda e: e.memset(t[:], float(v)), w=[t.R()])
                self._fconst[v] = t
            return self._fconst[v]
        be = fc(eps)
        bm = fc(float(np.log(mult)))
        s.op("act", lambda e: e.activation(out=out[0:np_, :], in_=ps[0:np_, :], func=AF.Ln, bias=be[0:np_, :], scale=float(scale)),
             r=[ps.R(), be.R()], w=[out.R()])
        s.op("act", lambda e: e.activation(out=out[0:np_, :], in_=out[0:np_, :], func=AF.Exp, bias=bm[0:np_, :], scale=-0.5),
             r=[out.R(), bm.R()], w=[out.R()])

    def ps_rot(self, lo, hi):
        key = (lo, hi)
        if not hasattr(self, "_psr"):
            self._psr = {}
        i = self._psr.get(key, 0)
        self._psr[key] = i + 1
        return self.psb[lo + i % (hi - lo)]

    def stage_B(self, l):
        s = self.s
        S, NB = self.S, self.NB
        self.stage_begin()
        QA = [self.sb(f"QA{i}", [70, S], BF16) for i in range(2)]
        KA = [self.sb(f"KA{i}", [70, S], BF16) for i in range(2)]
        Vb = [self.sb(f"Vb{i}", [128, S // 128, FOXW], BF16) for i in range(2)]
        PT = [self.sb(f"PT{i}", [128, TT], BF16) for i in range(4)]
        rden = [self.sb(f"rden{i}", [64, TT]) for i in range(2)]
        yt = [self.sb(f"yt{i}", [64, TT], BF16) for i in range(2)]
        fm = self.c["fullmask_b"]
        idb = self.c["ident_f_b"]
        onb = self.c["ones_b"]
        NQ = S // TT
        LOOK = 3
        groups = [(b, h) for b in range(NB) for h in range(8)]
        bufs = {}

        def load_group(gi):
            b, h = groups[gi]
            qa, ka = QA[gi % 2], KA[gi % 2]
            if h == 0:
                s.dma("sp", Vb[b % 2][:], self.vt.t[b * S:(b + 1) * S, :].rearrange("(c p) n -> p c n", p=128),
                      r=[self.vt.R(blk) for blk in range(b * S // 128, (b + 1) * S // 128)], w=[Vb[b % 2].R()])
            s.dma("sp", qa[:], self.qa.t[h, :, b * S:(b + 1) * S],
                  r=[self.qa.R(("ones", h))] + [self.qa.R(("qk", h, b * S + j * TT)) for j in range(NQ)] +
                    [self.qa.R(("c", jj, b * S + j * TT)) for j in range(NQ) for jj in range(3)], w=[qa.R()])
            s.dma("sp", ka[:], self.ka.t[h, :, b * S:(b + 1) * S],
                  r=[self.ka.R(("ones", h))] + [self.ka.R(("qk", h, b * S + j * TT)) for j in range(NQ)] +
                    [self.ka.R(("c", jj, b * S + j * TT)) for j in range(NQ) for jj in range(3)], w=[ka.R()])

        work = []
        for gi, (b, h) in enumerate(groups):
            for j in range(NQ):
                nch = 4 * (j + 1)
                for i in range(nch):
                    work.append((gi, b, h, j, i, nch))
        state = {}

        def emit_qk(w):
            gi, b, h, j, i, nch = w
            if j == 0 and i == 0:
                if gi == 0:
                    load_group(0)
                if gi + 1 < len(groups):
                    load_group(gi + 1)
            qa, ka = QA[gi % 2], KA[gi % 2]
            sc = self.ps_rot(4, 8)
            state[w] = sc
            r_ = i - 4 * j
            diag = r_ >= 0
            s.op("pe", lambda e: e.matmul(sc[:, :], lhsT=ka[:, i * 128:(i + 1) * 128], rhs=qa[:, j * TT:(j + 1) * TT], start=True,
                                          stop=not diag), r=[ka.R(), qa.R()], w=[sc.R()])
            if diag:
                s.op("pe", lambda e: e.matmul(sc[:, :], lhsT=idb[:], rhs=fm[:, r_ * 512:(r_ + 1) * 512], start=False, stop=True),
                     r=[idb.R(), fm.R()], w=[sc.R()])

        cnt = {"pt": 0, "acc": None, "den": None, "ep": 0}

        def emit_rest(w):
            gi, b, h, j, i, nch = w
            sc = state.pop(w)
            if i == 0:
                cnt["acc"] = self.ps_rot(0, 2)
                cnt["den"] = self.ps_rot(2, 4)
            acc, den = cnt["acc"], cnt["den"]
            pt = PT[cnt["pt"] % 4]
            cnt["pt"] += 1
            vb = Vb[b % 2]
            s.op("act", lambda e: e.activation(out=pt[:], in_=sc[:, :], func=AF.Exp), r=[sc.R()], w=[pt.R()])
            s.op("pe", lambda e: e.matmul(acc[0:64, :], lhsT=vb[:, i, h * 64:(h + 1) * 64], rhs=pt[:], start=(i == 0),
                                          stop=(i == nch - 1)), r=[vb.R(), pt.R()], w=[acc.R()])
            s.op("pe", lambda e: e.matmul(den[0:64, :], lhsT=onb[:, 0:64], rhs=pt[:], start=(i == 0), stop=(i == nch - 1)),
                 r=[onb.R(), pt.R()], w=[den.R()])
            if i == nch - 1:
                rd = rden[cnt["ep"] % 2]
                y = yt[cnt["ep"] % 2]
                cnt["ep"] += 1
                s.op("dve", lambda e: e.reciprocal(out=rd[:], in_=den[0:64, :]), r=[den.R()], w=[rd.R()])
                s.op("dve", lambda e: e.tensor_tensor(out=y[:], in0=acc[0:64, :], in1=rd[:], op=ALU.mult), r=[acc.R(), rd.R()],
                     w=[y.R()])
                t0 = b * S + j * TT
                s.dma("sp", self.yf.t[h * 64:(h + 1) * 64, t0:t0 + TT], y[:], r=[y.R()], w=[self.yf.R((h, t0))])

        for k in range(min(LOOK, len(work))):
            emit_qk(work[k])
        for k, w in enumerate(work):
            if k + LOOK < len(work):
                emit_qk(work[k + LOOK])
            emit_rest(w)

    def stage_C(self, l):
        import os
        cut = int(os.environ.get("CCUT", "9"))
        use_b = os.environ.get("RWDT", "bf16") == "bf16"
        RD = BF16 if use_b else mybir.dt.float32r
        s = self.s
        S, NB, T = self.S, self.NB, self.T
        W = self.W
        self.stage_begin()
        c = self.c
        idf, blkf, scanm, mS, mA, id8 = c["ident_f"], c["blk2_f"], c["scanmask"], c["mask_S"], c["mask_A"], c["ident8"]

        def V(E, fn, r, w):
            s.op(E, fn, r=[x.R() for x in r], w=[x.R() for x in w])

        def cp(E, out_ap, in_ap, r, w):
            if E == "act":
                V("act", lambda e: e.activation(out=out_ap, in_=in_ap, func=AF.Copy), r, w)
            else:
                V(E, lambda e: e.tensor_copy(out=out_ap, in_=in_ap), r, w)

        Wd = self.sb("Wd", [64, RW], BF16)
        Wa = self.sb("Wa", [64, RW], BF16)
        Wg = self.sb("Wg", [128, RW], BF16)
        s.dma("pool", Wd[:], W["w_decay_up"].t[l], r=[W["w_decay_up"].R()], w=[Wd.R()])
        s.dma("pool", Wa[:], W["w_aaa_up"].t[l], r=[W["w_aaa_up"].R()], w=[Wa.R()])
        s.dma("pool", Wg[:], W["w_gate_up"].t[l], r=[W["w_gate_up"].R()], w=[Wg.R()])
        omka = self.sb("omka", [128, 4])
        o_ka = l * NCOL + COLS["k_a"][0]
        V("dve", lambda e: e.tensor_scalar(out=omka[:], in0=self.colpack[:, o_ka:o_ka + 4], scalar1=-1.0, scalar2=1.0,
                                            op0=ALU.mult, op1=ALU.add), [self.colpack], [omka])
        epsg = self.sb("epsg", [64, 1])
        V("pool", lambda e: e.memset(epsg[:], GN_EPS), [], [epsg])
        AR = [self.sb(f"AR{i}", [128, 8, 2, CH], RD) for i in range(4)]
        BT = [self.sb(f"BT{i}", [128, TT], RD) for i in range(4)]
        KT = [self.sb(f"KT{i}", [128, TT], RD) for i in range(4)]
        ARo = [self.sb(f"ARo{i}", [64, 8, 2, CH], RD) for i in range(4)]
        BTo = [self.sb(f"BTo{i}", [64, TT], RD) for i in range(4)]
        KTo = [self.sb(f"KTo{i}", [64, TT], RD) for i in range(4)]
        VR = [self.sb(f"VR{i}", [128, TT]) for i in range(4)]
        G = [self.sb(f"G{i}", [128, TT], BF16) for i in range(4)]
        BG = [self.sb(f"BG{i}", [128, TT], BF16) for i in range(4)]
        PCp = self.sb("PCp", [128, 8]); PCo = self.sb("PCo", [64, 8])
        PCall = self.sb("PCall", [64, 8, 8])
        H = self.sb("H", [64, 512], RD)
        Hf = self.sb("Hf", [64, 512])
        Ht = self.sb("Ht", [64, 512])
        YN = self.sb("YN", [64, 8, 512])
        dwt = self.sb("dwt", [64, TT]); dat = self.sb("dat", [64, TT]); dgt = self.sb("dgt", [128, TT])
        tdw = self.sb("tdw", [64, TT], BF16); dab = self.sb("dab", [64, TT], BF16); sdg = self.sb("sdg", [128, TT], BF16)
        rT = self.sb("rT", [128, TT]); krT = self.sb("krT", [128, TT])
        sig = self.sb("sig", [128, TT]); aa = self.sb("aa", [128, TT]); kk = self.sb("kk", [128, TT])
        prod = self.sb("prod", [128, TT]); rn = self.sb("rn", [128, TT]); gf = self.sb("gf", [128, TT])
        Lc = self.sb("Lc", [128, TT])
        eL = self.sb("eL", [128, TT]); eLm = self.sb("eLm", [128, TT]); enL = self.sb("enL", [128, TT])
        TOK = [[self.sb(f"tok{i}{b}", [64, 512], RD) for i in range(3)] for b in range(2)]
        SMb = [[self.sb(f"SM{i}{b}", [64, 512], RD) for i in range(4)] for b in range(2)]
        Tfin = [self.sb(f"Tfin{b}", [64, 512], RD) for b in range(2)]
        Xa = [self.sb(f"Xa{i}", [64, 512], RD) for i in range(2)]
        XTa = [self.sb(f"XTa{i}", [64, 512], RD) for i in range(2)]
        TTa = [self.sb(f"TTa{i}", [64, 512], RD) for i in range(2)]
        W0s = self.sb("W0s", [64, 512], RD); Us = self.sb("Us", [64, 512], RD)
        YQ = self.sb("YQ", [64, 8, 512])
        st = {k: self.sb("st_" + k, [64, 64]) for k in ["sum", "sq", "m", "m2", "var", "rstd"]}
        po1 = self.sb("po1", [128, TT]); pob = self.sb("pob", [128, TT], BF16)

        def rr(ap):
            return ap

        def MM(e, out, lhsT, rhs, start, stop):
            return e.matmul(out, lhsT=rr(lhsT), rhs=rr(rhs), start=start, stop=stop)

        def colv(name, hp):
            return self.col(l, name, hp)

        def ar(h):
            return AR[h // 2] if h % 2 == 0 else ARo[h // 2]

        def bt(h):
            return BT[h // 2] if h % 2 == 0 else BTo[h // 2]

        def kt(h):
            return KT[h // 2] if h % 2 == 0 else KTo[h // 2]

        for tt in range(self.NT):
            t0 = tt * TT
            zr = self.zr
            s.dma("sp", dwt[:], zr.t[1536:1600, t0:t0 + TT], r=[zr.R((12, t0))], w=[dwt.R()])
            s.dma("sp", dat[:], zr.t[1600:1664, t0:t0 + TT], r=[zr.R((12, t0))], w=[dat.R()])
            s.dma("sp", dgt[:], zr.t[1664:1792, t0:t0 + TT], r=[zr.R((13, t0))], w=[dgt.R()])
            V("act", lambda e: e.activation(out=tdw[:], in_=dwt[:], func=AF.Tanh), [dwt], [tdw])
            V("pool", lambda e: e.tensor_copy(out=dab[:], in_=dat[:]), [dat], [dab])
            V("act", lambda e: e.activation(out=sdg[:], in_=dgt[:], func=AF.Sigmoid), [dgt], [sdg])
            for hp in range(4):
                hs = slice(hp * 128, (hp + 1) * 128)
                s.dma("sp", rT[:], zr.t[hp * 128:(hp + 1) * 128, t0:t0 + TT], r=[zr.R((hp, t0))], w=[rT.R()])
                s.dma("sp", krT[:], zr.t[512 + hp * 128:512 + (hp + 1) * 128, t0:t0 + TT], r=[zr.R((4 + hp, t0))], w=[krT.R()])
                s.dma("sp", VR[hp][:], zr.t[1024 + hp * 128:1024 + (hp + 1) * 128, t0:t0 + TT], r=[zr.R((8 + hp, t0))],
                      w=[VR[hp].R()])
                p1 = self.ps()
                V("pe", lambda e: e.matmul(p1[:, :], lhsT=Wd[:, hs], rhs=tdw[:], start=True, stop=True), [Wd, tdw], [p1])
                V("act", lambda e: e.activation(out=sig[:], in_=p1[:, :], func=AF.Sigmoid, bias=colv("w0", hp), scale=1.0),
                  [p1, self.colpack], [sig])
                p2 = self.ps()
                V("pe", lambda e: e.matmul(p2[:, :], lhsT=Wa[:, hs], rhs=dab[:], start=True, stop=True), [Wa, dab], [p2])
                V("act", lambda e: e.activation(out=aa[:], in_=p2[:, :], func=AF.Sigmoid, bias=colv("a0", hp), scale=1.0),
                  [p2, self.colpack], [aa])
                p3 = self.ps()
                V("pe", lambda e: e.matmul(p3[:, :], lhsT=Wg[:, hs], rhs=sdg[:], start=True, stop=True), [Wg, sdg], [p3])
                cp("act", gf[:], p3[:, :], [p3], [gf])
                cp("pool", G[hp][:], gf[:], [gf], [G[hp]])
                V("act", lambda e: e.activation(out=kk[:], in_=krT[:], func=AF.Copy, scale=colv("k_k", hp)),
                  [krT, self.colpack], [kk])
                V("pool", lambda e: e.tensor_tensor(out=prod[:], in0=kk[:], in1=kk[:], op=ALU.mult), [kk], [prod])
                p4 = self.ps()
                V("pe", lambda e: e.matmul(p4[:, :], lhsT=blkf[:], rhs=prod[:], start=True, stop=True), [blkf, prod], [p4])
                self.rsqrt_ps(rn, p4, 1.0, 1e-24, 1.0)
                V("dve", lambda e: e.tensor_tensor(out=kk[:], in0=kk[:], in1=rn[:], op=ALU.mult), [kk, rn], [kk])
                V("dve", lambda e: e.tensor_scalar(out=rn[:], in0=aa[:], scalar1=colv("k_a", hp), scalar2=omka[:, hp:hp + 1],
                                                    op0=ALU.mult, op1=ALU.add), [aa, self.colpack, omka], [rn])
                V("dve", lambda e: e.tensor_tensor(out=krT[:], in0=krT[:], in1=rn[:], op=ALU.mult), [krT, rn], [krT])
                V("pool", lambda e: e.tensor_tensor(out=aa[:], in0=kk[:], in1=aa[:], op=ALU.mult), [kk, aa], [aa])
                V("act", lambda e: e.activation(out=sig[:], in_=sig[:], func=AF.Copy, scale=-float(np.exp(-0.5))),
                  [sig], [sig])
                V("dve", lambda e: e.tensor_tensor_scan(out=Lc[:], data0=scanm[:], data1=sig[:], initial=0.0, op0=ALU.mult,
                                                         op1=ALU.add), [scanm, sig], [Lc])
                V("pool", lambda e: e.tensor_tensor(out=sig[:], in0=Lc[:], in1=sig[:], op=ALU.subtract), [Lc, sig], [sig])
                V("act", lambda e: e.activation(out=eL[:], in_=Lc[:], func=AF.Exp), [Lc], [eL])
                V("act", lambda e: e.activation(out=eLm[:], in_=sig[:], func=AF.Exp), [sig], [eLm])
                V("act", lambda e: e.activation(out=enL[:], in_=Lc[:], func=AF.Exp, scale=-1.0), [Lc], [enL])
                arv = AR[hp]
                V("dve", lambda e: e.scalar_tensor_tensor(out=arv[:, :, 0, :], in0=kk[:].rearrange("p (c t) -> p c t", t=CH),
                                                           scalar=-1.0, in1=eLm[:].rearrange("p (c t) -> p c t", t=CH),
                                                           op0=ALU.mult, op1=ALU.mult), [kk, eLm], [arv])
                V("dve", lambda e: e.tensor_tensor(out=arv[:, :, 1, :], in0=rT[:].rearrange("p (c t) -> p c t", t=CH),
                                                    in1=eL[:].rearrange("p (c t) -> p c t", t=CH), op=ALU.mult), [rT, eL], [arv])
                V("pool", lambda e: e.tensor_tensor(out=BT[hp][:], in0=aa[:], in1=enL[:], op=ALU.mult), [aa, enL], [BT[hp]])
                V("dve", lambda e: e.tensor_tensor(out=KT[hp][:], in0=krT[:], in1=enL[:], op=ALU.mult), [krT, enL], [KT[hp]])
                V("pool", lambda e: e.tensor_copy(out=PCp[:], in_=eL[:, CH - 1::CH]), [eL], [PCp])
                s.dma("sp", ARo[hp][:], AR[hp][64:128, :, :, :], r=[AR[hp].R()], w=[ARo[hp].R()])
                s.dma("sp", BTo[hp][:], BT[hp][64:128, :], r=[BT[hp].R()], w=[BTo[hp].R()])
                s.dma("sp", KTo[hp][:], KT[hp][64:128, :], r=[KT[hp].R()], w=[KTo[hp].R()])
                s.dma("sp", PCo[:], PCp[64:128, :], r=[PCp.R()], w=[PCo.R()])
                V("pool", lambda e: e.tensor_copy(out=PCall[:, :, 2 * hp], in_=PCp[0:64, :]), [PCp], [PCall])
                V("pool", lambda e: e.tensor_copy(out=PCall[:, :, 2 * hp + 1], in_=PCo[:]), [PCo], [PCall])
                V("dve", lambda e: e.scalar_tensor_tensor(out=prod[:], in0=rT[:], scalar=colv("r_k", hp), in1=krT[:],
                                                           op0=ALU.mult, op1=ALU.mult), [rT, krT, self.colpack], [prod])
                p5 = self.ps()
                V("pe", lambda e: e.matmul(p5[:, :], lhsT=blkf[:], rhs=prod[:], start=True, stop=True), [blkf, prod], [p5])
                V("dve", lambda e: e.tensor_tensor(out=rn[:], in0=p5[:, :], in1=VR[hp][:], op=ALU.mult), [p5, VR[hp]], [rn])
                V("pool", lambda e: e.tensor_tensor(out=BG[hp][:], in0=rn[:], in1=gf[:], op=ALU.mult), [rn, gf], [BG[hp]])
            def indep(cc):
                cs = slice(cc * CH, (cc + 1) * CH)
                b = cc % 2
                Btok, Ktok, Vtok = TOK[b]
                SM = SMb[b]
                for srcs, dst, eng in [(BT, Btok, "act"), (KT, Ktok, "dve"), (VR, Vtok, "act")]:
                    pt_ = self.ps()
                    if use_b and srcs is not VR:
                        pv_ = pt_.t.bitcast(BF16)
                        idb_ = c["ident_f_b"]
                        for hp in range(4):
                            V("pe", lambda e, hp=hp: e.transpose(out=pv_[0:64, hp * 128:(hp + 1) * 128], in_=srcs[hp][:, cs],
                                                                 identity=idb_[:]), [srcs[hp], idb_], [pt_])
                        cp(eng, dst[:], pv_[0:64, 0:512], [pt_], [dst])
                    else:
                        for hp in range(4):
                            V("pe", lambda e, hp=hp: e.transpose(out=pt_[0:64, hp * 128:(hp + 1) * 128],
                                                                 in_=(srcs[hp][:, cs] if srcs is VR else srcs[hp][:, cs].bitcast(F32)),
                                                                 identity=idf[:]), [srcs[hp], idf], [pt_])
                        cp(eng, dst[:], pt_[0:64, :], [pt_], [dst])
                    yield
                for hp in range(4):
                    pS = self.ps()
                    for par in range(2):
                        h = 2 * hp + par
                        rhs = ar(h)[0:64, cc, :, :].rearrange("p a t -> p (a t)")
                        V("pe", lambda e, par=par, h=h, rhs=rhs: MM(e, pS[0:64, par * 256:par * 256 + 128], lhsT=bt(h)[0:64, cs],
                                                                   rhs=rhs, start=True, stop=True), [bt(h), ar(h)], [pS])
                        V("pe", lambda e, par=par, h=h, rhs=rhs: MM(e, pS[0:64, par * 256 + 128:par * 256 + 256],
                                                                   lhsT=kt(h)[0:64, cs], rhs=rhs, start=True, stop=True),
                          [kt(h), ar(h)], [pS])
                    V("dve", lambda e, hp=hp, pS=pS: e.tensor_tensor(out=SM[hp][:], in0=pS[0:64, :], in1=mS[:], op=ALU.mult),
                      [pS, mS], [SM[hp]])
                    if hp % 2 == 1:
                        yield
                pA = self.ps()
                for h in range(8):
                    V("pe", lambda e, h=h: MM(e, pA[0:64, h * 64:(h + 1) * 64], lhsT=ar(h)[0:64, cc, 0, :],
                                              rhs=bt(h)[0:64, cs], start=True, stop=True), [ar(h), bt(h)], [pA])
                X, XT, Tt = Xa[0], XTa[0], TTa[0]
                V("dve", lambda e: e.tensor_tensor(out=X[:], in0=pA[0:64, :], in1=mA[:], op=ALU.mult), [pA, mA], [X])
                for hp in range(4):
                    V("pool", lambda e, hp=hp: e.tensor_copy(
                        out=XT[:, hp * 128:(hp + 1) * 128].rearrange("p (a t) -> p a t", a=2),
                        in_=SM[hp][:, :].rearrange("p (a t) -> p a t", a=2)[:, :, 0:64]), [SM[hp]], [XT])
                V("pool", lambda e: e.tensor_tensor(out=Tt[:], in0=XT[:], in1=id8[:], op=ALU.add), [XT, id8], [Tt])
                yield
                for k in range(1, 6):
                    Xn, XTn, Tn = Xa[k % 2], XTa[k % 2], (TTa[k % 2] if k < 5 else Tfin[b])
                    pX = self.ps()
                    for h in range(8):
                        hsl = slice(h * 64, (h + 1) * 64)
                        V("pe", lambda e, hsl=hsl: MM(e, pX[0:64, hsl], lhsT=XT[:, hsl], rhs=X[:, hsl], start=True, stop=True),
                          [XT, X], [pX])
                    if k < 5:
                        pXT = self.ps()
                        for h in range(8):
                            hsl = slice(h * 64, (h + 1) * 64)
                            V("pe", lambda e, hsl=hsl: MM(e, pXT[0:64, hsl], lhsT=X[:, hsl], rhs=XT[:, hsl], start=True,
                                                           stop=True), [XT, X], [pXT])
                    cp("act", Xn[:], pX[0:64, :], [pX], [Xn])
                    if k < 5:
                        cp("dve", XTn[:], pXT[0:64, :], [pXT], [XTn])
                    yield
                    pT = self.ps()
                    for h in range(8):
                        hsl = slice(h * 64, (h + 1) * 64)
                        V("pe", lambda e, hsl=hsl: MM(e, pT[0:64, hsl], lhsT=Xn[:, hsl], rhs=Tt[:, hsl], start=True, stop=True),
                          [Xn, Tt], [pT])
                    V("dve", lambda e: e.tensor_tensor(out=Tn[:], in0=pT[0:64, :], in1=Tt[:], op=ALU.add), [pT, Tt], [Tn])
                    X, XT, Tt = Xn, XTn, Tn
                    yield

            def dep(cc):
                cs = slice(cc * CH, (cc + 1) * CH)
                b = cc % 2
                Btok, Ktok, Vtok = TOK[b]
                SM = SMb[b]
                Tt = Tfin[b]
                if tt % self.TPS == 0 and cc == 0:
                    V("pool", lambda e: e.memset(Hf[:], 0.0), [], [Hf])
                    V("pool", lambda e: e.tensor_copy(out=H[:], in_=Hf[:]), [Hf], [H])

                def hd(h):
                    return h // 2, (h % 2) * 256, slice(h * 64, (h + 1) * 64)
                pW = self.ps()
                for h in range(8):
                    hp, b0, hsl = hd(h)
                    V("pe", lambda e, hp=hp, b0=b0, hsl=hsl: MM(e, pW[0:64, hsl], lhsT=SM[hp][:, b0 + 128:b0 + 192],
                                                               rhs=Vtok[:, hsl], start=True, stop=False), [SM[hp], Vtok], [pW])
                    V("pe", lambda e, h=h, hsl=hsl: MM(e, pW[0:64, hsl], lhsT=ar(h)[0:64, cc, 0, :], rhs=H[:, hsl], start=False,
                                                      stop=True), [ar(h), H], [pW])
                cp("act", W0s[:], pW[0:64, :], [pW], [W0s])
                yield
                pU = self.ps()
                for h in range(8):
                    hp, b0, hsl = hd(h)
                    V("pe", lambda e, hsl=hsl: MM(e, pU[0:64, hsl], lhsT=Tt[:, hsl], rhs=W0s[:, hsl], start=True, stop=True),
                      [Tt, W0s], [pU])
                cp("dve", Us[:], pU[0:64, :], [pU], [Us])
                yield
                pY = self.ps()
                for h in range(8):
                    hp, b0, hsl = hd(h)
                    V("pe", lambda e, hp=hp, b0=b0, hsl=hsl: MM(e, pY[0:64, hsl], lhsT=SM[hp][:, b0 + 192:b0 + 256],
                                                               rhs=Vtok[:, hsl], start=True, stop=False), [SM[hp], Vtok], [pY])
                    V("pe", lambda e, hp=hp, b0=b0, hsl=hsl: MM(e, pY[0:64, hsl], lhsT=SM[hp][:, b0 + 64:b0 + 128],
                                                               rhs=Us[:, hsl], start=False, stop=False), [SM[hp], Us], [pY])
                    V("pe", lambda e, h=h, hsl=hsl: MM(e, pY[0:64, hsl], lhsT=ar(h)[0:64, cc, 1, :], rhs=H[:, hsl], start=False,
                                                      stop=True), [ar(h), H], [pY])
                cp("act", YN[:, cc, :], pY[0:64, :], [pY], [YN])
                yield
                pH = self.ps()
                for h in range(8):
                    hp, b0, hsl = hd(h)
                    V("pe", lambda e, hsl=hsl: MM(e, pH[0:64, hsl], lhsT=Btok[:, hsl], rhs=Us[:, hsl], start=True, stop=False),
                      [Btok, Us], [pH])
                    V("pe", lambda e, hsl=hsl: MM(e, pH[0:64, hsl], lhsT=Ktok[:, hsl], rhs=Vtok[:, hsl], start=False, stop=True),
                      [Ktok, Vtok], [pH])
                V("dve", lambda e: e.tensor_tensor(out=Ht[:], in0=pH[0:64, :], in1=Hf[:], op=ALU.add), [pH, Hf], [Ht])
                yield
                V("dve", lambda e: e.tensor_tensor(out=Hf[:].rearrange("p (h v) -> p h v", h=8),
                                                    in0=Ht[:].rearrange("p (h v) -> p h v", h=8),
                                                    in1=PCall[:, cc, :].unsqueeze(2).broadcast_to([64, 8, 64]), op=ALU.mult),
                  [Ht, PCall], [Hf])
                V("act", lambda e: e.activation(out=H[:], in_=Hf[:], func=AF.Copy), [Hf], [H])
                yield

            def drive(gens):
                gens = [g for g in gens if g is not None]
                while gens:
                    for g in list(gens):
                        try:
                            next(g)
                        except StopIteration:
                            gens.remove(g)

            if cut >= 4:
                drive([indep(0)])
                for cc in range(8):
                    drive([dep(cc), indep(cc + 1) if cc + 1 < 8 else None])
            if cut >= 5:
                yr3 = YN[:].rearrange("p c (h v) -> p (c h) v", h=8)
                yq3 = YQ[:].rearrange("p c (h v) -> p (c h) v", h=8)
                V("act", lambda e: e.activation(out=YQ[:], in_=YN[:], func=AF.Square), [YN], [YQ])
                V("dve", lambda e: e.tensor_reduce(out=st["sum"][:], in_=yr3, axis=AX.X, op=ALU.add), [YN], [st["sum"]])
                V("dve", lambda e: e.tensor_reduce(out=st["sq"][:], in_=yq3, axis=AX.X, op=ALU.add), [YQ], [st["sq"]])
                V("pool", lambda e: e.tensor_scalar(out=st["m"][:], in0=st["sum"][:], scalar1=1.0 / 64, scalar2=None, op0=ALU.mult),
                  [st["sum"]], [st["m"]])
                V("pool", lambda e: e.tensor_tensor(out=st["m2"][:], in0=st["m"][:], in1=st["m"][:], op=ALU.mult), [st["m"]],
                  [st["m2"]])
                V("dve", lambda e: e.scalar_tensor_tensor(out=st["var"][:], in0=st["sq"][:], scalar=1.0 / 64, in1=st["m2"][:],
                                                           op0=ALU.mult, op1=ALU.subtract), [st["sq"], st["m2"]], [st["var"]])
                V("act", lambda e: e.activation(out=st["rstd"][:], in_=st["var"][:], func=AF.Ln, bias=epsg[:], scale=1.0),
                  [st["var"], epsg], [st["rstd"]])
                V("act", lambda e: e.activation(out=st["rstd"][:], in_=st["rstd"][:], func=AF.Exp, scale=-0.5), [st["rstd"]],
                  [st["rstd"]])
                V("dve", lambda e: e.tensor_tensor(out=yq3, in0=yr3, in1=st["m"][:].unsqueeze(2).broadcast_to([64, 64, 64]),
                                                    op=ALU.subtract), [YN, st["m"]], [YQ])
                V("dve", lambda e: e.tensor_tensor(out=yr3, in0=yq3, in1=st["rstd"][:].unsqueeze(2).broadcast_to([64, 64, 64]),
                                                    op=ALU.mult), [YQ, st["rstd"]], [YN])
            for hp in range(4 if cut >= 6 else 0):
                pO = self.ps()
                for cc in range(8):
                    V("pe", lambda e, cc=cc: e.transpose(out=pO[:, cc * CH:(cc + 1) * CH], in_=YN[:, cc, hp * 128:(hp + 1) * 128],
                                                         identity=idf[0:64, 0:64]), [YN, idf], [pO])
                V("dve", lambda e: e.tensor_scalar(out=po1[:], in0=pO[:, :], scalar1=colv("gn_g", hp), scalar2=colv("gn_b", hp),
                                                    op0=ALU.mult, op1=ALU.add), [pO, self.colpack], [po1])
                V("pool", lambda e: e.tensor_tensor(out=po1[:], in0=po1[:], in1=G[hp][:], op=ALU.mult), [po1, G[hp]], [po1])
                V("dve", lambda e: e.tensor_tensor(out=pob[:], in0=po1[:], in1=BG[hp][:], op=ALU.add), [po1, BG[hp]], [pob])
                s.dma("sp", self.yr.t[hp * 128:(hp + 1) * 128, t0:t0 + TT], pob[:], r=[pob.R()], w=[self.yr.R((hp, t0))])

    def load_w(self, dst, name, l, nk):
        Wt = self.W[name]
        srcv = Wt.t[l].rearrange("(k p) n -> p k n", p=128)
        for kc in range(nk):
            self.s.dma("pool", dst[:, kc, :], srcv[:, kc, :], r=[Wt.R()], w=[dst.R()])

    def stage_D(self, l, src):
        s = self.s
        self.stage_begin()
        wof = self.sb("wof", [128, 4, D], BF16)
        wor = self.sb("wor", [128, 4, D], BF16)
        wout = self.sb("wout", [128, KC, D], BF16)
        self.load_w(wof, "w_o_fox", l, 4)
        self.load_w(wor, "w_o_rwkv", l, 4)
        self.load_w(wout, "w_out", l, KC)
        yfT = [self.sb(f"yfT{i}", [128, 4, TT], BF16) for i in range(2)]
        yrT = [self.sb(f"yrT{i}", [128, 4, TT], BF16) for i in range(2)]
        gtT = [self.sb(f"gtT{i}", [128, 16, TT], BF16) for i in range(2)]
        xt_ = [self.sb(f"xD{i}", [128, KC, TT]) for i in range(2)]
        mg = self.sb("mg", [128, KC, TT], BF16)
        srcv = src.t.rearrange("(k p) t -> p k t", p=128)
        dstv = self.xs.t.rearrange("(k p) t -> p k t", p=128)
        for tt in range(self.NT):
            t0 = tt * TT
            yf, yr, gt, xt = yfT[tt % 2], yrT[tt % 2], gtT[tt % 2], xt_[tt % 2]
            s.dma("sp", yf[:], self.yf.t[:, t0:t0 + TT].rearrange("(k p) t -> p k t", p=128),
                  r=[self.yf.R((h, t0)) for h in range(8)], w=[yf.R()])
            s.dma("sp", yr[:], self.yr.t[:, t0:t0 + TT].rearrange("(k p) t -> p k t", p=128),
                  r=[self.yr.R((hp, t0)) for hp in range(4)], w=[yr.R()])
            s.dma("sp", gt[:], self.gt.t[:, t0:t0 + TT].rearrange("(k p) t -> p k t", p=128),
                  r=[self.gt.R((i, t0)) for i in range(16)], w=[gt.R()])
            for kc in range(KC):
                s.dma("sp", xt[:, kc, :], srcv[:, kc, t0:t0 + TT], r=[src.R(("tile", tt))], w=[xt.R()])
            for n in range(KC):
                ns = slice(n * 128, (n + 1) * 128)
                pa = self.ps()
                for kc in range(4):
                    s.op("pe", lambda e, kc=kc: e.matmul(pa[:, :], lhsT=wof[:, kc, ns], rhs=yf[:, kc, :], start=(kc == 0),
                                                         stop=(kc == 3)), r=[wof.R(), yf.R()], w=[pa.R()])
                pb = self.ps()
                for kc in range(4):
                    s.op("pe", lambda e, kc=kc: e.matmul(pb[:, :], lhsT=wor[:, kc, ns], rhs=yr[:, kc, :], start=(kc == 0),
                                                         stop=(kc == 3)), r=[wor.R(), yr.R()], w=[pb.R()])
                m1 = self.tmpA("m1", [128, TT])
                s.op("dve", lambda e, m1=m1: e.tensor_tensor(out=m1[:], in0=pa[:, :], in1=gt[:, n, :], op=ALU.mult),
                     r=[pa.R(), gt.R()], w=[m1.R()])
                m2 = self.tmpA("m2", [128, TT])
                s.op("dve", lambda e, m2=m2: e.tensor_tensor(out=m2[:], in0=pb[:, :], in1=gt[:, 8 + n, :], op=ALU.mult),
                     r=[pb.R(), gt.R()], w=[m2.R()])
                s.op("pool", lambda e, m1=m1, m2=m2: e.tensor_tensor(out=mg[:, n, :], in0=m1[:], in1=m2[:], op=ALU.add),
                     r=[m1.R(), m2.R()], w=[mg.R(n)])
            for n in range(KC):
                ns = slice(n * 128, (n + 1) * 128)
                po = self.ps()
                for kc in range(KC):
                    s.op("pe", lambda e, kc=kc: e.matmul(po[:, :], lhsT=wout[:, kc, ns], rhs=mg[:, kc, :], start=(kc == 0),
                                                         stop=(kc == KC - 1)), r=[wout.R(), mg.R(kc)], w=[po.R()])
                xo = self.tmpA("xo", [128, TT], nbuf=3)
                s.op("dve", lambda e, xo=xo: e.tensor_tensor(out=xo[:], in0=po[:, :], in1=xt[:, n, :], op=ALU.add),
                     r=[po.R(), xt.R()], w=[xo.R()])
                s.dma("sp", dstv[:, n, t0:t0 + TT], xo[:], r=[xo.R()], w=[self.xs.R(("tile", tt))])

    def stage_E1(self, l):
        s = self.s
        self.stage_begin()
        wup = self.sb("wup", [128, KC, 2 * DFF], BF16)
        self.load_w(wup, "w_up", l, KC)
        xt = self.sb("xE", [128, KC, TT])
        h2 = [self.sb(f"h2{i}", [128, KC, TT], BF16) for i in range(2)]
        ub = [self.sb(f"ub{i}", [128, TT + 2]) for i in range(2)]
        srcv = self.xs.t.rearrange("(k p) t -> p k t", p=128)
        K0 = float(2.0 * np.sqrt(2.0 / np.pi))
        def prep_tile(tt):
            for kc in range(KC):
                s.dma("sp", xt[:, kc, :], srcv[:, kc, tt * TT:(tt + 1) * TT], r=[self.xs.R(("tile", tt))], w=[xt.R()])
            self.rmsnorm_tile(l, "g_ffn", xt, h2[tt % 2], "E")

        prep_tile(0)
        for tt in range(self.NT):
            t0 = tt * TT
            seq_start = (tt % self.TPS == 0)
            ht = h2[tt % 2]
            hR = [ht.R(kc) for kc in range(KC)]
            for j in range(NJ):
                if j == 10 and tt + 1 < self.NT:
                    prep_tile(tt + 1)
                cres = []
                for half in range(2):
                    idx = half * NJ + j
                    c0 = idx * 128
                    pu = self.ps()
                    for kc in range(KC):
                        s.op("pe", lambda e, kc=kc, pu=pu, c0=c0: e.matmul(pu[:, :], lhsT=wup[:, kc, c0:c0 + 128], rhs=ht[:, kc, :],
                                                                           start=(kc == 0), stop=(kc == KC - 1)),
                             r=[hR[kc], wup.R()], w=[pu.R()])
                    u = ub[half]
                    s.op("act", lambda e, u=u, pu=pu: e.activation(out=u[:, 2:TT + 2], in_=pu[:, :], func=AF.Copy), r=[pu.R()],
                         w=[u.R()])
                    if seq_start:
                        s.op("pool", lambda e, u=u: e.memset(u[:, 0:2], 0.0), w=[u.R()])
                    else:
                        s.op("pool", lambda e, u=u, idx=idx: e.tensor_copy(out=u[:, 0:2], in_=self.ucarry[:, idx, :]),
                             r=[self.ucarry.R(idx)], w=[u.R()])
                    s.op("pool", lambda e, u=u, idx=idx: e.tensor_copy(out=self.ucarry[:, idx, :], in_=u[:, TT:TT + 2]),
                         r=[u.R()], w=[self.ucarry.R(idx)])
                    c1 = self.tmpA(f"c1{half}", [128, TT], nbuf=1)
                    s.op("act", lambda e, pu=pu, c1=c1, idx=idx: e.activation(out=c1[:], in_=pu[:, :], func=AF.Identity,
                                                                              scale=self.col(l, "cw2", idx),
                                                                              bias=self.col(l, "cb", idx)),
                         r=[pu.R(), self.colpack.R()], w=[c1.R()])
                    c2 = self.tmpA(f"c2{half}", [128, TT], nbuf=1)
                    s.op("dve", lambda e, u=u, c1=c1, c2=c2, idx=idx: e.scalar_tensor_tensor(out=c2[:], in0=u[:, 1:TT + 1],
                                                                                           scalar=self.col(l, "cw1", idx), in1=c1[:],
                                                                                           op0=ALU.mult, op1=ALU.add),
                         r=[u.R(), c1.R(), self.colpack.R()], w=[c2.R()])
                    c3 = self.tmpA(f"c3{half}", [128, TT], nbuf=2)
                    s.op("dve", lambda e, u=u, c2=c2, c3=c3, idx=idx: e.scalar_tensor_tensor(out=c3[:], in0=u[:, 0:TT],
                                                                                           scalar=self.col(l, "cw0", idx), in1=c2[:],
                                                                                           op0=ALU.mult, op1=ALU.add),
                         r=[u.R(), c2.R(), self.colpack.R()], w=[c3.R()])
                    cres.append(c3)
                u1, u2 = cres
                sg = self.tmpA("gsg", [128, TT], nbuf=2)
                s.op("act", lambda e, sg=sg, u1=u1: e.activation(out=sg[:], in_=u1[:], func=AF.Gelu_apprx_tanh), r=[u1.R()],
                     w=[sg.R()])
                ao = self.tmpA("gao", [128, TT], BF16, nbuf=3)
                s.op("pool", lambda e, sg=sg, u2=u2, ao=ao: e.tensor_tensor(out=ao[:], in0=sg[:], in1=u2[:], op=ALU.mult),
                     r=[sg.R(), u2.R()], w=[ao.R()])
                s.dma("sp", self.actT.t[j * 128:(j + 1) * 128, t0:t0 + TT], ao[:], r=[ao.R()], w=[self.actT.R((j, t0))])

    def stage_E2(self, l, last):
        s = self.s
        self.stage_begin()
        wdn = self.sb("wdn", [128, NJ, D], BF16)
        wpg = self.sb("wpg", [128, KC, D], BF16)
        wpu = self.sb("wpu", [128, 2, D], BF16)
        self.load_w(wdn, "w_down", l, NJ)
        self.load_w(wpg, "w_ple_gate", l, KC)
        self.load_w(wpu, "w_ple_up", l, 2)
        at_ = [self.sb(f"at{i}", [128, NJ, TT], BF16) for i in range(2)]
        pt_ = [self.sb(f"pp{i}", [128, 2, TT], BF16) for i in range(2)]
        x2_ = [self.sb(f"x2{i}", [128, KC, TT]) for i in range(2)]
        h3_ = [self.sb(f"h3{i}", [128, KC, TT], BF16) for i in range(2)]
        srcv = self.xs.t.rearrange("(k p) t -> p k t", p=128)
        dst = self.out if last else self.xs
        dstv = dst.t.rearrange("(k p) t -> p k t", p=128)
        pv = self.pT.t[l].rearrange("(k p) t -> p k t", p=128)

        def down(tt):
            t0 = tt * TT
            at, pp, x2, h3 = at_[tt % 2], pt_[tt % 2], x2_[tt % 2], h3_[tt % 2]
            s.dma("sp", at[:], self.actT.t[:, t0:t0 + TT].rearrange("(k p) t -> p k t", p=128),
                  r=[self.actT.R((j, t0)) for j in range(NJ)], w=[at.R()])
            s.dma("pool", pp[:], pv[:, :, t0:t0 + TT], r=[self.pT.R()], w=[pp.R()])
            for kc in range(KC):
                s.dma("sp", x2[:, kc, :], srcv[:, kc, t0:t0 + TT], r=[self.xs.R(("tile", tt))], w=[x2.R()])
            for n in range(KC):
                ns = slice(n * 128, (n + 1) * 128)
                pd = self.ps()
                for j in range(NJ):
                    s.op("pe", lambda e, j=j: e.matmul(pd[:, :], lhsT=wdn[:, j, ns], rhs=at[:, j, :], start=(j == 0),
                                                       stop=(j == NJ - 1)), r=[wdn.R(), at.R()], w=[pd.R()])
                s.op("dve", lambda e: e.tensor_tensor(out=x2[:, n, :], in0=pd[:, :], in1=x2[:, n, :], op=ALU.add),
                     r=[pd.R(), x2.R()], w=[x2.R()])
            self.rmsnorm_tile(l, "g_ple", x2, h3, "F")

        def ple(tt):
            t0 = tt * TT
            pp, x2, h3 = pt_[tt % 2], x2_[tt % 2], h3_[tt % 2]
            for n in range(KC):
                ns = slice(n * 128, (n + 1) * 128)
                pg = self.ps()
                for kc in range(KC):
                    s.op("pe", lambda e, kc=kc: e.matmul(pg[:, :], lhsT=wpg[:, kc, ns], rhs=h3[:, kc, :], start=(kc == 0),
                                                         stop=(kc == KC - 1)), r=[wpg.R(), h3.R(kc)], w=[pg.R()])
                sgt = self.tmpA("sgt", [128, TT])
                s.op("act", lambda e, sgt=sgt: e.activation(out=sgt[:], in_=pg[:, :], func=AF.Sigmoid), r=[pg.R()], w=[sgt.R()])
                pq = self.ps()
                for kc in range(2):
                    s.op("pe", lambda e, kc=kc: e.matmul(pq[:, :], lhsT=wpu[:, kc, ns], rhs=pp[:, kc, :], start=(kc == 0),
                                                         stop=(kc == 1)), r=[wpu.R(), pp.R()], w=[pq.R()])
                s.op("dve", lambda e, sgt=sgt: e.tensor_tensor(out=sgt[:], in0=pq[:, :], in1=sgt[:], op=ALU.mult),
                     r=[pq.R(), sgt.R()], w=[sgt.R()])
                xo = self.tmpA("xo2", [128, TT], nbuf=2)
                s.op("pool", lambda e, sgt=sgt, xo=xo: e.tensor_tensor(out=xo[:], in0=sgt[:], in1=x2[:, n, :], op=ALU.add),
                     r=[sgt.R(), x2.R()], w=[xo.R()])
                s.dma("sp", dstv[:, n, t0:t0 + TT], xo[:], r=[xo.R()], w=[dst.R(("tile", tt))])

        down(0)
        for tt in range(self.NT):
            if tt + 1 < self.NT:
                down(tt + 1)
            ple(tt)

    def tmpA(self, name, shape, dt=F32, nbuf=2):
        key = ("tmp", name)
        if key not in self.ep:
            self.ep[key] = [[self.sb(f"tA_{name}{i}", shape, dt) for i in range(nbuf)], 0]
        lst = self.ep[key]
        t = lst[0][lst[1] % nbuf]
        lst[1] += 1
        return t

    def epi_qk(self, l, kind, idx, ps, t0):
        s = self.s
        sq = self.tmpA("qsq", [128, TT], BF16)
        s.op("act", lambda e: e.activation(out=sq[:], in_=ps[:, :], func=AF.Square), r=[ps.R()], w=[sq.R()])
        self.deferred.append(lambda: self.epi_qk2(l, kind, idx, ps, t0, sq))

    def epi_qk2(self, l, kind, idx, ps, t0, sq):
        s = self.s
        ps2 = self.ps()
        s.op("pe", lambda e: e.matmul(ps2[:, :], lhsT=self.c["blk2_f_b"][:], rhs=sq[:], start=True, stop=True),
             r=[self.c["blk2_f_b"].R(), sq.R()], w=[ps2.R()])
        rs2 = self.tmpA("qrs2", [128, TT], nbuf=1)
        self.rsqrt_ps(rs2, ps2, 1.0 / 64, RMS_EPS, 0.125 if kind == "q" else 1.0)
        o = self.tmpA("qo", [128, TT], BF16)
        gname = "g_q" if kind == "q" else "g_k"
        s.op("dve", lambda e: e.scalar_tensor_tensor(out=o[:], in0=ps[:, :], scalar=self.col(l, gname), in1=rs2[:],
                                                      op0=ALU.mult, op1=ALU.mult),
             r=[ps.R(), rs2.R(), self.colpack.R()], w=[o.R()])
        dst = self.qa if kind == "q" else self.ka
        for hh in range(2):
            h = idx * 2 + hh
            s.dma("sp", dst[h, 0:64, t0:t0 + TT], o[hh * 64:(hh + 1) * 64, :], r=[o.R()], w=[dst.R(("qk", h, t0))])

    def epi_f(self, l, ps, t0, seq_start):
        s = self.s
        e1 = self.tmpA("fe", [8, TT], nbuf=1)
        s.op("act", lambda e: e.activation(out=e1[:], in_=ps[0:8, :], func=AF.Exp, bias=self.negb[:, l:l + 1], scale=-1.0),
             r=[ps.R(), self.negb.R()], w=[e1.R()])
        l1 = e1
        one = self.c["ones_f"]
        s.op("act", lambda e: e.activation(out=l1[:], in_=e1[:], func=AF.Ln, bias=one[0:8, 0:1], scale=1.0),
             r=[e1.R(), one.R()], w=[l1.R()])
        c = self.tmpA("fc", [8, TT], nbuf=1)
        if seq_start:
            init = 0.0
            rr = []
        else:
            init = self.ccar[:, 0:1]
            rr = [self.ccar.R()]
        s.op("dve", lambda e: e.tensor_tensor_scan(out=c[:], data0=self.c["ones_f"][0:8, :], data1=l1[:], initial=init,
                                                    op0=ALU.mult, op1=ALU.subtract),
             r=[self.c["ones_f"].R(), l1.R()] + rr, w=[c.R()])
        s.op("dve", lambda e: e.tensor_copy(out=self.ccar[:, 0:1], in_=c[:, TT - 1:TT]), r=[c.R()], w=[self.ccar.R()])
        hi = self.tmpA("fhi", [8, TT], BF16, nbuf=1)
        s.op("dve", lambda e: e.tensor_copy(out=hi[:], in_=c[:]), r=[c.R()], w=[hi.R()])
        r1 = self.tmpA("fr1", [8, TT], nbuf=1)
        s.op("dve", lambda e: e.tensor_tensor(out=r1[:], in0=c[:], in1=hi[:], op=ALU.subtract), r=[c.R(), hi.R()],
             w=[r1.R()])
        mid = self.tmpA("fmid", [8, TT], BF16, nbuf=1)
        s.op("dve", lambda e: e.tensor_copy(out=mid[:], in_=r1[:]), r=[r1.R()], w=[mid.R()])
        r2 = self.tmpA("fe", [8, TT], nbuf=1)
        s.op("dve", lambda e: e.tensor_tensor(out=r2[:], in0=r1[:], in1=mid[:], op=ALU.subtract), r=[r1.R(), mid.R()],
             w=[r2.R()])
        lo = self.tmpA("flo", [8, TT], BF16, nbuf=1)
        s.op("dve", lambda e: e.tensor_copy(out=lo[:], in_=r2[:]), r=[r2.R()], w=[lo.R()])
        for j, part in enumerate([hi, mid, lo]):
            s.dma("sp", self.qa.t[:, 64 + j, t0:t0 + TT], part[:], r=[part.R()], w=[self.qa.R(("c", j, t0))])
            ng = self.tmpA("fng", [8, TT], BF16, nbuf=1)
            s.op("pool", lambda e, ng=ng, part=part: e.tensor_scalar(out=ng[:], in0=part[:], scalar1=-1.0, scalar2=None,
                                                                      op0=ALU.mult), r=[part.R()], w=[ng.R()])
            s.dma("sp", self.ka.t[:, 67 + j, t0:t0 + TT], ng[:], r=[ng.R()], w=[self.ka.R(("c", j, t0))])

    def epi_rw(self, l, idx, ps, t0, tt, seq_start, vd):
        s = self.s
        zb = self.zraw[idx % 2]
        s.op("act", lambda e: e.activation(out=zb[:, 1:TT + 1], in_=ps[:, :], func=AF.Copy), r=[ps.R()], w=[zb.R()])
        if seq_start:
            s.op("pool", lambda e: e.memset(zb[:, 0:1], 0.0), w=[zb.R()])
        else:
            s.op("pool", lambda e: e.tensor_copy(out=zb[:, 0:1], in_=self.carry[:, idx:idx + 1]),
                 r=[self.carry.R(idx)], w=[zb.R()])
        s.op("pool", lambda e: e.tensor_copy(out=self.carry[:, idx:idx + 1], in_=zb[:, TT:TT + 1]), r=[zb.R()],
             w=[self.carry.R(idx)])
        d = self.tmpA("rwd", [128, TT])
        s.op("dve", lambda e: e.tensor_tensor(out=d[:], in0=zb[:, 0:TT], in1=zb[:, 1:TT + 1], op=ALU.subtract),
             r=[zb.R()], w=[d.R()])
        o = self.tmpA("rwo", [128, TT], nbuf=2)
        s.op("dve", lambda e: e.scalar_tensor_tensor(out=o[:], in0=d[:], scalar=self.col(l, "mu", idx), in1=zb[:, 1:TT + 1],
                                                      op0=ALU.mult, op1=ALU.add),
             r=[d.R(), zb.R(), self.colpack.R()], w=[o.R()])
        if 8 <= idx < 12:
            vi = idx - 8
            if l == 0:
                s.dma("sp", self.vf.t[vi * 128:(vi + 1) * 128, t0:t0 + TT], o[:], r=[o.R()], w=[self.vf.R((vi, t0))])
            else:
                ps2 = self.ps()
                s.op("pe", lambda e: e.matmul(ps2[:, :], lhsT=self.wvu[:, vi * 128:(vi + 1) * 128], rhs=vd[:], start=True,
                                              stop=True), r=[self.wvu.R(), vd.R()], w=[ps2.R()])
                vm = self.tmpA("vm", [128, TT], nbuf=1)
                s.op("act", lambda e: e.activation(out=vm[:], in_=ps2[:, :], func=AF.Sigmoid, bias=self.col(l, "v0", vi),
                                                   scale=1.0), r=[ps2.R(), self.colpack.R()], w=[vm.R()])
                vfl = self.tmpA("vfl", [128, TT], nbuf=1)
                s.dma("sp", vfl[:], self.vf.t[vi * 128:(vi + 1) * 128, t0:t0 + TT], r=[self.vf.R((vi, t0))], w=[vfl.R()])
                dd = self.tmpA("vdd", [128, TT], nbuf=1)
                s.op("pool", lambda e: e.tensor_tensor(out=dd[:], in0=vfl[:], in1=o[:], op=ALU.subtract),
                     r=[vfl.R(), o.R()], w=[dd.R()])
                s.op("pool", lambda e: e.tensor_tensor(out=dd[:], in0=dd[:], in1=vm[:], op=ALU.mult), r=[dd.R(), vm.R()],
                     w=[dd.R()])
                o2 = self.tmpA("rwo2", [128, TT], nbuf=1)
                s.op("dve", lambda e: e.tensor_tensor(out=o2[:], in0=o[:], in1=dd[:], op=ALU.add), r=[o.R(), dd.R()],
                     w=[o2.R()])
                o = o2
        s.dma("sp", self.zr.t[idx * 128:(idx + 1) * 128, t0:t0 + TT], o[:], r=[o.R()], w=[self.zr.R((idx, t0))])

    def epi_gate(self, l, idx, ps, t0):
        s = self.s
        o = self.tmpA("go", [128, TT], BF16, nbuf=3)
        s.op("act", lambda e: e.activation(out=o[:], in_=ps[:, :], func=AF.Sigmoid), r=[ps.R()], w=[o.R()])
        s.dma("sp", self.gt.t[idx * 128:(idx + 1) * 128, t0:t0 + TT], o[:], r=[o.R()], w=[self.gt.R((idx, t0))])


def host_inputs(inp, S, NB, depth, core):
    b0 = core * NB
    x = np.asarray(inp["x"], np.float32)[b0:b0 + NB].reshape(NB * S, D)
    p = np.asarray(inp["p"], np.float32)[:depth, b0:b0 + NB].reshape(depth, NB * S, PLE)
    m = {"xT": np.ascontiguousarray(x.T), "pT": np.ascontiguousarray(p.transpose(0, 2, 1))}
    return m


def shared_inputs(inp, depth):
    m = {}
    for name in ["w_in", "w_decay_up", "w_aaa_up", "w_gate_up", "w_o_fox", "w_o_rwkv", "w_out", "w_up", "w_down",
                 "w_ple_gate", "w_ple_up"]:
        m[name] = np.ascontiguousarray(np.asarray(inp[name], np.float32)[:depth])
    for name in ["w_vres_down", "w_vres_up"]:
        m[name] = np.ascontiguousarray(np.asarray(inp[name], np.float32)[:max(depth - 1, 1)])
    m["colpack"] = make_colpack({k: np.asarray(v, np.float32) for k, v in inp.items()}, depth)
    for k, v in make_consts().items():
        m["c_" + k] = v
    return m


_PROG_CACHE = {}


def kernel(**inp):
    S, NBT, depth = 2048, 16, 4
    ncores = 8
    NB = NBT // ncores
    key = (S, NB, depth)
    if key not in _PROG_CACHE:
        _PROG_CACHE[key] = Prog(S, NB, depth)
    prog = _PROG_CACHE[key]
    sh = shared_inputs(inp, depth)
    in_maps = []
    for c in range(ncores):
        m = dict(sh)
        m.update(host_inputs(inp, S, NB, depth, c))
        in_maps.append(m)
    res = run_bass_kernel_spmd(prog.nc, in_maps, core_ids=list(range(ncores)))
    outs = []
    for c in range(ncores):
        o = res.results[c]["outT"]
        outs.append(np.ascontiguousarray(o.T).reshape(NB, S, D))
    return np.concatenate(outs, axis=0).astype(np.float32)
```

```python
import numpy as np
import concourse.bass as bass
import concourse.mybir as mybir
from concourse.bass_utils import run_bass_kernel_spmd

F32 = mybir.dt.float32
BF16 = mybir.dt.bfloat16
AF = mybir.ActivationFunctionType
ALU = mybir.AluOpType
AX = mybir.AxisListType

D = 1024
KC = 8
FOXW = 512
RW = 512
NIN = 5384
DFF = 2816
NJ = 22
PLE = 256
RMS_EPS = 1e-6
GN_EPS = 64e-5
CH = 64
TT = 512


class Res:
    __slots__ = ("w", "rd")

    def __init__(self):
        self.w = None
        self.rd = {}


class Tl:
    def __init__(self, t):
        self.t = t
        self._r = {}

    def R(self, key=None):
        r = self._r.get(key)
        if r is None:
            r = Res()
            self._r[key] = r
        return r

    def __getitem__(self, idx):
        return self.t[idx]


class Sch:
    NDS = 6

    def __init__(self, nc):
        self.nc = nc
        self.e = dict(pe=nc.tensor, act=nc.scalar, dve=nc.vector, pool=nc.gpsimd, sp=nc.sync)
        self.sem = {k: nc.alloc_semaphore(name=f"s_{k}") for k in self.e}
        self.cnt = {k: 0 for k in self.e}
        self.seen = {k: {} for k in self.e}
        self.dsem = {q: [nc.alloc_semaphore(name=f"d_{q}{i}") for i in range(self.NDS)]
                     for q in ("sp", "pool", "act")}
        self.dcnt = {q: 0 for q in self.dsem}
        self.semh = {}
        for k in self.e:
            self.semh[k] = self.sem[k]
        for q in self.dsem:
            for i, h in enumerate(self.dsem[q]):
                self.semh[(q, i)] = h
        self.n_ins = 0

    def _wait(self, E, key, val):
        if self.seen[E].get(key, 0) >= val:
            return
        self.e[E].wait_ge(self.semh[key], val)
        self.seen[E][key] = val
        self.n_ins += 1

    def _collect(self, reads, writes):
        deps = {}

        def add(tok):
            k, v = tok
            if deps.get(k, 0) < v:
                deps[k] = v

        for r in reads:
            if r.w is not None:
                add(r.w)
        for w in writes:
            if w.w is not None:
                add(w.w)
            for k, v in w.rd.items():
                add((k, v))
        return deps

    def _commit(self, tok, reads, writes):
        k, v = tok
        for w in writes:
            w.w = tok
            w.rd = {}
        for r in reads:
            if r.rd.get(k, 0) < v:
                r.rd[k] = v

    def op(self, E, fn, r=(), w=()):
        deps = self._collect(r, w)
        for k, v in deps.items():
            if E == "pe" and k == "pe":
                continue
            self._wait(E, k, v)
        ins = fn(self.e[E])
        self.cnt[E] += 1
        ins.then_inc(self.sem[E], 1)
        self.n_ins += 1
        self._commit((E, self.cnt[E]), r, w)

    def dma(self, q, out, in_, r=(), w=(), **kw):
        deps = self._collect(r, w)
        for k, v in deps.items():
            self._wait(q, k, v)
        n = self.dcnt[q]
        i = n % self.NDS
        gen = n // self.NDS
        if gen > 0:
            self._wait(q, (q, i), 16 * gen)
        ins = self.e[q].dma_start(out=out, in_=in_, **kw)
        ins.then_inc(self.dsem[q][i], 16)
        self.dcnt[q] = n + 1
        self.n_ins += 1
        self._commit(((q, i), 16 * (gen + 1)), r, w)

    def finish(self):
        for q in self.dsem:
            n = self.dcnt[q]
            for i in range(self.NDS):
                cnt_i = (n - i + self.NDS - 1) // self.NDS
                if cnt_i > 0:
                    self._wait("sp", (q, i), 16 * cnt_i)
        for k in self.e:
            if k != "sp" and self.cnt[k] > 0:
                self._wait("sp", k, self.cnt[k])


def _cols(vec):
    v = np.asarray(vec, np.float32).reshape(-1)
    n = (v.size + 127) // 128
    buf = np.zeros(n * 128, np.float32)
    buf[: v.size] = v
    return buf.reshape(n, 128).T


def make_consts():
    c = {}
    c["ident_f"] = np.eye(128, dtype=np.float32)
    c["ones_f"] = np.ones((128, 512), np.float32)
    blk = np.zeros((128, 128), np.float32)
    blk[:64, :64] = 1.0
    blk[64:, 64:] = 1.0
    c["blk2_f"] = blk
    s = np.arange(128)[:, None]
    t = np.arange(128)[None, :]
    c["trimask_f"] = np.where(s > t, -30000.0, 0.0).astype(np.float32)
    t5 = np.arange(512)[None, :]
    c["fullmask"] = np.concatenate([np.where(t5 < r * 128 + s, -30000.0, 0.0).astype(np.float32) for r in range(4)], axis=1)
    sm = np.ones((128, 512), np.float32)
    sm[:, ::CH] = 0.0
    c["scanmask"] = sm
    s64 = np.arange(64)[:, None]
    t64 = np.arange(64)[None, :]
    strict_up = (t64 > s64).astype(np.float32)
    incl_up = (t64 >= s64).astype(np.float32)
    one = np.concatenate([strict_up, incl_up], axis=1)
    c["mask_S"] = np.tile(one, (1, 4))
    strict_lo = (t64 < s64).astype(np.float32)
    c["mask_A"] = np.tile(strict_lo, (1, 8))
    c["ident8"] = np.tile(np.eye(64, dtype=np.float32), (1, 8))
    return c


CONST_SHAPES = {"ident_f": (128, 128), "ones_f": (128, 512), "blk2_f": (128, 128), "trimask_f": (128, 128),
                "scanmask": (128, 512), "fullmask": (128, 2048), "mask_S": (64, 512), "mask_A": (64, 512), "ident8": (64, 512)}

COLS = {}
_o = 0
for _name, _n in [("g_mix", 8), ("g_ffn", 8), ("g_ple", 8), ("g_q", 1), ("g_k", 1), ("b_f", 1), ("mu", 14),
                  ("w0", 4), ("a0", 4), ("k_k", 4), ("k_a", 4), ("r_k", 4), ("gn_g", 4), ("gn_b", 4), ("v0", 4),
                  ("cw0", 44), ("cw1", 44), ("cw2", 44), ("cb", 44)]:
    COLS[_name] = (_o, _n)
    _o += _n
NCOL = _o


def make_colpack(inp, depth):
    pk = np.zeros((128, depth, NCOL), np.float32)

    def put(l, name, arr):
        o, n = COLS[name]
        a = _cols(arr)
        assert a.shape[1] == n, (name, a.shape, n)
        pk[:, l, o:o + n] = a

    for l in range(depth):
        put(l, "g_mix", inp["g_mix"][l])
        put(l, "g_ffn", inp["g_ffn"][l])
        put(l, "g_ple", inp["g_ple"][l])
        put(l, "g_q", np.tile(inp["g_qnorm"][l], 2))
        put(l, "g_k", np.tile(inp["g_knorm"][l], 2))
        put(l, "b_f", inp["b_f"][l])
        put(l, "mu", inp["mu_shift"][l])
        for nm, key in [("w0", "w0"), ("a0", "a0"), ("k_k", "k_k"), ("k_a", "k_a"), ("gn_g", "gn_g"),
                        ("gn_b", "gn_b")]:
            put(l, nm, inp[key][l])
        put(l, "r_k", inp["r_k"][l].reshape(-1))
        if l >= 1:
            put(l, "v0", inp["v0"][l - 1])
        for j in range(3):
            put(l, f"cw{j}", inp["conv_w"][l][j])
        put(l, "cb", inp["conv_b"][l])
    return pk.reshape(128, depth * NCOL)


def in_groups():
    g = []
    for i in range(4):
        g.append(("q", i * 128, 128, i))
    for i in range(4):
        g.append(("k", 512 + i * 128, 128, i))
    g.append(("f", 1536, 8, 0))
    for i in range(14):
        g.append(("rw", 1544 + i * 128, 128, i))
    for i in range(16):
        g.append(("gate", 3336 + i * 128, 128, i))
    return g


class Prog:
    def __init__(self, S, NB, depth, debug=False, stages="ABCDE"):
        self.S, self.NB, self.depth, self.debug, self.stages = S, NB, depth, debug, stages
        self.T = S * NB
        self.NT = self.T // TT
        self.TPS = S // TT
        nc = bass.Bass("TRN2", target_bir_lowering=False)
        self.nc = nc
        self.s = Sch(nc)
        self._ps_i = 0
        self.build()

    def psb_(self, name, shape, dt=F32):
        return Tl(self.nc.alloc_sbuf_tensor(name, list(shape), dt))

    def sb(self, name, shape, dt=F32):
        esz = 2 if dt == BF16 else 4
        nbytes = int(np.prod(shape[1:])) * esz
        nbytes = (nbytes + 63) // 64 * 64
        off = self.arena_off
        assert off + nbytes <= self.arena_end, (name, off, nbytes, self.arena_end)
        self.arena_off = off + nbytes
        self._uid += 1
        return Tl(self.nc.alloc_sbuf_tensor_at(f"{name}_{self._uid}", list(shape), dt, offset=off))

    def stage_begin(self):
        s = self.s
        for E in s.e:
            for k in s.e:
                if s.cnt[k] > 0:
                    s._wait(E, k, s.cnt[k])
            for q in s.dsem:
                n = s.dcnt[q]
                for i in range(s.NDS):
                    cnt_i = (n - i + s.NDS - 1) // s.NDS
                    if cnt_i > 0:
                        s._wait(E, (q, i), 16 * cnt_i)
        self.arena_off = self.arena_start
        self.ep = {}

    def dram(self, name, shape, dt, kind="Internal"):
        if kind == "Internal" and self.debug:
            kind = "ExternalOutput"
        return Tl(self.nc.dram_tensor(name, list(shape), dt, kind=kind))

    def ps(self):
        p = self.psb[self._ps_i % 8]
        self._ps_i += 1
        return p

    def build(self):
        nc, s = self.nc, self.s
        T, L = self.T, self.depth
        self.xT = self.dram("xT", [D, T], F32, kind="ExternalInput")
        self.pT = self.dram("pT", [L, PLE, T], F32, kind="ExternalInput")
        self.out = self.dram("outT", [D, T], F32, kind="ExternalOutput")
        W = {}
        for name, shp in [("w_in", (L, D, NIN)), ("w_decay_up", (L, 64, RW)), ("w_aaa_up", (L, 64, RW)),
                          ("w_gate_up", (L, 128, RW)), ("w_vres_down", (max(L - 1, 1), D, 32)),
                          ("w_vres_up", (max(L - 1, 1), 32, RW)), ("w_o_fox", (L, FOXW, D)),
                          ("w_o_rwkv", (L, RW, D)), ("w_out", (L, D, D)), ("w_up", (L, D, 2 * DFF)),
                          ("w_down", (L, DFF, D)), ("w_ple_gate", (L, D, D)), ("w_ple_up", (L, PLE, D))]:
            W[name] = self.dram(name, shp, F32, kind="ExternalInput")
        self.W = W
        self.colpack_d = self.dram("colpack", [128, L * NCOL], F32, kind="ExternalInput")
        self.const_d = {k: self.dram("c_" + k, list(v), F32, kind="ExternalInput") for k, v in CONST_SHAPES.items()}
        self.xs = self.dram("xs", [D, T], F32)
        self.qa = self.dram("qa", [8, 70, T], BF16)
        self.ka = self.dram("ka", [8, 70, T], BF16)
        self.zr = self.dram("zr", [1792, T], F32)
        self.vf = self.dram("vf", [RW, T], F32)
        self.gt = self.dram("gt", [2 * D, T], BF16)
        self.yf = self.dram("yf", [FOXW, T], BF16)
        self.yr = self.dram("yr", [RW, T], BF16)
        self.vt = self.dram("vt", [T, FOXW], BF16)
        self.actT = self.dram("actT", [DFF, T], BF16)
        self.psb = [Tl(nc.alloc_psum_tensor(f"ps{i}", [128, 512], F32)) for i in range(8)]
        self.colpack = self.psb_("colpack_s", [128, L * NCOL])
        s.dma("sp", self.colpack[:], self.colpack_d[:, :], r=[self.colpack_d.R()], w=[self.colpack.R()])
        self.c = {}
        for k, shp in CONST_SHAPES.items():
            if k in ("fullmask", "trimask_f"):
                continue
            self.c[k] = self.psb_("cs_" + k, shp)
            s.dma("sp", self.c[k][:], self.const_d[k][:, :], r=[self.const_d[k].R()], w=[self.c[k].R()])
        for k in ["ident_f", "blk2_f", "fullmask"]:
            self.c[k + "_b"] = self.psb_("cb_" + k, CONST_SHAPES[k], BF16)
            s.dma("pool", self.c[k + "_b"][:], self.const_d[k][:, :], r=[self.const_d[k].R()],
                  w=[self.c[k + "_b"].R()])
        self.c["ones_b"] = self.psb_("cb_ones", [128, 512], BF16)
        s.dma("pool", self.c["ones_b"][:], self.const_d["ones_f"][:, :], r=[self.const_d["ones_f"].R()],
              w=[self.c["ones_b"].R()])
        self.carry = self.psb_("carry", [128, 16])
        self.ccar = self.psb_("ccar", [8, 2])
        self.ucarry = self.psb_("ucarry", [128, 2 * NJ, 2])
        self.negb = self.psb_("negb", [8, 4])
        for ll in range(L):
            s.op("dve", lambda e, ll=ll: e.tensor_scalar(out=self.negb[:, ll:ll + 1], in0=self.col(ll, "b_f", 0, 0, 8),
                                                          scalar1=-1.0, scalar2=None, op0=ALU.mult),
                 r=[self.colpack.R()], w=[self.negb.R()])
        self._uid = 0
        base0 = int(nc.sbuf_base)
        self.arena_start = (base0 + 63) // 64 * 64
        left = (int(nc.sbuf_bytes_remaining) - 256 - (self.arena_start - base0)) // 64 * 64
        slab = nc.alloc_sbuf_tensor("arena", [128, (left + self.arena_start - base0) // 4], F32)
        self.arena_end = self.arena_start + left
        assert int(nc.sbuf_base) >= self.arena_end, (nc.sbuf_base, self.arena_end)
        self.stage_begin()
        ow = self.sb("ones_wide", [3, T], BF16)
        s.op("dve", lambda e: e.memset(ow[:], 1.0), w=[ow.R()])
        for h in range(8):
            s.dma("sp", self.qa[h, 67:70, :], ow[:], r=[ow.R()], w=[self.qa.R(("ones", h))])
            s.dma("sp", self.ka[h, 64:67, :], ow[:], r=[ow.R()], w=[self.ka.R(("ones", h))])
        for l in range(L):
            src = self.xT if l == 0 else self.xs
            if "A" in self.stages:
                self.stage_A(l, src)
            if "B" in self.stages:
                self.stage_B(l)
            if "C" in self.stages:
                self.stage_C(l)
            if "D" in self.stages:
                self.stage_D(l, src)
            if "E" in self.stages or "1" in self.stages:
                self.stage_E1(l)
            if "E" in self.stages or "2" in self.stages:
                self.stage_E2(l, last=(l == L - 1))
        s.finish()

    def col(self, l, name, j=0, p0=0, p1=128):
        o, n = COLS[name]
        assert j < n
        c0 = l * NCOL + o + j
        return self.colpack[p0:p1, c0:c0 + 1]

    def rmsnorm_tile(self, l, gname, xt, ht, tag):
        s = self.s
        sq = ht
        s.op("act", lambda e: e.activation(out=sq[:], in_=xt[:], func=AF.Square), r=[xt.R()],
             w=[ht.R(kc) for kc in range(KC)])
        ps = self.ps()
        for kc in range(KC):
            s.op("pe", lambda e, kc=kc: e.matmul(ps[:, :], lhsT=self.c["ones_b"][:, 0:128], rhs=sq[:, kc, :],
                                                 start=(kc == 0), stop=(kc == KC - 1)),
                 r=[self.c["ones_b"].R(), ht.R(kc)], w=[ps.R()])
        rs = self.tmpA("rn_rs", [128, TT], nbuf=1)
        self.rsqrt_ps(rs, ps, 1.0 / D, RMS_EPS, 1.0)
        for kc in range(KC):
            eng = "dve" if kc % 2 == 0 else "pool"
            if eng == "dve":
                s.op("dve", lambda e, kc=kc: e.scalar_tensor_tensor(out=ht[:, kc, :], in0=xt[:, kc, :],
                                                                     scalar=self.col(l, gname, kc), in1=rs[:],
                                                                     op0=ALU.mult, op1=ALU.mult),
                     r=[xt.R(), rs.R(), self.colpack.R()], w=[ht.R(kc)])
            else:
                tmp = self.tmpA("rn_nt", [128, TT], nbuf=2)
                s.op("pool", lambda e, kc=kc, tmp=tmp: e.tensor_tensor(out=tmp[:], in0=xt[:, kc, :], in1=rs[:], op=ALU.mult),
                     r=[xt.R(), rs.R()], w=[tmp.R()])
                s.op("pool", lambda e, kc=kc, tmp=tmp: e.tensor_scalar(out=ht[:, kc, :], in0=tmp[:],
                                                               scalar1=self.col(l, gname, kc), scalar2=None,
                                                               op0=ALU.mult),
                     r=[tmp.R(), self.colpack.R()], w=[ht.R(kc)])

    def stage_A(self, l, src):
        nc, s = self.nc, self.s
        T = self.T
        self.stage_begin()
        self.wbig = self.sb("wbig", [128, KC * NIN], BF16)
        self.wbig_R = self.wbig.R()
        self.xt_t = [self.sb("xt0", [128, KC, TT])] * 2
        self.ht_t = [self.sb(f"ht{i}", [128, KC, TT], BF16) for i in range(2)]
        self.zraw = [self.sb(f"zraw{i}", [128, TT + 1]) for i in range(2)]
        W = self.W
        win = self.wbig
        winv = self.wbig.t[:, 0:KC * NIN].rearrange("p (k n) -> p k n", k=KC)
        self.load_wb(self.wbig, winv, "w_in", l, [0, 1024, 1544, 2568, 3336, 4360, NIN], order=[1, 0, 2, 3, 4, 5])
        if l >= 1:
            self.wvd = self.sb("wvd", [128, KC, 32], BF16)
            self.wvu = self.sb("wvu", [32, RW], BF16)
            s.dma("pool", self.wvd[:], W["w_vres_down"].t[l - 1].rearrange("(k p) n -> p k n", p=128),
                  r=[W["w_vres_down"].R()], w=[self.wvd.R()])
            s.dma("pool", self.wvu[:], W["w_vres_up"].t[l - 1], r=[W["w_vres_up"].R()], w=[self.wvu.R()])
        groups = in_groups()
        srcv = src.t.rearrange("(k p) t -> p k t", p=128)
        self.deferred = []

        def run_deferred():
            run, self.deferred = self.deferred, []
            for f in run:
                f()

        def prep_tile(tt):
            xt = self.xt_t[tt % 2]
            ht = self.ht_t[tt % 2]
            for kc in range(KC):
                s.dma("sp", xt[:, kc, :], srcv[:, kc, tt * TT:(tt + 1) * TT], r=[src.R(("tile", tt))], w=[xt.R()])
            self.rmsnorm_tile(l, "g_mix", xt, ht, "A")

        prep_tile(0)
        for tt in range(self.NT):
            t0 = tt * TT
            seq_start = (tt % self.TPS == 0)
            xt = self.xt_t[tt % 2]
            ht = self.ht_t[tt % 2]
            hR = [ht.R(kc) for kc in range(KC)]
            for sub in range(4):
                ps = self.ps()
                for kc in range(KC):
                    s.op("pe", lambda e, kc=kc, sub=sub: e.matmul(ps[:, :], lhsT=ht[:, kc, sub * 128:(sub + 1) * 128],
                                                                   rhs=winv[:, kc, 1024:1536], start=(kc == 0),
                                                                   stop=(kc == KC - 1)),
                         r=[hR[kc], self.wres(self.wbig, 1024)], w=[ps.R()])
                blk = tt * 4 + sub
                vo = self.tmpA("vo", [128, FOXW], BF16)
                s.op("act", lambda e, vo=vo, ps=ps: e.activation(out=vo[:], in_=ps[:, :], func=AF.Copy),
                     r=[ps.R()], w=[vo.R()])
                s.dma("sp", self.vt.t[blk * 128:(blk + 1) * 128, :], vo[:], r=[vo.R()], w=[self.vt.R(blk)])
            if l >= 1:
                ps = self.ps()
                for kc in range(KC):
                    s.op("pe", lambda e, kc=kc, ps=ps: e.matmul(ps[0:32, :], lhsT=self.wvd[:, kc, :], rhs=ht[:, kc, :],
                                                                 start=(kc == 0), stop=(kc == KC - 1)),
                         r=[hR[kc], self.wvd.R()], w=[ps.R()])
                vd = self.tmpA("vd", [32, TT], BF16)
                s.op("act", lambda e, ps=ps: e.activation(out=vd[:], in_=ps[0:32, :], func=AF.Copy), r=[ps.R()],
                     w=[vd.R()])
            for gi, (kind, c0, wd, idx) in enumerate(groups):
                ps = self.ps()
                for kc in range(KC):
                    s.op("pe", lambda e, kc=kc, ps=ps, c0=c0, wd=wd: e.matmul(ps[0:wd, :], lhsT=winv[:, kc, c0:c0 + wd],
                                                                               rhs=ht[:, kc, :], start=(kc == 0),
                                                                               stop=(kc == KC - 1)),
                         r=[hR[kc], self.wres(self.wbig, c0)], w=[ps.R()])
                run_deferred()
                if gi == 24 and tt + 1 < self.NT:
                    prep_tile(tt + 1)
                if kind in ("q", "k"):
                    self.epi_qk(l, kind, idx, ps, t0)
                elif kind == "f":
                    self.epi_f(l, ps, t0, seq_start)
                elif kind == "rw":
                    self.epi_rw(l, idx, ps, t0, tt, seq_start, vd if l >= 1 else None)
                else:
                    self.epi_gate(l, idx, ps, t0)
        run_deferred()

    def rsqrt_ps(self, out, ps, scale, eps, mult, np_=128):
        s = self.s
        if not hasattr(self, "_fconst"):
            self._fconst = {}
        def fc(v):
            if v not in self._fconst:
                t = self.psb_(f"fc{len(self._fconst)}", [128, 1])
                s.op("pool", lambda e: e.memset(t[:], float(v)), w=[t.R()])
                self._fconst[v] = t
            return self._fconst[v]
        be = fc(eps)
        bm = fc(float(np.log(mult)))
        s.op("act", lambda e: e.activation(out=out[0:np_, :], in_=ps[0:np_, :], func=AF.Ln, bias=be[0:np_, :], scale=float(scale)),
             r=[ps.R(), be.R()], w=[out.R()])
        s.op("act", lambda e: e.activation(out=out[0:np_, :], in_=out[0:np_, :], func=AF.Exp, bias=bm[0:np_, :], scale=-0.5),
             r=[out.R(), bm.R()], w=[out.R()])

    def ps_rot(self, lo, hi):
        key = (lo, hi)
        if not hasattr(self, "_psr"):
            self._psr = {}
        i = self._psr.get(key, 0)
        self._psr[key] = i + 1
        return self.psb[lo + i % (hi - lo)]

    def stage_B(self, l):
        s = self.s
        S, NB = self.S, self.NB
        self.stage_begin()
        QA = [self.sb(f"QA{i}", [70, S], BF16) for i in range(2)]
        KA = [self.sb(f"KA{i}", [70, S], BF16) for i in range(2)]
        Vb = [self.sb(f"Vb{i}", [128, S // 128, FOXW], BF16) for i in range(2)]
        PT = [self.sb(f"PT{i}", [128, TT], BF16) for i in range(4)]
        rden = [self.sb(f"rden{i}", [64, TT]) for i in range(2)]
        yt = [self.sb(f"yt{i}", [64, TT], BF16) for i in range(2)]
        fm = self.c["fullmask_b"]
        idb = self.c["ident_f_b"]
        onb = self.c["ones_b"]
        NQ = S // TT
        LOOK = 3
        groups = [(b, h) for b in range(NB) for h in range(8)]
        bufs = {}

        def load_group(gi):
            b, h = groups[gi]
            qa, ka = QA[gi % 2], KA[gi % 2]
            if h == 0:
                s.dma("sp", Vb[b % 2][:], self.vt.t[b * S:(b + 1) * S, :].rearrange("(c p) n -> p c n", p=128),
                      r=[self.vt.R(blk) for blk in range(b * S // 128, (b + 1) * S // 128)], w=[Vb[b % 2].R()])
            s.dma("sp", qa[:], self.qa.t[h, :, b * S:(b + 1) * S],
                  r=[self.qa.R(("ones", h))] + [self.qa.R(("qk", h, b * S + j * TT)) for j in range(NQ)] +
                    [self.qa.R(("c", jj, b * S + j * TT)) for j in range(NQ) for jj in range(3)], w=[qa.R()])
            s.dma("sp", ka[:], self.ka.t[h, :, b * S:(b + 1) * S],
                  r=[self.ka.R(("ones", h))] + [self.ka.R(("qk", h, b * S + j * TT)) for j in range(NQ)] +
                    [self.ka.R(("c", jj, b * S + j * TT)) for j in range(NQ) for jj in range(3)], w=[ka.R()])

        work = []
        for gi, (b, h) in enumerate(groups):
            for j in range(NQ):
                nch = 4 * (j + 1)
                for i in range(nch):
                    work.append((gi, b, h, j, i, nch))
        state = {}

        def emit_qk(w):
            gi, b, h, j, i, nch = w
            if j == 0 and i == 0:
                if gi == 0:
                    load_group(0)
                if gi + 1 < len(groups):
                    load_group(gi + 1)
            qa, ka = QA[gi % 2], KA[gi % 2]
            sc = self.ps_rot(4, 8)
            state[w] = sc
            r_ = i - 4 * j
            diag = r_ >= 0
            s.op("pe", lambda e: e.matmul(sc[:, :], lhsT=ka[:, i * 128:(i + 1) * 128], rhs=qa[:, j * TT:(j + 1) * TT], start=True,
                                          stop=not diag), r=[ka.R(), qa.R()], w=[sc.R()])
            if diag:
                s.op("pe", lambda e: e.matmul(sc[:, :], lhsT=idb[:], rhs=fm[:, r_ * 512:(r_ + 1) * 512], start=False, stop=True),
                     r=[idb.R(), fm.R()], w=[sc.R()])

        cnt = {"pt": 0, "acc": None, "den": None, "ep": 0}

        def emit_rest(w):
            gi, b, h, j, i, nch = w
            sc = state.pop(w)
            if i == 0:
                cnt["acc"] = self.ps_rot(0, 2)
                cnt["den"] = self.ps_rot(2, 4)
            acc, den = cnt["acc"], cnt["den"]
            pt = PT[cnt["pt"] % 4]
            cnt["pt"] += 1
            vb = Vb[b % 2]
            s.op("act", lambda e: e.activation(out=pt[:], in_=sc[:, :], func=AF.Exp), r=[sc.R()], w=[pt.R()])
            s.op("pe", lambda e: e.matmul(acc[0:64, :], lhsT=vb[:, i, h * 64:(h + 1) * 64], rhs=pt[:], start=(i == 0),
                                          stop=(i == nch - 1)), r=[vb.R(), pt.R()], w=[acc.R()])
            s.op("pe", lambda e: e.matmul(den[0:64, :], lhsT=onb[:, 0:64], rhs=pt[:], start=(i == 0), stop=(i == nch - 1)),
                 r=[onb.R(), pt.R()], w=[den.R()])
            if i == nch - 1:
                rd = rden[cnt["ep"] % 2]
                y = yt[cnt["ep"] % 2]
                cnt["ep"] += 1
                s.op("dve", lambda e: e.reciprocal(out=rd[:], in_=den[0:64, :]), r=[den.R()], w=[rd.R()])
                s.op("dve", lambda e: e.tensor_tensor(out=y[:], in0=acc[0:64, :], in1=rd[:], op=ALU.mult), r=[acc.R(), rd.R()],
                     w=[y.R()])
                t0 = b * S + j * TT
                s.dma("sp", self.yf.t[h * 64:(h + 1) * 64, t0:t0 + TT], y[:], r=[y.R()], w=[self.yf.R((h, t0))])

        for k in range(min(LOOK, len(work))):
            emit_qk(work[k])
        for k, w in enumerate(work):
            if k + LOOK < len(work):
                emit_qk(work[k + LOOK])
            emit_rest(w)

    def stage_C(self, l):
        import os
        cut = int(os.environ.get("CCUT", "9"))
        use_b = os.environ.get("RWDT", "bf16") == "bf16"
        RD = BF16 if use_b else mybir.dt.float32r
        s = self.s
        S, NB, T = self.S, self.NB, self.T
        W = self.W
        self.stage_begin()
        c = self.c
        idf, blkf, scanm, mS, mA, id8 = c["ident_f"], c["blk2_f"], c["scanmask"], c["mask_S"], c["mask_A"], c["ident8"]

        def V(E, fn, r, w):
            s.op(E, fn, r=[x.R() for x in r], w=[x.R() for x in w])

        def cp(E, out_ap, in_ap, r, w):
            if E == "act":
                V("act", lambda e: e.activation(out=out_ap, in_=in_ap, func=AF.Copy), r, w)
            else:
                V(E, lambda e: e.tensor_copy(out=out_ap, in_=in_ap), r, w)

        Wd = self.sb("Wd", [64, RW], BF16)
        Wa = self.sb("Wa", [64, RW], BF16)
        Wg = self.sb("Wg", [128, RW], BF16)
        s.dma("pool", Wd[:], W["w_decay_up"].t[l], r=[W["w_decay_up"].R()], w=[Wd.R()])
        s.dma("pool", Wa[:], W["w_aaa_up"].t[l], r=[W["w_aaa_up"].R()], w=[Wa.R()])
        s.dma("pool", Wg[:], W["w_gate_up"].t[l], r=[W["w_gate_up"].R()], w=[Wg.R()])
        omka = self.sb("omka", [128, 4])
        o_ka = l * NCOL + COLS["k_a"][0]
        V("dve", lambda e: e.tensor_scalar(out=omka[:], in0=self.colpack[:, o_ka:o_ka + 4], scalar1=-1.0, scalar2=1.0,
                                            op0=ALU.mult, op1=ALU.add), [self.colpack], [omka])
        epsg = self.sb("epsg", [64, 1])
        V("pool", lambda e: e.memset(epsg[:], GN_EPS), [], [epsg])
        AR = [self.sb(f"AR{i}", [128, 8, 2, CH], RD) for i in range(4)]
        BT = [self.sb(f"BT{i}", [128, TT], RD) for i in range(4)]
        KT = [self.sb(f"KT{i}", [128, TT], RD) for i in range(4)]
        ARo = [self.sb(f"ARo{i}", [64, 8, 2, CH], RD) for i in range(4)]
        BTo = [self.sb(f"BTo{i}", [64, TT], RD) for i in range(4)]
        KTo = [self.sb(f"KTo{i}", [64, TT], RD) for i in range(4)]
        VR = [self.sb(f"VR{i}", [128, TT]) for i in range(4)]
        G = [self.sb(f"G{i}", [128, TT], BF16) for i in range(4)]
        BG = [self.sb(f"BG{i}", [128, TT], BF16) for i in range(4)]
        PCp = self.sb("PCp", [128, 8]); PCo = self.sb("PCo", [64, 8])
        PCall = self.sb("PCall", [64, 8, 8])
        H = self.sb("H", [64, 512], RD)
        Hf = self.sb("Hf", [64, 512])
        Ht = self.sb("Ht", [64, 512])
        YN = self.sb("YN", [64, 8, 512])
        dwt = self.sb("dwt", [64, TT]); dat = self.sb("dat", [64, TT]); dgt = self.sb("dgt", [128, TT])
        tdw = self.sb("tdw", [64, TT], BF16); dab = self.sb("dab", [64, TT], BF16); sdg = self.sb("sdg", [128, TT], BF16)
        rT = self.sb("rT", [128, TT]); krT = self.sb("krT", [128, TT])
        sig = self.sb("sig", [128, TT]); aa = self.sb("aa", [128, TT]); kk = self.sb("kk", [128, TT])
        prod = self.sb("prod", [128, TT]); rn = self.sb("rn", [128, TT]); gf = self.sb("gf", [128, TT])
        Lc = self.sb("Lc", [128, TT])
        eL = self.sb("eL", [128, TT]); eLm = self.sb("eLm", [128, TT]); enL = self.sb("enL", [128, TT])
        TOK = [[self.sb(f"tok{i}{b}", [64, 512], RD) for i in range(3)] for b in range(2)]
        SMb = [[self.sb(f"SM{i}{b}", [64, 512], RD) for i in range(4)] for b in range(2)]
        Tfin = [self.sb(f"Tfin{b}", [64, 512], RD) for b in range(2)]
        Xa = [self.sb(f"Xa{i}", [64, 512], RD) for i in range(2)]
        XTa = [self.sb(f"XTa{i}", [64, 512], RD) for i in range(2)]
        TTa = [self.sb(f"TTa{i}", [64, 512], RD) for i in range(2)]
        W0s = self.sb("W0s", [64, 512], RD); Us = self.sb("Us", [64, 512], RD)
        YQ = self.sb("YQ", [64, 8, 512])
        st = {k: self.sb("st_" + k, [64, 64]) for k in ["sum", "sq", "m", "m2", "var", "rstd"]}
        po1 = self.sb("po1", [128, TT]); pob = self.sb("pob", [128, TT], BF16)

        def rr(ap):
            return ap

        def MM(e, out, lhsT, rhs, start, stop):
            return e.matmul(out, lhsT=rr(lhsT), rhs=rr(rhs), start=start, stop=stop)

        def colv(name, hp):
            return self.col(l, name, hp)

        def ar(h):
            return AR[h // 2] if h % 2 == 0 else ARo[h // 2]

        def bt(h):
            return BT[h // 2] if h % 2 == 0 else BTo[h // 2]

        def kt(h):
            return KT[h // 2] if h % 2 == 0 else KTo[h // 2]

        for tt in range(self.NT):
            t0 = tt * TT
            zr = self.zr
            s.dma("sp", dwt[:], zr.t[1536:1600, t0:t0 + TT], r=[zr.R((12, t0))], w=[dwt.R()])
            s.dma("sp", dat[:], zr.t[1600:1664, t0:t0 + TT], r=[zr.R((12, t0))], w=[dat.R()])
            s.dma("sp", dgt[:], zr.t[1664:1792, t0:t0 + TT], r=[zr.R((13, t0))], w=[dgt.R()])
            V("act", lambda e: e.activation(out=tdw[:], in_=dwt[:], func=AF.Tanh), [dwt], [tdw])
            V("pool", lambda e: e.tensor_copy(out=dab[:], in_=dat[:]), [dat], [dab])
            V("act", lambda e: e.activation(out=sdg[:], in_=dgt[:], func=AF.Sigmoid), [dgt], [sdg])
            for hp in range(4):
                hs = slice(hp * 128, (hp + 1) * 128)
                s.dma("sp", rT[:], zr.t[hp * 128:(hp + 1) * 128, t0:t0 + TT], r=[zr.R((hp, t0))], w=[rT.R()])
                s.dma("sp", krT[:], zr.t[512 + hp * 128:512 + (hp + 1) * 128, t0:t0 + TT], r=[zr.R((4 + hp, t0))], w=[krT.R()])
                s.dma("sp", VR[hp][:], zr.t[1024 + hp * 128:1024 + (hp + 1) * 128, t0:t0 + TT], r=[zr.R((8 + hp, t0))],
                      w=[VR[hp].R()])
                p1 = self.ps()
                V("pe", lambda e: e.matmul(p1[:, :], lhsT=Wd[:, hs], rhs=tdw[:], start=True, stop=True), [Wd, tdw], [p1])
                V("act", lambda e: e.activation(out=sig[:], in_=p1[:, :], func=AF.Sigmoid, bias=colv("w0", hp), scale=1.0),
                  [p1, self.colpack], [sig])
                p2 = self.ps()
                V("pe", lambda e: e.matmul(p2[:, :], lhsT=Wa[:, hs], rhs=dab[:], start=True, stop=True), [Wa, dab], [p2])
                V("act", lambda e: e.activation(out=aa[:], in_=p2[:, :], func=AF.Sigmoid, bias=colv("a0", hp), scale=1.0),
                  [p2, self.colpack], [aa])
                p3 = self.ps()
                V("pe", lambda e: e.matmul(p3[:, :], lhsT=Wg[:, hs], rhs=sdg[:], start=True, stop=True), [Wg, sdg], [p3])
                cp("act", gf[:], p3[:, :], [p3], [gf])
                cp("pool", G[hp][:], gf[:], [gf], [G[hp]])
                V("act", lambda e: e.activation(out=kk[:], in_=krT[:], func=AF.Copy, scale=colv("k_k", hp)),
                  [krT, self.colpack], [kk])
                V("pool", lambda e: e.tensor_tensor(out=prod[:], in0=kk[:], in1=kk[:], op=ALU.mult), [kk], [prod])
                p4 = self.ps()
                V("pe", lambda e: e.matmul(p4[:, :], lhsT=blkf[:], rhs=prod[:], start=True, stop=True), [blkf, prod], [p4])
                self.rsqrt_ps(rn, p4, 1.0, 1e-24, 1.0)
                V("dve", lambda e: e.tensor_tensor(out=kk[:], in0=kk[:], in1=rn[:], op=ALU.mult), [kk, rn], [kk])
                V("dve", lambda e: e.tensor_scalar(out=rn[:], in0=aa[:], scalar1=colv("k_a", hp), scalar2=omka[:, hp:hp + 1],
                                                    op0=ALU.mult, op1=ALU.add), [aa, self.colpack, omka], [rn])
                V("dve", lambda e: e.tensor_tensor(out=krT[:], in0=krT[:], in1=rn[:], op=ALU.mult), [krT, rn], [krT])
                V("pool", lambda e: e.tensor_tensor(out=aa[:], in0=kk[:], in1=aa[:], op=ALU.mult), [kk, aa], [aa])
                V("act", lambda e: e.activation(out=sig[:], in_=sig[:], func=AF.Copy, scale=-float(np.exp(-0.5))),
                  [sig], [sig])
                V("dve", lambda e: e.tensor_tensor_scan(out=Lc[:], data0=scanm[:], data1=sig[:], initial=0.0, op0=ALU.mult,
                                                         op1=ALU.add), [scanm, sig], [Lc])
                V("pool", lambda e: e.tensor_tensor(out=sig[:], in0=Lc[:], in1=sig[:], op=ALU.subtract), [Lc, sig], [sig])
                V("act", lambda e: e.activation(out=eL[:], in_=Lc[:], func=AF.Exp), [Lc], [eL])
                V("act", lambda e: e.activation(out=eLm[:], in_=sig[:], func=AF.Exp), [sig], [eLm])
                V("act", lambda e: e.activation(out=enL[:], in_=Lc[:], func=AF.Exp, scale=-1.0), [Lc], [enL])
                arv = AR[hp]
                V("dve", lambda e: e.scalar_tensor_tensor(out=arv[:, :, 0, :], in0=kk[:].rearrange("p (c t) -> p c t", t=CH),
                                                           scalar=-1.0, in1=eLm[:].rearrange("p (c t) -> p c t", t=CH),
                                                           op0=ALU.mult, op1=ALU.mult), [kk, eLm], [arv])
                V("dve", lambda e: e.tensor_tensor(out=arv[:, :, 1, :], in0=rT[:].rearrange("p (c t) -> p c t", t=CH),
                                                    in1=eL[:].rearrange("p (c t) -> p c t", t=CH), op=ALU.mult), [rT, eL], [arv])
                V("pool", lambda e: e.tensor_tensor(out=BT[hp][:], in0=aa[:], in1=enL[:], op=ALU.mult), [aa, enL], [BT[hp]])
                V("dve", lambda e: e.tensor_tensor(out=KT[hp][:], in0=krT[:], in1=enL[:], op=ALU.mult), [krT, enL], [KT[hp]])
                V("pool", lambda e: e.tensor_copy(out=PCp[:], in_=eL[:, CH - 1::CH]), [eL], [PCp])
                s.dma("sp", ARo[hp][:], AR[hp][64:128, :, :, :], r=[AR[hp].R()], w=[ARo[hp].R()])
                s.dma("sp", BTo[hp][:], BT[hp][64:128, :], r=[BT[hp].R()], w=[BTo[hp].R()])
                s.dma("sp", KTo[hp][:], KT[hp][64:128, :], r=[KT[hp].R()], w=[KTo[hp].R()])
                s.dma("sp", PCo[:], PCp[64:128, :], r=[PCp.R()], w=[PCo.R()])
                V("pool", lambda e: e.tensor_copy(out=PCall[:, :, 2 * hp], in_=PCp[0:64, :]), [PCp], [PCall])
                V("pool", lambda e: e.tensor_copy(out=PCall[:, :, 2 * hp + 1], in_=PCo[:]), [PCo], [PCall])
                V("dve", lambda e: e.scalar_tensor_tensor(out=prod[:], in0=rT[:], scalar=colv("r_k", hp), in1=krT[:],
                                                           op0=ALU.mult, op1=ALU.mult), [rT, krT, self.colpack], [prod])
                p5 = self.ps()
                V("pe", lambda e: e.matmul(p5[:, :], lhsT=blkf[:], rhs=prod[:], start=True, stop=True), [blkf, prod], [p5])
                V("dve", lambda e: e.tensor_tensor(out=rn[:], in0=p5[:, :], in1=VR[hp][:], op=ALU.mult), [p5, VR[hp]], [rn])
                V("pool", lambda e: e.tensor_tensor(out=BG[hp][:], in0=rn[:], in1=gf[:], op=ALU.mult), [rn, gf], [BG[hp]])
            def indep(cc):
                cs = slice(cc * CH, (cc + 1) * CH)
                b = cc % 2
                Btok, Ktok, Vtok = TOK[b]
                SM = SMb[b]
                for srcs, dst, eng in [(BT, Btok, "act"), (KT, Ktok, "dve"), (VR, Vtok, "act")]:
                    pt_ = self.ps()
                    if use_b and srcs is not VR:
                        pv_ = pt_.t.bitcast(BF16)
                        idb_ = c["ident_f_b"]
                        for hp in range(4):
                            V("pe", lambda e, hp=hp: e.transpose(out=pv_[0:64, hp * 128:(hp + 1) * 128], in_=srcs[hp][:, cs],
                                                                 identity=idb_[:]), [srcs[hp], idb_], [pt_])
                        cp(eng, dst[:], pv_[0:64, 0:512], [pt_], [dst])
                    else:
                        for hp in range(4):
                            V("pe", lambda e, hp=hp: e.transpose(out=pt_[0:64, hp * 128:(hp + 1) * 128],
                                                                 in_=(srcs[hp][:, cs] if srcs is VR else srcs[hp][:, cs].bitcast(F32)),
                                                                 identity=idf[:]), [srcs[hp], idf], [pt_])
                        cp(eng, dst[:], pt_[0:64, :], [pt_], [dst])
                    yield
                for hp in range(4):
                    pS = self.ps()
                    for par in range(2):
                        h = 2 * hp + par
                        rhs = ar(h)[0:64, cc, :, :].rearrange("p a t -> p (a t)")
                        V("pe", lambda e, par=par, h=h, rhs=rhs: MM(e, pS[0:64, par * 256:par * 256 + 128], lhsT=bt(h)[0:64, cs],
                                                                   rhs=rhs, start=True, stop=True), [bt(h), ar(h)], [pS])
                        V("pe", lambda e, par=par, h=h, rhs=rhs: MM(e, pS[0:64, par * 256 + 128:par * 256 + 256],
                                                                   lhsT=kt(h)[0:64, cs], rhs=rhs, start=True, stop=True),
                          [kt(h), ar(h)], [pS])
                    V("dve", lambda e, hp=hp, pS=pS: e.tensor_tensor(out=SM[hp][:], in0=pS[0:64, :], in1=mS[:], op=ALU.mult),
                      [pS, mS], [SM[hp]])
                    if hp % 2 == 1:
                        yield
                pA = self.ps()
                for h in range(8):
                    V("pe", lambda e, h=h: MM(e, pA[0:64, h * 64:(h + 1) * 64], lhsT=ar(h)[0:64, cc, 0, :],
                                              rhs=bt(h)[0:64, cs], start=True, stop=True), [ar(h), bt(h)], [pA])
                X, XT, Tt = Xa[0], XTa[0], TTa[0]
                V("dve", lambda e: e.tensor_tensor(out=X[:], in0=pA[0:64, :], in1=mA[:], op=ALU.mult), [pA, mA], [X])
                for hp in range(4):
                    V("pool", lambda e, hp=hp: e.tensor_copy(
                        out=XT[:, hp * 128:(hp + 1) * 128].rearrange("p (a t) -> p a t", a=2),
                        in_=SM[hp][:, :].rearrange("p (a t) -> p a t", a=2)[:, :, 0:64]), [SM[hp]], [XT])
                V("pool", lambda e: e.tensor_tensor(out=Tt[:], in0=XT[:], in1=id8[:], op=ALU.add), [XT, id8], [Tt])
                yield
                for k in range(1, 6):
                    Xn, XTn, Tn = Xa[k % 2], XTa[k % 2], (TTa[k % 2] if k < 5 else Tfin[b])
                    pX = self.ps()
                    for h in range(8):
                        hsl = slice(h * 64, (h + 1) * 64)
                        V("pe", lambda e, hsl=hsl: MM(e, pX[0:64, hsl], lhsT=XT[:, hsl], rhs=X[:, hsl], start=True, stop=True),
                          [XT, X], [pX])
                    if k < 5:
                        pXT = self.ps()
                        for h in range(8):
                            hsl = slice(h * 64, (h + 1) * 64)
                            V("pe", lambda e, hsl=hsl: MM(e, pXT[0:64, hsl], lhsT=X[:, hsl], rhs=XT[:, hsl], start=True,
                                                           stop=True), [XT, X], [pXT])
                    cp("act", Xn[:], pX[0:64, :], [pX], [Xn])
                    if k < 5:
                        cp("dve", XTn[:], pXT[0:64, :], [pXT], [XTn])
                    yield
                    pT = self.ps()
                    for h in range(8):
                        hsl = slice(h * 64, (h + 1) * 64)
                        V("pe", lambda e, hsl=hsl: MM(e, pT[0:64, hsl], lhsT=Xn[:, hsl], rhs=Tt[:, hsl], start=True, stop=True),
                          [Xn, Tt], [pT])
                    V("dve", lambda e: e.tensor_tensor(out=Tn[:], in0=pT[0:64, :], in1=Tt[:], op=ALU.add), [pT, Tt], [Tn])
                    X, XT, Tt = Xn, XTn, Tn
                    yield

            def dep(cc):
                cs = slice(cc * CH, (cc + 1) * CH)
                b = cc % 2
                Btok, Ktok, Vtok = TOK[b]
                SM = SMb[b]
                Tt = Tfin[b]
                if tt % self.TPS == 0 and cc == 0:
                    V("pool", lambda e: e.memset(Hf[:], 0.0), [], [Hf])
                    V("pool", lambda e: e.tensor_copy(out=H[:], in_=Hf[:]), [Hf], [H])

                def hd(h):
                    return h // 2, (h % 2) * 256, slice(h * 64, (h + 1) * 64)
                pW = self.ps()
                for h in range(8):
                    hp, b0, hsl = hd(h)
                    V("pe", lambda e, hp=hp, b0=b0, hsl=hsl: MM(e, pW[0:64, hsl], lhsT=SM[hp][:, b0 + 128:b0 + 192],
                                                               rhs=Vtok[:, hsl], start=True, stop=False), [SM[hp], Vtok], [pW])
                    V("pe", lambda e, h=h, hsl=hsl: MM(e, pW[0:64, hsl], lhsT=ar(h)[0:64, cc, 0, :], rhs=H[:, hsl], start=False,
                                                      stop=True), [ar(h), H], [pW])
                cp("act", W0s[:], pW[0:64, :], [pW], [W0s])
                yield
                pU = self.ps()
                for h in range(8):
                    hp, b0, hsl = hd(h)
                    V("pe", lambda e, hsl=hsl: MM(e, pU[0:64, hsl], lhsT=Tt[:, hsl], rhs=W0s[:, hsl], start=True, stop=True),
                      [Tt, W0s], [pU])
                cp("dve", Us[:], pU[0:64, :], [pU], [Us])
                yield
                pY = self.ps()
                for h in range(8):
                    hp, b0, hsl = hd(h)
                    V("pe", lambda e, hp=hp, b0=b0, hsl=hsl: MM(e, pY[0:64, hsl], lhsT=SM[hp][:, b0 + 192:b0 + 256],
                                                               rhs=Vtok[:, hsl], start=True, stop=False), [SM[hp], Vtok], [pY])
                    V("pe", lambda e, hp=hp, b0=b0, hsl=hsl: MM(e, pY[0:64, hsl], lhsT=SM[hp][:, b0 + 64:b0 + 128],
                                                               rhs=Us[:, hsl], start=False, stop=False), [SM[hp], Us], [pY])
                    V("pe", lambda e, h=h, hsl=hsl: MM(e, pY[0:64, hsl], lhsT=ar(h)[0:64, cc, 1, :], rhs=H[:, hsl], start=False,
                                                      stop=True), [ar(h), H], [pY])
                cp("act", YN[:, cc, :], pY[0:64, :], [pY], [YN])
                yield
                pH = self.ps()
                for h in range(8):
                    hp, b0, hsl = hd(h)
                    V("pe", lambda e, hsl=hsl: MM(e, pH[0:64, hsl], lhsT=Btok[:, hsl], rhs=Us[:, hsl], start=True, stop=False),
                      [Btok, Us], [pH])
                    V("pe", lambda e, hsl=hsl: MM(e, pH[0:64, hsl], lhsT=Ktok[:, hsl], rhs=Vtok[:, hsl], start=False, stop=True),
                      [Ktok, Vtok], [pH])
                V("dve", lambda e: e.tensor_tensor(out=Ht[:], in0=pH[0:64, :], in1=Hf[:], op=ALU.add), [pH, Hf], [Ht])
                yield
                V("dve", lambda e: e.tensor_tensor(out=Hf[:].rearrange("p (h v) -> p h v", h=8),
                                                    in0=Ht[:].rearrange("p (h v) -> p h v", h=8),
                                                    in1=PCall[:, cc, :].unsqueeze(2).broadcast_to([64, 8, 64]), op=ALU.mult),
                  [Ht, PCall], [Hf])
                V("act", lambda e: e.activation(out=H[:], in_=Hf[:], func=AF.Copy), [Hf], [H])
                yield

            def drive(gens):
                gens = [g for g in gens if g is not None]
                while gens:
                    for g in list(gens):
                        try:
                            next(g)
                        except StopIteration:
                            gens.remove(g)

            if cut >= 4:
                drive([indep(0)])
                for cc in range(8):
                    drive([dep(cc), indep(cc + 1) if cc + 1 < 8 else None])
            if cut >= 5:
                yr3 = YN[:].rearrange("p c (h v) -> p (c h) v", h=8)
                yq3 = YQ[:].rearrange("p c (h v) -> p (c h) v", h=8)
                V("act", lambda e: e.activation(out=YQ[:], in_=YN[:], func=AF.Square), [YN], [YQ])
                V("dve", lambda e: e.tensor_reduce(out=st["sum"][:], in_=yr3, axis=AX.X, op=ALU.add), [YN], [st["sum"]])
                V("dve", lambda e: e.tensor_reduce(out=st["sq"][:], in_=yq3, axis=AX.X, op=ALU.add), [YQ], [st["sq"]])
                V("pool", lambda e: e.tensor_scalar(out=st["m"][:], in0=st["sum"][:], scalar1=1.0 / 64, scalar2=None, op0=ALU.mult),
                  [st["sum"]], [st["m"]])
                V("pool", lambda e: e.tensor_tensor(out=st["m2"][:], in0=st["m"][:], in1=st["m"][:], op=ALU.mult), [st["m"]],
                  [st["m2"]])
                V("dve", lambda e: e.scalar_tensor_tensor(out=st["var"][:], in0=st["sq"][:], scalar=1.0 / 64, in1=st["m2"][:],
                                                           op0=ALU.mult, op1=ALU.subtract), [st["sq"], st["m2"]], [st["var"]])
                V("act", lambda e: e.activation(out=st["rstd"][:], in_=st["var"][:], func=AF.Ln, bias=epsg[:], scale=1.0),
                  [st["var"], epsg], [st["rstd"]])
                V("act", lambda e: e.activation(out=st["rstd"][:], in_=st["rstd"][:], func=AF.Exp, scale=-0.5), [st["rstd"]],
                  [st["rstd"]])
                V("dve", lambda e: e.tensor_tensor(out=yq3, in0=yr3, in1=st["m"][:].unsqueeze(2).broadcast_to([64, 64, 64]),
                                                    op=ALU.subtract), [YN, st["m"]], [YQ])
                V("dve", lambda e: e.tensor_tensor(out=yr3, in0=yq3, in1=st["rstd"][:].unsqueeze(2).broadcast_to([64, 64, 64]),
                                                    op=ALU.mult), [YQ, st["rstd"]], [YN])
            for hp in range(4 if cut >= 6 else 0):
                pO = self.ps()
                for cc in range(8):
                    V("pe", lambda e, cc=cc: e.transpose(out=pO[:, cc * CH:(cc + 1) * CH], in_=YN[:, cc, hp * 128:(hp + 1) * 128],
                                                         identity=idf[0:64, 0:64]), [YN, idf], [pO])
                V("dve", lambda e: e.tensor_scalar(out=po1[:], in0=pO[:, :], scalar1=colv("gn_g", hp), scalar2=colv("gn_b", hp),
                                                    op0=ALU.mult, op1=ALU.add), [pO, self.colpack], [po1])
                V("pool", lambda e: e.tensor_tensor(out=po1[:], in0=po1[:], in1=G[hp][:], op=ALU.mult), [po1, G[hp]], [po1])
                V("dve", lambda e: e.tensor_tensor(out=pob[:], in0=po1[:], in1=BG[hp][:], op=ALU.add), [po1, BG[hp]], [pob])
                s.dma("sp", self.yr.t[hp * 128:(hp + 1) * 128, t0:t0 + TT], pob[:], r=[pob.R()], w=[self.yr.R((hp, t0))])

    def load_wb(self, dst, dstv, name, l, bounds, order=None):
        Wt = self.W[name]
        srcv = Wt.t[l].rearrange("(k p) n -> p k n", p=128)
        dst._bounds = list(bounds)
        for i in (order if order is not None else range(len(bounds) - 1)):
            c0, c1 = bounds[i], bounds[i + 1]
            self.s.dma("pool", dstv[:, :, c0:c1], srcv[:, :, c0:c1], r=[Wt.R()], w=[dst.R(("blk", i))])

    @staticmethod
    def wres(dst, col):
        import bisect
        return dst.R(("blk", bisect.bisect_right(dst._bounds, col) - 1))

    def load_w(self, dst, name, l, nk):
        Wt = self.W[name]
        srcv = Wt.t[l].rearrange("(k p) n -> p k n", p=128)
        for kc in range(nk):
            self.s.dma("pool", dst[:, kc, :], srcv[:, kc, :], r=[Wt.R()], w=[dst.R()])

    def stage_D(self, l, src):
        s = self.s
        self.stage_begin()
        wof = self.sb("wof", [128, 4, D], BF16)
        wor = self.sb("wor", [128, 4, D], BF16)
        wout = self.sb("wout", [128, KC, D], BF16)
        for blk in range(2):
            self.load_wb(wof, wof.t, "w_o_fox", l, [0, 512, 1024], order=[blk])
            self.load_wb(wor, wor.t, "w_o_rwkv", l, [0, 512, 1024], order=[blk])
        self.load_wb(wout, wout.t, "w_out", l, [0, 512, 1024])
        yfT = [self.sb(f"yfT{i}", [128, 4, TT], BF16) for i in range(2)]
        yrT = [self.sb(f"yrT{i}", [128, 4, TT], BF16) for i in range(2)]
        gtT = [self.sb(f"gtT{i}", [128, 16, TT], BF16) for i in range(2)]
        xt_ = [self.sb(f"xD{i}", [128, KC, TT]) for i in range(2)]
        mg = self.sb("mg", [128, KC, TT], BF16)
        srcv = src.t.rearrange("(k p) t -> p k t", p=128)
        dstv = self.xs.t.rearrange("(k p) t -> p k t", p=128)
        def loads(tt):
            t0 = tt * TT
            yf, yr, gt, xt = yfT[tt % 2], yrT[tt % 2], gtT[tt % 2], xt_[tt % 2]
            s.dma("sp", yf[:], self.yf.t[:, t0:t0 + TT].rearrange("(k p) t -> p k t", p=128),
                  r=[self.yf.R((h, t0)) for h in range(8)], w=[yf.R()])
            s.dma("sp", yr[:], self.yr.t[:, t0:t0 + TT].rearrange("(k p) t -> p k t", p=128),
                  r=[self.yr.R((hp, t0)) for hp in range(4)], w=[yr.R()])
            s.dma("sp", gt[:], self.gt.t[:, t0:t0 + TT].rearrange("(k p) t -> p k t", p=128),
                  r=[self.gt.R((i, t0)) for i in range(16)], w=[gt.R()])
            for kc in range(KC):
                s.dma("sp", xt[:, kc, :], srcv[:, kc, t0:t0 + TT], r=[src.R(("tile", tt))], w=[xt.R()])

        loads(0)
        for tt in range(self.NT):
            t0 = tt * TT
            yf, yr, gt, xt = yfT[tt % 2], yrT[tt % 2], gtT[tt % 2], xt_[tt % 2]
            if tt + 1 < self.NT:
                loads(tt + 1)
            for n in range(KC):
                ns = slice(n * 128, (n + 1) * 128)
                pa = self.ps()
                for kc in range(4):
                    s.op("pe", lambda e, kc=kc: e.matmul(pa[:, :], lhsT=wof[:, kc, ns], rhs=yf[:, kc, :], start=(kc == 0),
                                                         stop=(kc == 3)), r=[self.wres(wof, n * 128), yf.R()], w=[pa.R()])
                pb = self.ps()
                for kc in range(4):
                    s.op("pe", lambda e, kc=kc: e.matmul(pb[:, :], lhsT=wor[:, kc, ns], rhs=yr[:, kc, :], start=(kc == 0),
                                                         stop=(kc == 3)), r=[self.wres(wor, n * 128), yr.R()], w=[pb.R()])
                m1 = self.tmpA("m1", [128, TT])
                s.op("dve", lambda e, m1=m1: e.tensor_tensor(out=m1[:], in0=pa[:, :], in1=gt[:, n, :], op=ALU.mult),
                     r=[pa.R(), gt.R()], w=[m1.R()])
                m2 = self.tmpA("m2", [128, TT])
                s.op("dve", lambda e, m2=m2: e.tensor_tensor(out=m2[:], in0=pb[:, :], in1=gt[:, 8 + n, :], op=ALU.mult),
                     r=[pb.R(), gt.R()], w=[m2.R()])
                s.op("pool", lambda e, m1=m1, m2=m2: e.tensor_tensor(out=mg[:, n, :], in0=m1[:], in1=m2[:], op=ALU.add),
                     r=[m1.R(), m2.R()], w=[mg.R(n)])
            for n in range(KC):
                ns = slice(n * 128, (n + 1) * 128)
                po = self.ps()
                for kc in range(KC):
                    s.op("pe", lambda e, kc=kc: e.matmul(po[:, :], lhsT=wout[:, kc, ns], rhs=mg[:, kc, :], start=(kc == 0),
                                                         stop=(kc == KC - 1)), r=[self.wres(wout, n * 128), mg.R(kc)], w=[po.R()])
                xo = self.tmpA("xo", [128, TT], nbuf=3)
                s.op("dve", lambda e, xo=xo: e.tensor_tensor(out=xo[:], in0=po[:, :], in1=xt[:, n, :], op=ALU.add),
                     r=[po.R(), xt.R()], w=[xo.R()])
                s.dma("sp", dstv[:, n, t0:t0 + TT], xo[:], r=[xo.R()], w=[self.xs.R(("tile", tt))])

    def stage_E1(self, l):
        s = self.s
        self.stage_begin()
        wup = self.sb("wup", [128, KC, 2 * DFF], BF16)
        jb = [0, 4, 10, 16, NJ]
        self.load_wb(wup, wup.t, "w_up", l, [x * 128 for x in jb] + [DFF + x * 128 for x in jb[1:]],
                     order=[0, 4, 1, 5, 2, 6, 3, 7])
        xt = self.sb("xE", [128, KC, TT])
        h2 = [self.sb(f"h2{i}", [128, KC, TT], BF16) for i in range(2)]
        ub = [self.sb(f"ub{i}", [128, TT + 2]) for i in range(2)]
        srcv = self.xs.t.rearrange("(k p) t -> p k t", p=128)
        K0 = float(2.0 * np.sqrt(2.0 / np.pi))
        def prep_tile(tt):
            for kc in range(KC):
                s.dma("sp", xt[:, kc, :], srcv[:, kc, tt * TT:(tt + 1) * TT], r=[self.xs.R(("tile", tt))], w=[xt.R()])
            self.rmsnorm_tile(l, "g_ffn", xt, h2[tt % 2], "E")

        prep_tile(0)
        for tt in range(self.NT):
            t0 = tt * TT
            seq_start = (tt % self.TPS == 0)
            ht = h2[tt % 2]
            hR = [ht.R(kc) for kc in range(KC)]
            for j in range(NJ):
                if j == 10 and tt + 1 < self.NT:
                    prep_tile(tt + 1)
                cres = []
                for half in range(2):
                    idx = half * NJ + j
                    c0 = idx * 128
                    pu = self.ps()
                    for kc in range(KC):
                        s.op("pe", lambda e, kc=kc, pu=pu, c0=c0: e.matmul(pu[:, :], lhsT=wup[:, kc, c0:c0 + 128], rhs=ht[:, kc, :],
                                                                           start=(kc == 0), stop=(kc == KC - 1)),
                             r=[hR[kc], self.wres(wup, c0)], w=[pu.R()])
                    u = ub[half]
                    s.op("act", lambda e, u=u, pu=pu: e.activation(out=u[:, 2:TT + 2], in_=pu[:, :], func=AF.Copy), r=[pu.R()],
                         w=[u.R()])
                    if seq_start:
                        s.op("pool", lambda e, u=u: e.memset(u[:, 0:2], 0.0), w=[u.R()])
                    else:
                        s.op("pool", lambda e, u=u, idx=idx: e.tensor_copy(out=u[:, 0:2], in_=self.ucarry[:, idx, :]),
                             r=[self.ucarry.R(idx)], w=[u.R()])
                    s.op("pool", lambda e, u=u, idx=idx: e.tensor_copy(out=self.ucarry[:, idx, :], in_=u[:, TT:TT + 2]),
                         r=[u.R()], w=[self.ucarry.R(idx)])
                    c1 = self.tmpA(f"c1{half}", [128, TT], nbuf=1)
                    s.op("act", lambda e, pu=pu, c1=c1, idx=idx: e.activation(out=c1[:], in_=pu[:, :], func=AF.Identity,
                                                                              scale=self.col(l, "cw2", idx),
                                                                              bias=self.col(l, "cb", idx)),
                         r=[pu.R(), self.colpack.R()], w=[c1.R()])
                    c2 = self.tmpA(f"c2{half}", [128, TT], nbuf=1)
                    s.op("dve", lambda e, u=u, c1=c1, c2=c2, idx=idx: e.scalar_tensor_tensor(out=c2[:], in0=u[:, 1:TT + 1],
                                                                                           scalar=self.col(l, "cw1", idx), in1=c1[:],
                                                                                           op0=ALU.mult, op1=ALU.add),
                         r=[u.R(), c1.R(), self.colpack.R()], w=[c2.R()])
                    c3 = self.tmpA(f"c3{half}", [128, TT], nbuf=2)
                    s.op("dve", lambda e, u=u, c2=c2, c3=c3, idx=idx: e.scalar_tensor_tensor(out=c3[:], in0=u[:, 0:TT],
                                                                                           scalar=self.col(l, "cw0", idx), in1=c2[:],
                                                                                           op0=ALU.mult, op1=ALU.add),
                         r=[u.R(), c2.R(), self.colpack.R()], w=[c3.R()])
                    cres.append(c3)
                u1, u2 = cres
                sg = self.tmpA("gsg", [128, TT], nbuf=2)
                s.op("act", lambda e, sg=sg, u1=u1: e.activation(out=sg[:], in_=u1[:], func=AF.Gelu_apprx_tanh), r=[u1.R()],
                     w=[sg.R()])
                ao = self.tmpA("gao", [128, TT], BF16, nbuf=3)
                s.op("pool", lambda e, sg=sg, u2=u2, ao=ao: e.tensor_tensor(out=ao[:], in0=sg[:], in1=u2[:], op=ALU.mult),
                     r=[sg.R(), u2.R()], w=[ao.R()])
                s.dma("sp", self.actT.t[j * 128:(j + 1) * 128, t0:t0 + TT], ao[:], r=[ao.R()], w=[self.actT.R((j, t0))])

    def stage_E2(self, l, last):
        s = self.s
        self.stage_begin()
        wdn = self.sb("wdn", [128, NJ, D], BF16)
        wpg = self.sb("wpg", [128, KC, D], BF16)
        wpu = self.sb("wpu", [128, 2, D], BF16)
        self.load_wb(wdn, wdn.t, "w_down", l, [0, 256, 512, 1024])
        self.load_wb(wpg, wpg.t, "w_ple_gate", l, [0, 512, 1024])
        self.load_wb(wpu, wpu.t, "w_ple_up", l, [0, 1024])
        at_ = [self.sb(f"at{i}", [128, NJ, TT], BF16) for i in range(2)]
        pt_ = [self.sb(f"pp{i}", [128, 2, TT], BF16) for i in range(2)]
        x2_ = [self.sb(f"x2{i}", [128, KC, TT]) for i in range(2)]
        h3_ = [self.sb(f"h3{i}", [128, KC, TT], BF16) for i in range(2)]
        srcv = self.xs.t.rearrange("(k p) t -> p k t", p=128)
        dst = self.out if last else self.xs
        dstv = dst.t.rearrange("(k p) t -> p k t", p=128)
        pv = self.pT.t[l].rearrange("(k p) t -> p k t", p=128)

        def load_at(tt):
            t0 = tt * TT
            at = at_[tt % 2]
            s.dma("sp", at[:], self.actT.t[:, t0:t0 + TT].rearrange("(k p) t -> p k t", p=128),
                  r=[self.actT.R((j, t0)) for j in range(NJ)], w=[at.R()])

        def down(tt):
            t0 = tt * TT
            at, pp, x2, h3 = at_[tt % 2], pt_[tt % 2], x2_[tt % 2], h3_[tt % 2]
            if tt + 1 < self.NT:
                load_at(tt + 1)
            s.dma("pool", pp[:], pv[:, :, t0:t0 + TT], r=[self.pT.R()], w=[pp.R()])
            for kc in range(KC):
                s.dma("sp", x2[:, kc, :], srcv[:, kc, t0:t0 + TT], r=[self.xs.R(("tile", tt))], w=[x2.R()])
            for n in range(KC):
                ns = slice(n * 128, (n + 1) * 128)
                pd = self.ps()
                for j in range(NJ):
                    s.op("pe", lambda e, j=j: e.matmul(pd[:, :], lhsT=wdn[:, j, ns], rhs=at[:, j, :], start=(j == 0),
                                                       stop=(j == NJ - 1)), r=[self.wres(wdn, n * 128), at.R()], w=[pd.R()])
                s.op("dve", lambda e: e.tensor_tensor(out=x2[:, n, :], in0=pd[:, :], in1=x2[:, n, :], op=ALU.add),
                     r=[pd.R(), x2.R()], w=[x2.R()])
            self.rmsnorm_tile(l, "g_ple", x2, h3, "F")

        def ple(tt):
            t0 = tt * TT
            pp, x2, h3 = pt_[tt % 2], x2_[tt % 2], h3_[tt % 2]
            for n in range(KC):
                ns = slice(n * 128, (n + 1) * 128)
                pg = self.ps()
                for kc in range(KC):
                    s.op("pe", lambda e, kc=kc: e.matmul(pg[:, :], lhsT=wpg[:, kc, ns], rhs=h3[:, kc, :], start=(kc == 0),
                                                         stop=(kc == KC - 1)), r=[self.wres(wpg, n * 128), h3.R(kc)], w=[pg.R()])
                sgt = self.tmpA("sgt", [128, TT])
                s.op("act", lambda e, sgt=sgt: e.activation(out=sgt[:], in_=pg[:, :], func=AF.Sigmoid), r=[pg.R()], w=[sgt.R()])
                pq = self.ps()
                for kc in range(2):
                    s.op("pe", lambda e, kc=kc: e.matmul(pq[:, :], lhsT=wpu[:, kc, ns], rhs=pp[:, kc, :], start=(kc == 0),
                                                         stop=(kc == 1)), r=[self.wres(wpu, n * 128), pp.R()], w=[pq.R()])
                s.op("dve", lambda e, sgt=sgt: e.tensor_tensor(out=sgt[:], in0=pq[:, :], in1=sgt[:], op=ALU.mult),
                     r=[pq.R(), sgt.R()], w=[sgt.R()])
                xo = self.tmpA("xo2", [128, TT], nbuf=2)
                s.op("pool", lambda e, sgt=sgt, xo=xo: e.tensor_tensor(out=xo[:], in0=sgt[:], in1=x2[:, n, :], op=ALU.add),
                     r=[sgt.R(), x2.R()], w=[xo.R()])
                s.dma("sp", dstv[:, n, t0:t0 + TT], xo[:], r=[xo.R()], w=[dst.R(("tile", tt))])

        load_at(0)
        down(0)
        for tt in range(self.NT):
            if tt + 1 < self.NT:
                down(tt + 1)
            ple(tt)

    def tmpA(self, name, shape, dt=F32, nbuf=2):
        key = ("tmp", name)
        if key not in self.ep:
            self.ep[key] = [[self.sb(f"tA_{name}{i}", shape, dt) for i in range(nbuf)], 0]
        lst = self.ep[key]
        t = lst[0][lst[1] % nbuf]
        lst[1] += 1
        return t

    def epi_qk(self, l, kind, idx, ps, t0):
        s = self.s
        sq = self.tmpA("qsq", [128, TT], BF16)
        s.op("act", lambda e: e.activation(out=sq[:], in_=ps[:, :], func=AF.Square), r=[ps.R()], w=[sq.R()])
        self.deferred.append(lambda: self.epi_qk2(l, kind, idx, ps, t0, sq))

    def epi_qk2(self, l, kind, idx, ps, t0, sq):
        s = self.s
        ps2 = self.ps()
        s.op("pe", lambda e: e.matmul(ps2[:, :], lhsT=self.c["blk2_f_b"][:], rhs=sq[:], start=True, stop=True),
             r=[self.c["blk2_f_b"].R(), sq.R()], w=[ps2.R()])
        rs2 = self.tmpA("qrs2", [128, TT], nbuf=1)
        self.rsqrt_ps(rs2, ps2, 1.0 / 64, RMS_EPS, 0.125 if kind == "q" else 1.0)
        o = self.tmpA("qo", [128, TT], BF16)
        gname = "g_q" if kind == "q" else "g_k"
        s.op("dve", lambda e: e.scalar_tensor_tensor(out=o[:], in0=ps[:, :], scalar=self.col(l, gname), in1=rs2[:],
                                                      op0=ALU.mult, op1=ALU.mult),
             r=[ps.R(), rs2.R(), self.colpack.R()], w=[o.R()])
        dst = self.qa if kind == "q" else self.ka
        for hh in range(2):
            h = idx * 2 + hh
            s.dma("sp", dst[h, 0:64, t0:t0 + TT], o[hh * 64:(hh + 1) * 64, :], r=[o.R()], w=[dst.R(("qk", h, t0))])

    def epi_f(self, l, ps, t0, seq_start):
        s = self.s
        e1 = self.tmpA("fe", [8, TT], nbuf=1)
        s.op("act", lambda e: e.activation(out=e1[:], in_=ps[0:8, :], func=AF.Exp, bias=self.negb[:, l:l + 1], scale=-1.0),
             r=[ps.R(), self.negb.R()], w=[e1.R()])
        l1 = e1
        one = self.c["ones_f"]
        s.op("act", lambda e: e.activation(out=l1[:], in_=e1[:], func=AF.Ln, bias=one[0:8, 0:1], scale=1.0),
             r=[e1.R(), one.R()], w=[l1.R()])
        c = self.tmpA("fc", [8, TT], nbuf=1)
        if seq_start:
            init = 0.0
            rr = []
        else:
            init = self.ccar[:, 0:1]
            rr = [self.ccar.R()]
        s.op("dve", lambda e: e.tensor_tensor_scan(out=c[:], data0=self.c["ones_f"][0:8, :], data1=l1[:], initial=init,
                                                    op0=ALU.mult, op1=ALU.subtract),
             r=[self.c["ones_f"].R(), l1.R()] + rr, w=[c.R()])
        s.op("dve", lambda e: e.tensor_copy(out=self.ccar[:, 0:1], in_=c[:, TT - 1:TT]), r=[c.R()], w=[self.ccar.R()])
        hi = self.tmpA("fhi", [8, TT], BF16, nbuf=1)
        s.op("dve", lambda e: e.tensor_copy(out=hi[:], in_=c[:]), r=[c.R()], w=[hi.R()])
        r1 = self.tmpA("fr1", [8, TT], nbuf=1)
        s.op("dve", lambda e: e.tensor_tensor(out=r1[:], in0=c[:], in1=hi[:], op=ALU.subtract), r=[c.R(), hi.R()],
             w=[r1.R()])
        mid = self.tmpA("fmid", [8, TT], BF16, nbuf=1)
        s.op("dve", lambda e: e.tensor_copy(out=mid[:], in_=r1[:]), r=[r1.R()], w=[mid.R()])
        r2 = self.tmpA("fe", [8, TT], nbuf=1)
        s.op("dve", lambda e: e.tensor_tensor(out=r2[:], in0=r1[:], in1=mid[:], op=ALU.subtract), r=[r1.R(), mid.R()],
             w=[r2.R()])
        lo = self.tmpA("flo", [8, TT], BF16, nbuf=1)
        s.op("dve", lambda e: e.tensor_copy(out=lo[:], in_=r2[:]), r=[r2.R()], w=[lo.R()])
        for j, part in enumerate([hi, mid, lo]):
            s.dma("sp", self.qa.t[:, 64 + j, t0:t0 + TT], part[:], r=[part.R()], w=[self.qa.R(("c", j, t0))])
            ng = self.tmpA("fng", [8, TT], BF16, nbuf=1)
            s.op("pool", lambda e, ng=ng, part=part: e.tensor_scalar(out=ng[:], in0=part[:], scalar1=-1.0, scalar2=None,
                                                                      op0=ALU.mult), r=[part.R()], w=[ng.R()])
            s.dma("sp", self.ka.t[:, 67 + j, t0:t0 + TT], ng[:], r=[ng.R()], w=[self.ka.R(("c", j, t0))])

    def epi_rw(self, l, idx, ps, t0, tt, seq_start, vd):
        s = self.s
        zb = self.zraw[idx % 2]
        s.op("act", lambda e: e.activation(out=zb[:, 1:TT + 1], in_=ps[:, :], func=AF.Copy), r=[ps.R()], w=[zb.R()])
        if seq_start:
            s.op("pool", lambda e: e.memset(zb[:, 0:1], 0.0), w=[zb.R()])
        else:
            s.op("pool", lambda e: e.tensor_copy(out=zb[:, 0:1], in_=self.carry[:, idx:idx + 1]),
                 r=[self.carry.R(idx)], w=[zb.R()])
        s.op("pool", lambda e: e.tensor_copy(out=self.carry[:, idx:idx + 1], in_=zb[:, TT:TT + 1]), r=[zb.R()],
             w=[self.carry.R(idx)])
        d = self.tmpA("rwd", [128, TT])
        s.op("dve", lambda e: e.tensor_tensor(out=d[:], in0=zb[:, 0:TT], in1=zb[:, 1:TT + 1], op=ALU.subtract),
             r=[zb.R()], w=[d.R()])
        o = self.tmpA("rwo", [128, TT], nbuf=2)
        s.op("dve", lambda e: e.scalar_tensor_tensor(out=o[:], in0=d[:], scalar=self.col(l, "mu", idx), in1=zb[:, 1:TT + 1],
                                                      op0=ALU.mult, op1=ALU.add),
             r=[d.R(), zb.R(), self.colpack.R()], w=[o.R()])
        if 8 <= idx < 12:
            vi = idx - 8
            if l == 0:
                s.dma("sp", self.vf.t[vi * 128:(vi + 1) * 128, t0:t0 + TT], o[:], r=[o.R()], w=[self.vf.R((vi, t0))])
            else:
                ps2 = self.ps()
                s.op("pe", lambda e: e.matmul(ps2[:, :], lhsT=self.wvu[:, vi * 128:(vi + 1) * 128], rhs=vd[:], start=True,
                                              stop=True), r=[self.wvu.R(), vd.R()], w=[ps2.R()])
                vm = self.tmpA("vm", [128, TT], nbuf=1)
                s.op("act", lambda e: e.activation(out=vm[:], in_=ps2[:, :], func=AF.Sigmoid, bias=self.col(l, "v0", vi),
                                                   scale=1.0), r=[ps2.R(), self.colpack.R()], w=[vm.R()])
                vfl = self.tmpA("vfl", [128, TT], nbuf=1)
                s.dma("sp", vfl[:], self.vf.t[vi * 128:(vi + 1) * 128, t0:t0 + TT], r=[self.vf.R((vi, t0))], w=[vfl.R()])
                dd = self.tmpA("vdd", [128, TT], nbuf=1)
                s.op("pool", lambda e: e.tensor_tensor(out=dd[:], in0=vfl[:], in1=o[:], op=ALU.subtract),
                     r=[vfl.R(), o.R()], w=[dd.R()])
                s.op("pool", lambda e: e.tensor_tensor(out=dd[:], in0=dd[:], in1=vm[:], op=ALU.mult), r=[dd.R(), vm.R()],
                     w=[dd.R()])
                o2 = self.tmpA("rwo2", [128, TT], nbuf=1)
                s.op("dve", lambda e: e.tensor_tensor(out=o2[:], in0=o[:], in1=dd[:], op=ALU.add), r=[o.R(), dd.R()],
                     w=[o2.R()])
                o = o2
        s.dma("sp", self.zr.t[idx * 128:(idx + 1) * 128, t0:t0 + TT], o[:], r=[o.R()], w=[self.zr.R((idx, t0))])

    def epi_gate(self, l, idx, ps, t0):
        s = self.s
        o = self.tmpA("go", [128, TT], BF16, nbuf=3)
        s.op("act", lambda e: e.activation(out=o[:], in_=ps[:, :], func=AF.Sigmoid), r=[ps.R()], w=[o.R()])
        s.dma("sp", self.gt.t[idx * 128:(idx + 1) * 128, t0:t0 + TT], o[:], r=[o.R()], w=[self.gt.R((idx, t0))])


def host_inputs(inp, S, NB, depth, core):
    b0 = core * NB
    x = np.asarray(inp["x"], np.float32)[b0:b0 + NB].reshape(NB * S, D)
    p = np.asarray(inp["p"], np.float32)[:depth, b0:b0 + NB].reshape(depth, NB * S, PLE)
    m = {"xT": np.ascontiguousarray(x.T), "pT": np.ascontiguousarray(p.transpose(0, 2, 1))}
    return m


def shared_inputs(inp, depth):
    m = {}
    for name in ["w_in", "w_decay_up", "w_aaa_up", "w_gate_up", "w_o_fox", "w_o_rwkv", "w_out", "w_up", "w_down",
                 "w_ple_gate", "w_ple_up"]:
        m[name] = np.ascontiguousarray(np.asarray(inp[name], np.float32)[:depth])
    for name in ["w_vres_down", "w_vres_up"]:
        m[name] = np.ascontiguousarray(np.asarray(inp[name], np.float32)[:max(depth - 1, 1)])
    m["colpack"] = make_colpack({k: np.asarray(v, np.float32) for k, v in inp.items()}, depth)
    for k, v in make_consts().items():
        m["c_" + k] = v
    return m


_PROG_CACHE = {}


def kernel(**inp):
    S, NBT, depth = 2048, 16, 4
    ncores = 8
    NB = NBT // ncores
    key = (S, NB, depth)
    if key not in _PROG_CACHE:
        _PROG_CACHE[key] = Prog(S, NB, depth)
    prog = _PROG_CACHE[key]
    sh = shared_inputs(inp, depth)
    in_maps = []
    for c in range(ncores):
        m = dict(sh)
        m.update(host_inputs(inp, S, NB, depth, c))
        in_maps.append(m)
    res = run_bass_kernel_spmd(prog.nc, in_maps, core_ids=list(range(ncores)))
    outs = []
    for c in range(ncores):
        o = res.results[c]["outT"]
        outs.append(np.ascontiguousarray(o.T).reshape(NB, S, D))
    return np.concatenate(outs, axis=0).astype(np.float32)
```

```python
import numpy as np
import concourse.bass as bass
import concourse.mybir as mybir
from concourse.bass_utils import run_bass_kernel_spmd

F32 = mybir.dt.float32
BF16 = mybir.dt.bfloat16
AF = mybir.ActivationFunctionType
ALU = mybir.AluOpType
AX = mybir.AxisListType

D = 1024
KC = 8
FOXW = 512
RW = 512
NIN = 5384
DFF = 2816
NJ = 22
PLE = 256
RMS_EPS = 1e-6
GN_EPS = 64e-5
CH = 64
TT = 512


class Res:
    __slots__ = ("w", "rd")

    def __init__(self):
        self.w = None
        self.rd = {}


class Tl:
    def __init__(self, t):
        self.t = t
        self._r = {}

    def R(self, key=None):
        r = self._r.get(key)
        if r is None:
            r = Res()
            self._r[key] = r
        return r

    def __getitem__(self, idx):
        return self.t[idx]


class Sch:
    NDS = 6

    def __init__(self, nc):
        self.nc = nc
        self.e = dict(pe=nc.tensor, act=nc.scalar, dve=nc.vector, pool=nc.gpsimd, sp=nc.sync)
        self.sem = {k: nc.alloc_semaphore(name=f"s_{k}") for k in self.e}
        self.cnt = {k: 0 for k in self.e}
        self.seen = {k: {} for k in self.e}
        self.dsem = {q: [nc.alloc_semaphore(name=f"d_{q}{i}") for i in range(self.NDS)]
                     for q in ("sp", "pool", "act")}
        self.dcnt = {q: 0 for q in self.dsem}
        self.semh = {}
        for k in self.e:
            self.semh[k] = self.sem[k]
        for q in self.dsem:
            for i, h in enumerate(self.dsem[q]):
                self.semh[(q, i)] = h
        self.n_ins = 0

    def _wait(self, E, key, val):
        if self.seen[E].get(key, 0) >= val:
            return
        self.e[E].wait_ge(self.semh[key], val)
        self.seen[E][key] = val
        self.n_ins += 1

    def _collect(self, reads, writes):
        deps = {}

        def add(tok):
            k, v = tok
            if deps.get(k, 0) < v:
                deps[k] = v

        for r in reads:
            if r.w is not None:
                add(r.w)
        for w in writes:
            if w.w is not None:
                add(w.w)
            for k, v in w.rd.items():
                add((k, v))
        return deps

    def _commit(self, tok, reads, writes):
        k, v = tok
        for w in writes:
            w.w = tok
            w.rd = {}
        for r in reads:
            if r.rd.get(k, 0) < v:
                r.rd[k] = v

    def op(self, E, fn, r=(), w=()):
        deps = self._collect(r, w)
        for k, v in deps.items():
            if E == "pe" and k == "pe":
                continue
            self._wait(E, k, v)
        ins = fn(self.e[E])
        self.cnt[E] += 1
        ins.then_inc(self.sem[E], 1)
        self.n_ins += 1
        self._commit((E, self.cnt[E]), r, w)

    def dma(self, q, out, in_, r=(), w=(), **kw):
        deps = self._collect(r, w)
        for k, v in deps.items():
            self._wait(q, k, v)
        n = self.dcnt[q]
        i = n % self.NDS
        gen = n // self.NDS
        if gen > 0:
            self._wait(q, (q, i), 16 * gen)
        ins = self.e[q].dma_start(out=out, in_=in_, **kw)
        ins.then_inc(self.dsem[q][i], 16)
        self.dcnt[q] = n + 1
        self.n_ins += 1
        self._commit(((q, i), 16 * (gen + 1)), r, w)

    def finish(self):
        for q in self.dsem:
            n = self.dcnt[q]
            for i in range(self.NDS):
                cnt_i = (n - i + self.NDS - 1) // self.NDS
                if cnt_i > 0:
                    self._wait("sp", (q, i), 16 * cnt_i)
        for k in self.e:
            if k != "sp" and self.cnt[k] > 0:
                self._wait("sp", k, self.cnt[k])


def _cols(vec):
    v = np.asarray(vec, np.float32).reshape(-1)
    n = (v.size + 127) // 128
    buf = np.zeros(n * 128, np.float32)
    buf[: v.size] = v
    return buf.reshape(n, 128).T


def make_consts():
    c = {}
    c["ident_f"] = np.eye(128, dtype=np.float32)
    c["ones_f"] = np.ones((128, 512), np.float32)
    blk = np.zeros((128, 128), np.float32)
    blk[:64, :64] = 1.0
    blk[64:, 64:] = 1.0
    c["blk2_f"] = blk
    s = np.arange(128)[:, None]
    t = np.arange(128)[None, :]
    c["trimask_f"] = np.where(s > t, -30000.0, 0.0).astype(np.float32)
    t5 = np.arange(512)[None, :]
    c["fullmask"] = np.concatenate([np.where(t5 < r * 128 + s, -30000.0, 0.0).astype(np.float32) for r in range(4)], axis=1)
    sm = np.ones((128, 512), np.float32)
    sm[:, ::CH] = 0.0
    c["scanmask"] = sm
    s64 = np.arange(64)[:, None]
    t64 = np.arange(64)[None, :]
    strict_up = (t64 > s64).astype(np.float32)
    incl_up = (t64 >= s64).astype(np.float32)
    one = np.concatenate([strict_up, incl_up], axis=1)
    c["mask_S"] = np.tile(one, (1, 4))
    strict_lo = (t64 < s64).astype(np.float32)
    c["mask_A"] = np.tile(strict_lo, (1, 8))
    c["ident8"] = np.tile(np.eye(64, dtype=np.float32), (1, 8))
    return c


CONST_SHAPES = {"ident_f": (128, 128), "ones_f": (128, 512), "blk2_f": (128, 128), "trimask_f": (128, 128),
                "scanmask": (128, 512), "fullmask": (128, 2048), "mask_S": (64, 512), "mask_A": (64, 512), "ident8": (64, 512)}

COLS = {}
_o = 0
for _name, _n in [("g_mix", 8), ("g_ffn", 8), ("g_ple", 8), ("g_q", 1), ("g_k", 1), ("b_f", 1), ("mu", 14),
                  ("w0", 4), ("a0", 4), ("k_k", 4), ("k_a", 4), ("r_k", 4), ("gn_g", 4), ("gn_b", 4), ("v0", 4),
                  ("cw0", 44), ("cw1", 44), ("cw2", 44), ("cb", 44)]:
    COLS[_name] = (_o, _n)
    _o += _n
NCOL = _o


def make_colpack(inp, depth):
    pk = np.zeros((128, depth, NCOL), np.float32)

    def put(l, name, arr):
        o, n = COLS[name]
        a = _cols(arr)
        assert a.shape[1] == n, (name, a.shape, n)
        pk[:, l, o:o + n] = a

    for l in range(depth):
        put(l, "g_mix", inp["g_mix"][l])
        put(l, "g_ffn", inp["g_ffn"][l])
        put(l, "g_ple", inp["g_ple"][l])
        put(l, "g_q", np.tile(inp["g_qnorm"][l], 2))
        put(l, "g_k", np.tile(inp["g_knorm"][l], 2))
        put(l, "b_f", inp["b_f"][l])
        put(l, "mu", inp["mu_shift"][l])
        for nm, key in [("w0", "w0"), ("a0", "a0"), ("k_k", "k_k"), ("k_a", "k_a"), ("gn_g", "gn_g"),
                        ("gn_b", "gn_b")]:
            put(l, nm, inp[key][l])
        put(l, "r_k", inp["r_k"][l].reshape(-1))
        if l >= 1:
            put(l, "v0", inp["v0"][l - 1])
        for j in range(3):
            put(l, f"cw{j}", inp["conv_w"][l][j])
        put(l, "cb", inp["conv_b"][l])
    return pk.reshape(128, depth * NCOL)


def in_groups():
    g = []
    for i in range(4):
        g.append(("q", i * 128, 128, i))
    for i in range(4):
        g.append(("k", 512 + i * 128, 128, i))
    g.append(("f", 1536, 8, 0))
    for i in range(14):
        g.append(("rw", 1544 + i * 128, 128, i))
    for i in range(16):
        g.append(("gate", 3336 + i * 128, 128, i))
    return g


class Prog:
    def __init__(self, S, NB, depth, debug=False, stages="ABCDE"):
        self.S, self.NB, self.depth, self.debug, self.stages = S, NB, depth, debug, stages
        self.T = S * NB
        self.NT = self.T // TT
        self.TPS = S // TT
        nc = bass.Bass("TRN2", target_bir_lowering=False)
        self.nc = nc
        self.s = Sch(nc)
        self._ps_i = 0
        self.build()

    def psb_(self, name, shape, dt=F32):
        return Tl(self.nc.alloc_sbuf_tensor(name, list(shape), dt))

    def sb(self, name, shape, dt=F32):
        esz = 2 if dt == BF16 else 4
        nbytes = int(np.prod(shape[1:])) * esz
        nbytes = (nbytes + 63) // 64 * 64
        off = self.arena_off
        assert off + nbytes <= self.arena_end, (name, off, nbytes, self.arena_end)
        self.arena_off = off + nbytes
        self._uid += 1
        return Tl(self.nc.alloc_sbuf_tensor_at(f"{name}_{self._uid}", list(shape), dt, offset=off))

    def stage_begin(self):
        s = self.s
        for E in s.e:
            for k in s.e:
                if s.cnt[k] > 0:
                    s._wait(E, k, s.cnt[k])
            for q in s.dsem:
                n = s.dcnt[q]
                for i in range(s.NDS):
                    cnt_i = (n - i + s.NDS - 1) // s.NDS
                    if cnt_i > 0:
                        s._wait(E, (q, i), 16 * cnt_i)
        self.arena_off = self.arena_start
        self.ep = {}

    def dram(self, name, shape, dt, kind="Internal"):
        if kind == "Internal" and self.debug:
            kind = "ExternalOutput"
        return Tl(self.nc.dram_tensor(name, list(shape), dt, kind=kind))

    def ps(self):
        p = self.psb[self._ps_i % 8]
        self._ps_i += 1
        return p

    def build(self):
        nc, s = self.nc, self.s
        T, L = self.T, self.depth
        self.xT = self.dram("xT", [D, T], F32, kind="ExternalInput")
        self.pT = self.dram("pT", [L, PLE, T], F32, kind="ExternalInput")
        self.out = self.dram("outT", [D, T], F32, kind="ExternalOutput")
        W = {}
        for name, shp in [("w_in", (L, D, NIN)), ("w_decay_up", (L, 64, RW)), ("w_aaa_up", (L, 64, RW)),
                          ("w_gate_up", (L, 128, RW)), ("w_vres_down", (max(L - 1, 1), D, 32)),
                          ("w_vres_up", (max(L - 1, 1), 32, RW)), ("w_o_fox", (L, FOXW, D)),
                          ("w_o_rwkv", (L, RW, D)), ("w_out", (L, D, D)), ("w_up", (L, D, 2 * DFF)),
                          ("w_down", (L, DFF, D)), ("w_ple_gate", (L, D, D)), ("w_ple_up", (L, PLE, D))]:
            W[name] = self.dram(name, shp, F32, kind="ExternalInput")
        self.W = W
        self.colpack_d = self.dram("colpack", [128, L * NCOL], F32, kind="ExternalInput")
        self.const_d = {k: self.dram("c_" + k, list(v), F32, kind="ExternalInput") for k, v in CONST_SHAPES.items()}
        self.xs = self.dram("xs", [D, T], F32)
        self.qa = self.dram("qa", [8, 70, T], BF16)
        self.ka = self.dram("ka", [8, 70, T], BF16)
        self.zr = self.dram("zr", [1792, T], F32)
        self.vf = self.dram("vf", [RW, T], F32)
        self.gt = self.dram("gt", [2 * D, T], BF16)
        self.yf = self.dram("yf", [FOXW, T], BF16)
        self.yr = self.dram("yr", [RW, T], BF16)
        self.vt = self.dram("vt", [T, FOXW], BF16)
        self.actT = self.dram("actT", [DFF, T], BF16)
        self.psb = [Tl(nc.alloc_psum_tensor(f"ps{i}", [128, 512], F32)) for i in range(8)]
        self.colpack = self.psb_("colpack_s", [128, L * NCOL])
        s.dma("sp", self.colpack[:], self.colpack_d[:, :], r=[self.colpack_d.R()], w=[self.colpack.R()])
        self.c = {}
        for k, shp in CONST_SHAPES.items():
            if k in ("fullmask", "trimask_f"):
                continue
            self.c[k] = self.psb_("cs_" + k, shp)
            s.dma("sp", self.c[k][:], self.const_d[k][:, :], r=[self.const_d[k].R()], w=[self.c[k].R()])
        for k in ["ident_f", "blk2_f", "fullmask"]:
            self.c[k + "_b"] = self.psb_("cb_" + k, CONST_SHAPES[k], BF16)
            s.dma("pool", self.c[k + "_b"][:], self.const_d[k][:, :], r=[self.const_d[k].R()],
                  w=[self.c[k + "_b"].R()])
        self.c["ones_b"] = self.psb_("cb_ones", [128, 512], BF16)
        s.dma("pool", self.c["ones_b"][:], self.const_d["ones_f"][:, :], r=[self.const_d["ones_f"].R()],
              w=[self.c["ones_b"].R()])
        self.carry = self.psb_("carry", [128, 16])
        self.ccar = self.psb_("ccar", [8, 2])
        self.ucarry = self.psb_("ucarry", [128, 2 * NJ, 2])
        self.negb = self.psb_("negb", [8, 4])
        for ll in range(L):
            s.op("dve", lambda e, ll=ll: e.tensor_scalar(out=self.negb[:, ll:ll + 1], in0=self.col(ll, "b_f", 0, 0, 8),
                                                          scalar1=-1.0, scalar2=None, op0=ALU.mult),
                 r=[self.colpack.R()], w=[self.negb.R()])
        self._uid = 0
        base0 = int(nc.sbuf_base)
        self.arena_start = (base0 + 63) // 64 * 64
        left = (int(nc.sbuf_bytes_remaining) - 256 - (self.arena_start - base0)) // 64 * 64
        slab = nc.alloc_sbuf_tensor("arena", [128, (left + self.arena_start - base0) // 4], F32)
        self.arena_end = self.arena_start + left
        assert int(nc.sbuf_base) >= self.arena_end, (nc.sbuf_base, self.arena_end)
        self.stage_begin()
        ow = self.sb("ones_wide", [3, T], BF16)
        s.op("dve", lambda e: e.memset(ow[:], 1.0), w=[ow.R()])
        for h in range(8):
            s.dma("sp", self.qa[h, 67:70, :], ow[:], r=[ow.R()], w=[self.qa.R(("ones", h))])
            s.dma("sp", self.ka[h, 64:67, :], ow[:], r=[ow.R()], w=[self.ka.R(("ones", h))])
        for l in range(L):
            src = self.xT if l == 0 else self.xs
            if "A" in self.stages:
                self.stage_A(l, src)
            if "B" in self.stages:
                self.stage_B(l)
            if "C" in self.stages:
                self.stage_C(l)
            if "D" in self.stages:
                self.stage_D(l, src)
            if "E" in self.stages or "1" in self.stages:
                self.stage_E1(l)
            if "E" in self.stages or "2" in self.stages:
                self.stage_E2(l, last=(l == L - 1))
        s.finish()

    def col(self, l, name, j=0, p0=0, p1=128):
        o, n = COLS[name]
        assert j < n
        c0 = l * NCOL + o + j
        return self.colpack[p0:p1, c0:c0 + 1]

    def rmsnorm_tile(self, l, gname, xt, ht, tag):
        for _ in self.rn_gen(l, gname, xt, ht):
            pass

    def rn_gen(self, l, gname, xt, ht):
        s = self.s
        sq = ht
        s.op("act", lambda e: e.activation(out=sq[:], in_=xt[:], func=AF.Square), r=[xt.R()],
             w=[ht.R(kc) for kc in range(KC)])
        yield
        ps = self.ps()
        for kc in range(KC):
            s.op("pe", lambda e, kc=kc: e.matmul(ps[:, :], lhsT=self.c["ones_b"][:, 0:128], rhs=sq[:, kc, :],
                                                 start=(kc == 0), stop=(kc == KC - 1)),
                 r=[self.c["ones_b"].R(), ht.R(kc)], w=[ps.R()])
        rs = self.tmpA("rn_rs", [128, TT], nbuf=1)
        self.rsqrt_ps(rs, ps, 1.0 / D, RMS_EPS, 1.0)
        yield
        for kc in range(KC):
            eng = "dve" if kc % 2 == 0 else "pool"
            if eng == "dve":
                s.op("dve", lambda e, kc=kc: e.scalar_tensor_tensor(out=ht[:, kc, :], in0=xt[:, kc, :],
                                                                     scalar=self.col(l, gname, kc), in1=rs[:],
                                                                     op0=ALU.mult, op1=ALU.mult),
                     r=[xt.R(), rs.R(), self.colpack.R()], w=[ht.R(kc)])
            else:
                tmp = self.tmpA("rn_nt", [128, TT], nbuf=2)
                s.op("pool", lambda e, kc=kc, tmp=tmp: e.tensor_tensor(out=tmp[:], in0=xt[:, kc, :], in1=rs[:], op=ALU.mult),
                     r=[xt.R(), rs.R()], w=[tmp.R()])
                s.op("act", lambda e, kc=kc, tmp=tmp: e.activation(out=ht[:, kc, :], in_=tmp[:], func=AF.Copy,
                                                                   scale=self.col(l, gname, kc)),
                     r=[tmp.R(), self.colpack.R()], w=[ht.R(kc)])
        yield

    def stage_A(self, l, src):
        nc, s = self.nc, self.s
        T = self.T
        self.stage_begin()
        self.wbig = self.sb("wbig", [128, KC * NIN], BF16)
        self.wbig_R = self.wbig.R()
        self.xt_t = [self.sb("xt0", [128, KC, TT])] * 2
        self.ht_t = [self.sb(f"ht{i}", [128, KC, TT], BF16) for i in range(2)]
        self.zraw = [self.sb(f"zraw{i}", [128, TT + 1]) for i in range(2)]
        W = self.W
        win = self.wbig
        winv = self.wbig.t[:, 0:KC * NIN].rearrange("p (k n) -> p k n", k=KC)
        self.load_wb(self.wbig, winv, "w_in", l, [0, 1024, 1544, 2568, 3336, 4360, NIN], order=[1, 0, 2, 3, 4, 5])
        if l >= 1:
            self.wvd = self.sb("wvd", [128, KC, 32], BF16)
            self.wvu = self.sb("wvu", [32, RW], BF16)
            s.dma("pool", self.wvd[:], W["w_vres_down"].t[l - 1].rearrange("(k p) n -> p k n", p=128),
                  r=[W["w_vres_down"].R()], w=[self.wvd.R()])
            s.dma("pool", self.wvu[:], W["w_vres_up"].t[l - 1], r=[W["w_vres_up"].R()], w=[self.wvu.R()])
        groups = in_groups()
        srcv = src.t.rearrange("(k p) t -> p k t", p=128)
        self.deferred = []

        def run_deferred():
            run, self.deferred = self.deferred, []
            for f in run:
                f()

        def load_tile(tt):
            xt = self.xt_t[tt % 2]
            for kc in range(KC):
                s.dma("sp", xt[:, kc, :], srcv[:, kc, tt * TT:(tt + 1) * TT], r=[src.R(("tile", tt))], w=[xt.R()])

        def prep_tile(tt):
            load_tile(tt)
            self.rmsnorm_tile(l, "g_mix", self.xt_t[tt % 2], self.ht_t[tt % 2], "A")

        prep_tile(0)
        rng_ = [None]
        for tt in range(self.NT):
            t0 = tt * TT
            seq_start = (tt % self.TPS == 0)
            xt = self.xt_t[tt % 2]
            ht = self.ht_t[tt % 2]
            hR = [ht.R(kc) for kc in range(KC)]
            for sub in range(4):
                ps = self.ps()
                for kc in range(KC):
                    s.op("pe", lambda e, kc=kc, sub=sub: e.matmul(ps[:, :], lhsT=ht[:, kc, sub * 128:(sub + 1) * 128],
                                                                   rhs=winv[:, kc, 1024:1536], start=(kc == 0),
                                                                   stop=(kc == KC - 1)),
                         r=[hR[kc], self.wres(self.wbig, 1024)], w=[ps.R()])
                blk = tt * 4 + sub
                vo = self.tmpA("vo", [128, FOXW], BF16)
                s.op("act", lambda e, vo=vo, ps=ps: e.activation(out=vo[:], in_=ps[:, :], func=AF.Copy),
                     r=[ps.R()], w=[vo.R()])
                s.dma("sp", self.vt.t[blk * 128:(blk + 1) * 128, :], vo[:], r=[vo.R()], w=[self.vt.R(blk)])
            if l >= 1:
                ps = self.ps()
                for kc in range(KC):
                    s.op("pe", lambda e, kc=kc, ps=ps: e.matmul(ps[0:32, :], lhsT=self.wvd[:, kc, :], rhs=ht[:, kc, :],
                                                                 start=(kc == 0), stop=(kc == KC - 1)),
                         r=[hR[kc], self.wvd.R()], w=[ps.R()])
                vd = self.tmpA("vd", [32, TT], BF16)
                s.op("act", lambda e, ps=ps: e.activation(out=vd[:], in_=ps[0:32, :], func=AF.Copy), r=[ps.R()],
                     w=[vd.R()])
            for gi, (kind, c0, wd, idx) in enumerate(groups):
                ps = self.ps()
                for kc in range(KC):
                    s.op("pe", lambda e, kc=kc, ps=ps, c0=c0, wd=wd: e.matmul(ps[0:wd, :], lhsT=winv[:, kc, c0:c0 + wd],
                                                                               rhs=ht[:, kc, :], start=(kc == 0),
                                                                               stop=(kc == KC - 1)),
                         r=[hR[kc], self.wres(self.wbig, c0)], w=[ps.R()])
                run_deferred()
                if tt + 1 < self.NT:
                    if gi == 0:
                        load_tile(tt + 1)
                        rng_[0] = self.rn_gen(l, "g_mix", self.xt_t[(tt + 1) % 2], self.ht_t[(tt + 1) % 2])
                    elif gi in (5, 8, 9):
                        next(rng_[0])
                if kind in ("q", "k"):
                    self.epi_qk(l, kind, idx, ps, t0)
                elif kind == "f":
                    self.epi_f(l, ps, t0, seq_start)
                elif kind == "rw":
                    self.epi_rw(l, idx, ps, t0, tt, seq_start, vd if l >= 1 else None)
                else:
                    self.epi_gate(l, idx, ps, t0)
        run_deferred()

    def rsqrt_ps(self, out, ps, scale, eps, mult, np_=128):
        s = self.s
        if not hasattr(self, "_fconst"):
            self._fconst = {}
        def fc(v):
            if v not in self._fconst:
                t = self.psb_(f"fc{len(self._fconst)}", [128, 1])
                s.op("pool", lambda e: e.memset(t[:], float(v)), w=[t.R()])
                self._fconst[v] = t
            return self._fconst[v]
        be = fc(eps)
        bm = fc(float(np.log(mult)))
        s.op("act", lambda e: e.activation(out=out[0:np_, :], in_=ps[0:np_, :], func=AF.Ln, bias=be[0:np_, :], scale=float(scale)),
             r=[ps.R(), be.R()], w=[out.R()])
        s.op("act", lambda e: e.activation(out=out[0:np_, :], in_=out[0:np_, :], func=AF.Exp, bias=bm[0:np_, :], scale=-0.5),
             r=[out.R(), bm.R()], w=[out.R()])

    def ps_rot(self, lo, hi):
        key = (lo, hi)
        if not hasattr(self, "_psr"):
            self._psr = {}
        i = self._psr.get(key, 0)
        self._psr[key] = i + 1
        return self.psb[lo + i % (hi - lo)]

    def stage_B(self, l):
        s = self.s
        S, NB = self.S, self.NB
        self.stage_begin()
        QA = [self.sb(f"QA{i}", [70, S], BF16) for i in range(2)]
        KA = [self.sb(f"KA{i}", [70, S], BF16) for i in range(2)]
        Vb = [self.sb(f"Vb{i}", [128, S // 128, FOXW], BF16) for i in range(2)]
        PT = [self.sb(f"PT{i}", [128, TT], BF16) for i in range(4)]
        rden = [self.sb(f"rden{i}", [64, TT]) for i in range(2)]
        yt = [self.sb(f"yt{i}", [64, TT], BF16) for i in range(2)]
        fm = self.c["fullmask_b"]
        idb = self.c["ident_f_b"]
        onb = self.c["ones_b"]
        NQ = S // TT
        LOOK = 3
        groups = [(b, h) for b in range(NB) for h in range(8)]
        bufs = {}

        def load_group(gi):
            b, h = groups[gi]
            qa, ka = QA[gi % 2], KA[gi % 2]
            if h == 0:
                s.dma("sp", Vb[b % 2][:], self.vt.t[b * S:(b + 1) * S, :].rearrange("(c p) n -> p c n", p=128),
                      r=[self.vt.R(blk) for blk in range(b * S // 128, (b + 1) * S // 128)], w=[Vb[b % 2].R()])
            s.dma("sp", qa[:], self.qa.t[h, :, b * S:(b + 1) * S],
                  r=[self.qa.R(("ones", h))] + [self.qa.R(("qk", h, b * S + j * TT)) for j in range(NQ)] +
                    [self.qa.R(("c", jj, b * S + j * TT)) for j in range(NQ) for jj in range(3)], w=[qa.R()])
            s.dma("sp", ka[:], self.ka.t[h, :, b * S:(b + 1) * S],
                  r=[self.ka.R(("ones", h))] + [self.ka.R(("qk", h, b * S + j * TT)) for j in range(NQ)] +
                    [self.ka.R(("c", jj, b * S + j * TT)) for j in range(NQ) for jj in range(3)], w=[ka.R()])

        work = []
        for gi, (b, h) in enumerate(groups):
            for j in range(NQ):
                nch = 4 * (j + 1)
                for i in range(nch):
                    work.append((gi, b, h, j, i, nch))
        state = {}

        def emit_qk(w):
            gi, b, h, j, i, nch = w
            if j == 0 and i == 0:
                if gi == 0:
                    load_group(0)
                if gi + 1 < len(groups):
                    load_group(gi + 1)
            qa, ka = QA[gi % 2], KA[gi % 2]
            sc = self.ps_rot(4, 8)
            state[w] = sc
            r_ = i - 4 * j
            diag = r_ >= 0
            s.op("pe", lambda e: e.matmul(sc[:, :], lhsT=ka[:, i * 128:(i + 1) * 128], rhs=qa[:, j * TT:(j + 1) * TT], start=True,
                                          stop=not diag), r=[ka.R(), qa.R()], w=[sc.R()])
            if diag:
                s.op("pe", lambda e: e.matmul(sc[:, :], lhsT=idb[:], rhs=fm[:, r_ * 512:(r_ + 1) * 512], start=False, stop=True),
                     r=[idb.R(), fm.R()], w=[sc.R()])

        cnt = {"pt": 0, "acc": None, "den": None, "ep": 0}

        def emit_rest(w):
            gi, b, h, j, i, nch = w
            sc = state.pop(w)
            if i == 0:
                cnt["acc"] = self.ps_rot(0, 2)
                cnt["den"] = self.ps_rot(2, 4)
            acc, den = cnt["acc"], cnt["den"]
            pt = PT[cnt["pt"] % 4]
            cnt["pt"] += 1
            vb = Vb[b % 2]
            s.op("act", lambda e: e.activation(out=pt[:], in_=sc[:, :], func=AF.Exp), r=[sc.R()], w=[pt.R()])
            s.op("pe", lambda e: e.matmul(acc[0:64, :], lhsT=vb[:, i, h * 64:(h + 1) * 64], rhs=pt[:], start=(i == 0),
                                          stop=(i == nch - 1)), r=[vb.R(), pt.R()], w=[acc.R()])
            s.op("pe", lambda e: e.matmul(den[0:64, :], lhsT=onb[:, 0:64], rhs=pt[:], start=(i == 0), stop=(i == nch - 1)),
                 r=[onb.R(), pt.R()], w=[den.R()])
            if i == nch - 1:
                rd = rden[cnt["ep"] % 2]
                y = yt[cnt["ep"] % 2]
                cnt["ep"] += 1
                s.op("dve", lambda e: e.reciprocal(out=rd[:], in_=den[0:64, :]), r=[den.R()], w=[rd.R()])
                s.op("dve", lambda e: e.tensor_tensor(out=y[:], in0=acc[0:64, :], in1=rd[:], op=ALU.mult), r=[acc.R(), rd.R()],
                     w=[y.R()])
                t0 = b * S + j * TT
                s.dma("sp", self.yf.t[h * 64:(h + 1) * 64, t0:t0 + TT], y[:], r=[y.R()], w=[self.yf.R((h, t0))])

        for k in range(min(LOOK, len(work))):
            emit_qk(work[k])
        for k, w in enumerate(work):
            if k + LOOK < len(work):
                emit_qk(work[k + LOOK])
            emit_rest(w)

    def stage_C(self, l):
        import os
        cut = int(os.environ.get("CCUT", "9"))
        use_b = os.environ.get("RWDT", "bf16") == "bf16"
        RD = BF16 if use_b else mybir.dt.float32r
        s = self.s
        S, NB, T = self.S, self.NB, self.T
        W = self.W
        self.stage_begin()
        c = self.c
        idf, blkf, scanm, mS, mA, id8 = c["ident_f"], c["blk2_f"], c["scanmask"], c["mask_S"], c["mask_A"], c["ident8"]

        def V(E, fn, r, w):
            s.op(E, fn, r=[x.R() for x in r], w=[x.R() for x in w])

        def cp(E, out_ap, in_ap, r, w):
            if E == "act":
                V("act", lambda e: e.activation(out=out_ap, in_=in_ap, func=AF.Copy), r, w)
            else:
                V(E, lambda e: e.tensor_copy(out=out_ap, in_=in_ap), r, w)

        Wd = self.sb("Wd", [64, RW], BF16)
        Wa = self.sb("Wa", [64, RW], BF16)
        Wg = self.sb("Wg", [128, RW], BF16)
        s.dma("pool", Wd[:], W["w_decay_up"].t[l], r=[W["w_decay_up"].R()], w=[Wd.R()])
        s.dma("pool", Wa[:], W["w_aaa_up"].t[l], r=[W["w_aaa_up"].R()], w=[Wa.R()])
        s.dma("pool", Wg[:], W["w_gate_up"].t[l], r=[W["w_gate_up"].R()], w=[Wg.R()])
        omka = self.sb("omka", [128, 4])
        o_ka = l * NCOL + COLS["k_a"][0]
        V("dve", lambda e: e.tensor_scalar(out=omka[:], in0=self.colpack[:, o_ka:o_ka + 4], scalar1=-1.0, scalar2=1.0,
                                            op0=ALU.mult, op1=ALU.add), [self.colpack], [omka])
        epsg = self.sb("epsg", [64, 1])
        V("pool", lambda e: e.memset(epsg[:], GN_EPS), [], [epsg])
        AR = [self.sb(f"AR{i}", [128, 8, 2, CH], RD) for i in range(4)]
        BT = [self.sb(f"BT{i}", [128, TT], RD) for i in range(4)]
        KT = [self.sb(f"KT{i}", [128, TT], RD) for i in range(4)]
        ARo = [self.sb(f"ARo{i}", [64, 8, 2, CH], RD) for i in range(4)]
        BTo = [self.sb(f"BTo{i}", [64, TT], RD) for i in range(4)]
        KTo = [self.sb(f"KTo{i}", [64, TT], RD) for i in range(4)]
        VR = [self.sb(f"VR{i}", [128, TT]) for i in range(4)]
        G = [self.sb(f"G{i}", [128, TT], BF16) for i in range(4)]
        BG = [self.sb(f"BG{i}", [128, TT], BF16) for i in range(4)]
        PCp = self.sb("PCp", [128, 8]); PCo = self.sb("PCo", [64, 8])
        PCall = self.sb("PCall", [64, 8, 8])
        H = self.sb("H", [64, 512], RD)
        Hf = self.sb("Hf", [64, 512])
        Ht = self.sb("Ht", [64, 512])
        YN = self.sb("YN", [64, 8, 512])
        dwt = self.sb("dwt", [64, TT]); dat = self.sb("dat", [64, TT]); dgt = self.sb("dgt", [128, TT])
        tdw = self.sb("tdw", [64, TT], BF16); dab = self.sb("dab", [64, TT], BF16); sdg = self.sb("sdg", [128, TT], BF16)
        rT = self.sb("rT", [128, TT]); krT = self.sb("krT", [128, TT])
        sig = self.sb("sig", [128, TT]); aa = self.sb("aa", [128, TT]); kk = self.sb("kk", [128, TT])
        prod = self.sb("prod", [128, TT]); rn = self.sb("rn", [128, TT]); gf = self.sb("gf", [128, TT])
        Lc = self.sb("Lc", [128, TT])
        eL = self.sb("eL", [128, TT]); eLm = self.sb("eLm", [128, TT]); enL = self.sb("enL", [128, TT])
        TOK = [[self.sb(f"tok{i}{b}", [64, 512], RD) for i in range(3)] for b in range(2)]
        SMb = [[self.sb(f"SM{i}{b}", [64, 512], RD) for i in range(4)] for b in range(2)]
        Tfin = [self.sb(f"Tfin{b}", [64, 512], RD) for b in range(2)]
        Xa = [self.sb(f"Xa{i}", [64, 512], RD) for i in range(2)]
        XTa = [self.sb(f"XTa{i}", [64, 512], RD) for i in range(2)]
        TTa = [self.sb(f"TTa{i}", [64, 512], RD) for i in range(2)]
        W0s = self.sb("W0s", [64, 512], RD); Us = self.sb("Us", [64, 512], RD)
        YQ = self.sb("YQ", [64, 8, 512])
        st = {k: self.sb("st_" + k, [64, 64]) for k in ["sum", "sq", "m", "m2", "var", "rstd"]}
        po1 = self.sb("po1", [128, TT]); pob = self.sb("pob", [128, TT], BF16)

        def rr(ap):
            return ap

        def MM(e, out, lhsT, rhs, start, stop):
            return e.matmul(out, lhsT=rr(lhsT), rhs=rr(rhs), start=start, stop=stop)

        def colv(name, hp):
            return self.col(l, name, hp)

        def ar(h):
            return AR[h // 2] if h % 2 == 0 else ARo[h // 2]

        def bt(h):
            return BT[h // 2] if h % 2 == 0 else BTo[h // 2]

        def kt(h):
            return KT[h // 2] if h % 2 == 0 else KTo[h // 2]

        for tt in range(self.NT):
            t0 = tt * TT
            zr = self.zr
            s.dma("sp", dwt[:], zr.t[1536:1600, t0:t0 + TT], r=[zr.R((12, t0))], w=[dwt.R()])
            s.dma("sp", dat[:], zr.t[1600:1664, t0:t0 + TT], r=[zr.R((12, t0))], w=[dat.R()])
            s.dma("sp", dgt[:], zr.t[1664:1792, t0:t0 + TT], r=[zr.R((13, t0))], w=[dgt.R()])
            V("act", lambda e: e.activation(out=tdw[:], in_=dwt[:], func=AF.Tanh), [dwt], [tdw])
            V("pool", lambda e: e.tensor_copy(out=dab[:], in_=dat[:]), [dat], [dab])
            V("act", lambda e: e.activation(out=sdg[:], in_=dgt[:], func=AF.Sigmoid), [dgt], [sdg])
            for hp in range(4):
                hs = slice(hp * 128, (hp + 1) * 128)
                s.dma("sp", rT[:], zr.t[hp * 128:(hp + 1) * 128, t0:t0 + TT], r=[zr.R((hp, t0))], w=[rT.R()])
                s.dma("sp", krT[:], zr.t[512 + hp * 128:512 + (hp + 1) * 128, t0:t0 + TT], r=[zr.R((4 + hp, t0))], w=[krT.R()])
                s.dma("sp", VR[hp][:], zr.t[1024 + hp * 128:1024 + (hp + 1) * 128, t0:t0 + TT], r=[zr.R((8 + hp, t0))],
                      w=[VR[hp].R()])
                p1 = self.ps()
                V("pe", lambda e: e.matmul(p1[:, :], lhsT=Wd[:, hs], rhs=tdw[:], start=True, stop=True), [Wd, tdw], [p1])
                V("act", lambda e: e.activation(out=sig[:], in_=p1[:, :], func=AF.Sigmoid, bias=colv("w0", hp), scale=1.0),
                  [p1, self.colpack], [sig])
                p2 = self.ps()
                V("pe", lambda e: e.matmul(p2[:, :], lhsT=Wa[:, hs], rhs=dab[:], start=True, stop=True), [Wa, dab], [p2])
                V("act", lambda e: e.activation(out=aa[:], in_=p2[:, :], func=AF.Sigmoid, bias=colv("a0", hp), scale=1.0),
                  [p2, self.colpack], [aa])
                p3 = self.ps()
                V("pe", lambda e: e.matmul(p3[:, :], lhsT=Wg[:, hs], rhs=sdg[:], start=True, stop=True), [Wg, sdg], [p3])
                cp("act", gf[:], p3[:, :], [p3], [gf])
                cp("pool", G[hp][:], gf[:], [gf], [G[hp]])
                V("act", lambda e: e.activation(out=kk[:], in_=krT[:], func=AF.Copy, scale=colv("k_k", hp)),
                  [krT, self.colpack], [kk])
                V("pool", lambda e: e.tensor_tensor(out=prod[:], in0=kk[:], in1=kk[:], op=ALU.mult), [kk], [prod])
                p4 = self.ps()
                V("pe", lambda e: e.matmul(p4[:, :], lhsT=blkf[:], rhs=prod[:], start=True, stop=True), [blkf, prod], [p4])
                self.rsqrt_ps(rn, p4, 1.0, 1e-24, 1.0)
                V("dve", lambda e: e.tensor_tensor(out=kk[:], in0=kk[:], in1=rn[:], op=ALU.mult), [kk, rn], [kk])
                V("dve", lambda e: e.tensor_scalar(out=rn[:], in0=aa[:], scalar1=colv("k_a", hp), scalar2=omka[:, hp:hp + 1],
                                                    op0=ALU.mult, op1=ALU.add), [aa, self.colpack, omka], [rn])
                V("dve", lambda e: e.tensor_tensor(out=krT[:], in0=krT[:], in1=rn[:], op=ALU.mult), [krT, rn], [krT])
                V("pool", lambda e: e.tensor_tensor(out=aa[:], in0=kk[:], in1=aa[:], op=ALU.mult), [kk, aa], [aa])
                V("act", lambda e: e.activation(out=sig[:], in_=sig[:], func=AF.Copy, scale=-float(np.exp(-0.5))),
                  [sig], [sig])
                V("dve", lambda e: e.tensor_tensor_scan(out=Lc[:], data0=scanm[:], data1=sig[:], initial=0.0, op0=ALU.mult,
                                                         op1=ALU.add), [scanm, sig], [Lc])
                V("pool", lambda e: e.tensor_tensor(out=sig[:], in0=Lc[:], in1=sig[:], op=ALU.subtract), [Lc, sig], [sig])
                V("act", lambda e: e.activation(out=eL[:], in_=Lc[:], func=AF.Exp), [Lc], [eL])
                V("act", lambda e: e.activation(out=eLm[:], in_=sig[:], func=AF.Exp), [sig], [eLm])
                V("act", lambda e: e.activation(out=enL[:], in_=Lc[:], func=AF.Exp, scale=-1.0), [Lc], [enL])
                arv = AR[hp]
                V("dve", lambda e: e.scalar_tensor_tensor(out=arv[:, :, 0, :], in0=kk[:].rearrange("p (c t) -> p c t", t=CH),
                                                           scalar=-1.0, in1=eLm[:].rearrange("p (c t) -> p c t", t=CH),
                                                           op0=ALU.mult, op1=ALU.mult), [kk, eLm], [arv])
                V("dve", lambda e: e.tensor_tensor(out=arv[:, :, 1, :], in0=rT[:].rearrange("p (c t) -> p c t", t=CH),
                                                    in1=eL[:].rearrange("p (c t) -> p c t", t=CH), op=ALU.mult), [rT, eL], [arv])
                V("pool", lambda e: e.tensor_tensor(out=BT[hp][:], in0=aa[:], in1=enL[:], op=ALU.mult), [aa, enL], [BT[hp]])
                V("dve", lambda e: e.tensor_tensor(out=KT[hp][:], in0=krT[:], in1=enL[:], op=ALU.mult), [krT, enL], [KT[hp]])
                V("pool", lambda e: e.tensor_copy(out=PCp[:], in_=eL[:, CH - 1::CH]), [eL], [PCp])
                s.dma("sp", ARo[hp][:], AR[hp][64:128, :, :, :], r=[AR[hp].R()], w=[ARo[hp].R()])
                s.dma("sp", BTo[hp][:], BT[hp][64:128, :], r=[BT[hp].R()], w=[BTo[hp].R()])
                s.dma("sp", KTo[hp][:], KT[hp][64:128, :], r=[KT[hp].R()], w=[KTo[hp].R()])
                s.dma("sp", PCo[:], PCp[64:128, :], r=[PCp.R()], w=[PCo.R()])
                V("pool", lambda e: e.tensor_copy(out=PCall[:, :, 2 * hp], in_=PCp[0:64, :]), [PCp], [PCall])
                V("pool", lambda e: e.tensor_copy(out=PCall[:, :, 2 * hp + 1], in_=PCo[:]), [PCo], [PCall])
                V("dve", lambda e: e.scalar_tensor_tensor(out=prod[:], in0=rT[:], scalar=colv("r_k", hp), in1=krT[:],
                                                           op0=ALU.mult, op1=ALU.mult), [rT, krT, self.colpack], [prod])
                p5 = self.ps()
                V("pe", lambda e: e.matmul(p5[:, :], lhsT=blkf[:], rhs=prod[:], start=True, stop=True), [blkf, prod], [p5])
                V("dve", lambda e: e.tensor_tensor(out=rn[:], in0=p5[:, :], in1=VR[hp][:], op=ALU.mult), [p5, VR[hp]], [rn])
                V("pool", lambda e: e.tensor_tensor(out=BG[hp][:], in0=rn[:], in1=gf[:], op=ALU.mult), [rn, gf], [BG[hp]])
            def indep(cc):
                cs = slice(cc * CH, (cc + 1) * CH)
                b = cc % 2
                Btok, Ktok, Vtok = TOK[b]
                SM = SMb[b]
                for srcs, dst, eng in [(BT, Btok, "act"), (KT, Ktok, "dve"), (VR, Vtok, "act")]:
                    pt_ = self.ps()
                    if use_b and srcs is not VR:
                        pv_ = pt_.t.bitcast(BF16)
                        idb_ = c["ident_f_b"]
                        for hp in range(4):
                            V("pe", lambda e, hp=hp: e.transpose(out=pv_[0:64, hp * 128:(hp + 1) * 128], in_=srcs[hp][:, cs],
                                                                 identity=idb_[:]), [srcs[hp], idb_], [pt_])
                        cp(eng, dst[:], pv_[0:64, 0:512], [pt_], [dst])
                    else:
                        for hp in range(4):
                            V("pe", lambda e, hp=hp: e.transpose(out=pt_[0:64, hp * 128:(hp + 1) * 128],
                                                                 in_=(srcs[hp][:, cs] if srcs is VR else srcs[hp][:, cs].bitcast(F32)),
                                                                 identity=idf[:]), [srcs[hp], idf], [pt_])
                        cp(eng, dst[:], pt_[0:64, :], [pt_], [dst])
                    yield
                for hp in range(4):
                    pS = self.ps()
                    for par in range(2):
                        h = 2 * hp + par
                        rhs = ar(h)[0:64, cc, :, :].rearrange("p a t -> p (a t)")
                        V("pe", lambda e, par=par, h=h, rhs=rhs: MM(e, pS[0:64, par * 256:par * 256 + 128], lhsT=bt(h)[0:64, cs],
                                                                   rhs=rhs, start=True, stop=True), [bt(h), ar(h)], [pS])
                        V("pe", lambda e, par=par, h=h, rhs=rhs: MM(e, pS[0:64, par * 256 + 128:par * 256 + 256],
                                                                   lhsT=kt(h)[0:64, cs], rhs=rhs, start=True, stop=True),
                          [kt(h), ar(h)], [pS])
                    V("dve", lambda e, hp=hp, pS=pS: e.tensor_tensor(out=SM[hp][:], in0=pS[0:64, :], in1=mS[:], op=ALU.mult),
                      [pS, mS], [SM[hp]])
                    if hp % 2 == 1:
                        yield
                pA = self.ps()
                for h in range(8):
                    V("pe", lambda e, h=h: MM(e, pA[0:64, h * 64:(h + 1) * 64], lhsT=ar(h)[0:64, cc, 0, :],
                                              rhs=bt(h)[0:64, cs], start=True, stop=True), [ar(h), bt(h)], [pA])
                X, XT, Tt = Xa[0], XTa[0], TTa[0]
                V("dve", lambda e: e.tensor_tensor(out=X[:], in0=pA[0:64, :], in1=mA[:], op=ALU.mult), [pA, mA], [X])
                for hp in range(4):
                    V("pool", lambda e, hp=hp: e.tensor_copy(
                        out=XT[:, hp * 128:(hp + 1) * 128].rearrange("p (a t) -> p a t", a=2),
                        in_=SM[hp][:, :].rearrange("p (a t) -> p a t", a=2)[:, :, 0:64]), [SM[hp]], [XT])
                V("pool", lambda e: e.tensor_tensor(out=Tt[:], in0=XT[:], in1=id8[:], op=ALU.add), [XT, id8], [Tt])
                yield
                for k in range(1, 6):
                    Xn, XTn, Tn = Xa[k % 2], XTa[k % 2], (TTa[k % 2] if k < 5 else Tfin[b])
                    pX = self.ps()
                    for h in range(8):
                        hsl = slice(h * 64, (h + 1) * 64)
                        V("pe", lambda e, hsl=hsl: MM(e, pX[0:64, hsl], lhsT=XT[:, hsl], rhs=X[:, hsl], start=True, stop=True),
                          [XT, X], [pX])
                    if k < 5:
                        pXT = self.ps()
                        for h in range(8):
                            hsl = slice(h * 64, (h + 1) * 64)
                            V("pe", lambda e, hsl=hsl: MM(e, pXT[0:64, hsl], lhsT=X[:, hsl], rhs=XT[:, hsl], start=True,
                                                           stop=True), [XT, X], [pXT])
                    cp("act", Xn[:], pX[0:64, :], [pX], [Xn])
                    if k < 5:
                        cp("dve", XTn[:], pXT[0:64, :], [pXT], [XTn])
                    yield
                    pT = self.ps()
                    for h in range(8):
                        hsl = slice(h * 64, (h + 1) * 64)
                        V("pe", lambda e, hsl=hsl: MM(e, pT[0:64, hsl], lhsT=Xn[:, hsl], rhs=Tt[:, hsl], start=True, stop=True),
                          [Xn, Tt], [pT])
                    V("dve", lambda e: e.tensor_tensor(out=Tn[:], in0=pT[0:64, :], in1=Tt[:], op=ALU.add), [pT, Tt], [Tn])
                    X, XT, Tt = Xn, XTn, Tn
                    yield

            def dep(cc):
                cs = slice(cc * CH, (cc + 1) * CH)
                b = cc % 2
                Btok, Ktok, Vtok = TOK[b]
                SM = SMb[b]
                Tt = Tfin[b]
                if tt % self.TPS == 0 and cc == 0:
                    V("pool", lambda e: e.memset(Hf[:], 0.0), [], [Hf])
                    V("pool", lambda e: e.tensor_copy(out=H[:], in_=Hf[:]), [Hf], [H])

                def hd(h):
                    return h // 2, (h % 2) * 256, slice(h * 64, (h + 1) * 64)
                pW = self.ps()
                for h in range(8):
                    hp, b0, hsl = hd(h)
                    V("pe", lambda e, hp=hp, b0=b0, hsl=hsl: MM(e, pW[0:64, hsl], lhsT=SM[hp][:, b0 + 128:b0 + 192],
                                                               rhs=Vtok[:, hsl], start=True, stop=False), [SM[hp], Vtok], [pW])
                    V("pe", lambda e, h=h, hsl=hsl: MM(e, pW[0:64, hsl], lhsT=ar(h)[0:64, cc, 0, :], rhs=H[:, hsl], start=False,
                                                      stop=True), [ar(h), H], [pW])
                cp("act", W0s[:], pW[0:64, :], [pW], [W0s])
                yield
                pU = self.ps()
                for h in range(8):
                    hp, b0, hsl = hd(h)
                    V("pe", lambda e, hsl=hsl: MM(e, pU[0:64, hsl], lhsT=Tt[:, hsl], rhs=W0s[:, hsl], start=True, stop=True),
                      [Tt, W0s], [pU])
                cp("dve", Us[:], pU[0:64, :], [pU], [Us])
                yield
                pY = self.ps()
                for h in range(8):
                    hp, b0, hsl = hd(h)
                    V("pe", lambda e, hp=hp, b0=b0, hsl=hsl: MM(e, pY[0:64, hsl], lhsT=SM[hp][:, b0 + 192:b0 + 256],
                                                               rhs=Vtok[:, hsl], start=True, stop=False), [SM[hp], Vtok], [pY])
                    V("pe", lambda e, hp=hp, b0=b0, hsl=hsl: MM(e, pY[0:64, hsl], lhsT=SM[hp][:, b0 + 64:b0 + 128],
                                                               rhs=Us[:, hsl], start=False, stop=False), [SM[hp], Us], [pY])
                    V("pe", lambda e, h=h, hsl=hsl: MM(e, pY[0:64, hsl], lhsT=ar(h)[0:64, cc, 1, :], rhs=H[:, hsl], start=False,
                                                      stop=True), [ar(h), H], [pY])
                cp("act", YN[:, cc, :], pY[0:64, :], [pY], [YN])
                yield
                pH = self.ps()
                for h in range(8):
                    hp, b0, hsl = hd(h)
                    V("pe", lambda e, hsl=hsl: MM(e, pH[0:64, hsl], lhsT=Btok[:, hsl], rhs=Us[:, hsl], start=True, stop=False),
                      [Btok, Us], [pH])
                    V("pe", lambda e, hsl=hsl: MM(e, pH[0:64, hsl], lhsT=Ktok[:, hsl], rhs=Vtok[:, hsl], start=False, stop=True),
                      [Ktok, Vtok], [pH])
                V("dve", lambda e: e.tensor_tensor(out=Ht[:], in0=pH[0:64, :], in1=Hf[:], op=ALU.add), [pH, Hf], [Ht])
                yield
                V("dve", lambda e: e.tensor_tensor(out=Hf[:].rearrange("p (h v) -> p h v", h=8),
                                                    in0=Ht[:].rearrange("p (h v) -> p h v", h=8),
                                                    in1=PCall[:, cc, :].unsqueeze(2).broadcast_to([64, 8, 64]), op=ALU.mult),
                  [Ht, PCall], [Hf])
                V("act", lambda e: e.activation(out=H[:], in_=Hf[:], func=AF.Copy), [Hf], [H])
                yield

            def drive(gens):
                gens = [g for g in gens if g is not None]
                while gens:
                    for g in list(gens):
                        try:
                            next(g)
                        except StopIteration:
                            gens.remove(g)

            if cut >= 4:
                drive([indep(0)])
                for cc in range(8):
                    drive([dep(cc), indep(cc + 1) if cc + 1 < 8 else None])
            if cut >= 5:
                yr3 = YN[:].rearrange("p c (h v) -> p (c h) v", h=8)
                yq3 = YQ[:].rearrange("p c (h v) -> p (c h) v", h=8)
                V("act", lambda e: e.activation(out=YQ[:], in_=YN[:], func=AF.Square), [YN], [YQ])
                V("dve", lambda e: e.tensor_reduce(out=st["sum"][:], in_=yr3, axis=AX.X, op=ALU.add), [YN], [st["sum"]])
                V("dve", lambda e: e.tensor_reduce(out=st["sq"][:], in_=yq3, axis=AX.X, op=ALU.add), [YQ], [st["sq"]])
                V("act", lambda e: e.activation(out=st["m"][:], in_=st["sum"][:], func=AF.Copy, scale=1.0 / 64),
                  [st["sum"]], [st["m"]])
                V("pool", lambda e: e.tensor_tensor(out=st["m2"][:], in0=st["m"][:], in1=st["m"][:], op=ALU.mult), [st["m"]],
                  [st["m2"]])
                V("dve", lambda e: e.scalar_tensor_tensor(out=st["var"][:], in0=st["sq"][:], scalar=1.0 / 64, in1=st["m2"][:],
                                                           op0=ALU.mult, op1=ALU.subtract), [st["sq"], st["m2"]], [st["var"]])
                V("act", lambda e: e.activation(out=st["rstd"][:], in_=st["var"][:], func=AF.Ln, bias=epsg[:], scale=1.0),
                  [st["var"], epsg], [st["rstd"]])
                V("act", lambda e: e.activation(out=st["rstd"][:], in_=st["rstd"][:], func=AF.Exp, scale=-0.5), [st["rstd"]],
                  [st["rstd"]])
                V("dve", lambda e: e.tensor_tensor(out=yq3, in0=yr3, in1=st["m"][:].unsqueeze(2).broadcast_to([64, 64, 64]),
                                                    op=ALU.subtract), [YN, st["m"]], [YQ])
                V("dve", lambda e: e.tensor_tensor(out=yr3, in0=yq3, in1=st["rstd"][:].unsqueeze(2).broadcast_to([64, 64, 64]),
                                                    op=ALU.mult), [YQ, st["rstd"]], [YN])
            for hp in range(4 if cut >= 6 else 0):
                pO = self.ps()
                for cc in range(8):
                    V("pe", lambda e, cc=cc: e.transpose(out=pO[:, cc * CH:(cc + 1) * CH], in_=YN[:, cc, hp * 128:(hp + 1) * 128],
                                                         identity=idf[0:64, 0:64]), [YN, idf], [pO])
                V("dve", lambda e: e.tensor_scalar(out=po1[:], in0=pO[:, :], scalar1=colv("gn_g", hp), scalar2=colv("gn_b", hp),
                                                    op0=ALU.mult, op1=ALU.add), [pO, self.colpack], [po1])
                V("pool", lambda e: e.tensor_tensor(out=po1[:], in0=po1[:], in1=G[hp][:], op=ALU.mult), [po1, G[hp]], [po1])
                V("dve", lambda e: e.tensor_tensor(out=pob[:], in0=po1[:], in1=BG[hp][:], op=ALU.add), [po1, BG[hp]], [pob])
                s.dma("sp", self.yr.t[hp * 128:(hp + 1) * 128, t0:t0 + TT], pob[:], r=[pob.R()], w=[self.yr.R((hp, t0))])

    def load_wb(self, dst, dstv, name, l, bounds, order=None):
        Wt = self.W[name]
        srcv = Wt.t[l].rearrange("(k p) n -> p k n", p=128)
        dst._bounds = list(bounds)
        for i in (order if order is not None else range(len(bounds) - 1)):
            c0, c1 = bounds[i], bounds[i + 1]
            self.s.dma("pool", dstv[:, :, c0:c1], srcv[:, :, c0:c1], r=[Wt.R()], w=[dst.R(("blk", i))])

    @staticmethod
    def wres(dst, col):
        import bisect
        return dst.R(("blk", bisect.bisect_right(dst._bounds, col) - 1))

    def load_w(self, dst, name, l, nk):
        Wt = self.W[name]
        srcv = Wt.t[l].rearrange("(k p) n -> p k n", p=128)
        for kc in range(nk):
            self.s.dma("pool", dst[:, kc, :], srcv[:, kc, :], r=[Wt.R()], w=[dst.R()])

    def stage_D(self, l, src):
        s = self.s
        self.stage_begin()
        wof = self.sb("wof", [128, 4, D], BF16)
        wor = self.sb("wor", [128, 4, D], BF16)
        wout = self.sb("wout", [128, KC, D], BF16)
        for blk in range(2):
            self.load_wb(wof, wof.t, "w_o_fox", l, [0, 512, 1024], order=[blk])
            self.load_wb(wor, wor.t, "w_o_rwkv", l, [0, 512, 1024], order=[blk])
        self.load_wb(wout, wout.t, "w_out", l, [0, 512, 1024])
        yfT = [self.sb(f"yfT{i}", [128, 4, TT], BF16) for i in range(2)]
        yrT = [self.sb(f"yrT{i}", [128, 4, TT], BF16) for i in range(2)]
        gtT = [self.sb(f"gtT{i}", [128, 16, TT], BF16) for i in range(2)]
        xt_ = [self.sb(f"xD{i}", [128, KC, TT]) for i in range(2)]
        mg = self.sb("mg", [128, KC, TT], BF16)
        srcv = src.t.rearrange("(k p) t -> p k t", p=128)
        dstv = self.xs.t.rearrange("(k p) t -> p k t", p=128)
        def loads(tt):
            t0 = tt * TT
            yf, yr, gt, xt = yfT[tt % 2], yrT[tt % 2], gtT[tt % 2], xt_[tt % 2]
            s.dma("sp", yf[:], self.yf.t[:, t0:t0 + TT].rearrange("(k p) t -> p k t", p=128),
                  r=[self.yf.R((h, t0)) for h in range(8)], w=[yf.R()])
            s.dma("sp", yr[:], self.yr.t[:, t0:t0 + TT].rearrange("(k p) t -> p k t", p=128),
                  r=[self.yr.R((hp, t0)) for hp in range(4)], w=[yr.R()])
            s.dma("sp", gt[:], self.gt.t[:, t0:t0 + TT].rearrange("(k p) t -> p k t", p=128),
                  r=[self.gt.R((i, t0)) for i in range(16)], w=[gt.R()])
            for kc in range(KC):
                s.dma("sp", xt[:, kc, :], srcv[:, kc, t0:t0 + TT], r=[src.R(("tile", tt))], w=[xt.R()])

        loads(0)
        for tt in range(self.NT):
            t0 = tt * TT
            yf, yr, gt, xt = yfT[tt % 2], yrT[tt % 2], gtT[tt % 2], xt_[tt % 2]
            if tt + 1 < self.NT:
                loads(tt + 1)
            for n in range(KC):
                ns = slice(n * 128, (n + 1) * 128)
                pa = self.ps()
                for kc in range(4):
                    s.op("pe", lambda e, kc=kc: e.matmul(pa[:, :], lhsT=wof[:, kc, ns], rhs=yf[:, kc, :], start=(kc == 0),
                                                         stop=(kc == 3)), r=[self.wres(wof, n * 128), yf.R()], w=[pa.R()])
                pb = self.ps()
                for kc in range(4):
                    s.op("pe", lambda e, kc=kc: e.matmul(pb[:, :], lhsT=wor[:, kc, ns], rhs=yr[:, kc, :], start=(kc == 0),
                                                         stop=(kc == 3)), r=[self.wres(wor, n * 128), yr.R()], w=[pb.R()])
                m1 = self.tmpA("m1", [128, TT])
                s.op("dve", lambda e, m1=m1: e.tensor_tensor(out=m1[:], in0=pa[:, :], in1=gt[:, n, :], op=ALU.mult),
                     r=[pa.R(), gt.R()], w=[m1.R()])
                m2 = self.tmpA("m2", [128, TT])
                s.op("dve", lambda e, m2=m2: e.tensor_tensor(out=m2[:], in0=pb[:, :], in1=gt[:, 8 + n, :], op=ALU.mult),
                     r=[pb.R(), gt.R()], w=[m2.R()])
                s.op("pool", lambda e, m1=m1, m2=m2: e.tensor_tensor(out=mg[:, n, :], in0=m1[:], in1=m2[:], op=ALU.add),
                     r=[m1.R(), m2.R()], w=[mg.R(n)])
            for n in range(KC):
                ns = slice(n * 128, (n + 1) * 128)
                po = self.ps()
                for kc in range(KC):
                    s.op("pe", lambda e, kc=kc: e.matmul(po[:, :], lhsT=wout[:, kc, ns], rhs=mg[:, kc, :], start=(kc == 0),
                                                         stop=(kc == KC - 1)), r=[self.wres(wout, n * 128), mg.R(kc)], w=[po.R()])
                xo = self.tmpA("xo", [128, TT], nbuf=3)
                s.op("dve", lambda e, xo=xo: e.tensor_tensor(out=xo[:], in0=po[:, :], in1=xt[:, n, :], op=ALU.add),
                     r=[po.R(), xt.R()], w=[xo.R()])
                s.dma("sp", dstv[:, n, t0:t0 + TT], xo[:], r=[xo.R()], w=[self.xs.R(("tile", tt))])

    def stage_E1(self, l):
        s = self.s
        self.stage_begin()
        wup = self.sb("wup", [128, KC, 2 * DFF], BF16)
        jb = [0, 4, 10, 16, NJ]
        self.load_wb(wup, wup.t, "w_up", l, [x * 128 for x in jb] + [DFF + x * 128 for x in jb[1:]],
                     order=[0, 4, 1, 5, 2, 6, 3, 7])
        xt = self.sb("xE", [128, KC, TT])
        h2 = [self.sb(f"h2{i}", [128, KC, TT], BF16) for i in range(2)]
        ub = [self.sb(f"ub{i}", [128, TT + 2]) for i in range(2)]
        srcv = self.xs.t.rearrange("(k p) t -> p k t", p=128)
        K0 = float(2.0 * np.sqrt(2.0 / np.pi))
        def load_tile(tt):
            for kc in range(KC):
                s.dma("sp", xt[:, kc, :], srcv[:, kc, tt * TT:(tt + 1) * TT], r=[self.xs.R(("tile", tt))], w=[xt.R()])

        def prep_tile(tt):
            load_tile(tt)
            self.rmsnorm_tile(l, "g_ffn", xt, h2[tt % 2], "E")

        prep_tile(0)
        rng_ = [None]
        for tt in range(self.NT):
            t0 = tt * TT
            seq_start = (tt % self.TPS == 0)
            ht = h2[tt % 2]
            hR = [ht.R(kc) for kc in range(KC)]
            for j in range(NJ):
                if tt + 1 < self.NT:
                    if j == 0:
                        load_tile(tt + 1)
                        rng_[0] = self.rn_gen(l, "g_ffn", xt, h2[(tt + 1) % 2])
                    elif j in (3, 5, 6):
                        next(rng_[0])
                cres = []
                for half in range(2):
                    idx = half * NJ + j
                    c0 = idx * 128
                    pu = self.ps()
                    for kc in range(KC):
                        s.op("pe", lambda e, kc=kc, pu=pu, c0=c0: e.matmul(pu[:, :], lhsT=wup[:, kc, c0:c0 + 128], rhs=ht[:, kc, :],
                                                                           start=(kc == 0), stop=(kc == KC - 1)),
                             r=[hR[kc], self.wres(wup, c0)], w=[pu.R()])
                    u = ub[half]
                    s.op("act", lambda e, u=u, pu=pu: e.activation(out=u[:, 2:TT + 2], in_=pu[:, :], func=AF.Copy), r=[pu.R()],
                         w=[u.R()])
                    if seq_start:
                        s.op("pool", lambda e, u=u: e.memset(u[:, 0:2], 0.0), w=[u.R()])
                    else:
                        s.op("pool", lambda e, u=u, idx=idx: e.tensor_copy(out=u[:, 0:2], in_=self.ucarry[:, idx, :]),
                             r=[self.ucarry.R(idx)], w=[u.R()])
                    s.op("pool", lambda e, u=u, idx=idx: e.tensor_copy(out=self.ucarry[:, idx, :], in_=u[:, TT:TT + 2]),
                         r=[u.R()], w=[self.ucarry.R(idx)])
                    c1 = self.tmpA(f"c1{half}", [128, TT], nbuf=1)
                    s.op("act", lambda e, pu=pu, c1=c1, idx=idx: e.activation(out=c1[:], in_=pu[:, :], func=AF.Identity,
                                                                              scale=self.col(l, "cw2", idx),
                                                                              bias=self.col(l, "cb", idx)),
                         r=[pu.R(), self.colpack.R()], w=[c1.R()])
                    c2 = self.tmpA(f"c2{half}", [128, TT], nbuf=1)
                    s.op("dve", lambda e, u=u, c1=c1, c2=c2, idx=idx: e.scalar_tensor_tensor(out=c2[:], in0=u[:, 1:TT + 1],
                                                                                           scalar=self.col(l, "cw1", idx), in1=c1[:],
                                                                                           op0=ALU.mult, op1=ALU.add),
                         r=[u.R(), c1.R(), self.colpack.R()], w=[c2.R()])
                    c3 = self.tmpA(f"c3{half}", [128, TT], nbuf=2)
                    s.op("dve", lambda e, u=u, c2=c2, c3=c3, idx=idx: e.scalar_tensor_tensor(out=c3[:], in0=u[:, 0:TT],
                                                                                           scalar=self.col(l, "cw0", idx), in1=c2[:],
                                                                                           op0=ALU.mult, op1=ALU.add),
                         r=[u.R(), c2.R(), self.colpack.R()], w=[c3.R()])
                    cres.append(c3)
                u1, u2 = cres
                sg = self.tmpA("gsg", [128, TT], nbuf=2)
                s.op("act", lambda e, sg=sg, u1=u1: e.activation(out=sg[:], in_=u1[:], func=AF.Gelu_apprx_tanh), r=[u1.R()],
                     w=[sg.R()])
                ao = self.tmpA("gao", [128, TT], BF16, nbuf=3)
                s.op("pool", lambda e, sg=sg, u2=u2, ao=ao: e.tensor_tensor(out=ao[:], in0=sg[:], in1=u2[:], op=ALU.mult),
                     r=[sg.R(), u2.R()], w=[ao.R()])
                s.dma("sp", self.actT.t[j * 128:(j + 1) * 128, t0:t0 + TT], ao[:], r=[ao.R()], w=[self.actT.R((j, t0))])

    def stage_E2(self, l, last):
        s = self.s
        self.stage_begin()
        wdn = self.sb("wdn", [128, NJ, D], BF16)
        wpg = self.sb("wpg", [128, KC, D], BF16)
        wpu = self.sb("wpu", [128, 2, D], BF16)
        self.load_wb(wdn, wdn.t, "w_down", l, [0, 256, 512, 1024])
        self.load_wb(wpg, wpg.t, "w_ple_gate", l, [0, 512, 1024])
        self.load_wb(wpu, wpu.t, "w_ple_up", l, [0, 1024])
        at_ = [self.sb(f"at{i}", [128, NJ, TT], BF16) for i in range(2)]
        pt_ = [self.sb(f"pp{i}", [128, 2, TT], BF16) for i in range(2)]
        x2_ = [self.sb(f"x2{i}", [128, KC, TT]) for i in range(2)]
        h3_ = [self.sb(f"h3{i}", [128, KC, TT], BF16) for i in range(2)]
        srcv = self.xs.t.rearrange("(k p) t -> p k t", p=128)
        dst = self.out if last else self.xs
        dstv = dst.t.rearrange("(k p) t -> p k t", p=128)
        pv = self.pT.t[l].rearrange("(k p) t -> p k t", p=128)

        def load_at(tt):
            t0 = tt * TT
            at = at_[tt % 2]
            s.dma("sp", at[:], self.actT.t[:, t0:t0 + TT].rearrange("(k p) t -> p k t", p=128),
                  r=[self.actT.R((j, t0)) for j in range(NJ)], w=[at.R()])

        def down(tt):
            t0 = tt * TT
            at, pp, x2, h3 = at_[tt % 2], pt_[tt % 2], x2_[tt % 2], h3_[tt % 2]
            if tt + 1 < self.NT:
                load_at(tt + 1)
            s.dma("pool", pp[:], pv[:, :, t0:t0 + TT], r=[self.pT.R()], w=[pp.R()])
            for kc in range(KC):
                s.dma("sp", x2[:, kc, :], srcv[:, kc, t0:t0 + TT], r=[self.xs.R(("tile", tt))], w=[x2.R()])
            for n in range(KC):
                ns = slice(n * 128, (n + 1) * 128)
                pd = self.ps()
                for j in range(NJ):
                    s.op("pe", lambda e, j=j: e.matmul(pd[:, :], lhsT=wdn[:, j, ns], rhs=at[:, j, :], start=(j == 0),
                                                       stop=(j == NJ - 1)), r=[self.wres(wdn, n * 128), at.R()], w=[pd.R()])
                s.op("dve", lambda e: e.tensor_tensor(out=x2[:, n, :], in0=pd[:, :], in1=x2[:, n, :], op=ALU.add),
                     r=[pd.R(), x2.R()], w=[x2.R()])
            self.rmsnorm_tile(l, "g_ple", x2, h3, "F")

        def ple(tt):
            t0 = tt * TT
            pp, x2, h3 = pt_[tt % 2], x2_[tt % 2], h3_[tt % 2]
            for n in range(KC):
                ns = slice(n * 128, (n + 1) * 128)
                pg = self.ps()
                for kc in range(KC):
                    s.op("pe", lambda e, kc=kc: e.matmul(pg[:, :], lhsT=wpg[:, kc, ns], rhs=h3[:, kc, :], start=(kc == 0),
                                                         stop=(kc == KC - 1)), r=[self.wres(wpg, n * 128), h3.R(kc)], w=[pg.R()])
                sgt = self.tmpA("sgt", [128, TT])
                s.op("act", lambda e, sgt=sgt: e.activation(out=sgt[:], in_=pg[:, :], func=AF.Sigmoid), r=[pg.R()], w=[sgt.R()])
                pq = self.ps()
                for kc in range(2):
                    s.op("pe", lambda e, kc=kc: e.matmul(pq[:, :], lhsT=wpu[:, kc, ns], rhs=pp[:, kc, :], start=(kc == 0),
                                                         stop=(kc == 1)), r=[self.wres(wpu, n * 128), pp.R()], w=[pq.R()])
                s.op("dve", lambda e, sgt=sgt: e.tensor_tensor(out=sgt[:], in0=pq[:, :], in1=sgt[:], op=ALU.mult),
                     r=[pq.R(), sgt.R()], w=[sgt.R()])
                xo = self.tmpA("xo2", [128, TT], nbuf=2)
                s.op("pool", lambda e, sgt=sgt, xo=xo: e.tensor_tensor(out=xo[:], in0=sgt[:], in1=x2[:, n, :], op=ALU.add),
                     r=[sgt.R(), x2.R()], w=[xo.R()])
                s.dma("sp", dstv[:, n, t0:t0 + TT], xo[:], r=[xo.R()], w=[dst.R(("tile", tt))])

        load_at(0)
        down(0)
        for tt in range(self.NT):
            if tt + 1 < self.NT:
                down(tt + 1)
            ple(tt)

    def tmpA(self, name, shape, dt=F32, nbuf=2):
        key = ("tmp", name)
        if key not in self.ep:
            self.ep[key] = [[self.sb(f"tA_{name}{i}", shape, dt) for i in range(nbuf)], 0]
        lst = self.ep[key]
        t = lst[0][lst[1] % nbuf]
        lst[1] += 1
        return t

    def epi_qk(self, l, kind, idx, ps, t0):
        s = self.s
        sq = self.tmpA("qsq", [128, TT], BF16)
        s.op("act", lambda e: e.activation(out=sq[:], in_=ps[:, :], func=AF.Square), r=[ps.R()], w=[sq.R()])
        self.deferred.append(lambda: self.epi_qk2(l, kind, idx, ps, t0, sq))

    def epi_qk2(self, l, kind, idx, ps, t0, sq):
        s = self.s
        ps2 = self.ps()
        s.op("pe", lambda e: e.matmul(ps2[:, :], lhsT=self.c["blk2_f_b"][:], rhs=sq[:], start=True, stop=True),
             r=[self.c["blk2_f_b"].R(), sq.R()], w=[ps2.R()])
        rs2 = self.tmpA("qrs2", [128, TT], nbuf=1)
        self.rsqrt_ps(rs2, ps2, 1.0 / 64, RMS_EPS, 0.125 if kind == "q" else 1.0)
        o = self.tmpA("qo", [128, TT], BF16)
        gname = "g_q" if kind == "q" else "g_k"
        s.op("dve", lambda e: e.scalar_tensor_tensor(out=o[:], in0=ps[:, :], scalar=self.col(l, gname), in1=rs2[:],
                                                      op0=ALU.mult, op1=ALU.mult),
             r=[ps.R(), rs2.R(), self.colpack.R()], w=[o.R()])
        dst = self.qa if kind == "q" else self.ka
        for hh in range(2):
            h = idx * 2 + hh
            s.dma("sp", dst[h, 0:64, t0:t0 + TT], o[hh * 64:(hh + 1) * 64, :], r=[o.R()], w=[dst.R(("qk", h, t0))])

    def epi_f(self, l, ps, t0, seq_start):
        s = self.s
        e1 = self.tmpA("fe", [8, TT], nbuf=1)
        s.op("act", lambda e: e.activation(out=e1[:], in_=ps[0:8, :], func=AF.Exp, bias=self.negb[:, l:l + 1], scale=-1.0),
             r=[ps.R(), self.negb.R()], w=[e1.R()])
        l1 = e1
        one = self.c["ones_f"]
        s.op("act", lambda e: e.activation(out=l1[:], in_=e1[:], func=AF.Ln, bias=one[0:8, 0:1], scale=1.0),
             r=[e1.R(), one.R()], w=[l1.R()])
        c = self.tmpA("fc", [8, TT], nbuf=1)
        if seq_start:
            init = 0.0
            rr = []
        else:
            init = self.ccar[:, 0:1]
            rr = [self.ccar.R()]
        s.op("dve", lambda e: e.tensor_tensor_scan(out=c[:], data0=self.c["ones_f"][0:8, :], data1=l1[:], initial=init,
                                                    op0=ALU.mult, op1=ALU.subtract),
             r=[self.c["ones_f"].R(), l1.R()] + rr, w=[c.R()])
        s.op("dve", lambda e: e.tensor_copy(out=self.ccar[:, 0:1], in_=c[:, TT - 1:TT]), r=[c.R()], w=[self.ccar.R()])
        hi = self.tmpA("fhi", [8, TT], BF16, nbuf=1)
        s.op("dve", lambda e: e.tensor_copy(out=hi[:], in_=c[:]), r=[c.R()], w=[hi.R()])
        r1 = self.tmpA("fr1", [8, TT], nbuf=1)
        s.op("dve", lambda e: e.tensor_tensor(out=r1[:], in0=c[:], in1=hi[:], op=ALU.subtract), r=[c.R(), hi.R()],
             w=[r1.R()])
        mid = self.tmpA("fmid", [8, TT], BF16, nbuf=1)
        s.op("dve", lambda e: e.tensor_copy(out=mid[:], in_=r1[:]), r=[r1.R()], w=[mid.R()])
        r2 = self.tmpA("fe", [8, TT], nbuf=1)
        s.op("dve", lambda e: e.tensor_tensor(out=r2[:], in0=r1[:], in1=mid[:], op=ALU.subtract), r=[r1.R(), mid.R()],
             w=[r2.R()])
        lo = self.tmpA("flo", [8, TT], BF16, nbuf=1)
        s.op("dve", lambda e: e.tensor_copy(out=lo[:], in_=r2[:]), r=[r2.R()], w=[lo.R()])
        for j, part in enumerate([hi, mid, lo]):
            s.dma("sp", self.qa.t[:, 64 + j, t0:t0 + TT], part[:], r=[part.R()], w=[self.qa.R(("c", j, t0))])
            ng = self.tmpA("fng", [8, TT], BF16, nbuf=1)
            s.op("act", lambda e, ng=ng, part=part: e.activation(out=ng[:], in_=part[:], func=AF.Copy, scale=-1.0),
                 r=[part.R()], w=[ng.R()])
            s.dma("sp", self.ka.t[:, 67 + j, t0:t0 + TT], ng[:], r=[ng.R()], w=[self.ka.R(("c", j, t0))])

    def epi_rw(self, l, idx, ps, t0, tt, seq_start, vd):
        s = self.s
        zb = self.zraw[idx % 2]
        s.op("act", lambda e: e.activation(out=zb[:, 1:TT + 1], in_=ps[:, :], func=AF.Copy), r=[ps.R()], w=[zb.R()])
        if seq_start:
            s.op("pool", lambda e: e.memset(zb[:, 0:1], 0.0), w=[zb.R()])
        else:
            s.op("pool", lambda e: e.tensor_copy(out=zb[:, 0:1], in_=self.carry[:, idx:idx + 1]),
                 r=[self.carry.R(idx)], w=[zb.R()])
        s.op("pool", lambda e: e.tensor_copy(out=self.carry[:, idx:idx + 1], in_=zb[:, TT:TT + 1]), r=[zb.R()],
             w=[self.carry.R(idx)])
        d = self.tmpA("rwd", [128, TT])
        s.op("dve", lambda e: e.tensor_tensor(out=d[:], in0=zb[:, 0:TT], in1=zb[:, 1:TT + 1], op=ALU.subtract),
             r=[zb.R()], w=[d.R()])
        o = self.tmpA("rwo", [128, TT], nbuf=2)
        s.op("dve", lambda e: e.scalar_tensor_tensor(out=o[:], in0=d[:], scalar=self.col(l, "mu", idx), in1=zb[:, 1:TT + 1],
                                                      op0=ALU.mult, op1=ALU.add),
             r=[d.R(), zb.R(), self.colpack.R()], w=[o.R()])
        if 8 <= idx < 12:
            vi = idx - 8
            if l == 0:
                s.dma("sp", self.vf.t[vi * 128:(vi + 1) * 128, t0:t0 + TT], o[:], r=[o.R()], w=[self.vf.R((vi, t0))])
            else:
                ps2 = self.ps()
                s.op("pe", lambda e: e.matmul(ps2[:, :], lhsT=self.wvu[:, vi * 128:(vi + 1) * 128], rhs=vd[:], start=True,
                                              stop=True), r=[self.wvu.R(), vd.R()], w=[ps2.R()])
                vm = self.tmpA("vm", [128, TT], nbuf=1)
                s.op("act", lambda e: e.activation(out=vm[:], in_=ps2[:, :], func=AF.Sigmoid, bias=self.col(l, "v0", vi),
                                                   scale=1.0), r=[ps2.R(), self.colpack.R()], w=[vm.R()])
                vfl = self.tmpA("vfl", [128, TT], nbuf=1)
                s.dma("sp", vfl[:], self.vf.t[vi * 128:(vi + 1) * 128, t0:t0 + TT], r=[self.vf.R((vi, t0))], w=[vfl.R()])
                dd = self.tmpA("vdd", [128, TT], nbuf=1)
                s.op("pool", lambda e: e.tensor_tensor(out=dd[:], in0=vfl[:], in1=o[:], op=ALU.subtract),
                     r=[vfl.R(), o.R()], w=[dd.R()])
                s.op("pool", lambda e: e.tensor_tensor(out=dd[:], in0=dd[:], in1=vm[:], op=ALU.mult), r=[dd.R(), vm.R()],
                     w=[dd.R()])
                o2 = self.tmpA("rwo2", [128, TT], nbuf=1)
                s.op("dve", lambda e: e.tensor_tensor(out=o2[:], in0=o[:], in1=dd[:], op=ALU.add), r=[o.R(), dd.R()],
                     w=[o2.R()])
                o = o2
        s.dma("sp", self.zr.t[idx * 128:(idx + 1) * 128, t0:t0 + TT], o[:], r=[o.R()], w=[self.zr.R((idx, t0))])

    def epi_gate(self, l, idx, ps, t0):
        s = self.s
        o = self.tmpA("go", [128, TT], BF16, nbuf=3)
        s.op("act", lambda e: e.activation(out=o[:], in_=ps[:, :], func=AF.Sigmoid), r=[ps.R()], w=[o.R()])
        s.dma("sp", self.gt.t[idx * 128:(idx + 1) * 128, t0:t0 + TT], o[:], r=[o.R()], w=[self.gt.R((idx, t0))])


def host_inputs(inp, S, NB, depth, core):
    b0 = core * NB
    x = np.asarray(inp["x"], np.float32)[b0:b0 + NB].reshape(NB * S, D)
    p = np.asarray(inp["p"], np.float32)[:depth, b0:b0 + NB].reshape(depth, NB * S, PLE)
    m = {"xT": np.ascontiguousarray(x.T), "pT": np.ascontiguousarray(p.transpose(0, 2, 1))}
    return m


def shared_inputs(inp, depth):
    m = {}
    for name in ["w_in", "w_decay_up", "w_aaa_up", "w_gate_up", "w_o_fox", "w_o_rwkv", "w_out", "w_up", "w_down",
                 "w_ple_gate", "w_ple_up"]:
        m[name] = np.ascontiguousarray(np.asarray(inp[name], np.float32)[:depth])
    for name in ["w_vres_down", "w_vres_up"]:
        m[name] = np.ascontiguousarray(np.asarray(inp[name], np.float32)[:max(depth - 1, 1)])
    m["colpack"] = make_colpack({k: np.asarray(v, np.float32) for k, v in inp.items()}, depth)
    for k, v in make_consts().items():
        m["c_" + k] = v
    return m


_PROG_CACHE = {}


def kernel(**inp):
    S, NBT, depth = 2048, 16, 4
    ncores = 8
    NB = NBT // ncores
    key = (S, NB, depth)
    if key not in _PROG_CACHE:
        _PROG_CACHE[key] = Prog(S, NB, depth)
    prog = _PROG_CACHE[key]
    sh = shared_inputs(inp, depth)
    in_maps = []
    for c in range(ncores):
        m = dict(sh)
        m.update(host_inputs(inp, S, NB, depth, c))
        in_maps.append(m)
    res = run_bass_kernel_spmd(prog.nc, in_maps, core_ids=list(range(ncores)))
    outs = []
    for c in range(ncores):
        o = res.results[c]["outT"]
        outs.append(np.ascontiguousarray(o.T).reshape(NB, S, D))
    return np.concatenate(outs, axis=0).astype(np.float32)
```

```python
import numpy as np
import concourse.bass as bass
import concourse.mybir as mybir
from concourse.bass_utils import run_bass_kernel_spmd

F32 = mybir.dt.float32
BF16 = mybir.dt.bfloat16
AF = mybir.ActivationFunctionType
ALU = mybir.AluOpType
AX = mybir.AxisListType

D = 1024
KC = 8
FOXW = 512
RW = 512
NIN = 5384
DFF = 2816
NJ = 22
PLE = 256
RMS_EPS = 1e-6
GN_EPS = 64e-5
CH = 64
TT = 512


class Res:
    __slots__ = ("w", "rd")

    def __init__(self):
        self.w = None
        self.rd = {}


class Tl:
    def __init__(self, t):
        self.t = t
        self._r = {}

    def R(self, key=None):
        r = self._r.get(key)
        if r is None:
            r = Res()
            self._r[key] = r
        return r

    def __getitem__(self, idx):
        return self.t[idx]


class Sch:
    NDS = 6

    def __init__(self, nc):
        self.nc = nc
        self.e = dict(pe=nc.tensor, act=nc.scalar, dve=nc.vector, pool=nc.gpsimd, sp=nc.sync)
        self.sem = {k: nc.alloc_semaphore(name=f"s_{k}") for k in self.e}
        self.cnt = {k: 0 for k in self.e}
        self.seen = {k: {} for k in self.e}
        self.dsem = {q: [nc.alloc_semaphore(name=f"d_{q}{i}") for i in range(self.NDS)]
                     for q in ("sp", "pool", "act")}
        self.dcnt = {q: 0 for q in self.dsem}
        self.semh = {}
        for k in self.e:
            self.semh[k] = self.sem[k]
        for q in self.dsem:
            for i, h in enumerate(self.dsem[q]):
                self.semh[(q, i)] = h
        self.n_ins = 0

    def _wait(self, E, key, val):
        if self.seen[E].get(key, 0) >= val:
            return
        self.e[E].wait_ge(self.semh[key], val)
        self.seen[E][key] = val
        self.n_ins += 1

    def _collect(self, reads, writes):
        deps = {}

        def add(tok):
            k, v = tok
            if deps.get(k, 0) < v:
                deps[k] = v

        for r in reads:
            if r.w is not None:
                add(r.w)
        for w in writes:
            if w.w is not None:
                add(w.w)
            for k, v in w.rd.items():
                add((k, v))
        return deps

    def _commit(self, tok, reads, writes):
        k, v = tok
        for w in writes:
            w.w = tok
            w.rd = {}
        for r in reads:
            if r.rd.get(k, 0) < v:
                r.rd[k] = v

    def op(self, E, fn, r=(), w=()):
        deps = self._collect(r, w)
        for k, v in deps.items():
            if E == "pe" and k == "pe":
                continue
            self._wait(E, k, v)
        ins = fn(self.e[E])
        self.cnt[E] += 1
        ins.then_inc(self.sem[E], 1)
        self.n_ins += 1
        self._commit((E, self.cnt[E]), r, w)

    def dma(self, q, out, in_, r=(), w=(), **kw):
        deps = self._collect(r, w)
        for k, v in deps.items():
            self._wait(q, k, v)
        n = self.dcnt[q]
        i = n % self.NDS
        gen = n // self.NDS
        if gen > 0:
            self._wait(q, (q, i), 16 * gen)
        ins = self.e[q].dma_start(out=out, in_=in_, **kw)
        ins.then_inc(self.dsem[q][i], 16)
        self.dcnt[q] = n + 1
        self.n_ins += 1
        self._commit(((q, i), 16 * (gen + 1)), r, w)

    def finish(self):
        for q in self.dsem:
            n = self.dcnt[q]
            for i in range(self.NDS):
                cnt_i = (n - i + self.NDS - 1) // self.NDS
                if cnt_i > 0:
                    self._wait("sp", (q, i), 16 * cnt_i)
        for k in self.e:
            if k != "sp" and self.cnt[k] > 0:
                self._wait("sp", k, self.cnt[k])


def _cols(vec):
    v = np.asarray(vec, np.float32).reshape(-1)
    n = (v.size + 127) // 128
    buf = np.zeros(n * 128, np.float32)
    buf[: v.size] = v
    return buf.reshape(n, 128).T


def make_consts():
    c = {}
    c["ident_f"] = np.eye(128, dtype=np.float32)
    c["ones_f"] = np.ones((128, 512), np.float32)
    blk = np.zeros((128, 128), np.float32)
    blk[:64, :64] = 1.0
    blk[64:, 64:] = 1.0
    c["blk2_f"] = blk
    s = np.arange(128)[:, None]
    t = np.arange(128)[None, :]
    c["trimask_f"] = np.where(s > t, -30000.0, 0.0).astype(np.float32)
    t5 = np.arange(512)[None, :]
    c["fullmask"] = np.concatenate([np.where(t5 < r * 128 + s, -30000.0, 0.0).astype(np.float32) for r in range(4)], axis=1)
    sm = np.ones((128, 512), np.float32)
    sm[:, ::CH] = 0.0
    c["scanmask"] = sm
    s64 = np.arange(64)[:, None]
    t64 = np.arange(64)[None, :]
    strict_up = (t64 > s64).astype(np.float32)
    incl_up = (t64 >= s64).astype(np.float32)
    one = np.concatenate([strict_up, incl_up], axis=1)
    c["mask_S"] = np.tile(one, (1, 4))
    strict_lo = (t64 < s64).astype(np.float32)
    c["mask_A"] = np.tile(strict_lo, (1, 8))
    c["ident8"] = np.tile(np.eye(64, dtype=np.float32), (1, 8))
    return c


CONST_SHAPES = {"ident_f": (128, 128), "ones_f": (128, 512), "blk2_f": (128, 128), "trimask_f": (128, 128),
                "scanmask": (128, 512), "fullmask": (128, 2048), "mask_S": (64, 512), "mask_A": (64, 512), "ident8": (64, 512)}

COLS = {}
_o = 0
for _name, _n in [("g_mix", 8), ("g_ffn", 8), ("g_ple", 8), ("g_q", 1), ("g_k", 1), ("b_f", 1), ("mu", 14),
                  ("w0", 4), ("a0", 4), ("k_k", 4), ("k_a", 4), ("r_k", 4), ("gn_g", 4), ("gn_b", 4), ("v0", 4),
                  ("cw0", 44), ("cw1", 44), ("cw2", 44), ("cb", 44)]:
    COLS[_name] = (_o, _n)
    _o += _n
NCOL = _o


def make_colpack(inp, depth):
    pk = np.zeros((128, depth, NCOL), np.float32)

    def put(l, name, arr):
        o, n = COLS[name]
        a = _cols(arr)
        assert a.shape[1] == n, (name, a.shape, n)
        pk[:, l, o:o + n] = a

    for l in range(depth):
        put(l, "g_mix", inp["g_mix"][l])
        put(l, "g_ffn", inp["g_ffn"][l])
        put(l, "g_ple", inp["g_ple"][l])
        put(l, "g_q", np.tile(inp["g_qnorm"][l], 2))
        put(l, "g_k", np.tile(inp["g_knorm"][l], 2))
        put(l, "b_f", inp["b_f"][l])
        put(l, "mu", inp["mu_shift"][l])
        for nm, key in [("w0", "w0"), ("a0", "a0"), ("k_k", "k_k"), ("k_a", "k_a"), ("gn_g", "gn_g"),
                        ("gn_b", "gn_b")]:
            put(l, nm, inp[key][l])
        put(l, "r_k", inp["r_k"][l].reshape(-1))
        if l >= 1:
            put(l, "v0", inp["v0"][l - 1])
        for j in range(3):
            put(l, f"cw{j}", inp["conv_w"][l][j])
        put(l, "cb", inp["conv_b"][l])
    return pk.reshape(128, depth * NCOL)


def in_groups():
    g = []
    for i in range(4):
        g.append(("q", i * 128, 128, i))
    for i in range(4):
        g.append(("k", 512 + i * 128, 128, i))
    g.append(("f", 1536, 8, 0))
    for i in range(14):
        g.append(("rw", 1544 + i * 128, 128, i))
    for i in range(16):
        g.append(("gate", 3336 + i * 128, 128, i))
    return g


class Prog:
    def __init__(self, S, NB, depth, debug=False, stages="ABCDE"):
        self.S, self.NB, self.depth, self.debug, self.stages = S, NB, depth, debug, stages
        self.T = S * NB
        self.NT = self.T // TT
        self.TPS = S // TT
        nc = bass.Bass("TRN2", target_bir_lowering=False)
        self.nc = nc
        self.s = Sch(nc)
        self._ps_i = 0
        self.build()

    def psb_(self, name, shape, dt=F32):
        return Tl(self.nc.alloc_sbuf_tensor(name, list(shape), dt))

    def sb(self, name, shape, dt=F32):
        esz = 2 if dt == BF16 else 4
        nbytes = int(np.prod(shape[1:])) * esz
        nbytes = (nbytes + 63) // 64 * 64
        off = self.arena_off
        assert off + nbytes <= self.arena_end, (name, off, nbytes, self.arena_end)
        self.arena_off = off + nbytes
        self._uid += 1
        return Tl(self.nc.alloc_sbuf_tensor_at(f"{name}_{self._uid}", list(shape), dt, offset=off))

    def stage_begin(self):
        s = self.s
        for E in s.e:
            for k in s.e:
                if s.cnt[k] > 0:
                    s._wait(E, k, s.cnt[k])
            for q in s.dsem:
                n = s.dcnt[q]
                for i in range(s.NDS):
                    cnt_i = (n - i + s.NDS - 1) // s.NDS
                    if cnt_i > 0:
                        s._wait(E, (q, i), 16 * cnt_i)
        self.arena_off = self.arena_start
        self.ep = {}

    def dram(self, name, shape, dt, kind="Internal"):
        if kind == "Internal" and self.debug:
            kind = "ExternalOutput"
        return Tl(self.nc.dram_tensor(name, list(shape), dt, kind=kind))

    def ps(self):
        p = self.psb[self._ps_i % 8]
        self._ps_i += 1
        return p

    def build(self):
        nc, s = self.nc, self.s
        T, L = self.T, self.depth
        self.xT = self.dram("xT", [D, T], F32, kind="ExternalInput")
        self.pT = self.dram("pT", [L, PLE, T], F32, kind="ExternalInput")
        self.out = self.dram("outT", [D, T], F32, kind="ExternalOutput")
        W = {}
        for name, shp in [("w_in", (L, D, NIN)), ("w_decay_up", (L, 64, RW)), ("w_aaa_up", (L, 64, RW)),
                          ("w_gate_up", (L, 128, RW)), ("w_vres_down", (max(L - 1, 1), D, 32)),
                          ("w_vres_up", (max(L - 1, 1), 32, RW)), ("w_o_fox", (L, FOXW, D)),
                          ("w_o_rwkv", (L, RW, D)), ("w_out", (L, D, D)), ("w_up", (L, D, 2 * DFF)),
                          ("w_down", (L, DFF, D)), ("w_ple_gate", (L, D, D)), ("w_ple_up", (L, PLE, D))]:
            W[name] = self.dram(name, shp, F32, kind="ExternalInput")
        self.W = W
        self.colpack_d = self.dram("colpack", [128, L * NCOL], F32, kind="ExternalInput")
        self.const_d = {k: self.dram("c_" + k, list(v), F32, kind="ExternalInput") for k, v in CONST_SHAPES.items()}
        self.xs = self.dram("xs", [D, T], F32)
        self.qa = self.dram("qa", [8, 70, T], BF16)
        self.ka = self.dram("ka", [8, 70, T], BF16)
        self.zr = self.dram("zr", [1792, T], F32)
        self.vf = self.dram("vf", [RW, T], F32)
        self.gt = self.dram("gt", [2 * D, T], BF16)
        self.yf = self.dram("yf", [FOXW, T], BF16)
        self.yr = self.dram("yr", [RW, T], BF16)
        self.vt = self.dram("vt", [T, FOXW], BF16)
        self.actT = self.dram("actT", [DFF, T], BF16)
        self.psb = [Tl(nc.alloc_psum_tensor(f"ps{i}", [128, 512], F32)) for i in range(8)]
        self.colpack = self.psb_("colpack_s", [128, L * NCOL])
        s.dma("sp", self.colpack[:], self.colpack_d[:, :], r=[self.colpack_d.R()], w=[self.colpack.R()])
        self.c = {}
        for k, shp in CONST_SHAPES.items():
            if k in ("fullmask", "trimask_f"):
                continue
            self.c[k] = self.psb_("cs_" + k, shp)
            s.dma("sp", self.c[k][:], self.const_d[k][:, :], r=[self.const_d[k].R()], w=[self.c[k].R()])
        for k in ["ident_f", "blk2_f", "fullmask"]:
            self.c[k + "_b"] = self.psb_("cb_" + k, CONST_SHAPES[k], BF16)
            s.dma("pool", self.c[k + "_b"][:], self.const_d[k][:, :], r=[self.const_d[k].R()],
                  w=[self.c[k + "_b"].R()])
        self.c["ones_b"] = self.psb_("cb_ones", [128, 512], BF16)
        s.dma("pool", self.c["ones_b"][:], self.const_d["ones_f"][:, :], r=[self.const_d["ones_f"].R()],
              w=[self.c["ones_b"].R()])
        self.carry = self.psb_("carry", [128, 16])
        self.ccar = self.psb_("ccar", [8, 2])
        self.ucarry = self.psb_("ucarry", [128, 2 * NJ, 2])
        self.negb = self.psb_("negb", [8, 4])
        for ll in range(L):
            s.op("dve", lambda e, ll=ll: e.tensor_scalar(out=self.negb[:, ll:ll + 1], in0=self.col(ll, "b_f", 0, 0, 8),
                                                          scalar1=-1.0, scalar2=None, op0=ALU.mult),
                 r=[self.colpack.R()], w=[self.negb.R()])
        self._uid = 0
        base0 = int(nc.sbuf_base)
        self.arena_start = (base0 + 63) // 64 * 64
        left = (int(nc.sbuf_bytes_remaining) - 256 - (self.arena_start - base0)) // 64 * 64
        slab = nc.alloc_sbuf_tensor("arena", [128, (left + self.arena_start - base0) // 4], F32)
        self.arena_end = self.arena_start + left
        assert int(nc.sbuf_base) >= self.arena_end, (nc.sbuf_base, self.arena_end)
        self.stage_begin()
        ow = self.sb("ones_wide", [3, T], BF16)
        s.op("dve", lambda e: e.memset(ow[:], 1.0), w=[ow.R()])
        for h in range(8):
            s.dma("sp", self.qa[h, 67:70, :], ow[:], r=[ow.R()], w=[self.qa.R(("ones", h))])
            s.dma("sp", self.ka[h, 64:67, :], ow[:], r=[ow.R()], w=[self.ka.R(("ones", h))])
        for l in range(L):
            src = self.xT if l == 0 else self.xs
            if "A" in self.stages:
                self.stage_A(l, src)
            if "B" in self.stages:
                self.stage_B(l)
            if "C" in self.stages:
                self.stage_C(l)
            if "D" in self.stages:
                self.stage_D(l, src)
            if "E" in self.stages or "1" in self.stages:
                self.stage_E1(l)
            if "E" in self.stages or "2" in self.stages:
                self.stage_E2(l, last=(l == L - 1))
        s.finish()

    def col(self, l, name, j=0, p0=0, p1=128):
        o, n = COLS[name]
        assert j < n
        c0 = l * NCOL + o + j
        return self.colpack[p0:p1, c0:c0 + 1]

    def rmsnorm_tile(self, l, gname, xt, ht, tag):
        for _ in self.rn_gen(l, gname, xt, ht):
            pass

    def rn_gen(self, l, gname, xt, ht):
        s = self.s
        sq = ht
        s.op("act", lambda e: e.activation(out=sq[:], in_=xt[:], func=AF.Square), r=[xt.R()],
             w=[ht.R(kc) for kc in range(KC)])
        yield
        ps = self.ps()
        for kc in range(KC):
            s.op("pe", lambda e, kc=kc: e.matmul(ps[:, :], lhsT=self.c["ones_b"][:, 0:128], rhs=sq[:, kc, :],
                                                 start=(kc == 0), stop=(kc == KC - 1)),
                 r=[self.c["ones_b"].R(), ht.R(kc)], w=[ps.R()])
        rs = self.tmpA("rn_rs", [128, TT], nbuf=1)
        self.rsqrt_ps(rs, ps, 1.0 / D, RMS_EPS, 1.0)
        yield
        for kc in range(KC):
            eng = "dve" if kc % 2 == 0 else "pool"
            if eng == "dve":
                s.op("dve", lambda e, kc=kc: e.scalar_tensor_tensor(out=ht[:, kc, :], in0=xt[:, kc, :],
                                                                     scalar=self.col(l, gname, kc), in1=rs[:],
                                                                     op0=ALU.mult, op1=ALU.mult),
                     r=[xt.R(), rs.R(), self.colpack.R()], w=[ht.R(kc)])
            else:
                tmp = self.tmpA("rn_nt", [128, TT], nbuf=2)
                s.op("pool", lambda e, kc=kc, tmp=tmp: e.tensor_tensor(out=tmp[:], in0=xt[:, kc, :], in1=rs[:], op=ALU.mult),
                     r=[xt.R(), rs.R()], w=[tmp.R()])
                s.op("act", lambda e, kc=kc, tmp=tmp: e.activation(out=ht[:, kc, :], in_=tmp[:], func=AF.Copy,
                                                                   scale=self.col(l, gname, kc)),
                     r=[tmp.R(), self.colpack.R()], w=[ht.R(kc)])
        yield

    def stage_A(self, l, src):
        nc, s = self.nc, self.s
        T = self.T
        self.stage_begin()
        self.wbig = self.sb("wbig", [128, KC * NIN], BF16)
        self.wbig_R = self.wbig.R()
        self.xt_t = [self.sb("xt0", [128, KC, TT])] * 2
        self.ht_t = [self.sb(f"ht{i}", [128, KC, TT], BF16) for i in range(2)]
        self.zraw = [self.sb(f"zraw{i}", [128, TT + 1]) for i in range(2)]
        W = self.W
        win = self.wbig
        winv = self.wbig.t[:, 0:KC * NIN].rearrange("p (k n) -> p k n", k=KC)
        self.load_wb(self.wbig, winv, "w_in", l, [0, 1024, 1544, 2568, 3336, 4360, NIN], order=[1, 0, 2, 3, 4, 5])
        if l >= 1:
            self.wvd = self.sb("wvd", [128, KC, 32], BF16)
            self.wvu = self.sb("wvu", [32, RW], BF16)
            s.dma("pool", self.wvd[:], W["w_vres_down"].t[l - 1].rearrange("(k p) n -> p k n", p=128),
                  r=[W["w_vres_down"].R()], w=[self.wvd.R()])
            s.dma("pool", self.wvu[:], W["w_vres_up"].t[l - 1], r=[W["w_vres_up"].R()], w=[self.wvu.R()])
        groups = in_groups()
        srcv = src.t.rearrange("(k p) t -> p k t", p=128)
        self.deferred = []

        def run_deferred():
            run, self.deferred = self.deferred, []
            for f in run:
                f()

        def load_tile(tt):
            xt = self.xt_t[tt % 2]
            for kc in range(KC):
                s.dma("sp", xt[:, kc, :], srcv[:, kc, tt * TT:(tt + 1) * TT], r=[src.R(("tile", tt))], w=[xt.R()])

        def prep_tile(tt):
            load_tile(tt)
            self.rmsnorm_tile(l, "g_mix", self.xt_t[tt % 2], self.ht_t[tt % 2], "A")

        prep_tile(0)
        rng_ = [None]
        for tt in range(self.NT):
            t0 = tt * TT
            seq_start = (tt % self.TPS == 0)
            xt = self.xt_t[tt % 2]
            ht = self.ht_t[tt % 2]
            hR = [ht.R(kc) for kc in range(KC)]
            for sub in range(4):
                ps = self.ps()
                for kc in range(KC):
                    s.op("pe", lambda e, kc=kc, sub=sub: e.matmul(ps[:, :], lhsT=ht[:, kc, sub * 128:(sub + 1) * 128],
                                                                   rhs=winv[:, kc, 1024:1536], start=(kc == 0),
                                                                   stop=(kc == KC - 1)),
                         r=[hR[kc], self.wres(self.wbig, 1024)], w=[ps.R()])
                blk = tt * 4 + sub
                vo = self.tmpA("vo", [128, FOXW], BF16)
                s.op("act", lambda e, vo=vo, ps=ps: e.activation(out=vo[:], in_=ps[:, :], func=AF.Copy),
                     r=[ps.R()], w=[vo.R()])
                s.dma("sp", self.vt.t[blk * 128:(blk + 1) * 128, :], vo[:], r=[vo.R()], w=[self.vt.R(blk)])
            if l >= 1:
                ps = self.ps()
                for kc in range(KC):
                    s.op("pe", lambda e, kc=kc, ps=ps: e.matmul(ps[0:32, :], lhsT=self.wvd[:, kc, :], rhs=ht[:, kc, :],
                                                                 start=(kc == 0), stop=(kc == KC - 1)),
                         r=[hR[kc], self.wvd.R()], w=[ps.R()])
                vd = self.tmpA("vd", [32, TT], BF16)
                s.op("act", lambda e, ps=ps: e.activation(out=vd[:], in_=ps[0:32, :], func=AF.Copy), r=[ps.R()],
                     w=[vd.R()])
            for gi, (kind, c0, wd, idx) in enumerate(groups):
                ps = self.ps()
                for kc in range(KC):
                    s.op("pe", lambda e, kc=kc, ps=ps, c0=c0, wd=wd: e.matmul(ps[0:wd, :], lhsT=winv[:, kc, c0:c0 + wd],
                                                                               rhs=ht[:, kc, :], start=(kc == 0),
                                                                               stop=(kc == KC - 1)),
                         r=[hR[kc], self.wres(self.wbig, c0)], w=[ps.R()])
                run_deferred()
                if tt + 1 < self.NT:
                    if gi == 0:
                        load_tile(tt + 1)
                        rng_[0] = self.rn_gen(l, "g_mix", self.xt_t[(tt + 1) % 2], self.ht_t[(tt + 1) % 2])
                    elif gi in (5, 8, 9):
                        next(rng_[0])
                if kind in ("q", "k"):
                    self.epi_qk(l, kind, idx, ps, t0)
                elif kind == "f":
                    self.epi_f(l, ps, t0, seq_start)
                elif kind == "rw":
                    self.epi_rw(l, idx, ps, t0, tt, seq_start, vd if l >= 1 else None)
                else:
                    self.epi_gate(l, idx, ps, t0)
        run_deferred()

    def rsqrt_ps(self, out, ps, scale, eps, mult, np_=128):
        s = self.s
        if not hasattr(self, "_fconst"):
            self._fconst = {}
        def fc(v):
            if v not in self._fconst:
                t = self.psb_(f"fc{len(self._fconst)}", [128, 1])
                s.op("pool", lambda e: e.memset(t[:], float(v)), w=[t.R()])
                self._fconst[v] = t
            return self._fconst[v]
        be = fc(eps)
        bm = fc(float(np.log(mult)))
        s.op("act", lambda e: e.activation(out=out[0:np_, :], in_=ps[0:np_, :], func=AF.Ln, bias=be[0:np_, :], scale=float(scale)),
             r=[ps.R(), be.R()], w=[out.R()])
        s.op("act", lambda e: e.activation(out=out[0:np_, :], in_=out[0:np_, :], func=AF.Exp, bias=bm[0:np_, :], scale=-0.5),
             r=[out.R(), bm.R()], w=[out.R()])

    def ps_rot(self, lo, hi):
        key = (lo, hi)
        if not hasattr(self, "_psr"):
            self._psr = {}
        i = self._psr.get(key, 0)
        self._psr[key] = i + 1
        return self.psb[lo + i % (hi - lo)]

    def stage_B(self, l):
        s = self.s
        S, NB = self.S, self.NB
        self.stage_begin()
        QA = [self.sb(f"QA{i}", [70, S], BF16) for i in range(2)]
        KA = [self.sb(f"KA{i}", [70, S], BF16) for i in range(2)]
        Vb = [self.sb(f"Vb{i}", [128, S // 128, FOXW], BF16) for i in range(2)]
        PT = [self.sb(f"PT{i}", [128, TT], BF16) for i in range(4)]
        rden = [self.sb(f"rden{i}", [64, TT]) for i in range(2)]
        yt = [self.sb(f"yt{i}", [64, TT], BF16) for i in range(2)]
        fm = self.c["fullmask_b"]
        idb = self.c["ident_f_b"]
        onb = self.c["ones_b"]
        NQ = S // TT
        LOOK = 3
        groups = [(b, h) for b in range(NB) for h in range(8)]
        bufs = {}

        def load_group(gi):
            b, h = groups[gi]
            qa, ka = QA[gi % 2], KA[gi % 2]
            if h == 0:
                s.dma("sp", Vb[b % 2][:], self.vt.t[b * S:(b + 1) * S, :].rearrange("(c p) n -> p c n", p=128),
                      r=[self.vt.R(blk) for blk in range(b * S // 128, (b + 1) * S // 128)], w=[Vb[b % 2].R()])
            s.dma("sp", qa[:], self.qa.t[h, :, b * S:(b + 1) * S],
                  r=[self.qa.R(("ones", h))] + [self.qa.R(("qk", h, b * S + j * TT)) for j in range(NQ)] +
                    [self.qa.R(("c", jj, b * S + j * TT)) for j in range(NQ) for jj in range(3)], w=[qa.R()])
            s.dma("sp", ka[:], self.ka.t[h, :, b * S:(b + 1) * S],
                  r=[self.ka.R(("ones", h))] + [self.ka.R(("qk", h, b * S + j * TT)) for j in range(NQ)] +
                    [self.ka.R(("c", jj, b * S + j * TT)) for j in range(NQ) for jj in range(3)], w=[ka.R()])

        work = []
        for gi, (b, h) in enumerate(groups):
            for j in range(NQ):
                nch = 4 * (j + 1)
                for i in range(nch):
                    work.append((gi, b, h, j, i, nch))
        state = {}

        def emit_qk(w):
            gi, b, h, j, i, nch = w
            if j == 0 and i == 0:
                if gi == 0:
                    load_group(0)
                if gi + 1 < len(groups):
                    load_group(gi + 1)
            qa, ka = QA[gi % 2], KA[gi % 2]
            sc = self.ps_rot(4, 8)
            state[w] = sc
            r_ = i - 4 * j
            diag = r_ >= 0
            s.op("pe", lambda e: e.matmul(sc[:, :], lhsT=ka[:, i * 128:(i + 1) * 128], rhs=qa[:, j * TT:(j + 1) * TT], start=True,
                                          stop=not diag), r=[ka.R(), qa.R()], w=[sc.R()])
            if diag:
                s.op("pe", lambda e: e.matmul(sc[:, :], lhsT=idb[:], rhs=fm[:, r_ * 512:(r_ + 1) * 512], start=False, stop=True),
                     r=[idb.R(), fm.R()], w=[sc.R()])

        cnt = {"pt": 0, "acc": None, "den": None, "ep": 0}

        def emit_rest(w):
            gi, b, h, j, i, nch = w
            sc = state.pop(w)
            if i == 0:
                cnt["acc"] = self.ps_rot(0, 2)
                cnt["den"] = self.ps_rot(2, 4)
            acc, den = cnt["acc"], cnt["den"]
            pt = PT[cnt["pt"] % 4]
            cnt["pt"] += 1
            vb = Vb[b % 2]
            s.op("act", lambda e: e.activation(out=pt[:], in_=sc[:, :], func=AF.Exp), r=[sc.R()], w=[pt.R()])
            s.op("pe", lambda e: e.matmul(acc[0:64, :], lhsT=vb[:, i, h * 64:(h + 1) * 64], rhs=pt[:], start=(i == 0),
                                          stop=(i == nch - 1)), r=[vb.R(), pt.R()], w=[acc.R()])
            s.op("pe", lambda e: e.matmul(den[0:64, :], lhsT=onb[:, 0:64], rhs=pt[:], start=(i == 0), stop=(i == nch - 1)),
                 r=[onb.R(), pt.R()], w=[den.R()])
            if i == nch - 1:
                rd = rden[cnt["ep"] % 2]
                y = yt[cnt["ep"] % 2]
                cnt["ep"] += 1
                s.op("dve", lambda e: e.reciprocal(out=rd[:], in_=den[0:64, :]), r=[den.R()], w=[rd.R()])
                s.op("dve", lambda e: e.tensor_tensor(out=y[:], in0=acc[0:64, :], in1=rd[:], op=ALU.mult), r=[acc.R(), rd.R()],
                     w=[y.R()])
                t0 = b * S + j * TT
                s.dma("sp", self.yf.t[h * 64:(h + 1) * 64, t0:t0 + TT], y[:], r=[y.R()], w=[self.yf.R((h, t0))])

        for k in range(min(LOOK, len(work))):
            emit_qk(work[k])
        for k, w in enumerate(work):
            if k + LOOK < len(work):
                emit_qk(work[k + LOOK])
            emit_rest(w)

    def stage_C(self, l):
        import os
        cut = int(os.environ.get("CCUT", "9"))
        use_b = os.environ.get("RWDT", "bf16") == "bf16"
        RD = BF16 if use_b else mybir.dt.float32r
        s = self.s
        S, NB, T = self.S, self.NB, self.T
        W = self.W
        self.stage_begin()
        c = self.c
        idf, blkf, scanm, mS, mA, id8 = c["ident_f"], c["blk2_f"], c["scanmask"], c["mask_S"], c["mask_A"], c["ident8"]

        def V(E, fn, r, w):
            s.op(E, fn, r=[x.R() for x in r], w=[x.R() for x in w])

        def cp(E, out_ap, in_ap, r, w):
            if E == "act":
                V("act", lambda e: e.activation(out=out_ap, in_=in_ap, func=AF.Copy), r, w)
            else:
                V(E, lambda e: e.tensor_copy(out=out_ap, in_=in_ap), r, w)

        Wd = self.sb("Wd", [64, RW], BF16)
        Wa = self.sb("Wa", [64, RW], BF16)
        Wg = self.sb("Wg", [128, RW], BF16)
        s.dma("pool", Wd[:], W["w_decay_up"].t[l], r=[W["w_decay_up"].R()], w=[Wd.R()])
        s.dma("pool", Wa[:], W["w_aaa_up"].t[l], r=[W["w_aaa_up"].R()], w=[Wa.R()])
        s.dma("pool", Wg[:], W["w_gate_up"].t[l], r=[W["w_gate_up"].R()], w=[Wg.R()])
        omka = self.sb("omka", [128, 4])
        o_ka = l * NCOL + COLS["k_a"][0]
        V("dve", lambda e: e.tensor_scalar(out=omka[:], in0=self.colpack[:, o_ka:o_ka + 4], scalar1=-1.0, scalar2=1.0,
                                            op0=ALU.mult, op1=ALU.add), [self.colpack], [omka])
        epsg = self.sb("epsg", [64, 1])
        V("pool", lambda e: e.memset(epsg[:], GN_EPS), [], [epsg])
        AR = [self.sb(f"AR{i}", [128, 8, 2, CH], RD) for i in range(4)]
        BT = [self.sb(f"BT{i}", [128, TT], RD) for i in range(4)]
        KT = [self.sb(f"KT{i}", [128, TT], RD) for i in range(4)]
        ARo = [self.sb(f"ARo{i}", [64, 8, 2, CH], RD) for i in range(4)]
        BTo = [self.sb(f"BTo{i}", [64, TT], RD) for i in range(4)]
        KTo = [self.sb(f"KTo{i}", [64, TT], RD) for i in range(4)]
        VR = [self.sb(f"VR{i}", [128, TT]) for i in range(4)]
        G = [[self.sb(f"G{i}{b}", [128, TT], BF16) for i in range(4)] for b in range(2)]
        BG = [[self.sb(f"BG{i}{b}", [128, TT], BF16) for i in range(4)] for b in range(2)]
        PCp = self.sb("PCp", [128, 8]); PCo = self.sb("PCo", [64, 8])
        PCall = self.sb("PCall", [64, 8, 8])
        H = self.sb("H", [64, 512], RD)
        Hf = self.sb("Hf", [64, 512])
        Ht = self.sb("Ht", [64, 512])
        YN = self.sb("YN", [64, 8, 512])
        dwt = self.sb("dwt", [64, TT]); dat = self.sb("dat", [64, TT]); dgt = self.sb("dgt", [128, TT])
        tdw = self.sb("tdw", [64, TT], BF16); dab = self.sb("dab", [64, TT], BF16); sdg = self.sb("sdg", [128, TT], BF16)
        rT = self.sb("rT", [128, TT]); krT = self.sb("krT", [128, TT])
        sig = self.sb("sig", [128, TT]); aa = self.sb("aa", [128, TT]); kk = self.sb("kk", [128, TT])
        prod = self.sb("prod", [128, TT]); rn = self.sb("rn", [128, TT]); gf = self.sb("gf", [128, TT])
        Lc = self.sb("Lc", [128, TT])
        eL = self.sb("eL", [128, TT]); eLm = self.sb("eLm", [128, TT]); enL = self.sb("enL", [128, TT])
        TOK = [[self.sb(f"tok{i}{b}", [64, 512], RD) for i in range(3)] for b in range(2)]
        SMb = [[self.sb(f"SM{i}{b}", [64, 512], RD) for i in range(4)] for b in range(2)]
        Tfin = [self.sb(f"Tfin{b}", [64, 512], RD) for b in range(2)]
        Xa = [self.sb(f"Xa{i}", [64, 512], RD) for i in range(2)]
        XTa = [self.sb(f"XTa{i}", [64, 512], RD) for i in range(2)]
        TTa = [self.sb(f"TTa{i}", [64, 512], RD) for i in range(2)]
        W0s = self.sb("W0s", [64, 512], RD); Us = self.sb("Us", [64, 512], RD)
        YQ = self.sb("YQ", [64, 8, 512])
        st = {k: self.sb("st_" + k, [64, 64]) for k in ["sum", "sq", "m", "m2", "var", "rstd"]}
        po1 = self.sb("po1", [128, TT]); pob = self.sb("pob", [128, TT], BF16)

        def rr(ap):
            return ap

        def MM(e, out, lhsT, rhs, start, stop):
            return e.matmul(out, lhsT=rr(lhsT), rhs=rr(rhs), start=start, stop=stop)

        def colv(name, hp):
            return self.col(l, name, hp)

        def ar(h):
            return AR[h // 2] if h % 2 == 0 else ARo[h // 2]

        def bt(h):
            return BT[h // 2] if h % 2 == 0 else BTo[h // 2]

        def kt(h):
            return KT[h // 2] if h % 2 == 0 else KTo[h // 2]

        def prep(tt):
            t0 = tt * TT
            zr = self.zr
            s.dma("sp", dwt[:], zr.t[1536:1600, t0:t0 + TT], r=[zr.R((12, t0))], w=[dwt.R()])
            s.dma("sp", dat[:], zr.t[1600:1664, t0:t0 + TT], r=[zr.R((12, t0))], w=[dat.R()])
            s.dma("sp", dgt[:], zr.t[1664:1792, t0:t0 + TT], r=[zr.R((13, t0))], w=[dgt.R()])
            V("act", lambda e: e.activation(out=tdw[:], in_=dwt[:], func=AF.Tanh), [dwt], [tdw])
            V("pool", lambda e: e.tensor_copy(out=dab[:], in_=dat[:]), [dat], [dab])
            V("act", lambda e: e.activation(out=sdg[:], in_=dgt[:], func=AF.Sigmoid), [dgt], [sdg])
            for hp in range(4):
                hs = slice(hp * 128, (hp + 1) * 128)
                s.dma("sp", rT[:], zr.t[hp * 128:(hp + 1) * 128, t0:t0 + TT], r=[zr.R((hp, t0))], w=[rT.R()])
                s.dma("sp", krT[:], zr.t[512 + hp * 128:512 + (hp + 1) * 128, t0:t0 + TT], r=[zr.R((4 + hp, t0))], w=[krT.R()])
                s.dma("sp", VR[hp][:], zr.t[1024 + hp * 128:1024 + (hp + 1) * 128, t0:t0 + TT], r=[zr.R((8 + hp, t0))],
                      w=[VR[hp].R()])
                p1 = self.ps()
                V("pe", lambda e: e.matmul(p1[:, :], lhsT=Wd[:, hs], rhs=tdw[:], start=True, stop=True), [Wd, tdw], [p1])
                V("act", lambda e: e.activation(out=sig[:], in_=p1[:, :], func=AF.Sigmoid, bias=colv("w0", hp), scale=1.0),
                  [p1, self.colpack], [sig])
                p2 = self.ps()
                V("pe", lambda e: e.matmul(p2[:, :], lhsT=Wa[:, hs], rhs=dab[:], start=True, stop=True), [Wa, dab], [p2])
                V("act", lambda e: e.activation(out=aa[:], in_=p2[:, :], func=AF.Sigmoid, bias=colv("a0", hp), scale=1.0),
                  [p2, self.colpack], [aa])
                p3 = self.ps()
                V("pe", lambda e: e.matmul(p3[:, :], lhsT=Wg[:, hs], rhs=sdg[:], start=True, stop=True), [Wg, sdg], [p3])
                cp("act", gf[:], p3[:, :], [p3], [gf])
                cp("pool", G[tt % 2][hp][:], gf[:], [gf], [G[tt % 2][hp]])
                V("act", lambda e: e.activation(out=kk[:], in_=krT[:], func=AF.Copy, scale=colv("k_k", hp)),
                  [krT, self.colpack], [kk])
                V("pool", lambda e: e.tensor_tensor(out=prod[:], in0=kk[:], in1=kk[:], op=ALU.mult), [kk], [prod])
                p4 = self.ps()
                V("pe", lambda e: e.matmul(p4[:, :], lhsT=blkf[:], rhs=prod[:], start=True, stop=True), [blkf, prod], [p4])
                self.rsqrt_ps(rn, p4, 1.0, 1e-24, 1.0)
                V("dve", lambda e: e.tensor_tensor(out=kk[:], in0=kk[:], in1=rn[:], op=ALU.mult), [kk, rn], [kk])
                V("dve", lambda e: e.tensor_scalar(out=rn[:], in0=aa[:], scalar1=colv("k_a", hp), scalar2=omka[:, hp:hp + 1],
                                                    op0=ALU.mult, op1=ALU.add), [aa, self.colpack, omka], [rn])
                V("dve", lambda e: e.tensor_tensor(out=krT[:], in0=krT[:], in1=rn[:], op=ALU.mult), [krT, rn], [krT])
                V("pool", lambda e: e.tensor_tensor(out=aa[:], in0=kk[:], in1=aa[:], op=ALU.mult), [kk, aa], [aa])
                V("act", lambda e: e.activation(out=sig[:], in_=sig[:], func=AF.Copy, scale=-float(np.exp(-0.5))),
                  [sig], [sig])
                V("dve", lambda e: e.tensor_tensor_scan(out=Lc[:], data0=scanm[:], data1=sig[:], initial=0.0, op0=ALU.mult,
                                                         op1=ALU.add), [scanm, sig], [Lc])
                V("pool", lambda e: e.tensor_tensor(out=sig[:], in0=Lc[:], in1=sig[:], op=ALU.subtract), [Lc, sig], [sig])
                V("act", lambda e: e.activation(out=eL[:], in_=Lc[:], func=AF.Exp), [Lc], [eL])
                V("act", lambda e: e.activation(out=eLm[:], in_=sig[:], func=AF.Exp), [sig], [eLm])
                V("act", lambda e: e.activation(out=enL[:], in_=Lc[:], func=AF.Exp, scale=-1.0), [Lc], [enL])
                arv = AR[hp]
                V("dve", lambda e: e.scalar_tensor_tensor(out=arv[:, :, 0, :], in0=kk[:].rearrange("p (c t) -> p c t", t=CH),
                                                           scalar=-1.0, in1=eLm[:].rearrange("p (c t) -> p c t", t=CH),
                                                           op0=ALU.mult, op1=ALU.mult), [kk, eLm], [arv])
                V("dve", lambda e: e.tensor_tensor(out=arv[:, :, 1, :], in0=rT[:].rearrange("p (c t) -> p c t", t=CH),
                                                    in1=eL[:].rearrange("p (c t) -> p c t", t=CH), op=ALU.mult), [rT, eL], [arv])
                V("pool", lambda e: e.tensor_tensor(out=BT[hp][:], in0=aa[:], in1=enL[:], op=ALU.mult), [aa, enL], [BT[hp]])
                V("dve", lambda e: e.tensor_tensor(out=KT[hp][:], in0=krT[:], in1=enL[:], op=ALU.mult), [krT, enL], [KT[hp]])
                V("pool", lambda e: e.tensor_copy(out=PCp[:], in_=eL[:, CH - 1::CH]), [eL], [PCp])
                s.dma("sp", ARo[hp][:], AR[hp][64:128, :, :, :], r=[AR[hp].R()], w=[ARo[hp].R()])
                s.dma("sp", BTo[hp][:], BT[hp][64:128, :], r=[BT[hp].R()], w=[BTo[hp].R()])
                s.dma("sp", KTo[hp][:], KT[hp][64:128, :], r=[KT[hp].R()], w=[KTo[hp].R()])
                s.dma("sp", PCo[:], PCp[64:128, :], r=[PCp.R()], w=[PCo.R()])
                V("pool", lambda e: e.tensor_copy(out=PCall[:, :, 2 * hp], in_=PCp[0:64, :]), [PCp], [PCall])
                V("pool", lambda e: e.tensor_copy(out=PCall[:, :, 2 * hp + 1], in_=PCo[:]), [PCo], [PCall])
                V("dve", lambda e: e.scalar_tensor_tensor(out=prod[:], in0=rT[:], scalar=colv("r_k", hp), in1=krT[:],
                                                           op0=ALU.mult, op1=ALU.mult), [rT, krT, self.colpack], [prod])
                p5 = self.ps()
                V("pe", lambda e: e.matmul(p5[:, :], lhsT=blkf[:], rhs=prod[:], start=True, stop=True), [blkf, prod], [p5])
                V("dve", lambda e: e.tensor_tensor(out=rn[:], in0=p5[:, :], in1=VR[hp][:], op=ALU.mult), [p5, VR[hp]], [rn])
                V("pool", lambda e: e.tensor_tensor(out=BG[tt % 2][hp][:], in0=rn[:], in1=gf[:], op=ALU.mult), [rn, gf], [BG[tt % 2][hp]])
                yield

        def gnpost(tt):
            t0 = tt * TT
            if cut >= 5:
                yr3 = YN[:].rearrange("p c (h v) -> p (c h) v", h=8)
                yq3 = YQ[:].rearrange("p c (h v) -> p (c h) v", h=8)
                V("act", lambda e: e.activation(out=YQ[:], in_=YN[:], func=AF.Square), [YN], [YQ])
                V("dve", lambda e: e.tensor_reduce(out=st["sum"][:], in_=yr3, axis=AX.X, op=ALU.add), [YN], [st["sum"]])
                V("dve", lambda e: e.tensor_reduce(out=st["sq"][:], in_=yq3, axis=AX.X, op=ALU.add), [YQ], [st["sq"]])
                V("act", lambda e: e.activation(out=st["m"][:], in_=st["sum"][:], func=AF.Copy, scale=1.0 / 64),
                  [st["sum"]], [st["m"]])
                V("pool", lambda e: e.tensor_tensor(out=st["m2"][:], in0=st["m"][:], in1=st["m"][:], op=ALU.mult), [st["m"]],
                  [st["m2"]])
                V("dve", lambda e: e.scalar_tensor_tensor(out=st["var"][:], in0=st["sq"][:], scalar=1.0 / 64, in1=st["m2"][:],
                                                           op0=ALU.mult, op1=ALU.subtract), [st["sq"], st["m2"]], [st["var"]])
                V("act", lambda e: e.activation(out=st["rstd"][:], in_=st["var"][:], func=AF.Ln, bias=epsg[:], scale=1.0),
                  [st["var"], epsg], [st["rstd"]])
                V("act", lambda e: e.activation(out=st["rstd"][:], in_=st["rstd"][:], func=AF.Exp, scale=-0.5), [st["rstd"]],
                  [st["rstd"]])
                yield
                V("dve", lambda e: e.tensor_tensor(out=yq3, in0=yr3, in1=st["m"][:].unsqueeze(2).broadcast_to([64, 64, 64]),
                                                    op=ALU.subtract), [YN, st["m"]], [YQ])
                V("dve", lambda e: e.tensor_tensor(out=yr3, in0=yq3, in1=st["rstd"][:].unsqueeze(2).broadcast_to([64, 64, 64]),
                                                    op=ALU.mult), [YQ, st["rstd"]], [YN])
            yield
            for hp in range(4 if cut >= 6 else 0):
                pO = self.ps()
                for cc in range(8):
                    V("pe", lambda e, cc=cc: e.transpose(out=pO[:, cc * CH:(cc + 1) * CH], in_=YN[:, cc, hp * 128:(hp + 1) * 128],
                                                         identity=idf[0:64, 0:64]), [YN, idf], [pO])
                V("dve", lambda e: e.tensor_scalar(out=po1[:], in0=pO[:, :], scalar1=colv("gn_g", hp), scalar2=colv("gn_b", hp),
                                                    op0=ALU.mult, op1=ALU.add), [pO, self.colpack], [po1])
                V("pool", lambda e: e.tensor_tensor(out=po1[:], in0=po1[:], in1=G[tt % 2][hp][:], op=ALU.mult), [po1, G[tt % 2][hp]], [po1])
                V("dve", lambda e: e.tensor_tensor(out=pob[:], in0=po1[:], in1=BG[tt % 2][hp][:], op=ALU.add), [po1, BG[tt % 2][hp]], [pob])
                s.dma("sp", self.yr.t[hp * 128:(hp + 1) * 128, t0:t0 + TT], pob[:], r=[pob.R()], w=[self.yr.R((hp, t0))])
                yield

        for _ in prep(0):
            pass
        for tt in range(self.NT):
            t0 = tt * TT
            zr = self.zr
            def indep(cc):
                cs = slice(cc * CH, (cc + 1) * CH)
                b = cc % 2
                Btok, Ktok, Vtok = TOK[b]
                SM = SMb[b]
                for srcs, dst, eng in [(BT, Btok, "act"), (KT, Ktok, "dve"), (VR, Vtok, "act")]:
                    pt_ = self.ps()
                    if use_b and srcs is not VR:
                        pv_ = pt_.t.bitcast(BF16)
                        idb_ = c["ident_f_b"]
                        for hp in range(4):
                            V("pe", lambda e, hp=hp: e.transpose(out=pv_[0:64, hp * 128:(hp + 1) * 128], in_=srcs[hp][:, cs],
                                                                 identity=idb_[:]), [srcs[hp], idb_], [pt_])
                        cp(eng, dst[:], pv_[0:64, 0:512], [pt_], [dst])
                    else:
                        for hp in range(4):
                            V("pe", lambda e, hp=hp: e.transpose(out=pt_[0:64, hp * 128:(hp + 1) * 128],
                                                                 in_=(srcs[hp][:, cs] if srcs is VR else srcs[hp][:, cs].bitcast(F32)),
                                                                 identity=idf[:]), [srcs[hp], idf], [pt_])
                        cp(eng, dst[:], pt_[0:64, :], [pt_], [dst])
                    yield
                for hp in range(4):
                    pS = self.ps()
                    for par in range(2):
                        h = 2 * hp + par
                        rhs = ar(h)[0:64, cc, :, :].rearrange("p a t -> p (a t)")
                        V("pe", lambda e, par=par, h=h, rhs=rhs: MM(e, pS[0:64, par * 256:par * 256 + 128], lhsT=bt(h)[0:64, cs],
                                                                   rhs=rhs, start=True, stop=True), [bt(h), ar(h)], [pS])
                        V("pe", lambda e, par=par, h=h, rhs=rhs: MM(e, pS[0:64, par * 256 + 128:par * 256 + 256],
                                                                   lhsT=kt(h)[0:64, cs], rhs=rhs, start=True, stop=True),
                          [kt(h), ar(h)], [pS])
                    V("dve", lambda e, hp=hp, pS=pS: e.tensor_tensor(out=SM[hp][:], in0=pS[0:64, :], in1=mS[:], op=ALU.mult),
                      [pS, mS], [SM[hp]])
                    if hp % 2 == 1:
                        yield
                pA = self.ps()
                for h in range(8):
                    V("pe", lambda e, h=h: MM(e, pA[0:64, h * 64:(h + 1) * 64], lhsT=ar(h)[0:64, cc, 0, :],
                                              rhs=bt(h)[0:64, cs], start=True, stop=True), [ar(h), bt(h)], [pA])
                X, XT, Tt = Xa[0], XTa[0], TTa[0]
                V("dve", lambda e: e.tensor_tensor(out=X[:], in0=pA[0:64, :], in1=mA[:], op=ALU.mult), [pA, mA], [X])
                for hp in range(4):
                    V("pool", lambda e, hp=hp: e.tensor_copy(
                        out=XT[:, hp * 128:(hp + 1) * 128].rearrange("p (a t) -> p a t", a=2),
                        in_=SM[hp][:, :].rearrange("p (a t) -> p a t", a=2)[:, :, 0:64]), [SM[hp]], [XT])
                V("pool", lambda e: e.tensor_tensor(out=Tt[:], in0=XT[:], in1=id8[:], op=ALU.add), [XT, id8], [Tt])
                yield
                for k in range(1, 6):
                    Xn, XTn, Tn = Xa[k % 2], XTa[k % 2], (TTa[k % 2] if k < 5 else Tfin[b])
                    pX = self.ps()
                    for h in range(8):
                        hsl = slice(h * 64, (h + 1) * 64)
                        V("pe", lambda e, hsl=hsl: MM(e, pX[0:64, hsl], lhsT=XT[:, hsl], rhs=X[:, hsl], start=True, stop=True),
                          [XT, X], [pX])
                    if k < 5:
                        pXT = self.ps()
                        for h in range(8):
                            hsl = slice(h * 64, (h + 1) * 64)
                            V("pe", lambda e, hsl=hsl: MM(e, pXT[0:64, hsl], lhsT=X[:, hsl], rhs=XT[:, hsl], start=True,
                                                           stop=True), [XT, X], [pXT])
                    cp("act", Xn[:], pX[0:64, :], [pX], [Xn])
                    if k < 5:
                        cp("dve", XTn[:], pXT[0:64, :], [pXT], [XTn])
                    yield
                    pT = self.ps()
                    for h in range(8):
                        hsl = slice(h * 64, (h + 1) * 64)
                        V("pe", lambda e, hsl=hsl: MM(e, pT[0:64, hsl], lhsT=Xn[:, hsl], rhs=Tt[:, hsl], start=True, stop=True),
                          [Xn, Tt], [pT])
                    V("dve", lambda e: e.tensor_tensor(out=Tn[:], in0=pT[0:64, :], in1=Tt[:], op=ALU.add), [pT, Tt], [Tn])
                    X, XT, Tt = Xn, XTn, Tn
                    yield

            def dep(cc):
                cs = slice(cc * CH, (cc + 1) * CH)
                b = cc % 2
                Btok, Ktok, Vtok = TOK[b]
                SM = SMb[b]
                Tt = Tfin[b]
                if tt % self.TPS == 0 and cc == 0:
                    V("pool", lambda e: e.memset(Hf[:], 0.0), [], [Hf])
                    V("pool", lambda e: e.tensor_copy(out=H[:], in_=Hf[:]), [Hf], [H])

                def hd(h):
                    return h // 2, (h % 2) * 256, slice(h * 64, (h + 1) * 64)
                pW = self.ps()
                for h in range(8):
                    hp, b0, hsl = hd(h)
                    V("pe", lambda e, hp=hp, b0=b0, hsl=hsl: MM(e, pW[0:64, hsl], lhsT=SM[hp][:, b0 + 128:b0 + 192],
                                                               rhs=Vtok[:, hsl], start=True, stop=False), [SM[hp], Vtok], [pW])
                    V("pe", lambda e, h=h, hsl=hsl: MM(e, pW[0:64, hsl], lhsT=ar(h)[0:64, cc, 0, :], rhs=H[:, hsl], start=False,
                                                      stop=True), [ar(h), H], [pW])
                cp("act", W0s[:], pW[0:64, :], [pW], [W0s])
                yield
                pU = self.ps()
                for h in range(8):
                    hp, b0, hsl = hd(h)
                    V("pe", lambda e, hsl=hsl: MM(e, pU[0:64, hsl], lhsT=Tt[:, hsl], rhs=W0s[:, hsl], start=True, stop=True),
                      [Tt, W0s], [pU])
                cp("dve", Us[:], pU[0:64, :], [pU], [Us])
                yield
                pY = self.ps()
                for h in range(8):
                    hp, b0, hsl = hd(h)
                    V("pe", lambda e, hp=hp, b0=b0, hsl=hsl: MM(e, pY[0:64, hsl], lhsT=SM[hp][:, b0 + 192:b0 + 256],
                                                               rhs=Vtok[:, hsl], start=True, stop=False), [SM[hp], Vtok], [pY])
                    V("pe", lambda e, hp=hp, b0=b0, hsl=hsl: MM(e, pY[0:64, hsl], lhsT=SM[hp][:, b0 + 64:b0 + 128],
                                                               rhs=Us[:, hsl], start=False, stop=False), [SM[hp], Us], [pY])
                    V("pe", lambda e, h=h, hsl=hsl: MM(e, pY[0:64, hsl], lhsT=ar(h)[0:64, cc, 1, :], rhs=H[:, hsl], start=False,
                                                      stop=True), [ar(h), H], [pY])
                cp("act", YN[:, cc, :], pY[0:64, :], [pY], [YN])
                yield
                pH = self.ps()
                for h in range(8):
                    hp, b0, hsl = hd(h)
                    V("pe", lambda e, hsl=hsl: MM(e, pH[0:64, hsl], lhsT=Btok[:, hsl], rhs=Us[:, hsl], start=True, stop=False),
                      [Btok, Us], [pH])
                    V("pe", lambda e, hsl=hsl: MM(e, pH[0:64, hsl], lhsT=Ktok[:, hsl], rhs=Vtok[:, hsl], start=False, stop=True),
                      [Ktok, Vtok], [pH])
                V("dve", lambda e: e.tensor_tensor(out=Ht[:], in0=pH[0:64, :], in1=Hf[:], op=ALU.add), [pH, Hf], [Ht])
                yield
                V("dve", lambda e: e.tensor_tensor(out=Hf[:].rearrange("p (h v) -> p h v", h=8),
                                                    in0=Ht[:].rearrange("p (h v) -> p h v", h=8),
                                                    in1=PCall[:, cc, :].unsqueeze(2).broadcast_to([64, 8, 64]), op=ALU.mult),
                  [Ht, PCall], [Hf])
                V("act", lambda e: e.activation(out=H[:], in_=Hf[:], func=AF.Copy), [Hf], [H])
                yield

            def drive(gens):
                gens = [g for g in gens if g is not None]
                while gens:
                    for g in list(gens):
                        try:
                            next(g)
                        except StopIteration:
                            gens.remove(g)

            if cut >= 4:
                drive([indep(0)])
                for cc in range(8):
                    drive([dep(cc), indep(cc + 1) if cc + 1 < 8 else None])
            drive([gnpost(tt), prep(tt + 1) if tt + 1 < self.NT else None])

    def load_wb(self, dst, dstv, name, l, bounds, order=None):
        Wt = self.W[name]
        srcv = Wt.t[l].rearrange("(k p) n -> p k n", p=128)
        dst._bounds = list(bounds)
        for i in (order if order is not None else range(len(bounds) - 1)):
            c0, c1 = bounds[i], bounds[i + 1]
            self.s.dma("pool", dstv[:, :, c0:c1], srcv[:, :, c0:c1], r=[Wt.R()], w=[dst.R(("blk", i))])

    @staticmethod
    def wres(dst, col):
        import bisect
        return dst.R(("blk", bisect.bisect_right(dst._bounds, col) - 1))

    def load_w(self, dst, name, l, nk):
        Wt = self.W[name]
        srcv = Wt.t[l].rearrange("(k p) n -> p k n", p=128)
        for kc in range(nk):
            self.s.dma("pool", dst[:, kc, :], srcv[:, kc, :], r=[Wt.R()], w=[dst.R()])

    def stage_D(self, l, src):
        s = self.s
        self.stage_begin()
        wof = self.sb("wof", [128, 4, D], BF16)
        wor = self.sb("wor", [128, 4, D], BF16)
        wout = self.sb("wout", [128, KC, D], BF16)
        for blk in range(2):
            self.load_wb(wof, wof.t, "w_o_fox", l, [0, 512, 1024], order=[blk])
            self.load_wb(wor, wor.t, "w_o_rwkv", l, [0, 512, 1024], order=[blk])
        self.load_wb(wout, wout.t, "w_out", l, [0, 512, 1024])
        yfT = [self.sb(f"yfT{i}", [128, 4, TT], BF16) for i in range(2)]
        yrT = [self.sb(f"yrT{i}", [128, 4, TT], BF16) for i in range(2)]
        gtT = [self.sb(f"gtT{i}", [128, 16, TT], BF16) for i in range(2)]
        xt_ = [self.sb(f"xD{i}", [128, KC, TT]) for i in range(2)]
        mg = self.sb("mg", [128, KC, TT], BF16)
        srcv = src.t.rearrange("(k p) t -> p k t", p=128)
        dstv = self.xs.t.rearrange("(k p) t -> p k t", p=128)
        def loads(tt):
            t0 = tt * TT
            yf, yr, gt, xt = yfT[tt % 2], yrT[tt % 2], gtT[tt % 2], xt_[tt % 2]
            s.dma("sp", yf[:], self.yf.t[:, t0:t0 + TT].rearrange("(k p) t -> p k t", p=128),
                  r=[self.yf.R((h, t0)) for h in range(8)], w=[yf.R()])
            s.dma("sp", yr[:], self.yr.t[:, t0:t0 + TT].rearrange("(k p) t -> p k t", p=128),
                  r=[self.yr.R((hp, t0)) for hp in range(4)], w=[yr.R()])
            s.dma("sp", gt[:], self.gt.t[:, t0:t0 + TT].rearrange("(k p) t -> p k t", p=128),
                  r=[self.gt.R((i, t0)) for i in range(16)], w=[gt.R()])
            for kc in range(KC):
                s.dma("sp", xt[:, kc, :], srcv[:, kc, t0:t0 + TT], r=[src.R(("tile", tt))], w=[xt.R()])

        loads(0)
        for tt in range(self.NT):
            t0 = tt * TT
            yf, yr, gt, xt = yfT[tt % 2], yrT[tt % 2], gtT[tt % 2], xt_[tt % 2]
            if tt + 1 < self.NT:
                loads(tt + 1)
            for n in range(KC):
                ns = slice(n * 128, (n + 1) * 128)
                pa = self.ps()
                for kc in range(4):
                    s.op("pe", lambda e, kc=kc: e.matmul(pa[:, :], lhsT=wof[:, kc, ns], rhs=yf[:, kc, :], start=(kc == 0),
                                                         stop=(kc == 3)), r=[self.wres(wof, n * 128), yf.R()], w=[pa.R()])
                pb = self.ps()
                for kc in range(4):
                    s.op("pe", lambda e, kc=kc: e.matmul(pb[:, :], lhsT=wor[:, kc, ns], rhs=yr[:, kc, :], start=(kc == 0),
                                                         stop=(kc == 3)), r=[self.wres(wor, n * 128), yr.R()], w=[pb.R()])
                m1 = self.tmpA("m1", [128, TT])
                s.op("dve", lambda e, m1=m1: e.tensor_tensor(out=m1[:], in0=pa[:, :], in1=gt[:, n, :], op=ALU.mult),
                     r=[pa.R(), gt.R()], w=[m1.R()])
                m2 = self.tmpA("m2", [128, TT])
                s.op("dve", lambda e, m2=m2: e.tensor_tensor(out=m2[:], in0=pb[:, :], in1=gt[:, 8 + n, :], op=ALU.mult),
                     r=[pb.R(), gt.R()], w=[m2.R()])
                s.op("pool", lambda e, m1=m1, m2=m2: e.tensor_tensor(out=mg[:, n, :], in0=m1[:], in1=m2[:], op=ALU.add),
                     r=[m1.R(), m2.R()], w=[mg.R(n)])
            for n in range(KC):
                ns = slice(n * 128, (n + 1) * 128)
                po = self.ps()
                for kc in range(KC):
                    s.op("pe", lambda e, kc=kc: e.matmul(po[:, :], lhsT=wout[:, kc, ns], rhs=mg[:, kc, :], start=(kc == 0),
                                                         stop=(kc == KC - 1)), r=[self.wres(wout, n * 128), mg.R(kc)], w=[po.R()])
                xo = self.tmpA("xo", [128, TT], nbuf=3)
                s.op("dve", lambda e, xo=xo: e.tensor_tensor(out=xo[:], in0=po[:, :], in1=xt[:, n, :], op=ALU.add),
                     r=[po.R(), xt.R()], w=[xo.R()])
                s.dma("sp", dstv[:, n, t0:t0 + TT], xo[:], r=[xo.R()], w=[self.xs.R(("tile", tt))])

    def stage_E1(self, l):
        s = self.s
        self.stage_begin()
        wup = self.sb("wup", [128, KC, 2 * DFF], BF16)
        jb = [0, 4, 10, 16, NJ]
        self.load_wb(wup, wup.t, "w_up", l, [x * 128 for x in jb] + [DFF + x * 128 for x in jb[1:]],
                     order=[0, 4, 1, 5, 2, 6, 3, 7])
        xt = self.sb("xE", [128, KC, TT])
        h2 = [self.sb(f"h2{i}", [128, KC, TT], BF16) for i in range(2)]
        ub = [self.sb(f"ub{i}", [128, TT + 2]) for i in range(2)]
        srcv = self.xs.t.rearrange("(k p) t -> p k t", p=128)
        K0 = float(2.0 * np.sqrt(2.0 / np.pi))
        def load_tile(tt):
            for kc in range(KC):
                s.dma("sp", xt[:, kc, :], srcv[:, kc, tt * TT:(tt + 1) * TT], r=[self.xs.R(("tile", tt))], w=[xt.R()])

        def prep_tile(tt):
            load_tile(tt)
            self.rmsnorm_tile(l, "g_ffn", xt, h2[tt % 2], "E")

        prep_tile(0)
        rng_ = [None]
        for tt in range(self.NT):
            t0 = tt * TT
            seq_start = (tt % self.TPS == 0)
            ht = h2[tt % 2]
            hR = [ht.R(kc) for kc in range(KC)]
            for j in range(NJ):
                if tt + 1 < self.NT:
                    if j == 0:
                        load_tile(tt + 1)
                        rng_[0] = self.rn_gen(l, "g_ffn", xt, h2[(tt + 1) % 2])
                    elif j in (3, 5, 6):
                        next(rng_[0])
                cres = []
                for half in range(2):
                    idx = half * NJ + j
                    c0 = idx * 128
                    pu = self.ps()
                    for kc in range(KC):
                        s.op("pe", lambda e, kc=kc, pu=pu, c0=c0: e.matmul(pu[:, :], lhsT=wup[:, kc, c0:c0 + 128], rhs=ht[:, kc, :],
                                                                           start=(kc == 0), stop=(kc == KC - 1)),
                             r=[hR[kc], self.wres(wup, c0)], w=[pu.R()])
                    u = ub[half]
                    s.op("act", lambda e, u=u, pu=pu: e.activation(out=u[:, 2:TT + 2], in_=pu[:, :], func=AF.Copy), r=[pu.R()],
                         w=[u.R()])
                    if seq_start:
                        s.op("pool", lambda e, u=u: e.memset(u[:, 0:2], 0.0), w=[u.R()])
                    else:
                        s.op("pool", lambda e, u=u, idx=idx: e.tensor_copy(out=u[:, 0:2], in_=self.ucarry[:, idx, :]),
                             r=[self.ucarry.R(idx)], w=[u.R()])
                    s.op("pool", lambda e, u=u, idx=idx: e.tensor_copy(out=self.ucarry[:, idx, :], in_=u[:, TT:TT + 2]),
                         r=[u.R()], w=[self.ucarry.R(idx)])
                    c1 = self.tmpA(f"c1{half}", [128, TT], nbuf=1)
                    s.op("act", lambda e, pu=pu, c1=c1, idx=idx: e.activation(out=c1[:], in_=pu[:, :], func=AF.Identity,
                                                                              scale=self.col(l, "cw2", idx),
                                                                              bias=self.col(l, "cb", idx)),
                         r=[pu.R(), self.colpack.R()], w=[c1.R()])
                    c2 = self.tmpA(f"c2{half}", [128, TT], nbuf=1)
                    s.op("dve", lambda e, u=u, c1=c1, c2=c2, idx=idx: e.scalar_tensor_tensor(out=c2[:], in0=u[:, 1:TT + 1],
                                                                                           scalar=self.col(l, "cw1", idx), in1=c1[:],
                                                                                           op0=ALU.mult, op1=ALU.add),
                         r=[u.R(), c1.R(), self.colpack.R()], w=[c2.R()])
                    c3 = self.tmpA(f"c3{half}", [128, TT], nbuf=2)
                    s.op("dve", lambda e, u=u, c2=c2, c3=c3, idx=idx: e.scalar_tensor_tensor(out=c3[:], in0=u[:, 0:TT],
                                                                                           scalar=self.col(l, "cw0", idx), in1=c2[:],
                                                                                           op0=ALU.mult, op1=ALU.add),
                         r=[u.R(), c2.R(), self.colpack.R()], w=[c3.R()])
                    cres.append(c3)
                u1, u2 = cres
                sg = self.tmpA("gsg", [128, TT], nbuf=2)
                s.op("act", lambda e, sg=sg, u1=u1: e.activation(out=sg[:], in_=u1[:], func=AF.Gelu_apprx_tanh), r=[u1.R()],
                     w=[sg.R()])
                ao = self.tmpA("gao", [128, TT], BF16, nbuf=3)
                s.op("pool", lambda e, sg=sg, u2=u2, ao=ao: e.tensor_tensor(out=ao[:], in0=sg[:], in1=u2[:], op=ALU.mult),
                     r=[sg.R(), u2.R()], w=[ao.R()])
                s.dma("sp", self.actT.t[j * 128:(j + 1) * 128, t0:t0 + TT], ao[:], r=[ao.R()], w=[self.actT.R((j, t0))])

    def stage_E2(self, l, last):
        s = self.s
        self.stage_begin()
        wdn = self.sb("wdn", [128, NJ, D], BF16)
        wpg = self.sb("wpg", [128, KC, D], BF16)
        wpu = self.sb("wpu", [128, 2, D], BF16)
        self.load_wb(wdn, wdn.t, "w_down", l, [0, 256, 512, 1024])
        self.load_wb(wpg, wpg.t, "w_ple_gate", l, [0, 512, 1024])
        self.load_wb(wpu, wpu.t, "w_ple_up", l, [0, 1024])
        at_ = [self.sb(f"at{i}", [128, NJ, TT], BF16) for i in range(2)]
        pt_ = [self.sb(f"pp{i}", [128, 2, TT], BF16) for i in range(2)]
        x2_ = [self.sb(f"x2{i}", [128, KC, TT]) for i in range(2)]
        h3_ = [self.sb(f"h3{i}", [128, KC, TT], BF16) for i in range(2)]
        srcv = self.xs.t.rearrange("(k p) t -> p k t", p=128)
        dst = self.out if last else self.xs
        dstv = dst.t.rearrange("(k p) t -> p k t", p=128)
        pv = self.pT.t[l].rearrange("(k p) t -> p k t", p=128)

        def load_at(tt):
            t0 = tt * TT
            at = at_[tt % 2]
            s.dma("sp", at[:], self.actT.t[:, t0:t0 + TT].rearrange("(k p) t -> p k t", p=128),
                  r=[self.actT.R((j, t0)) for j in range(NJ)], w=[at.R()])

        def down(tt):
            t0 = tt * TT
            at, pp, x2, h3 = at_[tt % 2], pt_[tt % 2], x2_[tt % 2], h3_[tt % 2]
            if tt + 1 < self.NT:
                load_at(tt + 1)
            s.dma("pool", pp[:], pv[:, :, t0:t0 + TT], r=[self.pT.R()], w=[pp.R()])
            for kc in range(KC):
                s.dma("sp", x2[:, kc, :], srcv[:, kc, t0:t0 + TT], r=[self.xs.R(("tile", tt))], w=[x2.R()])
            for n in range(KC):
                ns = slice(n * 128, (n + 1) * 128)
                pd = self.ps()
                for j in range(NJ):
                    s.op("pe", lambda e, j=j: e.matmul(pd[:, :], lhsT=wdn[:, j, ns], rhs=at[:, j, :], start=(j == 0),
                                                       stop=(j == NJ - 1)), r=[self.wres(wdn, n * 128), at.R()], w=[pd.R()])
                s.op("dve", lambda e: e.tensor_tensor(out=x2[:, n, :], in0=pd[:, :], in1=x2[:, n, :], op=ALU.add),
                     r=[pd.R(), x2.R()], w=[x2.R()])
            self.rmsnorm_tile(l, "g_ple", x2, h3, "F")

        def ple(tt):
            t0 = tt * TT
            pp, x2, h3 = pt_[tt % 2], x2_[tt % 2], h3_[tt % 2]
            for n in range(KC):
                ns = slice(n * 128, (n + 1) * 128)
                pg = self.ps()
                for kc in range(KC):
                    s.op("pe", lambda e, kc=kc: e.matmul(pg[:, :], lhsT=wpg[:, kc, ns], rhs=h3[:, kc, :], start=(kc == 0),
                                                         stop=(kc == KC - 1)), r=[self.wres(wpg, n * 128), h3.R(kc)], w=[pg.R()])
                sgt = self.tmpA("sgt", [128, TT])
                s.op("act", lambda e, sgt=sgt: e.activation(out=sgt[:], in_=pg[:, :], func=AF.Sigmoid), r=[pg.R()], w=[sgt.R()])
                pq = self.ps()
                for kc in range(2):
                    s.op("pe", lambda e, kc=kc: e.matmul(pq[:, :], lhsT=wpu[:, kc, ns], rhs=pp[:, kc, :], start=(kc == 0),
                                                         stop=(kc == 1)), r=[self.wres(wpu, n * 128), pp.R()], w=[pq.R()])
                s.op("dve", lambda e, sgt=sgt: e.tensor_tensor(out=sgt[:], in0=pq[:, :], in1=sgt[:], op=ALU.mult),
                     r=[pq.R(), sgt.R()], w=[sgt.R()])
                xo = self.tmpA("xo2", [128, TT], nbuf=2)
                s.op("pool", lambda e, sgt=sgt, xo=xo: e.tensor_tensor(out=xo[:], in0=sgt[:], in1=x2[:, n, :], op=ALU.add),
                     r=[sgt.R(), x2.R()], w=[xo.R()])
                s.dma("sp", dstv[:, n, t0:t0 + TT], xo[:], r=[xo.R()], w=[dst.R(("tile", tt))])

        load_at(0)
        down(0)
        for tt in range(self.NT):
            if tt + 1 < self.NT:
                down(tt + 1)
            ple(tt)

    def tmpA(self, name, shape, dt=F32, nbuf=2):
        key = ("tmp", name)
        if key not in self.ep:
            self.ep[key] = [[self.sb(f"tA_{name}{i}", shape, dt) for i in range(nbuf)], 0]
        lst = self.ep[key]
        t = lst[0][lst[1] % nbuf]
        lst[1] += 1
        return t

    def epi_qk(self, l, kind, idx, ps, t0):
        s = self.s
        sq = self.tmpA("qsq", [128, TT], BF16)
        s.op("act", lambda e: e.activation(out=sq[:], in_=ps[:, :], func=AF.Square), r=[ps.R()], w=[sq.R()])
        self.deferred.append(lambda: self.epi_qk2(l, kind, idx, ps, t0, sq))

    def epi_qk2(self, l, kind, idx, ps, t0, sq):
        s = self.s
        ps2 = self.ps()
        s.op("pe", lambda e: e.matmul(ps2[:, :], lhsT=self.c["blk2_f_b"][:], rhs=sq[:], start=True, stop=True),
             r=[self.c["blk2_f_b"].R(), sq.R()], w=[ps2.R()])
        rs2 = self.tmpA("qrs2", [128, TT], nbuf=1)
        self.rsqrt_ps(rs2, ps2, 1.0 / 64, RMS_EPS, 0.125 if kind == "q" else 1.0)
        o = self.tmpA("qo", [128, TT], BF16)
        gname = "g_q" if kind == "q" else "g_k"
        s.op("dve", lambda e: e.scalar_tensor_tensor(out=o[:], in0=ps[:, :], scalar=self.col(l, gname), in1=rs2[:],
                                                      op0=ALU.mult, op1=ALU.mult),
             r=[ps.R(), rs2.R(), self.colpack.R()], w=[o.R()])
        dst = self.qa if kind == "q" else self.ka
        for hh in range(2):
            h = idx * 2 + hh
            s.dma("sp", dst[h, 0:64, t0:t0 + TT], o[hh * 64:(hh + 1) * 64, :], r=[o.R()], w=[dst.R(("qk", h, t0))])

    def epi_f(self, l, ps, t0, seq_start):
        s = self.s
        e1 = self.tmpA("fe", [8, TT], nbuf=1)
        s.op("act", lambda e: e.activation(out=e1[:], in_=ps[0:8, :], func=AF.Exp, bias=self.negb[:, l:l + 1], scale=-1.0),
             r=[ps.R(), self.negb.R()], w=[e1.R()])
        l1 = e1
        one = self.c["ones_f"]
        s.op("act", lambda e: e.activation(out=l1[:], in_=e1[:], func=AF.Ln, bias=one[0:8, 0:1], scale=1.0),
             r=[e1.R(), one.R()], w=[l1.R()])
        c = self.tmpA("fc", [8, TT], nbuf=1)
        if seq_start:
            init = 0.0
            rr = []
        else:
            init = self.ccar[:, 0:1]
            rr = [self.ccar.R()]
        s.op("dve", lambda e: e.tensor_tensor_scan(out=c[:], data0=self.c["ones_f"][0:8, :], data1=l1[:], initial=init,
                                                    op0=ALU.mult, op1=ALU.subtract),
             r=[self.c["ones_f"].R(), l1.R()] + rr, w=[c.R()])
        s.op("dve", lambda e: e.tensor_copy(out=self.ccar[:, 0:1], in_=c[:, TT - 1:TT]), r=[c.R()], w=[self.ccar.R()])
        hi = self.tmpA("fhi", [8, TT], BF16, nbuf=1)
        s.op("dve", lambda e: e.tensor_copy(out=hi[:], in_=c[:]), r=[c.R()], w=[hi.R()])
        r1 = self.tmpA("fr1", [8, TT], nbuf=1)
        s.op("dve", lambda e: e.tensor_tensor(out=r1[:], in0=c[:], in1=hi[:], op=ALU.subtract), r=[c.R(), hi.R()],
             w=[r1.R()])
        mid = self.tmpA("fmid", [8, TT], BF16, nbuf=1)
        s.op("dve", lambda e: e.tensor_copy(out=mid[:], in_=r1[:]), r=[r1.R()], w=[mid.R()])
        r2 = self.tmpA("fe", [8, TT], nbuf=1)
        s.op("dve", lambda e: e.tensor_tensor(out=r2[:], in0=r1[:], in1=mid[:], op=ALU.subtract), r=[r1.R(), mid.R()],
             w=[r2.R()])
        lo = self.tmpA("flo", [8, TT], BF16, nbuf=1)
        s.op("dve", lambda e: e.tensor_copy(out=lo[:], in_=r2[:]), r=[r2.R()], w=[lo.R()])
        for j, part in enumerate([hi, mid, lo]):
            s.dma("sp", self.qa.t[:, 64 + j, t0:t0 + TT], part[:], r=[part.R()], w=[self.qa.R(("c", j, t0))])
            ng = self.tmpA("fng", [8, TT], BF16, nbuf=1)
            s.op("act", lambda e, ng=ng, part=part: e.activation(out=ng[:], in_=part[:], func=AF.Copy, scale=-1.0),
                 r=[part.R()], w=[ng.R()])
            s.dma("sp", self.ka.t[:, 67 + j, t0:t0 + TT], ng[:], r=[ng.R()], w=[self.ka.R(("c", j, t0))])

    def epi_rw(self, l, idx, ps, t0, tt, seq_start, vd):
        s = self.s
        zb = self.zraw[idx % 2]
        s.op("act", lambda e: e.activation(out=zb[:, 1:TT + 1], in_=ps[:, :], func=AF.Copy), r=[ps.R()], w=[zb.R()])
        if seq_start:
            s.op("pool", lambda e: e.memset(zb[:, 0:1], 0.0), w=[zb.R()])
        else:
            s.op("pool", lambda e: e.tensor_copy(out=zb[:, 0:1], in_=self.carry[:, idx:idx + 1]),
                 r=[self.carry.R(idx)], w=[zb.R()])
        s.op("pool", lambda e: e.tensor_copy(out=self.carry[:, idx:idx + 1], in_=zb[:, TT:TT + 1]), r=[zb.R()],
             w=[self.carry.R(idx)])
        d = self.tmpA("rwd", [128, TT])
        s.op("dve", lambda e: e.tensor_tensor(out=d[:], in0=zb[:, 0:TT], in1=zb[:, 1:TT + 1], op=ALU.subtract),
             r=[zb.R()], w=[d.R()])
        o = self.tmpA("rwo", [128, TT], nbuf=2)
        s.op("dve", lambda e: e.scalar_tensor_tensor(out=o[:], in0=d[:], scalar=self.col(l, "mu", idx), in1=zb[:, 1:TT + 1],
                                                      op0=ALU.mult, op1=ALU.add),
             r=[d.R(), zb.R(), self.colpack.R()], w=[o.R()])
        if 8 <= idx < 12:
            vi = idx - 8
            if l == 0:
                s.dma("sp", self.vf.t[vi * 128:(vi + 1) * 128, t0:t0 + TT], o[:], r=[o.R()], w=[self.vf.R((vi, t0))])
            else:
                ps2 = self.ps()
                s.op("pe", lambda e: e.matmul(ps2[:, :], lhsT=self.wvu[:, vi * 128:(vi + 1) * 128], rhs=vd[:], start=True,
                                              stop=True), r=[self.wvu.R(), vd.R()], w=[ps2.R()])
                vm = self.tmpA("vm", [128, TT], nbuf=1)
                s.op("act", lambda e: e.activation(out=vm[:], in_=ps2[:, :], func=AF.Sigmoid, bias=self.col(l, "v0", vi),
                                                   scale=1.0), r=[ps2.R(), self.colpack.R()], w=[vm.R()])
                vfl = self.tmpA("vfl", [128, TT], nbuf=1)
                s.dma("sp", vfl[:], self.vf.t[vi * 128:(vi + 1) * 128, t0:t0 + TT], r=[self.vf.R((vi, t0))], w=[vfl.R()])
                dd = self.tmpA("vdd", [128, TT], nbuf=1)
                s.op("pool", lambda e: e.tensor_tensor(out=dd[:], in0=vfl[:], in1=o[:], op=ALU.subtract),
                     r=[vfl.R(), o.R()], w=[dd.R()])
                s.op("pool", lambda e: e.tensor_tensor(out=dd[:], in0=dd[:], in1=vm[:], op=ALU.mult), r=[dd.R(), vm.R()],
                     w=[dd.R()])
                o2 = self.tmpA("rwo2", [128, TT], nbuf=1)
                s.op("dve", lambda e: e.tensor_tensor(out=o2[:], in0=o[:], in1=dd[:], op=ALU.add), r=[o.R(), dd.R()],
                     w=[o2.R()])
                o = o2
        s.dma("sp", self.zr.t[idx * 128:(idx + 1) * 128, t0:t0 + TT], o[:], r=[o.R()], w=[self.zr.R((idx, t0))])

    def epi_gate(self, l, idx, ps, t0):
        s = self.s
        o = self.tmpA("go", [128, TT], BF16, nbuf=3)
        s.op("act", lambda e: e.activation(out=o[:], in_=ps[:, :], func=AF.Sigmoid), r=[ps.R()], w=[o.R()])
        s.dma("sp", self.gt.t[idx * 128:(idx + 1) * 128, t0:t0 + TT], o[:], r=[o.R()], w=[self.gt.R((idx, t0))])


def host_inputs(inp, S, NB, depth, core):
    b0 = core * NB
    x = np.asarray(inp["x"], np.float32)[b0:b0 + NB].reshape(NB * S, D)
    p = np.asarray(inp["p"], np.float32)[:depth, b0:b0 + NB].reshape(depth, NB * S, PLE)
    m = {"xT": np.ascontiguousarray(x.T), "pT": np.ascontiguousarray(p.transpose(0, 2, 1))}
    return m


def shared_inputs(inp, depth):
    m = {}
    for name in ["w_in", "w_decay_up", "w_aaa_up", "w_gate_up", "w_o_fox", "w_o_rwkv", "w_out", "w_up", "w_down",
                 "w_ple_gate", "w_ple_up"]:
        m[name] = np.ascontiguousarray(np.asarray(inp[name], np.float32)[:depth])
    for name in ["w_vres_down", "w_vres_up"]:
        m[name] = np.ascontiguousarray(np.asarray(inp[name], np.float32)[:max(depth - 1, 1)])
    m["colpack"] = make_colpack({k: np.asarray(v, np.float32) for k, v in inp.items()}, depth)
    for k, v in make_consts().items():
        m["c_" + k] = v
    return m


_PROG_CACHE = {}


def kernel(**inp):
    S, NBT, depth = 2048, 16, 4
    ncores = 8
    NB = NBT // ncores
    key = (S, NB, depth)
    if key not in _PROG_CACHE:
        _PROG_CACHE[key] = Prog(S, NB, depth)
    prog = _PROG_CACHE[key]
    sh = shared_inputs(inp, depth)
    in_maps = []
    for c in range(ncores):
        m = dict(sh)
        m.update(host_inputs(inp, S, NB, depth, c))
        in_maps.append(m)
    res = run_bass_kernel_spmd(prog.nc, in_maps, core_ids=list(range(ncores)))
    outs = []
    for c in range(ncores):
        o = res.results[c]["outT"]
        outs.append(np.ascontiguousarray(o.T).reshape(NB, S, D))
    return np.concatenate(outs, axis=0).astype(np.float32)
```

```python
import numpy as np
import concourse.bass as bass
import concourse.mybir as mybir
from concourse.bass_utils import run_bass_kernel_spmd

F32 = mybir.dt.float32
BF16 = mybir.dt.bfloat16
AF = mybir.ActivationFunctionType
ALU = mybir.AluOpType
AX = mybir.AxisListType

D = 1024
KC = 8
FOXW = 512
RW = 512
NIN = 5384
DFF = 2816
NJ = 22
PLE = 256
RMS_EPS = 1e-6
GN_EPS = 64e-5
CH = 64
TT = 512


class Res:
    __slots__ = ("w", "rd")

    def __init__(self):
        self.w = None
        self.rd = {}


class Tl:
    def __init__(self, t):
        self.t = t
        self._r = {}

    def R(self, key=None):
        r = self._r.get(key)
        if r is None:
            r = Res()
            self._r[key] = r
        return r

    def __getitem__(self, idx):
        return self.t[idx]


class Sch:
    NDS = 16

    def __init__(self, nc):
        self.nc = nc
        self.e = dict(pe=nc.tensor, act=nc.scalar, dve=nc.vector, pool=nc.gpsimd, sp=nc.sync)
        self.sem = {k: nc.alloc_semaphore(name=f"s_{k}") for k in self.e}
        self.cnt = {k: 0 for k in self.e}
        self.seen = {k: {} for k in self.e}
        self.dsem = {q: [nc.alloc_semaphore(name=f"d_{q}{i}") for i in range(self.NDS)]
                     for q in ("sp", "pool", "act")}
        self.dcnt = {q: 0 for q in self.dsem}
        self.semh = {}
        for k in self.e:
            self.semh[k] = self.sem[k]
        for q in self.dsem:
            for i, h in enumerate(self.dsem[q]):
                self.semh[(q, i)] = h
        self.n_ins = 0

    def _wait(self, E, key, val):
        if self.seen[E].get(key, 0) >= val:
            return
        self.e[E].wait_ge(self.semh[key], val)
        self.seen[E][key] = val
        self.n_ins += 1

    def _collect(self, reads, writes):
        deps = {}

        def add(tok):
            k, v = tok
            if deps.get(k, 0) < v:
                deps[k] = v

        for r in reads:
            if r.w is not None:
                add(r.w)
        for w in writes:
            if w.w is not None:
                add(w.w)
            for k, v in w.rd.items():
                add((k, v))
        return deps

    def _commit(self, tok, reads, writes):
        k, v = tok
        for w in writes:
            w.w = tok
            w.rd = {}
        for r in reads:
            if r.rd.get(k, 0) < v:
                r.rd[k] = v

    def op(self, E, fn, r=(), w=()):
        deps = self._collect(r, w)
        for k, v in deps.items():
            if E == "pe" and k == "pe":
                continue
            self._wait(E, k, v)
        ins = fn(self.e[E])
        self.cnt[E] += 1
        ins.then_inc(self.sem[E], 1)
        self.n_ins += 1
        self._commit((E, self.cnt[E]), r, w)

    def dma(self, q, out, in_, r=(), w=(), **kw):
        deps = self._collect(r, w)
        for k, v in deps.items():
            self._wait(q, k, v)
        n = self.dcnt[q]
        i = n % self.NDS
        gen = n // self.NDS
        if gen > 0:
            self._wait(q, (q, i), 16 * gen)
        ins = self.e[q].dma_start(out=out, in_=in_, **kw)
        ins.then_inc(self.dsem[q][i], 16)
        self.dcnt[q] = n + 1
        self.n_ins += 1
        self._commit(((q, i), 16 * (gen + 1)), r, w)

    def finish(self):
        for q in self.dsem:
            n = self.dcnt[q]
            for i in range(self.NDS):
                cnt_i = (n - i + self.NDS - 1) // self.NDS
                if cnt_i > 0:
                    self._wait("sp", (q, i), 16 * cnt_i)
        for k in self.e:
            if k != "sp" and self.cnt[k] > 0:
                self._wait("sp", k, self.cnt[k])


def _cols(vec):
    v = np.asarray(vec, np.float32).reshape(-1)
    n = (v.size + 127) // 128
    buf = np.zeros(n * 128, np.float32)
    buf[: v.size] = v
    return buf.reshape(n, 128).T


def make_consts():
    c = {}
    c["ident_f"] = np.eye(128, dtype=np.float32)
    c["ones_f"] = np.ones((128, 512), np.float32)
    blk = np.zeros((128, 128), np.float32)
    blk[:64, :64] = 1.0
    blk[64:, 64:] = 1.0
    c["blk2_f"] = blk
    s = np.arange(128)[:, None]
    t = np.arange(128)[None, :]
    c["trimask_f"] = np.where(s > t, -30000.0, 0.0).astype(np.float32)
    t5 = np.arange(512)[None, :]
    c["fullmask"] = np.concatenate([np.where(t5 < r * 128 + s, -30000.0, 0.0).astype(np.float32) for r in range(4)], axis=1)
    sm = np.ones((128, 512), np.float32)
    sm[:, ::CH] = 0.0
    c["scanmask"] = sm
    s64 = np.arange(64)[:, None]
    t64 = np.arange(64)[None, :]
    strict_up = (t64 > s64).astype(np.float32)
    incl_up = (t64 >= s64).astype(np.float32)
    one = np.concatenate([strict_up, incl_up], axis=1)
    c["mask_S"] = np.tile(one, (1, 4))
    strict_lo = (t64 < s64).astype(np.float32)
    c["mask_A"] = np.tile(strict_lo, (1, 8))
    c["ident8"] = np.tile(np.eye(64, dtype=np.float32), (1, 8))
    return c


CONST_SHAPES = {"ident_f": (128, 128), "ones_f": (128, 512), "blk2_f": (128, 128), "trimask_f": (128, 128),
                "scanmask": (128, 512), "fullmask": (128, 2048), "mask_S": (64, 512), "mask_A": (64, 512), "ident8": (64, 512)}

COLS = {}
_o = 0
for _name, _n in [("g_mix", 8), ("g_ffn", 8), ("g_ple", 8), ("g_q", 1), ("g_k", 1), ("b_f", 1), ("mu", 14),
                  ("w0", 4), ("a0", 4), ("k_k", 4), ("k_a", 4), ("r_k", 4), ("gn_g", 4), ("gn_b", 4), ("v0", 4),
                  ("cw0", 44), ("cw1", 44), ("cw2", 44), ("cb", 44)]:
    COLS[_name] = (_o, _n)
    _o += _n
NCOL = _o


def make_colpack(inp, depth):
    pk = np.zeros((128, depth, NCOL), np.float32)

    def put(l, name, arr):
        o, n = COLS[name]
        a = _cols(arr)
        assert a.shape[1] == n, (name, a.shape, n)
        pk[:, l, o:o + n] = a

    for l in range(depth):
        put(l, "g_mix", inp["g_mix"][l])
        put(l, "g_ffn", inp["g_ffn"][l])
        put(l, "g_ple", inp["g_ple"][l])
        put(l, "g_q", np.tile(inp["g_qnorm"][l], 2))
        put(l, "g_k", np.tile(inp["g_knorm"][l], 2))
        put(l, "b_f", inp["b_f"][l])
        put(l, "mu", inp["mu_shift"][l])
        for nm, key in [("w0", "w0"), ("a0", "a0"), ("k_k", "k_k"), ("k_a", "k_a"), ("gn_g", "gn_g"),
                        ("gn_b", "gn_b")]:
            put(l, nm, inp[key][l])
        put(l, "r_k", inp["r_k"][l].reshape(-1))
        if l >= 1:
            put(l, "v0", inp["v0"][l - 1])
        for j in range(3):
            put(l, f"cw{j}", inp["conv_w"][l][j])
        put(l, "cb", inp["conv_b"][l])
    return pk.reshape(128, depth * NCOL)


def in_groups():
    g = []
    for i in range(4):
        g.append(("q", i * 128, 128, i))
    for i in range(4):
        g.append(("k", 512 + i * 128, 128, i))
    g.append(("f", 1536, 8, 0))
    for i in range(14):
        g.append(("rw", 1544 + i * 128, 128, i))
    for i in range(16):
        g.append(("gate", 3336 + i * 128, 128, i))
    return g


class Prog:
    def __init__(self, S, NB, depth, debug=False, stages="ABCDE"):
        self.S, self.NB, self.depth, self.debug, self.stages = S, NB, depth, debug, stages
        self.T = S * NB
        self.NT = self.T // TT
        self.TPS = S // TT
        nc = bass.Bass("TRN2", target_bir_lowering=False)
        self.nc = nc
        self.s = Sch(nc)
        self._ps_i = 0
        self.build()

    def psb_(self, name, shape, dt=F32):
        return Tl(self.nc.alloc_sbuf_tensor(name, list(shape), dt))

    def sb(self, name, shape, dt=F32):
        esz = 2 if dt == BF16 else 4
        nbytes = int(np.prod(shape[1:])) * esz
        nbytes = (nbytes + 63) // 64 * 64
        off = self.arena_off
        assert off + nbytes <= self.arena_end, (name, off, nbytes, self.arena_end)
        self.arena_off = off + nbytes
        self._uid += 1
        return Tl(self.nc.alloc_sbuf_tensor_at(f"{name}_{self._uid}", list(shape), dt, offset=off))

    def stage_begin(self):
        s = self.s
        for E in s.e:
            for k in s.e:
                if s.cnt[k] > 0:
                    s._wait(E, k, s.cnt[k])
            for q in s.dsem:
                n = s.dcnt[q]
                for i in range(s.NDS):
                    cnt_i = (n - i + s.NDS - 1) // s.NDS
                    if cnt_i > 0:
                        s._wait(E, (q, i), 16 * cnt_i)
        self.arena_off = self.arena_start
        self.ep = {}

    def dram(self, name, shape, dt, kind="Internal"):
        if kind == "Internal" and self.debug:
            kind = "ExternalOutput"
        return Tl(self.nc.dram_tensor(name, list(shape), dt, kind=kind))

    def ps(self):
        p = self.psb[self._ps_i % 8]
        self._ps_i += 1
        return p

    def build(self):
        nc, s = self.nc, self.s
        T, L = self.T, self.depth
        self.xT = self.dram("xT", [D, T], F32, kind="ExternalInput")
        self.pT = self.dram("pT", [L, PLE, T], F32, kind="ExternalInput")
        self.out = self.dram("outT", [D, T], F32, kind="ExternalOutput")
        W = {}
        for name, shp in [("w_in", (L, D, NIN)), ("w_decay_up", (L, 64, RW)), ("w_aaa_up", (L, 64, RW)),
                          ("w_gate_up", (L, 128, RW)), ("w_vres_down", (max(L - 1, 1), D, 32)),
                          ("w_vres_up", (max(L - 1, 1), 32, RW)), ("w_o_fox", (L, FOXW, D)),
                          ("w_o_rwkv", (L, RW, D)), ("w_out", (L, D, D)), ("w_up", (L, D, 2 * DFF)),
                          ("w_down", (L, DFF, D)), ("w_ple_gate", (L, D, D)), ("w_ple_up", (L, PLE, D))]:
            W[name] = self.dram(name, shp, F32, kind="ExternalInput")
        self.W = W
        self.colpack_d = self.dram("colpack", [128, L * NCOL], F32, kind="ExternalInput")
        self.const_d = {k: self.dram("c_" + k, list(v), F32, kind="ExternalInput") for k, v in CONST_SHAPES.items()}
        self.xs = self.dram("xs", [D, T], F32)
        self.qa = self.dram("qa", [8, 70, T], BF16)
        self.ka = self.dram("ka", [8, 70, T], BF16)
        self.zr = self.dram("zr", [1792, T], F32)
        self.vf = self.dram("vf", [RW, T], F32)
        self.gt = self.dram("gt", [2 * D, T], BF16)
        self.yf = self.dram("yf", [FOXW, T], BF16)
        self.yr = self.dram("yr", [RW, T], BF16)
        self.vt = self.dram("vt", [T, FOXW], BF16)
        self.actT = self.dram("actT", [DFF, T], BF16)
        self.psb = [Tl(nc.alloc_psum_tensor(f"ps{i}", [128, 512], F32)) for i in range(8)]
        self.colpack = self.psb_("colpack_s", [128, L * NCOL])
        s.dma("sp", self.colpack[:], self.colpack_d[:, :], r=[self.colpack_d.R()], w=[self.colpack.R()])
        self.c = {}
        for k, shp in CONST_SHAPES.items():
            if k in ("fullmask", "trimask_f"):
                continue
            self.c[k] = self.psb_("cs_" + k, shp)
            s.dma("sp", self.c[k][:], self.const_d[k][:, :], r=[self.const_d[k].R()], w=[self.c[k].R()])
        for k in ["ident_f", "blk2_f", "fullmask"]:
            self.c[k + "_b"] = self.psb_("cb_" + k, CONST_SHAPES[k], BF16)
            s.dma("pool", self.c[k + "_b"][:], self.const_d[k][:, :], r=[self.const_d[k].R()],
                  w=[self.c[k + "_b"].R()])
        self.c["ones_b"] = self.psb_("cb_ones", [128, 512], BF16)
        s.dma("pool", self.c["ones_b"][:], self.const_d["ones_f"][:, :], r=[self.const_d["ones_f"].R()],
              w=[self.c["ones_b"].R()])
        self.carry = self.psb_("carry", [128, 16])
        self.ccar = self.psb_("ccar", [8, 2])
        self.ucarry = self.psb_("ucarry", [128, 2 * NJ, 2])
        self.negb = self.psb_("negb", [8, 4])
        for ll in range(L):
            s.op("dve", lambda e, ll=ll: e.tensor_scalar(out=self.negb[:, ll:ll + 1], in0=self.col(ll, "b_f", 0, 0, 8),
                                                          scalar1=-1.0, scalar2=None, op0=ALU.mult),
                 r=[self.colpack.R()], w=[self.negb.R()])
        self._uid = 0
        base0 = int(nc.sbuf_base)
        self.arena_start = (base0 + 63) // 64 * 64
        left = (int(nc.sbuf_bytes_remaining) - 256 - (self.arena_start - base0)) // 64 * 64
        slab = nc.alloc_sbuf_tensor("arena", [128, (left + self.arena_start - base0) // 4], F32)
        self.arena_end = self.arena_start + left
        assert int(nc.sbuf_base) >= self.arena_end, (nc.sbuf_base, self.arena_end)
        self.stage_begin()
        ow = self.sb("ones_wide", [3, T], BF16)
        s.op("dve", lambda e: e.memset(ow[:], 1.0), w=[ow.R()])
        for h in range(8):
            s.dma("sp", self.qa[h, 67:70, :], ow[:], r=[ow.R()], w=[self.qa.R(("ones", h))])
            s.dma("sp", self.ka[h, 64:67, :], ow[:], r=[ow.R()], w=[self.ka.R(("ones", h))])
        for l in range(L):
            src = self.xT if l == 0 else self.xs
            if "A" in self.stages:
                self.stage_A(l, src)
            if "B" in self.stages:
                self.stage_B(l)
            if "C" in self.stages:
                self.stage_C(l)
            if "D" in self.stages:
                self.stage_D(l, src)
            if "E" in self.stages or "1" in self.stages:
                self.stage_E1(l)
            if "E" in self.stages or "2" in self.stages:
                self.stage_E2(l, last=(l == L - 1))
        s.finish()

    def col(self, l, name, j=0, p0=0, p1=128):
        o, n = COLS[name]
        assert j < n
        c0 = l * NCOL + o + j
        return self.colpack[p0:p1, c0:c0 + 1]

    def rmsnorm_tile(self, l, gname, xt, ht, tag):
        for _ in self.rn_gen(l, gname, xt, ht):
            pass

    def rn_gen(self, l, gname, xt, ht):
        s = self.s
        sq = ht
        s.op("act", lambda e: e.activation(out=sq[:], in_=xt[:], func=AF.Square), r=[xt.R()],
             w=[ht.R(kc) for kc in range(KC)])
        yield
        ps = self.ps()
        for kc in range(KC):
            s.op("pe", lambda e, kc=kc: e.matmul(ps[:, :], lhsT=self.c["ones_b"][:, 0:128], rhs=sq[:, kc, :],
                                                 start=(kc == 0), stop=(kc == KC - 1)),
                 r=[self.c["ones_b"].R(), ht.R(kc)], w=[ps.R()])
        rs = self.tmpA("rn_rs", [128, TT], nbuf=1)
        self.rsqrt_ps(rs, ps, 1.0 / D, RMS_EPS, 1.0)
        yield
        for kc in range(KC):
            eng = "dve" if kc % 2 == 0 else "pool"
            if eng == "dve":
                s.op("dve", lambda e, kc=kc: e.scalar_tensor_tensor(out=ht[:, kc, :], in0=xt[:, kc, :],
                                                                     scalar=self.col(l, gname, kc), in1=rs[:],
                                                                     op0=ALU.mult, op1=ALU.mult),
                     r=[xt.R(), rs.R(), self.colpack.R()], w=[ht.R(kc)])
            else:
                tmp = self.tmpA("rn_nt", [128, TT], nbuf=2)
                s.op("pool", lambda e, kc=kc, tmp=tmp: e.tensor_tensor(out=tmp[:], in0=xt[:, kc, :], in1=rs[:], op=ALU.mult),
                     r=[xt.R(), rs.R()], w=[tmp.R()])
                s.op("act", lambda e, kc=kc, tmp=tmp: e.activation(out=ht[:, kc, :], in_=tmp[:], func=AF.Copy,
                                                                   scale=self.col(l, gname, kc)),
                     r=[tmp.R(), self.colpack.R()], w=[ht.R(kc)])
        yield

    def stage_A(self, l, src):
        nc, s = self.nc, self.s
        T = self.T
        self.stage_begin()
        self.wbig = self.sb("wbig", [128, KC * NIN], BF16)
        self.wbig_R = self.wbig.R()
        self.xt_t = [self.sb("xt0", [128, KC, TT])] * 2
        self.ht_t = [self.sb(f"ht{i}", [128, KC, TT], BF16) for i in range(2)]
        self.zraw = [self.sb(f"zraw{i}", [128, TT + 1]) for i in range(2)]
        W = self.W
        win = self.wbig
        winv = self.wbig.t[:, 0:KC * NIN].rearrange("p (k n) -> p k n", k=KC)
        self.load_wb(self.wbig, winv, "w_in", l, [0, 1024, 1544, 2568, 3336, 4360, NIN], order=[1, 0, 2, 3, 4, 5])
        if l >= 1:
            self.wvd = self.sb("wvd", [128, KC, 32], BF16)
            self.wvu = self.sb("wvu", [32, RW], BF16)
            s.dma("pool", self.wvd[:], W["w_vres_down"].t[l - 1].rearrange("(k p) n -> p k n", p=128),
                  r=[W["w_vres_down"].R()], w=[self.wvd.R()])
            s.dma("pool", self.wvu[:], W["w_vres_up"].t[l - 1], r=[W["w_vres_up"].R()], w=[self.wvu.R()])
        groups = in_groups()
        srcv = src.t.rearrange("(k p) t -> p k t", p=128)
        self.deferred = []

        def run_deferred():
            run, self.deferred = self.deferred, []
            for f in run:
                f()

        def load_tile(tt):
            xt = self.xt_t[tt % 2]
            for kc in range(KC):
                s.dma("pool" if tt > 0 else "sp", xt[:, kc, :], srcv[:, kc, tt * TT:(tt + 1) * TT], r=[src.R(("tile", tt))],
                      w=[xt.R()])

        def prep_tile(tt):
            load_tile(tt)
            self.rmsnorm_tile(l, "g_mix", self.xt_t[tt % 2], self.ht_t[tt % 2], "A")

        prep_tile(0)
        rng_ = [None]
        for tt in range(self.NT):
            t0 = tt * TT
            seq_start = (tt % self.TPS == 0)
            xt = self.xt_t[tt % 2]
            ht = self.ht_t[tt % 2]
            hR = [ht.R(kc) for kc in range(KC)]
            for sub in range(4):
                ps = self.ps()
                for kc in range(KC):
                    s.op("pe", lambda e, kc=kc, sub=sub: e.matmul(ps[:, :], lhsT=ht[:, kc, sub * 128:(sub + 1) * 128],
                                                                   rhs=winv[:, kc, 1024:1536], start=(kc == 0),
                                                                   stop=(kc == KC - 1)),
                         r=[hR[kc], self.wres(self.wbig, 1024)], w=[ps.R()])
                blk = tt * 4 + sub
                vo = self.tmpA("vo", [128, FOXW], BF16)
                s.op("act", lambda e, vo=vo, ps=ps: e.activation(out=vo[:], in_=ps[:, :], func=AF.Copy),
                     r=[ps.R()], w=[vo.R()])
                s.dma("sp", self.vt.t[blk * 128:(blk + 1) * 128, :], vo[:], r=[vo.R()], w=[self.vt.R(blk)])
            if l >= 1:
                ps = self.ps()
                for kc in range(KC):
                    s.op("pe", lambda e, kc=kc, ps=ps: e.matmul(ps[0:32, :], lhsT=self.wvd[:, kc, :], rhs=ht[:, kc, :],
                                                                 start=(kc == 0), stop=(kc == KC - 1)),
                         r=[hR[kc], self.wvd.R()], w=[ps.R()])
                vd = self.tmpA("vd", [32, TT], BF16)
                s.op("act", lambda e, ps=ps: e.activation(out=vd[:], in_=ps[0:32, :], func=AF.Copy), r=[ps.R()],
                     w=[vd.R()])
            for gi, (kind, c0, wd, idx) in enumerate(groups):
                ps = self.ps()
                for kc in range(KC):
                    s.op("pe", lambda e, kc=kc, ps=ps, c0=c0, wd=wd: e.matmul(ps[0:wd, :], lhsT=winv[:, kc, c0:c0 + wd],
                                                                               rhs=ht[:, kc, :], start=(kc == 0),
                                                                               stop=(kc == KC - 1)),
                         r=[hR[kc], self.wres(self.wbig, c0)], w=[ps.R()])
                run_deferred()
                if tt + 1 < self.NT:
                    if gi == 0:
                        load_tile(tt + 1)
                        rng_[0] = self.rn_gen(l, "g_mix", self.xt_t[(tt + 1) % 2], self.ht_t[(tt + 1) % 2])
                    elif gi in (5, 8, 9):
                        next(rng_[0])
                if kind in ("q", "k"):
                    self.epi_qk(l, kind, idx, ps, t0)
                elif kind == "f":
                    self.epi_f(l, ps, t0, seq_start)
                elif kind == "rw":
                    self.epi_rw(l, idx, ps, t0, tt, seq_start, vd if l >= 1 else None)
                else:
                    self.epi_gate(l, idx, ps, t0)
        run_deferred()

    def rsqrt_ps(self, out, ps, scale, eps, mult, np_=128):
        s = self.s
        if not hasattr(self, "_fconst"):
            self._fconst = {}
        def fc(v):
            if v not in self._fconst:
                t = self.psb_(f"fc{len(self._fconst)}", [128, 1])
                s.op("pool", lambda e: e.memset(t[:], float(v)), w=[t.R()])
                self._fconst[v] = t
            return self._fconst[v]
        be = fc(eps)
        bm = fc(float(np.log(mult)))
        s.op("act", lambda e: e.activation(out=out[0:np_, :], in_=ps[0:np_, :], func=AF.Ln, bias=be[0:np_, :], scale=float(scale)),
             r=[ps.R(), be.R()], w=[out.R()])
        s.op("act", lambda e: e.activation(out=out[0:np_, :], in_=out[0:np_, :], func=AF.Exp, bias=bm[0:np_, :], scale=-0.5),
             r=[out.R(), bm.R()], w=[out.R()])

    def ps_rot(self, lo, hi):
        key = (lo, hi)
        if not hasattr(self, "_psr"):
            self._psr = {}
        i = self._psr.get(key, 0)
        self._psr[key] = i + 1
        return self.psb[lo + i % (hi - lo)]

    def stage_B(self, l):
        s = self.s
        S, NB = self.S, self.NB
        self.stage_begin()
        QA = [self.sb(f"QA{i}", [70, S], BF16) for i in range(2)]
        KA = [self.sb(f"KA{i}", [70, S], BF16) for i in range(2)]
        Vb = [self.sb(f"Vb{i}", [128, S // 128, FOXW], BF16) for i in range(2)]
        PT = [self.sb(f"PT{i}", [128, TT], BF16) for i in range(4)]
        rden = [self.sb(f"rden{i}", [64, TT]) for i in range(2)]
        yt = [self.sb(f"yt{i}", [64, TT], BF16) for i in range(2)]
        fm = self.c["fullmask_b"]
        idb = self.c["ident_f_b"]
        onb = self.c["ones_b"]
        NQ = S // TT
        LOOK = 3
        groups = [(b, h) for b in range(NB) for h in range(8)]
        bufs = {}

        def load_group(gi):
            b, h = groups[gi]
            qa, ka = QA[gi % 2], KA[gi % 2]
            if h == 0:
                s.dma("sp", Vb[b % 2][:], self.vt.t[b * S:(b + 1) * S, :].rearrange("(c p) n -> p c n", p=128),
                      r=[self.vt.R(blk) for blk in range(b * S // 128, (b + 1) * S // 128)], w=[Vb[b % 2].R()])
            s.dma("sp", qa[:], self.qa.t[h, :, b * S:(b + 1) * S],
                  r=[self.qa.R(("ones", h))] + [self.qa.R(("qk", h, b * S + j * TT)) for j in range(NQ)] +
                    [self.qa.R(("c", jj, b * S + j * TT)) for j in range(NQ) for jj in range(3)], w=[qa.R()])
            s.dma("sp", ka[:], self.ka.t[h, :, b * S:(b + 1) * S],
                  r=[self.ka.R(("ones", h))] + [self.ka.R(("qk", h, b * S + j * TT)) for j in range(NQ)] +
                    [self.ka.R(("c", jj, b * S + j * TT)) for j in range(NQ) for jj in range(3)], w=[ka.R()])

        work = []
        for gi, (b, h) in enumerate(groups):
            for j in range(NQ):
                nch = 4 * (j + 1)
                for i in range(nch):
                    work.append((gi, b, h, j, i, nch))
        state = {}

        def emit_qk(w):
            gi, b, h, j, i, nch = w
            if j == 0 and i == 0:
                if gi == 0:
                    load_group(0)
                if gi + 1 < len(groups):
                    load_group(gi + 1)
            qa, ka = QA[gi % 2], KA[gi % 2]
            sc = self.ps_rot(4, 8)
            state[w] = sc
            r_ = i - 4 * j
            diag = r_ >= 0
            s.op("pe", lambda e: e.matmul(sc[:, :], lhsT=ka[:, i * 128:(i + 1) * 128], rhs=qa[:, j * TT:(j + 1) * TT], start=True,
                                          stop=not diag), r=[ka.R(), qa.R()], w=[sc.R()])
            if diag:
                s.op("pe", lambda e: e.matmul(sc[:, :], lhsT=idb[:], rhs=fm[:, r_ * 512:(r_ + 1) * 512], start=False, stop=True),
                     r=[idb.R(), fm.R()], w=[sc.R()])

        cnt = {"pt": 0, "acc": None, "den": None, "ep": 0}

        def emit_rest(w):
            gi, b, h, j, i, nch = w
            sc = state.pop(w)
            if i == 0:
                cnt["acc"] = self.ps_rot(0, 2)
                cnt["den"] = self.ps_rot(2, 4)
            acc, den = cnt["acc"], cnt["den"]
            pt = PT[cnt["pt"] % 4]
            cnt["pt"] += 1
            vb = Vb[b % 2]
            s.op("act", lambda e: e.activation(out=pt[:], in_=sc[:, :], func=AF.Exp), r=[sc.R()], w=[pt.R()])
            s.op("pe", lambda e: e.matmul(acc[0:64, :], lhsT=vb[:, i, h * 64:(h + 1) * 64], rhs=pt[:], start=(i == 0),
                                          stop=(i == nch - 1)), r=[vb.R(), pt.R()], w=[acc.R()])
            s.op("pe", lambda e: e.matmul(den[0:64, :], lhsT=onb[:, 0:64], rhs=pt[:], start=(i == 0), stop=(i == nch - 1)),
                 r=[onb.R(), pt.R()], w=[den.R()])
            if i == nch - 1:
                rd = rden[cnt["ep"] % 2]
                y = yt[cnt["ep"] % 2]
                cnt["ep"] += 1
                s.op("dve", lambda e: e.reciprocal(out=rd[:], in_=den[0:64, :]), r=[den.R()], w=[rd.R()])
                s.op("dve", lambda e: e.tensor_tensor(out=y[:], in0=acc[0:64, :], in1=rd[:], op=ALU.mult), r=[acc.R(), rd.R()],
                     w=[y.R()])
                t0 = b * S + j * TT
                s.dma("sp", self.yf.t[h * 64:(h + 1) * 64, t0:t0 + TT], y[:], r=[y.R()], w=[self.yf.R((h, t0))])

        for k in range(min(LOOK, len(work))):
            emit_qk(work[k])
        for k, w in enumerate(work):
            if k + LOOK < len(work):
                emit_qk(work[k + LOOK])
            emit_rest(w)

    def stage_C(self, l):
        import os
        cut = int(os.environ.get("CCUT", "9"))
        use_b = os.environ.get("RWDT", "bf16") == "bf16"
        RD = BF16 if use_b else mybir.dt.float32r
        s = self.s
        S, NB, T = self.S, self.NB, self.T
        W = self.W
        self.stage_begin()
        c = self.c
        idf, blkf, scanm, mS, mA, id8 = c["ident_f"], c["blk2_f"], c["scanmask"], c["mask_S"], c["mask_A"], c["ident8"]

        def V(E, fn, r, w):
            s.op(E, fn, r=[x.R() for x in r], w=[x.R() for x in w])

        def cp(E, out_ap, in_ap, r, w):
            if E == "act":
                V("act", lambda e: e.activation(out=out_ap, in_=in_ap, func=AF.Copy), r, w)
            else:
                V(E, lambda e: e.tensor_copy(out=out_ap, in_=in_ap), r, w)

        Wd = self.sb("Wd", [64, RW], BF16)
        Wa = self.sb("Wa", [64, RW], BF16)
        Wg = self.sb("Wg", [128, RW], BF16)
        s.dma("pool", Wd[:], W["w_decay_up"].t[l], r=[W["w_decay_up"].R()], w=[Wd.R()])
        s.dma("pool", Wa[:], W["w_aaa_up"].t[l], r=[W["w_aaa_up"].R()], w=[Wa.R()])
        s.dma("pool", Wg[:], W["w_gate_up"].t[l], r=[W["w_gate_up"].R()], w=[Wg.R()])
        omka = self.sb("omka", [128, 4])
        o_ka = l * NCOL + COLS["k_a"][0]
        V("dve", lambda e: e.tensor_scalar(out=omka[:], in0=self.colpack[:, o_ka:o_ka + 4], scalar1=-1.0, scalar2=1.0,
                                            op0=ALU.mult, op1=ALU.add), [self.colpack], [omka])
        epsg = self.sb("epsg", [64, 1])
        V("pool", lambda e: e.memset(epsg[:], GN_EPS), [], [epsg])
        AR = [self.sb(f"AR{i}", [128, 8, 2, CH], RD) for i in range(4)]
        BT = [self.sb(f"BT{i}", [128, TT], RD) for i in range(4)]
        KT = [self.sb(f"KT{i}", [128, TT], RD) for i in range(4)]
        ARo = [self.sb(f"ARo{i}", [64, 8, 2, CH], RD) for i in range(4)]
        BTo = [self.sb(f"BTo{i}", [64, TT], RD) for i in range(4)]
        KTo = [self.sb(f"KTo{i}", [64, TT], RD) for i in range(4)]
        VR = [self.sb(f"VR{i}", [128, TT]) for i in range(4)]
        G = [[self.sb(f"G{i}{b}", [128, TT], BF16) for i in range(4)] for b in range(2)]
        BG = [[self.sb(f"BG{i}{b}", [128, TT], BF16) for i in range(4)] for b in range(2)]
        PCp = self.sb("PCp", [128, 8]); PCo = self.sb("PCo", [64, 8])
        PCall = self.sb("PCall", [64, 8, 8])
        H = self.sb("H", [64, 512], RD)
        Hf = self.sb("Hf", [64, 512])
        Ht = self.sb("Ht", [64, 512])
        YN = self.sb("YN", [64, 8, 512])
        dwt = self.sb("dwt", [64, TT]); dat = self.sb("dat", [64, TT]); dgt = self.sb("dgt", [128, TT])
        tdw = self.sb("tdw", [64, TT], BF16); dab = self.sb("dab", [64, TT], BF16); sdg = self.sb("sdg", [128, TT], BF16)
        rT = self.sb("rT", [128, TT]); krT = self.sb("krT", [128, TT])
        sig = self.sb("sig", [128, TT]); aa = self.sb("aa", [128, TT]); kk = self.sb("kk", [128, TT])
        prod = self.sb("prod", [128, TT]); rn = self.sb("rn", [128, TT]); gf = self.sb("gf", [128, TT])
        Lc = self.sb("Lc", [128, TT])
        eL = self.sb("eL", [128, TT]); eLm = self.sb("eLm", [128, TT]); enL = self.sb("enL", [128, TT])
        TOK = [[self.sb(f"tok{i}{b}", [64, 512], RD) for i in range(3)] for b in range(2)]
        SMb = [[self.sb(f"SM{i}{b}", [64, 512], RD) for i in range(4)] for b in range(2)]
        Tfin = [self.sb(f"Tfin{b}", [64, 512], RD) for b in range(2)]
        Xa = [self.sb(f"Xa{i}", [64, 512], RD) for i in range(2)]
        XTa = [self.sb(f"XTa{i}", [64, 512], RD) for i in range(2)]
        TTa = [self.sb(f"TTa{i}", [64, 512], RD) for i in range(2)]
        W0s = self.sb("W0s", [64, 512], RD); Us = self.sb("Us", [64, 512], RD)
        YQ = self.sb("YQ", [64, 8, 512])
        st = {k: self.sb("st_" + k, [64, 64]) for k in ["sum", "sq", "m", "m2", "var", "rstd"]}
        po1 = self.sb("po1", [128, TT]); pob = self.sb("pob", [128, TT], BF16)

        def rr(ap):
            return ap

        def MM(e, out, lhsT, rhs, start, stop):
            return e.matmul(out, lhsT=rr(lhsT), rhs=rr(rhs), start=start, stop=stop)

        def colv(name, hp):
            return self.col(l, name, hp)

        def ar(h):
            return AR[h // 2] if h % 2 == 0 else ARo[h // 2]

        def bt(h):
            return BT[h // 2] if h % 2 == 0 else BTo[h // 2]

        def kt(h):
            return KT[h // 2] if h % 2 == 0 else KTo[h // 2]

        def prep(tt):
            t0 = tt * TT
            zr = self.zr
            s.dma("sp", dwt[:], zr.t[1536:1600, t0:t0 + TT], r=[zr.R((12, t0))], w=[dwt.R()])
            s.dma("sp", dat[:], zr.t[1600:1664, t0:t0 + TT], r=[zr.R((12, t0))], w=[dat.R()])
            s.dma("sp", dgt[:], zr.t[1664:1792, t0:t0 + TT], r=[zr.R((13, t0))], w=[dgt.R()])
            V("act", lambda e: e.activation(out=tdw[:], in_=dwt[:], func=AF.Tanh), [dwt], [tdw])
            V("pool", lambda e: e.tensor_copy(out=dab[:], in_=dat[:]), [dat], [dab])
            V("act", lambda e: e.activation(out=sdg[:], in_=dgt[:], func=AF.Sigmoid), [dgt], [sdg])
            for hp in range(4):
                hs = slice(hp * 128, (hp + 1) * 128)
                s.dma("sp", rT[:], zr.t[hp * 128:(hp + 1) * 128, t0:t0 + TT], r=[zr.R((hp, t0))], w=[rT.R()])
                s.dma("sp", krT[:], zr.t[512 + hp * 128:512 + (hp + 1) * 128, t0:t0 + TT], r=[zr.R((4 + hp, t0))], w=[krT.R()])
                s.dma("sp", VR[hp][:], zr.t[1024 + hp * 128:1024 + (hp + 1) * 128, t0:t0 + TT], r=[zr.R((8 + hp, t0))],
                      w=[VR[hp].R()])
                p1 = self.ps()
                V("pe", lambda e: e.matmul(p1[:, :], lhsT=Wd[:, hs], rhs=tdw[:], start=True, stop=True), [Wd, tdw], [p1])
                V("act", lambda e: e.activation(out=sig[:], in_=p1[:, :], func=AF.Sigmoid, bias=colv("w0", hp), scale=1.0),
                  [p1, self.colpack], [sig])
                p2 = self.ps()
                V("pe", lambda e: e.matmul(p2[:, :], lhsT=Wa[:, hs], rhs=dab[:], start=True, stop=True), [Wa, dab], [p2])
                V("act", lambda e: e.activation(out=aa[:], in_=p2[:, :], func=AF.Sigmoid, bias=colv("a0", hp), scale=1.0),
                  [p2, self.colpack], [aa])
                p3 = self.ps()
                V("pe", lambda e: e.matmul(p3[:, :], lhsT=Wg[:, hs], rhs=sdg[:], start=True, stop=True), [Wg, sdg], [p3])
                cp("act", gf[:], p3[:, :], [p3], [gf])
                cp("pool", G[tt % 2][hp][:], gf[:], [gf], [G[tt % 2][hp]])
                V("act", lambda e: e.activation(out=kk[:], in_=krT[:], func=AF.Copy, scale=colv("k_k", hp)),
                  [krT, self.colpack], [kk])
                V("pool", lambda e: e.tensor_tensor(out=prod[:], in0=kk[:], in1=kk[:], op=ALU.mult), [kk], [prod])
                p4 = self.ps()
                V("pe", lambda e: e.matmul(p4[:, :], lhsT=blkf[:], rhs=prod[:], start=True, stop=True), [blkf, prod], [p4])
                self.rsqrt_ps(rn, p4, 1.0, 1e-24, 1.0)
                V("dve", lambda e: e.tensor_tensor(out=kk[:], in0=kk[:], in1=rn[:], op=ALU.mult), [kk, rn], [kk])
                V("dve", lambda e: e.tensor_scalar(out=rn[:], in0=aa[:], scalar1=colv("k_a", hp), scalar2=omka[:, hp:hp + 1],
                                                    op0=ALU.mult, op1=ALU.add), [aa, self.colpack, omka], [rn])
                V("dve", lambda e: e.tensor_tensor(out=krT[:], in0=krT[:], in1=rn[:], op=ALU.mult), [krT, rn], [krT])
                V("pool", lambda e: e.tensor_tensor(out=aa[:], in0=kk[:], in1=aa[:], op=ALU.mult), [kk, aa], [aa])
                V("act", lambda e: e.activation(out=sig[:], in_=sig[:], func=AF.Copy, scale=-float(np.exp(-0.5))),
                  [sig], [sig])
                V("dve", lambda e: e.tensor_tensor_scan(out=Lc[:], data0=scanm[:], data1=sig[:], initial=0.0, op0=ALU.mult,
                                                         op1=ALU.add), [scanm, sig], [Lc])
                V("pool", lambda e: e.tensor_tensor(out=sig[:], in0=Lc[:], in1=sig[:], op=ALU.subtract), [Lc, sig], [sig])
                V("act", lambda e: e.activation(out=eL[:], in_=Lc[:], func=AF.Exp), [Lc], [eL])
                V("act", lambda e: e.activation(out=eLm[:], in_=sig[:], func=AF.Exp), [sig], [eLm])
                V("act", lambda e: e.activation(out=enL[:], in_=Lc[:], func=AF.Exp, scale=-1.0), [Lc], [enL])
                arv = AR[hp]
                V("dve", lambda e: e.scalar_tensor_tensor(out=arv[:, :, 0, :], in0=kk[:].rearrange("p (c t) -> p c t", t=CH),
                                                           scalar=-1.0, in1=eLm[:].rearrange("p (c t) -> p c t", t=CH),
                                                           op0=ALU.mult, op1=ALU.mult), [kk, eLm], [arv])
                V("dve", lambda e: e.tensor_tensor(out=arv[:, :, 1, :], in0=rT[:].rearrange("p (c t) -> p c t", t=CH),
                                                    in1=eL[:].rearrange("p (c t) -> p c t", t=CH), op=ALU.mult), [rT, eL], [arv])
                V("pool", lambda e: e.tensor_tensor(out=BT[hp][:], in0=aa[:], in1=enL[:], op=ALU.mult), [aa, enL], [BT[hp]])
                V("dve", lambda e: e.tensor_tensor(out=KT[hp][:], in0=krT[:], in1=enL[:], op=ALU.mult), [krT, enL], [KT[hp]])
                V("pool", lambda e: e.tensor_copy(out=PCp[:], in_=eL[:, CH - 1::CH]), [eL], [PCp])
                s.dma("sp", ARo[hp][:], AR[hp][64:128, :, :, :], r=[AR[hp].R()], w=[ARo[hp].R()])
                s.dma("sp", BTo[hp][:], BT[hp][64:128, :], r=[BT[hp].R()], w=[BTo[hp].R()])
                s.dma("sp", KTo[hp][:], KT[hp][64:128, :], r=[KT[hp].R()], w=[KTo[hp].R()])
                s.dma("sp", PCo[:], PCp[64:128, :], r=[PCp.R()], w=[PCo.R()])
                V("pool", lambda e: e.tensor_copy(out=PCall[:, :, 2 * hp], in_=PCp[0:64, :]), [PCp], [PCall])
                V("pool", lambda e: e.tensor_copy(out=PCall[:, :, 2 * hp + 1], in_=PCo[:]), [PCo], [PCall])
                V("dve", lambda e: e.scalar_tensor_tensor(out=prod[:], in0=rT[:], scalar=colv("r_k", hp), in1=krT[:],
                                                           op0=ALU.mult, op1=ALU.mult), [rT, krT, self.colpack], [prod])
                p5 = self.ps()
                V("pe", lambda e: e.matmul(p5[:, :], lhsT=blkf[:], rhs=prod[:], start=True, stop=True), [blkf, prod], [p5])
                V("dve", lambda e: e.tensor_tensor(out=rn[:], in0=p5[:, :], in1=VR[hp][:], op=ALU.mult), [p5, VR[hp]], [rn])
                V("pool", lambda e: e.tensor_tensor(out=BG[tt % 2][hp][:], in0=rn[:], in1=gf[:], op=ALU.mult), [rn, gf], [BG[tt % 2][hp]])
                yield

        def gnpost(tt):
            t0 = tt * TT
            if cut >= 5:
                yr3 = YN[:].rearrange("p c (h v) -> p (c h) v", h=8)
                yq3 = YQ[:].rearrange("p c (h v) -> p (c h) v", h=8)
                V("act", lambda e: e.activation(out=YQ[:], in_=YN[:], func=AF.Square), [YN], [YQ])
                V("dve", lambda e: e.tensor_reduce(out=st["sum"][:], in_=yr3, axis=AX.X, op=ALU.add), [YN], [st["sum"]])
                V("dve", lambda e: e.tensor_reduce(out=st["sq"][:], in_=yq3, axis=AX.X, op=ALU.add), [YQ], [st["sq"]])
                V("act", lambda e: e.activation(out=st["m"][:], in_=st["sum"][:], func=AF.Copy, scale=1.0 / 64),
                  [st["sum"]], [st["m"]])
                V("pool", lambda e: e.tensor_tensor(out=st["m2"][:], in0=st["m"][:], in1=st["m"][:], op=ALU.mult), [st["m"]],
                  [st["m2"]])
                V("dve", lambda e: e.scalar_tensor_tensor(out=st["var"][:], in0=st["sq"][:], scalar=1.0 / 64, in1=st["m2"][:],
                                                           op0=ALU.mult, op1=ALU.subtract), [st["sq"], st["m2"]], [st["var"]])
                V("act", lambda e: e.activation(out=st["rstd"][:], in_=st["var"][:], func=AF.Ln, bias=epsg[:], scale=1.0),
                  [st["var"], epsg], [st["rstd"]])
                V("act", lambda e: e.activation(out=st["rstd"][:], in_=st["rstd"][:], func=AF.Exp, scale=-0.5), [st["rstd"]],
                  [st["rstd"]])
                yield
                V("dve", lambda e: e.tensor_tensor(out=yq3, in0=yr3, in1=st["m"][:].unsqueeze(2).broadcast_to([64, 64, 64]),
                                                    op=ALU.subtract), [YN, st["m"]], [YQ])
                V("dve", lambda e: e.tensor_tensor(out=yr3, in0=yq3, in1=st["rstd"][:].unsqueeze(2).broadcast_to([64, 64, 64]),
                                                    op=ALU.mult), [YQ, st["rstd"]], [YN])
            yield
            for hp in range(4 if cut >= 6 else 0):
                pO = self.ps()
                for cc in range(8):
                    V("pe", lambda e, cc=cc: e.transpose(out=pO[:, cc * CH:(cc + 1) * CH], in_=YN[:, cc, hp * 128:(hp + 1) * 128],
                                                         identity=idf[0:64, 0:64]), [YN, idf], [pO])
                V("dve", lambda e: e.tensor_scalar(out=po1[:], in0=pO[:, :], scalar1=colv("gn_g", hp), scalar2=colv("gn_b", hp),
                                                    op0=ALU.mult, op1=ALU.add), [pO, self.colpack], [po1])
                V("pool", lambda e: e.tensor_tensor(out=po1[:], in0=po1[:], in1=G[tt % 2][hp][:], op=ALU.mult), [po1, G[tt % 2][hp]], [po1])
                V("dve", lambda e: e.tensor_tensor(out=pob[:], in0=po1[:], in1=BG[tt % 2][hp][:], op=ALU.add), [po1, BG[tt % 2][hp]], [pob])
                s.dma("sp", self.yr.t[hp * 128:(hp + 1) * 128, t0:t0 + TT], pob[:], r=[pob.R()], w=[self.yr.R((hp, t0))])
                yield

        for _ in prep(0):
            pass
        for tt in range(self.NT):
            t0 = tt * TT
            zr = self.zr
            def indep(cc):
                cs = slice(cc * CH, (cc + 1) * CH)
                b = cc % 2
                Btok, Ktok, Vtok = TOK[b]
                SM = SMb[b]
                for srcs, dst, eng in [(BT, Btok, "act"), (KT, Ktok, "dve"), (VR, Vtok, "act")]:
                    pt_ = self.ps()
                    if use_b and srcs is not VR:
                        pv_ = pt_.t.bitcast(BF16)
                        idb_ = c["ident_f_b"]
                        for hp in range(4):
                            V("pe", lambda e, hp=hp: e.transpose(out=pv_[0:64, hp * 128:(hp + 1) * 128], in_=srcs[hp][:, cs],
                                                                 identity=idb_[:]), [srcs[hp], idb_], [pt_])
                        cp(eng, dst[:], pv_[0:64, 0:512], [pt_], [dst])
                    else:
                        for hp in range(4):
                            V("pe", lambda e, hp=hp: e.transpose(out=pt_[0:64, hp * 128:(hp + 1) * 128],
                                                                 in_=(srcs[hp][:, cs] if srcs is VR else srcs[hp][:, cs].bitcast(F32)),
                                                                 identity=idf[:]), [srcs[hp], idf], [pt_])
                        cp(eng, dst[:], pt_[0:64, :], [pt_], [dst])
                    yield
                for hp in range(4):
                    pS = self.ps()
                    for par in range(2):
                        h = 2 * hp + par
                        rhs = ar(h)[0:64, cc, :, :].rearrange("p a t -> p (a t)")
                        V("pe", lambda e, par=par, h=h, rhs=rhs: MM(e, pS[0:64, par * 256:par * 256 + 128], lhsT=bt(h)[0:64, cs],
                                                                   rhs=rhs, start=True, stop=True), [bt(h), ar(h)], [pS])
                        V("pe", lambda e, par=par, h=h, rhs=rhs: MM(e, pS[0:64, par * 256 + 128:par * 256 + 256],
                                                                   lhsT=kt(h)[0:64, cs], rhs=rhs, start=True, stop=True),
                          [kt(h), ar(h)], [pS])
                    V("dve", lambda e, hp=hp, pS=pS: e.tensor_tensor(out=SM[hp][:], in0=pS[0:64, :], in1=mS[:], op=ALU.mult),
                      [pS, mS], [SM[hp]])
                    if hp % 2 == 1:
                        yield
                pA = self.ps()
                for h in range(8):
                    V("pe", lambda e, h=h: MM(e, pA[0:64, h * 64:(h + 1) * 64], lhsT=ar(h)[0:64, cc, 0, :],
                                              rhs=bt(h)[0:64, cs], start=True, stop=True), [ar(h), bt(h)], [pA])
                X, XT, Tt = Xa[0], XTa[0], TTa[0]
                V("dve", lambda e: e.tensor_tensor(out=X[:], in0=pA[0:64, :], in1=mA[:], op=ALU.mult), [pA, mA], [X])
                for hp in range(4):
                    V("pool", lambda e, hp=hp: e.tensor_copy(
                        out=XT[:, hp * 128:(hp + 1) * 128].rearrange("p (a t) -> p a t", a=2),
                        in_=SM[hp][:, :].rearrange("p (a t) -> p a t", a=2)[:, :, 0:64]), [SM[hp]], [XT])
                V("pool", lambda e: e.tensor_tensor(out=Tt[:], in0=XT[:], in1=id8[:], op=ALU.add), [XT, id8], [Tt])
                yield
                for k in range(1, 6):
                    Xn, XTn, Tn = Xa[k % 2], XTa[k % 2], (TTa[k % 2] if k < 5 else Tfin[b])
                    pX = self.ps()
                    for h in range(8):
                        hsl = slice(h * 64, (h + 1) * 64)
                        V("pe", lambda e, hsl=hsl: MM(e, pX[0:64, hsl], lhsT=XT[:, hsl], rhs=X[:, hsl], start=True, stop=True),
                          [XT, X], [pX])
                    if k < 5:
                        pXT = self.ps()
                        for h in range(8):
                            hsl = slice(h * 64, (h + 1) * 64)
                            V("pe", lambda e, hsl=hsl: MM(e, pXT[0:64, hsl], lhsT=X[:, hsl], rhs=XT[:, hsl], start=True,
                                                           stop=True), [XT, X], [pXT])
                    cp("act", Xn[:], pX[0:64, :], [pX], [Xn])
                    if k < 5:
                        cp("dve", XTn[:], pXT[0:64, :], [pXT], [XTn])
                    yield
                    pT = self.ps()
                    for h in range(8):
                        hsl = slice(h * 64, (h + 1) * 64)
                        V("pe", lambda e, hsl=hsl: MM(e, pT[0:64, hsl], lhsT=Xn[:, hsl], rhs=Tt[:, hsl], start=True, stop=True),
                          [Xn, Tt], [pT])
                    V("dve", lambda e: e.tensor_tensor(out=Tn[:], in0=pT[0:64, :], in1=Tt[:], op=ALU.add), [pT, Tt], [Tn])
                    X, XT, Tt = Xn, XTn, Tn
                    yield

            def dep(cc):
                cs = slice(cc * CH, (cc + 1) * CH)
                b = cc % 2
                Btok, Ktok, Vtok = TOK[b]
                SM = SMb[b]
                Tt = Tfin[b]
                if tt % self.TPS == 0 and cc == 0:
                    V("pool", lambda e: e.memset(Hf[:], 0.0), [], [Hf])
                    V("pool", lambda e: e.tensor_copy(out=H[:], in_=Hf[:]), [Hf], [H])

                def hd(h):
                    return h // 2, (h % 2) * 256, slice(h * 64, (h + 1) * 64)
                pW = self.ps()
                for h in range(8):
                    hp, b0, hsl = hd(h)
                    V("pe", lambda e, hp=hp, b0=b0, hsl=hsl: MM(e, pW[0:64, hsl], lhsT=SM[hp][:, b0 + 128:b0 + 192],
                                                               rhs=Vtok[:, hsl], start=True, stop=False), [SM[hp], Vtok], [pW])
                    V("pe", lambda e, h=h, hsl=hsl: MM(e, pW[0:64, hsl], lhsT=ar(h)[0:64, cc, 0, :], rhs=H[:, hsl], start=False,
                                                      stop=True), [ar(h), H], [pW])
                cp("act", W0s[:], pW[0:64, :], [pW], [W0s])
                yield
                pU = self.ps()
                for h in range(8):
                    hp, b0, hsl = hd(h)
                    V("pe", lambda e, hsl=hsl: MM(e, pU[0:64, hsl], lhsT=Tt[:, hsl], rhs=W0s[:, hsl], start=True, stop=True),
                      [Tt, W0s], [pU])
                cp("dve", Us[:], pU[0:64, :], [pU], [Us])
                yield
                pY = self.ps()
                for h in range(8):
                    hp, b0, hsl = hd(h)
                    V("pe", lambda e, hp=hp, b0=b0, hsl=hsl: MM(e, pY[0:64, hsl], lhsT=SM[hp][:, b0 + 192:b0 + 256],
                                                               rhs=Vtok[:, hsl], start=True, stop=False), [SM[hp], Vtok], [pY])
                    V("pe", lambda e, hp=hp, b0=b0, hsl=hsl: MM(e, pY[0:64, hsl], lhsT=SM[hp][:, b0 + 64:b0 + 128],
                                                               rhs=Us[:, hsl], start=False, stop=False), [SM[hp], Us], [pY])
                    V("pe", lambda e, h=h, hsl=hsl: MM(e, pY[0:64, hsl], lhsT=ar(h)[0:64, cc, 1, :], rhs=H[:, hsl], start=False,
                                                      stop=True), [ar(h), H], [pY])
                cp("act", YN[:, cc, :], pY[0:64, :], [pY], [YN])
                yield
                pH = self.ps()
                for h in range(8):
                    hp, b0, hsl = hd(h)
                    V("pe", lambda e, hsl=hsl: MM(e, pH[0:64, hsl], lhsT=Btok[:, hsl], rhs=Us[:, hsl], start=True, stop=False),
                      [Btok, Us], [pH])
                    V("pe", lambda e, hsl=hsl: MM(e, pH[0:64, hsl], lhsT=Ktok[:, hsl], rhs=Vtok[:, hsl], start=False, stop=True),
                      [Ktok, Vtok], [pH])
                V("dve", lambda e: e.tensor_tensor(out=Ht[:], in0=pH[0:64, :], in1=Hf[:], op=ALU.add), [pH, Hf], [Ht])
                yield
                V("dve", lambda e: e.tensor_tensor(out=Hf[:].rearrange("p (h v) -> p h v", h=8),
                                                    in0=Ht[:].rearrange("p (h v) -> p h v", h=8),
                                                    in1=PCall[:, cc, :].unsqueeze(2).broadcast_to([64, 8, 64]), op=ALU.mult),
                  [Ht, PCall], [Hf])
                V("act", lambda e: e.activation(out=H[:], in_=Hf[:], func=AF.Copy), [Hf], [H])
                yield

            def drive(gens):
                gens = [g for g in gens if g is not None]
                while gens:
                    for g in list(gens):
                        try:
                            next(g)
                        except StopIteration:
                            gens.remove(g)

            if cut >= 4:
                drive([indep(0)])
                for cc in range(8):
                    drive([dep(cc), indep(cc + 1) if cc + 1 < 8 else None])
            drive([gnpost(tt), prep(tt + 1) if tt + 1 < self.NT else None])

    def load_wb(self, dst, dstv, name, l, bounds, order=None):
        Wt = self.W[name]
        srcv = Wt.t[l].rearrange("(k p) n -> p k n", p=128)
        dst._bounds = list(bounds)
        for i in (order if order is not None else range(len(bounds) - 1)):
            c0, c1 = bounds[i], bounds[i + 1]
            self.s.dma("pool", dstv[:, :, c0:c1], srcv[:, :, c0:c1], r=[Wt.R()], w=[dst.R(("blk", i))])

    @staticmethod
    def wres(dst, col):
        import bisect
        return dst.R(("blk", bisect.bisect_right(dst._bounds, col) - 1))

    def load_w(self, dst, name, l, nk):
        Wt = self.W[name]
        srcv = Wt.t[l].rearrange("(k p) n -> p k n", p=128)
        for kc in range(nk):
            self.s.dma("pool", dst[:, kc, :], srcv[:, kc, :], r=[Wt.R()], w=[dst.R()])

    def stage_D(self, l, src):
        s = self.s
        self.stage_begin()
        wof = self.sb("wof", [128, 4, D], BF16)
        wor = self.sb("wor", [128, 4, D], BF16)
        wout = self.sb("wout", [128, KC, D], BF16)
        for blk in range(2):
            self.load_wb(wof, wof.t, "w_o_fox", l, [0, 512, 1024], order=[blk])
            self.load_wb(wor, wor.t, "w_o_rwkv", l, [0, 512, 1024], order=[blk])
        self.load_wb(wout, wout.t, "w_out", l, [0, 512, 1024])
        yfT = [self.sb(f"yfT{i}", [128, 4, TT], BF16) for i in range(2)]
        yrT = [self.sb(f"yrT{i}", [128, 4, TT], BF16) for i in range(2)]
        gtT = [self.sb(f"gtT{i}", [128, 16, TT], BF16) for i in range(2)]
        xt_ = [self.sb(f"xD{i}", [128, KC, TT]) for i in range(2)]
        mg = self.sb("mg", [128, KC, TT], BF16)
        srcv = src.t.rearrange("(k p) t -> p k t", p=128)
        dstv = self.xs.t.rearrange("(k p) t -> p k t", p=128)
        def loads(tt):
            t0 = tt * TT
            yf, yr, gt, xt = yfT[tt % 2], yrT[tt % 2], gtT[tt % 2], xt_[tt % 2]
            s.dma("sp", yf[:], self.yf.t[:, t0:t0 + TT].rearrange("(k p) t -> p k t", p=128),
                  r=[self.yf.R((h, t0)) for h in range(8)], w=[yf.R()])
            s.dma("sp", yr[:], self.yr.t[:, t0:t0 + TT].rearrange("(k p) t -> p k t", p=128),
                  r=[self.yr.R((hp, t0)) for hp in range(4)], w=[yr.R()])
            s.dma("sp", gt[:], self.gt.t[:, t0:t0 + TT].rearrange("(k p) t -> p k t", p=128),
                  r=[self.gt.R((i, t0)) for i in range(16)], w=[gt.R()])
            for kc in range(KC):
                s.dma("sp", xt[:, kc, :], srcv[:, kc, t0:t0 + TT], r=[src.R(("tile", tt))], w=[xt.R()])

        loads(0)
        for tt in range(self.NT):
            t0 = tt * TT
            yf, yr, gt, xt = yfT[tt % 2], yrT[tt % 2], gtT[tt % 2], xt_[tt % 2]
            if tt + 1 < self.NT:
                loads(tt + 1)
            for n in range(KC):
                ns = slice(n * 128, (n + 1) * 128)
                pa = self.ps()
                for kc in range(4):
                    s.op("pe", lambda e, kc=kc: e.matmul(pa[:, :], lhsT=wof[:, kc, ns], rhs=yf[:, kc, :], start=(kc == 0),
                                                         stop=(kc == 3)), r=[self.wres(wof, n * 128), yf.R()], w=[pa.R()])
                pb = self.ps()
                for kc in range(4):
                    s.op("pe", lambda e, kc=kc: e.matmul(pb[:, :], lhsT=wor[:, kc, ns], rhs=yr[:, kc, :], start=(kc == 0),
                                                         stop=(kc == 3)), r=[self.wres(wor, n * 128), yr.R()], w=[pb.R()])
                m1 = self.tmpA("m1", [128, TT])
                s.op("dve", lambda e, m1=m1: e.tensor_tensor(out=m1[:], in0=pa[:, :], in1=gt[:, n, :], op=ALU.mult),
                     r=[pa.R(), gt.R()], w=[m1.R()])
                m2 = self.tmpA("m2", [128, TT])
                s.op("dve", lambda e, m2=m2: e.tensor_tensor(out=m2[:], in0=pb[:, :], in1=gt[:, 8 + n, :], op=ALU.mult),
                     r=[pb.R(), gt.R()], w=[m2.R()])
                s.op("pool", lambda e, m1=m1, m2=m2: e.tensor_tensor(out=mg[:, n, :], in0=m1[:], in1=m2[:], op=ALU.add),
                     r=[m1.R(), m2.R()], w=[mg.R(n)])
            for n in range(KC):
                ns = slice(n * 128, (n + 1) * 128)
                po = self.ps()
                for kc in range(KC):
                    s.op("pe", lambda e, kc=kc: e.matmul(po[:, :], lhsT=wout[:, kc, ns], rhs=mg[:, kc, :], start=(kc == 0),
                                                         stop=(kc == KC - 1)), r=[self.wres(wout, n * 128), mg.R(kc)], w=[po.R()])
                xo = self.tmpA("xo", [128, TT], nbuf=3)
                s.op("dve", lambda e, xo=xo: e.tensor_tensor(out=xo[:], in0=po[:, :], in1=xt[:, n, :], op=ALU.add),
                     r=[po.R(), xt.R()], w=[xo.R()])
                s.dma("sp", dstv[:, n, t0:t0 + TT], xo[:], r=[xo.R()], w=[self.xs.R(("tile", tt))])

    def stage_E1(self, l):
        s = self.s
        self.stage_begin()
        wup = self.sb("wup", [128, KC, 2 * DFF], BF16)
        jb = [0, 4, 10, 16, NJ]
        self.load_wb(wup, wup.t, "w_up", l, [x * 128 for x in jb] + [DFF + x * 128 for x in jb[1:]],
                     order=[0, 4, 1, 5, 2, 6, 3, 7])
        xt = self.sb("xE", [128, KC, TT])
        h2 = [self.sb(f"h2{i}", [128, KC, TT], BF16) for i in range(2)]
        ub = [self.sb(f"ub{i}", [128, TT + 2]) for i in range(2)]
        srcv = self.xs.t.rearrange("(k p) t -> p k t", p=128)
        K0 = float(2.0 * np.sqrt(2.0 / np.pi))
        def load_tile(tt):
            for kc in range(KC):
                s.dma("sp", xt[:, kc, :], srcv[:, kc, tt * TT:(tt + 1) * TT], r=[self.xs.R(("tile", tt))], w=[xt.R()])

        def prep_tile(tt):
            load_tile(tt)
            self.rmsnorm_tile(l, "g_ffn", xt, h2[tt % 2], "E")

        prep_tile(0)
        rng_ = [None]
        for tt in range(self.NT):
            t0 = tt * TT
            seq_start = (tt % self.TPS == 0)
            ht = h2[tt % 2]
            hR = [ht.R(kc) for kc in range(KC)]
            for j in range(NJ):
                if tt + 1 < self.NT:
                    if j == 0:
                        load_tile(tt + 1)
                        rng_[0] = self.rn_gen(l, "g_ffn", xt, h2[(tt + 1) % 2])
                    elif j in (3, 5, 6):
                        next(rng_[0])
                cres = []
                for half in range(2):
                    idx = half * NJ + j
                    c0 = idx * 128
                    pu = self.ps()
                    for kc in range(KC):
                        s.op("pe", lambda e, kc=kc, pu=pu, c0=c0: e.matmul(pu[:, :], lhsT=wup[:, kc, c0:c0 + 128], rhs=ht[:, kc, :],
                                                                           start=(kc == 0), stop=(kc == KC - 1)),
                             r=[hR[kc], self.wres(wup, c0)], w=[pu.R()])
                    u = ub[half]
                    s.op("act", lambda e, u=u, pu=pu: e.activation(out=u[:, 2:TT + 2], in_=pu[:, :], func=AF.Copy), r=[pu.R()],
                         w=[u.R()])
                    if seq_start:
                        s.op("pool", lambda e, u=u: e.memset(u[:, 0:2], 0.0), w=[u.R()])
                    else:
                        s.op("pool", lambda e, u=u, idx=idx: e.tensor_copy(out=u[:, 0:2], in_=self.ucarry[:, idx, :]),
                             r=[self.ucarry.R(idx)], w=[u.R()])
                    s.op("pool", lambda e, u=u, idx=idx: e.tensor_copy(out=self.ucarry[:, idx, :], in_=u[:, TT:TT + 2]),
                         r=[u.R()], w=[self.ucarry.R(idx)])
                    c1 = self.tmpA(f"c1{half}", [128, TT], nbuf=1)
                    s.op("act", lambda e, pu=pu, c1=c1, idx=idx: e.activation(out=c1[:], in_=pu[:, :], func=AF.Identity,
                                                                              scale=self.col(l, "cw2", idx),
                                                                              bias=self.col(l, "cb", idx)),
                         r=[pu.R(), self.colpack.R()], w=[c1.R()])
                    c2 = self.tmpA(f"c2{half}", [128, TT], nbuf=1)
                    s.op("dve", lambda e, u=u, c1=c1, c2=c2, idx=idx: e.scalar_tensor_tensor(out=c2[:], in0=u[:, 1:TT + 1],
                                                                                           scalar=self.col(l, "cw1", idx), in1=c1[:],
                                                                                           op0=ALU.mult, op1=ALU.add),
                         r=[u.R(), c1.R(), self.colpack.R()], w=[c2.R()])
                    c3 = self.tmpA(f"c3{half}", [128, TT], nbuf=2)
                    s.op("dve", lambda e, u=u, c2=c2, c3=c3, idx=idx: e.scalar_tensor_tensor(out=c3[:], in0=u[:, 0:TT],
                                                                                           scalar=self.col(l, "cw0", idx), in1=c2[:],
                                                                                           op0=ALU.mult, op1=ALU.add),
                         r=[u.R(), c2.R(), self.colpack.R()], w=[c3.R()])
                    cres.append(c3)
                u1, u2 = cres
                sg = self.tmpA("gsg", [128, TT], nbuf=2)
                s.op("act", lambda e, sg=sg, u1=u1: e.activation(out=sg[:], in_=u1[:], func=AF.Gelu_apprx_tanh), r=[u1.R()],
                     w=[sg.R()])
                ao = self.tmpA("gao", [128, TT], BF16, nbuf=3)
                s.op("pool", lambda e, sg=sg, u2=u2, ao=ao: e.tensor_tensor(out=ao[:], in0=sg[:], in1=u2[:], op=ALU.mult),
                     r=[sg.R(), u2.R()], w=[ao.R()])
                s.dma("sp", self.actT.t[j * 128:(j + 1) * 128, t0:t0 + TT], ao[:], r=[ao.R()], w=[self.actT.R((j, t0))])

    def stage_E2(self, l, last):
        s = self.s
        self.stage_begin()
        wdn = self.sb("wdn", [128, NJ, D], BF16)
        wpg = self.sb("wpg", [128, KC, D], BF16)
        wpu = self.sb("wpu", [128, 2, D], BF16)
        self.load_wb(wdn, wdn.t, "w_down", l, [0, 256, 512, 1024])
        self.load_wb(wpg, wpg.t, "w_ple_gate", l, [0, 512, 1024])
        self.load_wb(wpu, wpu.t, "w_ple_up", l, [0, 1024])
        at_ = [self.sb(f"at{i}", [128, NJ, TT], BF16) for i in range(2)]
        pt_ = [self.sb(f"pp{i}", [128, 2, TT], BF16) for i in range(2)]
        x2_ = [self.sb(f"x2{i}", [128, KC, TT]) for i in range(2)]
        h3_ = [self.sb(f"h3{i}", [128, KC, TT], BF16) for i in range(2)]
        srcv = self.xs.t.rearrange("(k p) t -> p k t", p=128)
        dst = self.out if last else self.xs
        dstv = dst.t.rearrange("(k p) t -> p k t", p=128)
        pv = self.pT.t[l].rearrange("(k p) t -> p k t", p=128)

        def load_at(tt):
            t0 = tt * TT
            at = at_[tt % 2]
            s.dma("sp", at[:], self.actT.t[:, t0:t0 + TT].rearrange("(k p) t -> p k t", p=128),
                  r=[self.actT.R((j, t0)) for j in range(NJ)], w=[at.R()])

        def down(tt):
            t0 = tt * TT
            at, pp, x2, h3 = at_[tt % 2], pt_[tt % 2], x2_[tt % 2], h3_[tt % 2]
            if tt + 1 < self.NT:
                load_at(tt + 1)
            s.dma("pool", pp[:], pv[:, :, t0:t0 + TT], r=[self.pT.R()], w=[pp.R()])
            for kc in range(KC):
                s.dma("sp", x2[:, kc, :], srcv[:, kc, t0:t0 + TT], r=[self.xs.R(("tile", tt))], w=[x2.R()])
            for n in range(KC):
                ns = slice(n * 128, (n + 1) * 128)
                pd = self.ps()
                for j in range(NJ):
                    s.op("pe", lambda e, j=j: e.matmul(pd[:, :], lhsT=wdn[:, j, ns], rhs=at[:, j, :], start=(j == 0),
                                                       stop=(j == NJ - 1)), r=[self.wres(wdn, n * 128), at.R()], w=[pd.R()])
                s.op("dve", lambda e: e.tensor_tensor(out=x2[:, n, :], in0=pd[:, :], in1=x2[:, n, :], op=ALU.add),
                     r=[pd.R(), x2.R()], w=[x2.R()])
            self.rmsnorm_tile(l, "g_ple", x2, h3, "F")

        def ple(tt):
            t0 = tt * TT
            pp, x2, h3 = pt_[tt % 2], x2_[tt % 2], h3_[tt % 2]
            for n in range(KC):
                ns = slice(n * 128, (n + 1) * 128)
                pg = self.ps()
                for kc in range(KC):
                    s.op("pe", lambda e, kc=kc: e.matmul(pg[:, :], lhsT=wpg[:, kc, ns], rhs=h3[:, kc, :], start=(kc == 0),
                                                         stop=(kc == KC - 1)), r=[self.wres(wpg, n * 128), h3.R(kc)], w=[pg.R()])
                sgt = self.tmpA("sgt", [128, TT])
                s.op("act", lambda e, sgt=sgt: e.activation(out=sgt[:], in_=pg[:, :], func=AF.Sigmoid), r=[pg.R()], w=[sgt.R()])
                pq = self.ps()
                for kc in range(2):
                    s.op("pe", lambda e, kc=kc: e.matmul(pq[:, :], lhsT=wpu[:, kc, ns], rhs=pp[:, kc, :], start=(kc == 0),
                                                         stop=(kc == 1)), r=[self.wres(wpu, n * 128), pp.R()], w=[pq.R()])
                s.op("dve", lambda e, sgt=sgt: e.tensor_tensor(out=sgt[:], in0=pq[:, :], in1=sgt[:], op=ALU.mult),
                     r=[pq.R(), sgt.R()], w=[sgt.R()])
                xo = self.tmpA("xo2", [128, TT], nbuf=2)
                s.op("pool", lambda e, sgt=sgt, xo=xo: e.tensor_tensor(out=xo[:], in0=sgt[:], in1=x2[:, n, :], op=ALU.add),
                     r=[sgt.R(), x2.R()], w=[xo.R()])
                s.dma("sp", dstv[:, n, t0:t0 + TT], xo[:], r=[xo.R()], w=[dst.R(("tile", tt))])

        load_at(0)
        down(0)
        for tt in range(self.NT):
            if tt + 1 < self.NT:
                down(tt + 1)
            ple(tt)

    def tmpA(self, name, shape, dt=F32, nbuf=2):
        key = ("tmp", name)
        if key not in self.ep:
            self.ep[key] = [[self.sb(f"tA_{name}{i}", shape, dt) for i in range(nbuf)], 0]
        lst = self.ep[key]
        t = lst[0][lst[1] % nbuf]
        lst[1] += 1
        return t

    def epi_qk(self, l, kind, idx, ps, t0):
        s = self.s
        sq = self.tmpA("qsq", [128, TT], BF16)
        s.op("act", lambda e: e.activation(out=sq[:], in_=ps[:, :], func=AF.Square), r=[ps.R()], w=[sq.R()])
        self.deferred.append(lambda: self.epi_qk2(l, kind, idx, ps, t0, sq))

    def epi_qk2(self, l, kind, idx, ps, t0, sq):
        s = self.s
        ps2 = self.ps()
        s.op("pe", lambda e: e.matmul(ps2[:, :], lhsT=self.c["blk2_f_b"][:], rhs=sq[:], start=True, stop=True),
             r=[self.c["blk2_f_b"].R(), sq.R()], w=[ps2.R()])
        rs2 = self.tmpA("qrs2", [128, TT], nbuf=1)
        self.rsqrt_ps(rs2, ps2, 1.0 / 64, RMS_EPS, 0.125 if kind == "q" else 1.0)
        o = self.tmpA("qo", [128, TT], BF16)
        gname = "g_q" if kind == "q" else "g_k"
        s.op("dve", lambda e: e.scalar_tensor_tensor(out=o[:], in0=ps[:, :], scalar=self.col(l, gname), in1=rs2[:],
                                                      op0=ALU.mult, op1=ALU.mult),
             r=[ps.R(), rs2.R(), self.colpack.R()], w=[o.R()])
        dst = self.qa if kind == "q" else self.ka
        for hh in range(2):
            h = idx * 2 + hh
            s.dma("sp", dst[h, 0:64, t0:t0 + TT], o[hh * 64:(hh + 1) * 64, :], r=[o.R()], w=[dst.R(("qk", h, t0))])

    def epi_f(self, l, ps, t0, seq_start):
        s = self.s
        e1 = self.tmpA("fe", [8, TT], nbuf=1)
        s.op("act", lambda e: e.activation(out=e1[:], in_=ps[0:8, :], func=AF.Exp, bias=self.negb[:, l:l + 1], scale=-1.0),
             r=[ps.R(), self.negb.R()], w=[e1.R()])
        l1 = e1
        one = self.c["ones_f"]
        s.op("act", lambda e: e.activation(out=l1[:], in_=e1[:], func=AF.Ln, bias=one[0:8, 0:1], scale=1.0),
             r=[e1.R(), one.R()], w=[l1.R()])
        c = self.tmpA("fc", [8, TT], nbuf=1)
        if seq_start:
            init = 0.0
            rr = []
        else:
            init = self.ccar[:, 0:1]
            rr = [self.ccar.R()]
        s.op("dve", lambda e: e.tensor_tensor_scan(out=c[:], data0=self.c["ones_f"][0:8, :], data1=l1[:], initial=init,
                                                    op0=ALU.mult, op1=ALU.subtract),
             r=[self.c["ones_f"].R(), l1.R()] + rr, w=[c.R()])
        s.op("dve", lambda e: e.tensor_copy(out=self.ccar[:, 0:1], in_=c[:, TT - 1:TT]), r=[c.R()], w=[self.ccar.R()])
        hi = self.tmpA("fhi", [8, TT], BF16, nbuf=1)
        s.op("dve", lambda e: e.tensor_copy(out=hi[:], in_=c[:]), r=[c.R()], w=[hi.R()])
        r1 = self.tmpA("fr1", [8, TT], nbuf=1)
        s.op("dve", lambda e: e.tensor_tensor(out=r1[:], in0=c[:], in1=hi[:], op=ALU.subtract), r=[c.R(), hi.R()],
             w=[r1.R()])
        mid = self.tmpA("fmid", [8, TT], BF16, nbuf=1)
        s.op("dve", lambda e: e.tensor_copy(out=mid[:], in_=r1[:]), r=[r1.R()], w=[mid.R()])
        r2 = self.tmpA("fe", [8, TT], nbuf=1)
        s.op("dve", lambda e: e.tensor_tensor(out=r2[:], in0=r1[:], in1=mid[:], op=ALU.subtract), r=[r1.R(), mid.R()],
             w=[r2.R()])
        lo = self.tmpA("flo", [8, TT], BF16, nbuf=1)
        s.op("dve", lambda e: e.tensor_copy(out=lo[:], in_=r2[:]), r=[r2.R()], w=[lo.R()])
        for j, part in enumerate([hi, mid, lo]):
            s.dma("sp", self.qa.t[:, 64 + j, t0:t0 + TT], part[:], r=[part.R()], w=[self.qa.R(("c", j, t0))])
            ng = self.tmpA("fng", [8, TT], BF16, nbuf=1)
            s.op("act", lambda e, ng=ng, part=part: e.activation(out=ng[:], in_=part[:], func=AF.Copy, scale=-1.0),
                 r=[part.R()], w=[ng.R()])
            s.dma("sp", self.ka.t[:, 67 + j, t0:t0 + TT], ng[:], r=[ng.R()], w=[self.ka.R(("c", j, t0))])

    def epi_rw(self, l, idx, ps, t0, tt, seq_start, vd):
        s = self.s
        zb = self.zraw[idx % 2]
        s.op("act", lambda e: e.activation(out=zb[:, 1:TT + 1], in_=ps[:, :], func=AF.Copy), r=[ps.R()], w=[zb.R()])
        if seq_start:
            s.op("pool", lambda e: e.memset(zb[:, 0:1], 0.0), w=[zb.R()])
        else:
            s.op("pool", lambda e: e.tensor_copy(out=zb[:, 0:1], in_=self.carry[:, idx:idx + 1]),
                 r=[self.carry.R(idx)], w=[zb.R()])
        s.op("pool", lambda e: e.tensor_copy(out=self.carry[:, idx:idx + 1], in_=zb[:, TT:TT + 1]), r=[zb.R()],
             w=[self.carry.R(idx)])
        d = self.tmpA("rwd", [128, TT])
        s.op("dve", lambda e: e.tensor_tensor(out=d[:], in0=zb[:, 0:TT], in1=zb[:, 1:TT + 1], op=ALU.subtract),
             r=[zb.R()], w=[d.R()])
        o = self.tmpA("rwo", [128, TT], nbuf=2)
        s.op("dve", lambda e: e.scalar_tensor_tensor(out=o[:], in0=d[:], scalar=self.col(l, "mu", idx), in1=zb[:, 1:TT + 1],
                                                      op0=ALU.mult, op1=ALU.add),
             r=[d.R(), zb.R(), self.colpack.R()], w=[o.R()])
        if 8 <= idx < 12:
            vi = idx - 8
            if l == 0:
                s.dma("sp", self.vf.t[vi * 128:(vi + 1) * 128, t0:t0 + TT], o[:], r=[o.R()], w=[self.vf.R((vi, t0))])
            else:
                ps2 = self.ps()
                s.op("pe", lambda e: e.matmul(ps2[:, :], lhsT=self.wvu[:, vi * 128:(vi + 1) * 128], rhs=vd[:], start=True,
                                              stop=True), r=[self.wvu.R(), vd.R()], w=[ps2.R()])
                vm = self.tmpA("vm", [128, TT], nbuf=1)
                s.op("act", lambda e: e.activation(out=vm[:], in_=ps2[:, :], func=AF.Sigmoid, bias=self.col(l, "v0", vi),
                                                   scale=1.0), r=[ps2.R(), self.colpack.R()], w=[vm.R()])
                vfl = self.tmpA("vfl", [128, TT], nbuf=1)
                s.dma("sp", vfl[:], self.vf.t[vi * 128:(vi + 1) * 128, t0:t0 + TT], r=[self.vf.R((vi, t0))], w=[vfl.R()])
                dd = self.tmpA("vdd", [128, TT], nbuf=1)
                s.op("pool", lambda e: e.tensor_tensor(out=dd[:], in0=vfl[:], in1=o[:], op=ALU.subtract),
                     r=[vfl.R(), o.R()], w=[dd.R()])
                s.op("pool", lambda e: e.tensor_tensor(out=dd[:], in0=dd[:], in1=vm[:], op=ALU.mult), r=[dd.R(), vm.R()],
                     w=[dd.R()])
                o2 = self.tmpA("rwo2", [128, TT], nbuf=1)
                s.op("dve", lambda e: e.tensor_tensor(out=o2[:], in0=o[:], in1=dd[:], op=ALU.add), r=[o.R(), dd.R()],
                     w=[o2.R()])
                o = o2
        s.dma("sp", self.zr.t[idx * 128:(idx + 1) * 128, t0:t0 + TT], o[:], r=[o.R()], w=[self.zr.R((idx, t0))])

    def epi_gate(self, l, idx, ps, t0):
        s = self.s
        o = self.tmpA("go", [128, TT], BF16, nbuf=3)
        s.op("act", lambda e: e.activation(out=o[:], in_=ps[:, :], func=AF.Sigmoid), r=[ps.R()], w=[o.R()])
        s.dma("sp", self.gt.t[idx * 128:(idx + 1) * 128, t0:t0 + TT], o[:], r=[o.R()], w=[self.gt.R((idx, t0))])


def host_inputs(inp, S, NB, depth, core):
    b0 = core * NB
    x = np.asarray(inp["x"], np.float32)[b0:b0 + NB].reshape(NB * S, D)
    p = np.asarray(inp["p"], np.float32)[:depth, b0:b0 + NB].reshape(depth, NB * S, PLE)
    m = {"xT": np.ascontiguousarray(x.T), "pT": np.ascontiguousarray(p.transpose(0, 2, 1))}
    return m


def shared_inputs(inp, depth):
    m = {}
    for name in ["w_in", "w_decay_up", "w_aaa_up", "w_gate_up", "w_o_fox", "w_o_rwkv", "w_out", "w_up", "w_down",
                 "w_ple_gate", "w_ple_up"]:
        m[name] = np.ascontiguousarray(np.asarray(inp[name], np.float32)[:depth])
    for name in ["w_vres_down", "w_vres_up"]:
        m[name] = np.ascontiguousarray(np.asarray(inp[name], np.float32)[:max(depth - 1, 1)])
    m["colpack"] = make_colpack({k: np.asarray(v, np.float32) for k, v in inp.items()}, depth)
    for k, v in make_consts().items():
        m["c_" + k] = v
    return m


_PROG_CACHE = {}


def kernel(**inp):
    S, NBT, depth = 2048, 16, 4
    ncores = 8
    NB = NBT // ncores
    key = (S, NB, depth)
    if key not in _PROG_CACHE:
        _PROG_CACHE[key] = Prog(S, NB, depth)
    prog = _PROG_CACHE[key]
    sh = shared_inputs(inp, depth)
    in_maps = []
    for c in range(ncores):
        m = dict(sh)
        m.update(host_inputs(inp, S, NB, depth, c))
        in_maps.append(m)
    res = run_bass_kernel_spmd(prog.nc, in_maps, core_ids=list(range(ncores)))
    outs = []
    for c in range(ncores):
        o = res.results[c]["outT"]
        outs.append(np.ascontiguousarray(o.T).reshape(NB, S, D))
    return np.concatenate(outs, axis=0).astype(np.float32)
```
